# Optimizing a Trainium2 kernel written in Bass

```python
import math
import numpy as np
import jax
import jax.numpy as jnp
from jax import lax

D_MODEL = 1024
BATCH = 8
SEQ = 2048
DEPTH = 4

GRID_W = 64
CTX_LEN = 256
Q_BLOCK = 128
ROPE_THETA = 10000.0
NORM_EPS = 1e-6
NEG_INF = -1e30

MLA_HEADS = 4
MLA_NOPE = 64
MLA_ROPE = 32
MLA_V = 64
MLA_Q_RANK = 256
MLA_KV_RANK = 128
DIFF_HEADS = 4
DIFF_D = 32
NA_HEADS = 4
NA_D = 64
NA_ROWS = 8
NA_COLS = 16
NA_QCOLS = 16
NA_BAND = 2 * NA_COLS
GQA_HEADS = 4
GQA_KV_HEADS = 2
GQA_D = 64
D_FF = 2816
CONV_W = 3

MIX_SPLITS = (
    MLA_Q_RANK, MLA_KV_RANK, MLA_ROPE,
    DIFF_HEADS * 2 * DIFF_D, DIFF_HEADS * 2 * DIFF_D, DIFF_HEADS * 2 * DIFF_D,
    NA_HEADS * NA_D, NA_HEADS * NA_D, NA_HEADS * NA_D,
    GQA_HEADS * GQA_D, GQA_KV_HEADS * GQA_D, GQA_KV_HEADS * GQA_D,
)
IN_COLS = sum(MIX_SPLITS)
MIX_WIDTH = MLA_HEADS * MLA_V + DIFF_HEADS * 2 * DIFF_D + NA_HEADS * NA_D + GQA_HEADS * GQA_D

kernel_name = 'hybrid_parallel_head_dit_block'


def rmsnorm(x, g):
    xf = x.astype(jnp.float32)
    y = xf * lax.rsqrt(jnp.mean(xf * xf, axis=-1, keepdims=True) + NORM_EPS)
    return (y * g.astype(jnp.float32)).astype(x.dtype)


def modulate(x, g, shift, scale):
    return rmsnorm(x, g) * (1.0 + scale) + shift


def split_cols(p, sizes):
    idx = np.cumsum(np.array(sizes))[:-1].tolist()
    return jnp.split(p, idx, axis=-1)


def to_heads(t, h):
    b, n, _ = t.shape
    return t.reshape(b, n, h, -1).transpose(0, 2, 1, 3)


def from_heads(t):
    b, h, n, d = t.shape
    return t.transpose(0, 2, 1, 3).reshape(b, n, h * d)


def axial_rope_tables(n_tok, rot_dim):
    n_freq = rot_dim // 4
    inv = jnp.power(ROPE_THETA, -jnp.arange(n_freq, dtype=jnp.float32) / n_freq)
    t = jnp.arange(n_tok)
    row = (t // GRID_W).astype(jnp.float32)
    col = (t % GRID_W).astype(jnp.float32)
    ar = row[:, None] * inv
    ac = col[:, None] * inv
    ang = jnp.concatenate([ar, ar, ac, ac], axis=-1)
    return jnp.cos(ang), jnp.sin(ang)


def apply_axial_rope(x, cos, sin):
    a1, a2, b1, b2 = jnp.split(x, 4, axis=-1)
    rot = jnp.concatenate([-a2, a1, -b2, b1], axis=-1)
    return x * cos.astype(x.dtype) + rot * sin.astype(x.dtype)


def attend(q, k, v, scale):
    s = jnp.matmul(q, jnp.swapaxes(k, -1, -2)).astype(jnp.float32) * scale
    p = jax.nn.softmax(s, axis=-1)
    return jnp.matmul(p.astype(v.dtype), v)


def sweep_query_blocks(fn, *qs):
    n_tok = qs[0].shape[-2]
    nb = n_tok // Q_BLOCK

    def to_blocks(q):
        q = q.reshape(q.shape[:-2] + (nb, Q_BLOCK, q.shape[-1]))
        return jnp.moveaxis(q, -3, 0)

    out = lax.map(lambda blk: fn(*blk), tuple(to_blocks(q) for q in qs))
    out = jnp.moveaxis(out, 0, -3)
    return out.reshape(out.shape[:-3] + (n_tok, out.shape[-1]))


def mla_mixer(p_lat, p_ctx, q_a_g, w_uq, kv_a_g, w_ukv, q_g, k_g, cos, sin, with_ctx):
    scale = (MLA_NOPE + MLA_ROPE) ** -0.5

    def q_of(cq):
        q = to_heads(rmsnorm(cq, q_a_g) @ w_uq, MLA_HEADS)
        return rmsnorm(q, q_g)

    def kv_of(ckv, kr):
        kv = to_heads(rmsnorm(ckv, kv_a_g) @ w_ukv, MLA_HEADS)
        k_nope, v = kv[..., :MLA_NOPE], kv[..., MLA_NOPE:]
        kr = jnp.broadcast_to(kr[:, None], k_nope.shape[:-1] + (MLA_ROPE,))
        return rmsnorm(jnp.concatenate([k_nope, kr], axis=-1), k_g), v

    def rope_tail(t):
        return jnp.concatenate([t[..., :MLA_NOPE], apply_axial_rope(t[..., MLA_NOPE:], cos, sin)], axis=-1)

    cq_l, ckv_l, kr_l = p_lat
    cq_c, ckv_c, kr_c = p_ctx
    q_l = rope_tail(q_of(cq_l))
    k_l, v_l = kv_of(ckv_l, kr_l)
    k_l = rope_tail(k_l)
    k_c, v_c = kv_of(ckv_c, kr_c)
    k_all = jnp.concatenate([k_l, k_c], axis=2)
    v_all = jnp.concatenate([v_l, v_c], axis=2)
    o_l = sweep_query_blocks(lambda qb: attend(qb, k_all, v_all, scale), q_l)
    o_c = from_heads(attend(q_of(cq_c), k_c, v_c, scale)) if with_ctx else None
    return from_heads(o_l), o_c


def diff_mixer(p_lat, p_ctx, q_g, k_g, lq1, lk1, lq2, lk2, subln_g, lambda_init, cos, sin, with_ctx):
    scale = DIFF_D ** -0.5
    lq1f, lk1f = lq1.astype(jnp.float32), lk1.astype(jnp.float32)
    lq2f, lk2f = lq2.astype(jnp.float32), lk2.astype(jnp.float32)
    lam = jnp.exp(jnp.sum(lq1f * lk1f)) - jnp.exp(jnp.sum(lq2f * lk2f)) + lambda_init

    def pair(t, g):
        b, n, _ = t.shape
        t = rmsnorm(t.reshape(b, n, DIFF_HEADS, 2, DIFF_D), g).transpose(3, 0, 2, 1, 4)
        return t[0], t[1]

    def diff_attend(q1, q2, k1, k2, v):
        s1 = jnp.matmul(q1, jnp.swapaxes(k1, -1, -2)).astype(jnp.float32) * scale
        s2 = jnp.matmul(q2, jnp.swapaxes(k2, -1, -2)).astype(jnp.float32) * scale
        w = jax.nn.softmax(s1, axis=-1) - lam * jax.nn.softmax(s2, axis=-1)
        return jnp.matmul(w.astype(v.dtype), v)

    def finish(o):
        return from_heads(rmsnorm(o, subln_g) * (1.0 - lambda_init))

    q_l, k_l, v_l = p_lat
    q_c, k_c, v_c = p_ctx
    q1l, q2l = pair(q_l, q_g)
    k1l, k2l = pair(k_l, k_g)
    q1l, q2l = apply_axial_rope(q1l, cos, sin), apply_axial_rope(q2l, cos, sin)
    k1l, k2l = apply_axial_rope(k1l, cos, sin), apply_axial_rope(k2l, cos, sin)
    k1c, k2c = pair(k_c, k_g)
    vl = to_heads(v_l, DIFF_HEADS)
    vc = to_heads(v_c, DIFF_HEADS)
    k1a = jnp.concatenate([k1l, k1c], axis=2)
    k2a = jnp.concatenate([k2l, k2c], axis=2)
    va = jnp.concatenate([vl, vc], axis=2)
    o_l = sweep_query_blocks(lambda a, b: diff_attend(a, b, k1a, k2a, va), q1l, q2l)
    if with_ctx:
        q1c, q2c = pair(q_c, q_g)
        o_c = finish(diff_attend(q1c, q2c, k1c, k2c, vc))
    else:
        o_c = None
    return finish(o_l), o_c


def na_mixer(p_lat, p_ctx, q_g, k_g, rpb, with_ctx):
    scale = NA_D ** -0.5
    q_l, k_l, v_l = [to_heads(t, NA_HEADS) for t in p_lat]
    q_c, k_c, v_c = [to_heads(t, NA_HEADS) for t in p_ctx]
    q_l, k_l = rmsnorm(q_l, q_g), rmsnorm(k_l, k_g)
    q_c, k_c = rmsnorm(q_c, q_g), rmsnorm(k_c, k_g)
    b, h, n_tok, d = q_l.shape
    rows = n_tok // GRID_W
    wr = min(NA_ROWS, rows)
    ncb = GRID_W // NA_QCOLS
    n_loc = wr * NA_BAND

    r = jnp.arange(rows)
    row_start = jnp.clip(r - wr // 2, 0, rows - wr)
    row_idx = row_start[:, None] + jnp.arange(wr)
    qcol = jnp.arange(ncb)[:, None] * NA_QCOLS + jnp.arange(NA_QCOLS)
    band_start = jnp.clip(jnp.arange(ncb) * NA_QCOLS - NA_COLS // 2, 0, GRID_W - NA_BAND)
    col_idx = band_start[:, None] + jnp.arange(NA_BAND)
    win_start = jnp.clip(qcol - NA_COLS // 2, 0, GRID_W - NA_COLS)
    valid = (col_idx[:, None, :] >= win_start[..., None]) & (col_idx[:, None, :] < win_start[..., None] + NA_COLS)
    valid = jnp.broadcast_to(valid[:, :, None, :], (ncb, NA_QCOLS, wr, NA_BAND)).reshape(ncb, NA_QCOLS, n_loc)
    row_off = row_idx - r[:, None] + (NA_ROWS - 1)
    col_off = jnp.clip(col_idx[:, None, :] - qcol[..., None] + (NA_COLS - 1), 0, 2 * NA_COLS - 2)
    bias = rpb[:, row_off[:, None, None, :, None], col_off[None, :, :, None, :]]
    bias = bias.reshape(h, rows, ncb, NA_QCOLS, n_loc).astype(jnp.float32)

    def band(t):
        g = t.reshape(b, h, rows, GRID_W, d)[:, :, row_idx[:, None, :, None], col_idx[None, :, None, :]]
        return g.reshape(b, h, rows, ncb, n_loc, d)

    kb, vb = band(k_l), band(v_l)
    qb = q_l.reshape(b, h, rows, ncb, NA_QCOLS, d)
    s_loc = jnp.einsum('bhrjqd,bhrjkd->bhrjqk', qb, kb).astype(jnp.float32) * scale + bias[None]
    s_loc = jnp.where(valid, s_loc, NEG_INF)
    s_ctx = jnp.einsum('bhrjqd,bhcd->bhrjqc', qb, k_c).astype(jnp.float32) * scale
    p = jax.nn.softmax(jnp.concatenate([s_loc, s_ctx], axis=-1), axis=-1).astype(v_l.dtype)
    o = (jnp.einsum('bhrjqk,bhrjkd->bhrjqd', p[..., :n_loc], vb)
         + jnp.einsum('bhrjqc,bhcd->bhrjqd', p[..., n_loc:], v_c))
    o_l = from_heads(o.reshape(b, h, n_tok, d))
    o_c = from_heads(attend(q_c, k_c, v_c, scale)) if with_ctx else None
    return o_l, o_c


def gqa_mixer(p_lat, p_ctx, q_g, k_g, cos, sin, with_ctx):
    scale = GQA_D ** -0.5
    rep = GQA_HEADS // GQA_KV_HEADS

    def q_of(t):
        b, n, _ = t.shape
        q = rmsnorm(t.reshape(b, n, GQA_KV_HEADS, rep, GQA_D), q_g)
        return q.transpose(0, 2, 3, 1, 4)

    def k_of(t):
        return rmsnorm(to_heads(t, GQA_KV_HEADS), k_g)

    q_l, k_l, v_l = p_lat
    q_c, k_c, v_c = p_ctx
    ql = apply_axial_rope(q_of(q_l), cos, sin)
    kl = apply_axial_rope(k_of(k_l), cos, sin)[:, :, None]
    kc = k_of(k_c)[:, :, None]
    vl = to_heads(v_l, GQA_KV_HEADS)[:, :, None]
    vc = to_heads(v_c, GQA_KV_HEADS)[:, :, None]
    k_all = jnp.concatenate([kl, kc], axis=-2)
    v_all = jnp.concatenate([vl, vc], axis=-2)
    o = sweep_query_blocks(lambda qb: attend(qb, k_all, v_all, scale), ql)
    b, _, _, n_tok, d = o.shape
    o_l = from_heads(o.reshape(b, GQA_HEADS, n_tok, d))
    if with_ctx:
        oc = attend(q_of(q_c), kc, vc, scale)
        o_c = from_heads(oc.reshape(oc.shape[0], GQA_HEADS, oc.shape[3], d))
    else:
        o_c = None
    return o_l, o_c


def conv_ffn(h, w_up, conv_w, conv_b, w_down):
    u = h @ w_up
    n_tok = u.shape[1]
    pad = CONV_W // 2
    up = jnp.pad(u, ((0, 0), (pad, pad), (0, 0)))
    u = sum(up[:, i:i + n_tok] * conv_w[i] for i in range(CONV_W)) + conv_b
    a, g = jnp.split(u, 2, axis=-1)
    return (jax.nn.silu(g) * a) @ w_down


def setup_inputs(seed: int = 0) -> dict:
    key = jax.random.key(seed)
    ks = iter(jax.random.split(key, 40))
    L, D = DEPTH, D_MODEL

    def nrm(shape, s):
        return jax.random.normal(next(ks), shape, jnp.float32) * s

    def gain(shape):
        return 1.0 + nrm(shape, 0.05)

    return {
        'x': nrm((BATCH, SEQ, D), 1.0),
        'c': nrm((BATCH, D), 1.0),
        'ctx': nrm((BATCH, CTX_LEN, D), 1.0),
        'c_ctx': nrm((D,), 1.0),
        'w_mod': nrm((L, D, 6 * D), 0.5 * D ** -0.5),
        'b_mod': nrm((L, 6 * D), 0.01),
        'g_mix': gain((L, D)),
        'w_in': nrm((L, D, IN_COLS), D ** -0.5),
        'w_out': nrm((L, MIX_WIDTH, D), MIX_WIDTH ** -0.5),
        'mla_q_a_g': gain((L, MLA_Q_RANK)),
        'mla_w_uq': nrm((L, MLA_Q_RANK, MLA_HEADS * (MLA_NOPE + MLA_ROPE)), MLA_Q_RANK ** -0.5),
        'mla_kv_a_g': gain((L, MLA_KV_RANK)),
        'mla_w_ukv': nrm((L, MLA_KV_RANK, MLA_HEADS * (MLA_NOPE + MLA_V)), MLA_KV_RANK ** -0.5),
        'mla_q_g': gain((L, MLA_NOPE + MLA_ROPE)),
        'mla_k_g': gain((L, MLA_NOPE + MLA_ROPE)),
        'diff_q_g': gain((L, DIFF_D)),
        'diff_k_g': gain((L, DIFF_D)),
        'diff_lq1': nrm((L, DIFF_D), 0.1),
        'diff_lk1': nrm((L, DIFF_D), 0.1),
        'diff_lq2': nrm((L, DIFF_D), 0.1),
        'diff_lk2': nrm((L, DIFF_D), 0.1),
        'diff_subln_g': gain((L, 2 * DIFF_D)),
        'na_q_g': gain((L, NA_D)),
        'na_k_g': gain((L, NA_D)),
        'na_rpb': nrm((L, NA_HEADS, 2 * NA_ROWS - 1, 2 * NA_COLS - 1), 0.1),
        'gqa_q_g': gain((L, GQA_D)),
        'gqa_k_g': gain((L, GQA_D)),
        'g_ffn': gain((L, D)),
        'w_up': nrm((L, D, 2 * D_FF), D ** -0.5),
        'conv_w': nrm((L, CONV_W, 2 * D_FF), CONV_W ** -0.5),
        'conv_b': nrm((L, 2 * D_FF), 0.01),
        'w_down': nrm((L, D_FF, D), D_FF ** -0.5),
    }


def reference(x, c, ctx, c_ctx, w_mod, b_mod, g_mix, w_in, w_out,
              mla_q_a_g, mla_w_uq, mla_kv_a_g, mla_w_ukv, mla_q_g, mla_k_g,
              diff_q_g, diff_k_g, diff_lq1, diff_lk1, diff_lq2, diff_lk2, diff_subln_g,
              na_q_g, na_k_g, na_rpb, gqa_q_g, gqa_k_g,
              g_ffn, w_up, conv_w, conv_b, w_down):
    n_tok = x.shape[1]
    cos_a, sin_a = axial_rope_tables(n_tok, MLA_ROPE)
    cos_b, sin_b = axial_rope_tables(n_tok, DIFF_D)
    cos_d, sin_d = axial_rope_tables(n_tok, GQA_D)
    sc_lat = jax.nn.silu(c)
    sc_ctx = jax.nn.silu(c_ctx)
    for l in range(DEPTH):
        with_ctx = l < DEPTH - 1
        lambda_init = 0.8 - 0.6 * math.exp(-0.3 * l)
        m_lat = (sc_lat @ w_mod[l] + b_mod[l])[:, None, :]
        m_ctx = sc_ctx @ w_mod[l] + b_mod[l]
        sh1, s1, g1, sh2, s2, g2 = jnp.split(m_lat, 6, axis=-1)
        csh1, cs1, cg1, csh2, cs2, cg2 = jnp.split(m_ctx, 6, axis=-1)

        p_lat = split_cols(modulate(x, g_mix[l], sh1, s1) @ w_in[l], MIX_SPLITS)
        p_ctx = split_cols(modulate(ctx, g_mix[l], csh1, cs1) @ w_in[l], MIX_SPLITS)

        oa = mla_mixer(p_lat[0:3], p_ctx[0:3], mla_q_a_g[l], mla_w_uq[l], mla_kv_a_g[l], mla_w_ukv[l],
                       mla_q_g[l], mla_k_g[l], cos_a, sin_a, with_ctx)
        ob = diff_mixer(p_lat[3:6], p_ctx[3:6], diff_q_g[l], diff_k_g[l], diff_lq1[l], diff_lk1[l],
                        diff_lq2[l], diff_lk2[l], diff_subln_g[l], lambda_init, cos_b, sin_b, with_ctx)
        oc = na_mixer(p_lat[6:9], p_ctx[6:9], na_q_g[l], na_k_g[l], na_rpb[l], with_ctx)
        od = gqa_mixer(p_lat[9:12], p_ctx[9:12], gqa_q_g[l], gqa_k_g[l], cos_d, sin_d, with_ctx)

        mix_lat = jnp.concatenate([oa[0], ob[0], oc[0], od[0]], axis=-1) @ w_out[l]
        x = x + g1 * mix_lat
        x = x + g2 * conv_ffn(modulate(x, g_ffn[l], sh2, s2), w_up[l], conv_w[l], conv_b[l], w_down[l])
        if with_ctx:
            mix_ctx = jnp.concatenate([oa[1], ob[1], oc[1], od[1]], axis=-1) @ w_out[l]
            ctx = ctx + cg1 * mix_ctx
            ctx = ctx + cg2 * conv_ffn(modulate(ctx, g_ffn[l], csh2, cs2), w_up[l], conv_w[l], conv_b[l], w_down[l])
    return x
```

```python
import math
import numpy as np
import concourse.bass as bass
import concourse.mybir as mybir
from concourse.bass_utils import run_bass_kernel_spmd

F32 = mybir.dt.float32
BF16 = mybir.dt.bfloat16
AF = mybir.ActivationFunctionType
ALU = mybir.AluOpType
AX = mybir.AxisListType

ENGS = ['pe', 'act', 'dve', 'pool', 'sp']
CELL = 256
_ESZ = {}


def esz(dt):
    if dt not in _ESZ:
        _ESZ[dt] = mybir.dt.size(dt)
    return _ESZ[dt]


def ap_cells(ap):
    space = str(ap.space)
    sp = 0 if space == 'SB' else 1
    dims = ap.ap
    pstep, pcount = dims[0]
    e = esz(ap.dtype)
    off = ap.offset
    p0 = off // pstep
    foff = off % pstep
    ranges = [(foff, foff + 1)]
    for (st, cnt) in dims[1:]:
        if cnt <= 1:
            continue
        if len(ranges) * cnt <= 512 and abs(st) * e >= CELL:
            ranges = [(lo + i * st, hi + i * st) for (lo, hi) in ranges for i in range(cnt)]
        else:
            ext = (cnt - 1) * st
            if ext >= 0:
                ranges = [(lo, hi + ext) for (lo, hi) in ranges]
            else:
                ranges = [(lo + ext, hi) for (lo, hi) in ranges]
    cs = set()
    for (lo, hi) in ranges:
        c0 = (lo * e) // CELL
        c1 = (hi * e - 1) // CELL
        for c in range(c0, c1 + 1):
            cs.add(c)
    if sp == 1:
        return sorted(set(4 * 4096 + (c * CELL) // 2048 for c in cs))
    q0 = p0 // 32
    q1 = (p0 + pcount - 1) // 32
    out = []
    for q in range(q0, q1 + 1):
        base = (sp * 4 + q) * 4096
        for c in cs:
            out.append(base + c)
    return out


class Sched:
    def __init__(self, nc, n_lanes=48, same_eng_sync=True):
        self.nc = nc
        self.ops = {e: [] for e in ENGS}
        self.cw = {}
        self.cr = {}
        self.known = {e: {} for e in ENGS}
        self.snap = {}
        self.n_lanes = n_lanes
        self.lane_count = [0] * n_lanes
        self.next_lane = 0
        self.same_eng_sync = same_eng_sync
        self.eng_sem = {e: nc.alloc_semaphore("sem_" + e) for e in ENGS}
        self.lane_sem = [nc.alloc_semaphore("lane%d" % i) for i in range(n_lanes)]

    def _cells(self, items):
        cs = []
        for it in items:
            if it is None:
                continue
            if isinstance(it, (str, tuple)):
                cs.append(it)
            else:
                cs.extend(ap_cells(it))
        return cs

    def add(self, eng, fn, reads=(), writes=(), dma=False):
        ops = self.ops[eng]
        idx = len(ops)
        rc = self._cells(reads)
        wc = self._cells(writes)
        need = {}

        def want(tok, war=False):
            key, seq = tok
            if key == eng:
                if eng == 'pe' or war or not self.same_eng_sync:
                    return
            if need.get(key, -1) < seq:
                need[key] = seq

        for c in rc:
            t = self.cw.get(c)
            if t is not None:
                want(t)
        for c in wc:
            t = self.cw.get(c)
            if t is not None:
                want(t)
            rs = self.cr.get(c)
            if rs:
                for k, s in rs.items():
                    want((k, s), war=True)
        if dma:
            lane = self.next_lane
            self.next_lane = (lane + 1) % self.n_lanes
            cnt = self.lane_count[lane] + 1
            self.lane_count[lane] = cnt
            tok = (('L', lane), cnt)
            if cnt > 1:
                want((('L', lane), cnt - 1))
        else:
            tok = (eng, idx)
        kn = self.known[eng]
        waits = []
        for key, seq in need.items():
            if kn.get(key, -1) >= seq:
                continue
            waits.append((key, seq))
            kn[key] = seq
            if isinstance(key, str):
                self.ops[key][seq]['signal'] = True
                sn = self.snap.get((key, seq))
                if sn:
                    for k2, s2 in sn.items():
                        if kn.get(k2, -1) < s2:
                            kn[k2] = s2
        if not dma:
            self.snap[(eng, idx)] = dict(kn)
        ops.append(dict(fn=fn, waits=waits, tok=tok, dma=dma, signal=False))
        for c in wc:
            self.cw[c] = tok
            self.cr[c] = {}
        for c in rc:
            d = self.cr.get(c)
            if d is None:
                d = {}
                self.cr[c] = d
            if d.get(tok[0], -1) < tok[1]:
                d[tok[0]] = tok[1]
        return tok

    def emit(self):
        nc = self.nc
        for e in ENGS:
            c = 0
            for op in self.ops[e]:
                if op['signal'] and not op['dma']:
                    c += 1
                    op['sigval'] = c
        emap = {'pe': 'tensor', 'act': 'scalar', 'dve': 'vector', 'pool': 'gpsimd', 'sp': 'sync'}

        def run(e, E):
            for op in self.ops[e]:
                for key, seq in op['waits']:
                    if isinstance(key, str):
                        E.wait_ge(self.eng_sem[key], self.ops[key][seq]['sigval'])
                    else:
                        E.wait_ge(self.lane_sem[key[1]], 16 * seq)
                inst = op['fn'](E)
                if op['dma']:
                    inst.then_inc(self.lane_sem[op['tok'][0][1]], 16)
                elif op['signal']:
                    inst.then_inc(self.eng_sem[e], 1)
            if e == 'sp':
                for i, cnt in enumerate(self.lane_count):
                    if cnt:
                        E.wait_ge(self.lane_sem[i], 16 * cnt)

        with nc.Block() as block:
            for e in ENGS:
                getattr(block, emap[e])(lambda E, e=e: run(e, E))

    def stats(self):
        return {e: (len(self.ops[e]), sum(len(o['waits']) for o in self.ops[e])) for e in ENGS}


class Ring:
    def __init__(self, items):
        self.items = list(items)
        self.i = 0

    def next(self):
        r = self.items[self.i]
        self.i = (self.i + 1) % len(self.items)
        return r


D = 1024
L = 4
TL = 2048
TC = 256
T = TL + TC
NKT = T // 128
TBS = [(0, 512), (512, 512), (1024, 512), (1536, 512), (2048, 256)]
IN_COLS = 2464
DFF = 2816
NCH = DFF // 128
EPS = 1e-6
BIG = 30000.0
NA_KT = {0: range(0, 6), 1: range(2, 10), 2: range(6, 14), 3: range(10, 16)}
NAW = 22 * 64
FFG = [(0, 6), (6, 6), (12, 6), (18, 4)]

PV = {}
_c = 0
for _n, _w in [('g_mix', 8), ('g_ffn', 8), ('qa_g', 2), ('kva_g', 1), ('mq_g', 1), ('mk_g', 1),
               ('dq_g', 1), ('dk_g', 1), ('dsub_g', 1), ('nq_g', 1), ('nk_g', 1), ('gq_g', 1), ('gk_g', 1),
               ('cw0', 44), ('cw1', 44), ('cw2', 44), ('cb', 44), ('lvec', 128)]:
    PV[_n] = (_c, _w)
    _c += _w
NV = _c


def lambda_init(l):
    return 0.8 - 0.6 * math.exp(-0.3 * l)


class StopBuild(Exception):
    pass


def build_program(n_layers=L, dbg=False, stop_after=None):
    nc = bass.Bass("TRN2", target_bir_lowering=False)

    def ck(name):
        if stop_after == name:
            raise StopBuild()

    def din(name, shape):
        return nc.dram_tensor(name, list(shape), F32, kind="ExternalInput").ap()

    xc_d = din("xc", [T, D])
    cT_d = din("cT", [128, 16])
    wmod_d = din("w_mod", [L, D, 6 * D])
    bmod_d = din("b_mod", [L, 6 * D])
    win_d = din("w_in", [L, D, IN_COLS])
    wout_d = din("w_out", [L, D, D])
    wuq_d = din("w_uq", [L, 256, 384])
    wukv_d = din("w_ukv", [L, 128, 512])
    wup_d = din("w_up", [L, D, 2 * DFF])
    wdn_d = din("w_down", [L, DFF, D])
    pv_d = din("pv", [128, L * NV])
    cf_d = din("cf32", [128, 256])
    cb_d = din("cbf", [128, 6 * 128])
    aug_d = din("aug", [2, 32, T])
    tabs_d = din("tabs", [3, 128, 2, TL])
    nab_d = din("nab", [L, 4, 128, NAW])
    y_d = nc.dram_tensor("y", [TL, D], F32, kind="ExternalOutput").ap()
    xT_d = nc.dram_tensor("xT_scr", [128, 8, T], F32).ap()
    if dbg:
        dbg_d = nc.dram_tensor("dbg", [128, 8, T], F32, kind="ExternalOutput").ap()

    ARENA_F32 = 53000
    arena = nc.alloc_sbuf_tensor("arena", [128, ARENA_F32], F32)
    psum = nc.alloc_psum_tensor("psum", [128, 4096], F32)
    S = Sched(nc)

    pos = [0]

    def alloc_b(nbytes):
        a = pos[0]
        n = (nbytes + 255) // 256 * 256
        pos[0] += n
        assert pos[0] <= ARENA_F32 * 4, pos[0]
        return a

    def f32v(boff, n):
        return arena[:, boff // 4: boff // 4 + n]

    def bf16v(boff, n):
        return arena[:, boff // 4: boff // 4 + (n + 1) // 2].bitcast(BF16)

    def PB(i):
        return psum[:, i * 512:(i + 1) * 512]

    ident = f32v(alloc_b(512), 128)
    ones_f = f32v(alloc_b(512), 128)
    cbf = bf16v(alloc_b(6 * 256), 6 * 128).rearrange("p (a b) -> p a b", a=6)
    allones, blk32, blk64, Rmla, Rdiff, Rgqa = [cbf[:, i, :] for i in range(6)]
    pv = f32v(alloc_b(L * NV * 4), L * NV).rearrange("p (l n) -> p l n", l=L)
    modT = f32v(alloc_b(L * 96 * 4), L * 96).rearrange("p (l s w) -> p l s w", l=L, s=48)
    cT = f32v(alloc_b(64), 16)
    scT_f = f32v(alloc_b(64), 16)
    scT = bf16v(alloc_b(32), 16).rearrange("p (a b) -> p a b", a=8)
    Gv = f32v(alloc_b(4 * 16 * 4), 64).rearrange("p (a j w) -> p a j w", a=2, j=8)
    misc = f32v(alloc_b(64 * 4), 64)
    hT = bf16v(alloc_b(8 * T * 2), 8 * T).rearrange("p (a t) -> p a t", a=8)
    mix_off = alloc_b(8 * T * 2)
    mixT = bf16v(mix_off, 8 * T).rearrange("p (a t) -> p a t", a=8)
    qkv_off = alloc_b(3 * 4 * T * 2)
    QT = bf16v(qkv_off, 4 * T).rearrange("p (a t) -> p a t", a=4)
    KT = bf16v(qkv_off + 4 * T * 2, 4 * T).rearrange("p (a t) -> p a t", a=4)
    VA = bf16v(qkv_off + 8 * T * 2, NKT * 4 * 128).rearrange("p (k h c) -> p k h c", k=NKT, h=4)
    wout_sb = bf16v(qkv_off, 8 * 1024).rearrange("p (a n) -> p a n", a=8)
    xblk = f32v(qkv_off + 16384, 8 * 512).rearrange("p (a n) -> p a n", a=8)
    xld = f32v(qkv_off + 32768, 4 * 1024).rearrange("p (a n) -> p a n", a=4)
    fo = mix_off
    actT = bf16v(fo, 6 * T).rearrange("p (a t) -> p a t", a=6); fo += 6 * T * 2
    wup_sb = bf16v(fo, 8 * 2 * 768).rearrange("p (k g n) -> p k g n", k=8, g=2); fo += 8 * 2 * 768 * 2
    wdn_sb = bf16v(fo, 6 * 1024).rearrange("p (a n) -> p a n", a=6); fo += 6 * 1024 * 2
    UW = T + 4
    UWP = (UW * 4 + 255) // 256 * 256
    ubuf = [f32v(fo + i * UWP, UW) for i in range(2)]; fo += 2 * UWP
    ctile = [f32v(fo + i * 2048, 512) for i in range(4)]; fo += 4 * 2048
    assert fo <= qkv_off + 3 * 4 * T * 2, (fo, qkv_off + 3 * 4 * T * 2)
    tab_off = alloc_b(2 * TL * 4)
    ropeT = f32v(tab_off, 2 * TL).rearrange("p (a t) -> p a t", a=2)
    natab = bf16v(tab_off, 4 * NAW).rearrange("p (h n) -> p h n", h=4)
    wA = bf16v(alloc_b(8 * 768 * 2), 8 * 768).rearrange("p (a n) -> p a n", a=8)
    wuq_sb = bf16v(alloc_b(2 * 384 * 2), 2 * 384).rearrange("p (a n) -> p a n", a=2)
    wukv_sb = bf16v(alloc_b(512 * 2), 512)
    ckvnT = bf16v(alloc_b(T * 2), T)
    cqn = bf16v(alloc_b(2 * 512 * 2), 2 * 512).rearrange("p (a n) -> p a n", a=2)
    tmp_off = alloc_b(24 * 1024)

    def tf(i):
        return f32v(tmp_off + i * 2048, 512)

    def tb16(i, half=0):
        return bf16v(tmp_off + i * 2048 + half * 1024, 512)

    kraw = Ring([tf(0), tf(1)])
    sqr = Ring([tb16(2, 0), tb16(2, 1)])
    lnr = Ring([tf(3), tf(4)])
    rsr = Ring([tf(5), tf(6)])
    qnr = Ring([tb16(7, 0), tb16(7, 1)])
    t1r = Ring([tf(8), tf(9)])
    t2r = Ring([tf(10), tf(11)])
    Pr = Ring([tb16(0, 0), tb16(0, 1), tb16(1, 0), tb16(1, 1)])
    P2r = Ring([tb16(2, 0), tb16(2, 1)])
    Osb = Ring([tf(3), tf(4)])
    zrow = Ring([tf(5), tf(6)])
    ABo = [tf(7), tf(8), tf(9)]
    xring = Ring([tf(8), tf(9), tf(10), tf(11)])
    nastg = f32v(tmp_off, NAW)
    wmr = Ring([bf16v(tmp_off + i * 4096, 2048) for i in range(3)])
    m_sb = f32v(qkv_off, 6 * D)
    bm_sb = f32v(qkv_off + 24576, 6 * D)

    def mm(out, lhsT, rhs, start=True, stop=True):
        S.add('pe', lambda E: E.matmul(out, lhsT, rhs, start=start, stop=stop), reads=[lhsT, rhs], writes=[out])

    def tr(out, in_, idn):
        S.add('pe', lambda E: E.transpose(out, in_, idn), reads=[in_, idn], writes=[out])

    def act(out, in_, func, scale=None, bias=None, eng='act'):
        kw = {}
        rd = [in_]
        if scale is not None:
            kw['scale'] = scale
            if not isinstance(scale, float):
                rd.append(scale)
        if bias is not None:
            kw['bias'] = bias
            if not isinstance(bias, float):
                rd.append(bias)
        S.add('act', lambda E: E.activation(out, in_, func, **kw), reads=rd, writes=[out])

    def stt(out, in0, scalar, in1, op0, op1):
        rd = [in0, in1] + ([] if isinstance(scalar, float) else [scalar])
        S.add('dve', lambda E: E.scalar_tensor_tensor(out=out, in0=in0, scalar=scalar, in1=in1, op0=op0, op1=op1),
              reads=rd, writes=[out])

    def tt(out, in0, in1, op, eng='dve'):
        S.add(eng, lambda E: E.tensor_tensor(out=out, in0=in0, in1=in1, op=op), reads=[in0, in1], writes=[out])

    def ts(out, in0, s1, s2, op0, op1=None, eng='dve'):
        rd = [in0] + [s for s in (s1, s2) if s is not None and not isinstance(s, float)]
        if op1 is None:
            S.add(eng, lambda E: E.tensor_scalar(out=out, in0=in0, scalar1=s1, scalar2=None, op0=op0), reads=rd, writes=[out])
        else:
            S.add(eng, lambda E: E.tensor_scalar(out=out, in0=in0, scalar1=s1, scalar2=s2, op0=op0, op1=op1), reads=rd, writes=[out])

    def cp(out, in_, eng='dve'):
        S.add(eng, lambda E: E.tensor_copy(out=out, in_=in_), reads=[in_], writes=[out])

    def vcopy(out, in_, bank, use_act):
        if use_act:
            S.add('act', lambda E: E.activation(out, in_, AF.Copy), reads=[bank], writes=[out])
        else:
            S.add('dve', lambda E: E.tensor_copy(out=out, in_=in_), reads=[bank], writes=[out])

    def mset(ap, val, eng='dve'):
        S.add(eng, lambda E: E.memset(ap, val), writes=[ap])

    def dma(q, out, in_, reads=None, writes=None):
        r = [in_] if reads is None else reads
        w = [out] if writes is None else writes
        r = [a for a in r if isinstance(a, (str, tuple)) or str(a.space) != 'DRAM']
        w = [a for a in w if isinstance(a, (str, tuple)) or str(a.space) != 'DRAM']
        S.add(q, lambda E: E.dma_start(out=out, in_=in_), reads=r, writes=w, dma=True)

    def xkey(tbi):
        return "x:%d" % tbi

    def xkeys(tbi):
        return ["x:%d" % tbi] + ["xf:%d:%d" % (tbi, j) for j in range(8)]

    try:
        import os as _os
        if _os.environ.get('SKIP_CONST'):
            raise StopBuild()
        dma('sp', ident, cf_d[:, 0:128])
        dma('sp', ones_f, cf_d[:, 128:256])
        dma('pool', cbf, cb_d.rearrange("p (a b) -> p a b", a=6))
        dma('sp', pv, pv_d.rearrange("p (l n) -> p l n", l=L))
        dma('sp', cT, cT_d)
        act(scT_f, cT, AF.Silu)
        cp(scT, scT_f.rearrange("p (a b) -> p a b", a=8))

        ck('c0')
        for tbi, (t0, n) in enumerate(TBS):
            ntt = n // 128
            dma('sp', xld[:, 0:ntt, :], xc_d[t0:t0 + n, :].rearrange("(a p) d -> p a d", p=128))
            for j in range(8):
                pb = PB(j % 2)
                for a in range(ntt):
                    tr(pb[:, a * 128:(a + 1) * 128], xld[:, a, j * 128:(j + 1) * 128], ident)
                if j % 2 == 0:
                    act(xblk[:, j, 0:n], pb[:, 0:n], AF.Copy)
                else:
                    cp(xblk[:, j, 0:n], pb[:, 0:n])
            dma('sp', xT_d[:, :, t0:t0 + n], xblk[:, :, 0:n], writes=xkeys(tbi))

        ck('xt')
        for l in range(n_layers):
            dma('sp', bm_sb[0:1, :], bmod_d[l:l + 1, :])
            dma('sp', bm_sb[1:2, :], bmod_d[l:l + 1, :])
            for cg in range(3):
                for kc in range(8):
                    piece = wmr.next()
                    dma('pool', piece, wmod_d[l, kc * 128:(kc + 1) * 128, cg * 2048:(cg + 1) * 2048])
                    for i in range(4):
                        mm(PB(i)[0:2, :], scT[:, kc, :], piece[:, i * 512:(i + 1) * 512], start=(kc == 0), stop=(kc == 7))
                for i in range(4):
                    c0 = cg * 2048 + i * 512
                    tt(m_sb[0:2, c0:c0 + 512], PB(i)[0:2, :], bm_sb[0:2, c0:c0 + 512], ALU.add)
            pt = PB(4)
            for s in range(48):
                tr(pt[:, 2 * s:2 * s + 2], m_sb[0:2, s * 128:(s + 1) * 128], ident[0:2, 0:2])
            cp(modT[:, l].rearrange("p s w -> p (s w)"), pt[:, 0:96])

        ck('mod')
        def rstd_from(src, R, N, ones_m, inv_d):
            sq = sqr.next()[0:R, 0:N]
            act(sq, src, AF.Square)
            ssp = ss_ring.next()[0:R, 0:N]
            mm(ssp, ones_m, sq)
            ln = lnr.next()[0:R, 0:N]
            act(ln, ssp, AF.Ln, scale=float(inv_d), bias=float(EPS))
            rs = rsr.next()[0:R, 0:N]
            act(rs, ln, AF.Exp, scale=-0.5)
            return rs

        def norm_rope(src, R, N, ones_m, inv_d, gain, out, rope=None):
            rs = rstd_from(src, R, N, ones_m, inv_d)
            if rope is None:
                stt(out, src, gain, rs, ALU.mult, ALU.mult)
                return
            Rm, cosT, sinT = rope
            qn = qnr.next()[0:R, 0:N]
            stt(qn, src, gain, rs, ALU.mult, ALU.mult)
            rp = rot_ring.next()[0:R, 0:N]
            mm(rp, Rm, qn)
            t1 = t1r.next()[0:R, 0:N]
            tt(t1, qn, cosT, ALU.mult, eng='pool')
            t2 = t2r.next()[0:R, 0:N]
            tt(t2, rp, sinT, ALU.mult)
            tt(out, t1, t2, ALU.add, eng='pool')

        def modulate(l, which, xsrc_loader):
            for tbi, (t0, n) in enumerate(TBS):
                w = 0 if tbi < 4 else 1
                xsrc_loader(tbi, t0, n)
                ssp = ss_ring.next()[:, 0:n]
                for j in range(8):
                    sq = sqr.next()[:, 0:n]
                    act(sq, xblk[:, j, 0:n], AF.Square)
                    mm(ssp, allones, sq, start=(j == 0), stop=(j == 7))
                ln = lnr.next()[:, 0:n]
                act(ln, ssp, AF.Ln, scale=1.0 / D, bias=float(EPS))
                rs = rsr.next()[:, 0:n]
                act(rs, ln, AF.Exp, scale=-0.5)
                for j in range(8):
                    t1 = t1r.next()[:, 0:n]
                    stt(t1, xblk[:, j, 0:n], Gv[:, which, j, w:w + 1], rs, ALU.mult, ALU.mult)
                    shift = modT[:, l, (3 * which) * 8 + j, w:w + 1]
                    act(hT[:, j, t0:t0 + n], t1, AF.Identity, bias=shift)

        def load_x_block(tbi, t0, n):
            dma('sp', xblk[:, :, 0:n], xT_d[:, :, t0:t0 + n], reads=xkeys(tbi))

        def attention(q_of, keytiles, N, scale, vrows):
            Op = o_ring.next()
            nk = len(keytiles)
            for i, (k_ap, v_ap, tab) in enumerate(keytiles):
                Sp = s_ring.next()[:, 0:N]
                mm(Sp, k_ap, q_of)
                P = Pr.next()[:, 0:N]
                act(P, Sp, AF.Exp, scale=float(scale))
                if tab is not None:
                    P2 = P2r.next()[:, 0:N]
                    tt(P2, P, tab[:, 0:N], ALU.mult)
                    P = P2
                mm(Op[0:vrows, 0:N], v_ap, P, start=(i == 0), stop=(i == nk - 1))
            return Op

        def normalize(Op, odd, N, out_sb):
            zp = 0 if odd else 64
            r0 = 64 if odd else 0
            zr = zrow.next()
            act(zr[zp:zp + 1, 0:N], Op[zp:zp + 1, 0:N], AF.Ln)
            zr2 = zrow.next()
            act(zr2[zp:zp + 1, 0:N], zr[zp:zp + 1, 0:N], AF.Exp, scale=-1.0)
            bc = bc_ring.next()
            mm(bc[:, 0:N], ones_f[zp:zp + 1, :], zr2[zp:zp + 1, 0:N])
            osb = Osb.next()
            act(osb[r0:r0 + 64, 0:N], Op[r0:r0 + 64, 0:N], AF.Copy)
            tt(out_sb[r0:r0 + 64, 0:N], osb[r0:r0 + 64, 0:N], bc[r0:r0 + 64, 0:N], ALU.mult)

        def v_slot(kt, h):
            odd = h % 2
            return VA[:, kt, h, 0:128] if odd else VA[:, kt, h, 0:65]

        def v_dst(kt, h):
            odd = h % 2
            return VA[:, kt, h, 64:128] if odd else VA[:, kt, h, 0:64]

        def init_va():
            mset(VA.rearrange("p k h c -> p (k h c)"), 0.0, eng='dve')
            for h in range(4):
                col = 0 if h % 2 else 64
                mset(VA[:, :, h, col:col + 1], 1.0, eng='dve')

        for l in range(n_layers):
            with_ctx = l < L - 1
            li = lambda_init(l)
            q_tbs = TBS if with_ctx else TBS[:4]
            ss_ring = Ring([PB(3), PB(4)])
            rot_ring = Ring([PB(5)])
            raw_ring = Ring([PB(0), PB(1), PB(2)])
            vp_ring = Ring([PB(int(_os.environ.get("VPB", "6")))])
            s_ring = Ring([PB(0), PB(1), PB(2)])
            o_ring = Ring([PB(3), PB(4)])
            bc_ring = Ring([PB(5)])

            def pvc(name, j=0, rows=128):
                c0, w = PV[name]
                return pv[0:rows, l, c0 + j:c0 + j + 1]

            for which, gname in ((0, 'g_mix'), (1, 'g_ffn')):
                c0, _ = PV[gname]
                for w in range(2):
                    sc = modT[:, l, (3 * which + 1) * 8:(3 * which + 2) * 8, w]
                    stt(Gv[:, which, :, w], sc, 1.0, pv[:, l, c0:c0 + 8], ALU.add, ALU.mult)
            c0, _ = PV['lvec']
            lv = pv[:, l, c0:c0 + 128].rearrange("p (a d) -> p a d", a=4)
            prod = tf(0)[:, 0:64].rearrange("p (a d) -> p a d", a=2)
            tt(prod[:, 0, :], lv[:, 0, :], lv[:, 1, :], ALU.mult)
            tt(prod[:, 1, :], lv[:, 2, :], lv[:, 3, :], ALU.mult)
            S.add('dve', lambda E, prod=prod: E.tensor_reduce(out=misc[:, 0:2], in_=prod, axis=AX.X, op=ALU.add),
                  reads=[prod], writes=[misc[:, 0:2]])
            act(misc[:, 2:4], misc[:, 0:2], AF.Exp)
            tt(misc[:, 4:5], misc[:, 3:4], misc[:, 2:3], ALU.subtract)
            ts(misc[:, 5:6], misc[:, 4:5], float(-li), None, ALU.add)
            ts(misc[:, 6:7], pvc('dsub_g'), float(1.0 - li), None, ALU.mult)
            nlam = misc[:, 5:6]
            gsub = misc[:, 6:7]

            modulate(l, 0, load_x_block)
            ck('m1')
            init_va()
            ck('va')

            def proj_fm(out_ps, wcols, t0, n):
                for kc in range(8):
                    mm(out_ps, wA[:, kc, wcols[0]:wcols[1]], hT[:, kc, t0:t0 + n], start=(kc == 0), stop=(kc == 7))

            def proj_v(col0, heads, ncols_per_head=64, slot_of=None):
                nh = len(heads)
                for kt in range(NKT):
                    vp = vp_ring.next()
                    for kc in range(8):
                        mm(vp[:, 0:nh * 64], hT[:, kc, kt * 128:(kt + 1) * 128], wA[:, kc, col0:col0 + nh * 64],
                           start=(kc == 0), stop=(kc == 7))
                    for i, hs in enumerate(heads):
                        for h in hs:
                            vcopy(v_dst(kt, h), vp[:, i * 64:(i + 1) * 64], vp, True)

            def run_attention(mixer, head_q, head_k, krows, scale, tab_of=None, na=False, diff=False):
                for h in range(4):
                    odd = h % 2
                    chunk = 2 * mixer + h // 2
                    for tbi, (t0, n) in enumerate(q_tbs):
                        if tbi < 4:
                            if na:
                                kts = list(NA_KT[tbi]) + [16, 17]
                            else:
                                kts = list(range(NKT))
                        else:
                            kts = [16, 17]
                        vrows = 128 if odd else 65
                        if not diff:
                            tiles = []
                            for kt in kts:
                                tab = None
                                if na and kt < 16:
                                    i0 = 8 * tbi - 2 * kt + 7
                                    tab = natab[:, h, (i0 + 3) * 64:(i0 + 3) * 64 + 512]
                                tiles.append((KT[0:krows, head_k(h), kt * 128:(kt + 1) * 128], v_slot(kt, h), tab))
                            Op = attention(QT[0:krows, h, t0:t0 + n], tiles, n, scale, vrows)
                            normalize(Op, odd, n, mixT[:, chunk, t0:t0 + n])
                        else:
                            r0 = 64 if odd else 0
                            AB = []
                            for pr in range(2):
                                tiles = [(KT[32 * pr:32 * pr + 32, h, kt * 128:(kt + 1) * 128], v_slot(kt, h), None) for kt in kts]
                                Op = attention(QT[32 * pr:32 * pr + 32, h, t0:t0 + n], tiles, n, scale, vrows)
                                normalize(Op, odd, n, ABo[pr])
                                AB.append(ABo[pr])
                            o = ABo[2]
                            stt(o[r0:r0 + 64, 0:n], AB[1][r0:r0 + 64, 0:n], nlam[r0:r0 + 64, :], AB[0][r0:r0 + 64, 0:n], ALU.mult, ALU.add)
                            sq = sqr.next()
                            act(sq[r0:r0 + 64, 0:n], o[r0:r0 + 64, 0:n], AF.Square)
                            ssp = PB(6)
                            mm(ssp[r0:r0 + 64, 0:n], blk64[r0:r0 + 64, r0:r0 + 64], sq[r0:r0 + 64, 0:n])
                            ln = lnr.next()
                            act(ln[r0:r0 + 64, 0:n], ssp[r0:r0 + 64, 0:n], AF.Ln, scale=1.0 / 64, bias=float(EPS))
                            rs = rsr.next()
                            act(rs[r0:r0 + 64, 0:n], ln[r0:r0 + 64, 0:n], AF.Exp, scale=-0.5)
                            stt(mixT[r0:r0 + 64, chunk, t0:t0 + n], o[r0:r0 + 64, 0:n], gsub[r0:r0 + 64, :], rs[r0:r0 + 64, 0:n],
                                ALU.mult, ALU.mult)

            dma('pool', wA[:, :, 0:416], win_d[l, :, 0:416].rearrange("(a p) n -> p a n", p=128))
            dma('pool', wuq_sb, wuq_d[l].rearrange("(a p) n -> p a n", p=128))
            dma('pool', wukv_sb, wukv_d[l])
            dma('sp', ropeT, tabs_d[0])
            for tbi, (t0, n) in enumerate(TBS):
                lat = tbi < 4
                raws = []
                for c in range(2):
                    rp = raw_ring.next()[:, 0:n]
                    proj_fm(rp, (c * 128, (c + 1) * 128), t0, n)
                    raws.append(rp)
                ssp = ss_ring.next()[:, 0:n]
                for c in range(2):
                    sq = sqr.next()[:, 0:n]
                    act(sq, raws[c], AF.Square)
                    mm(ssp, allones, sq, start=(c == 0), stop=(c == 1))
                ln = lnr.next()[:, 0:n]
                act(ln, ssp, AF.Ln, scale=1.0 / 256, bias=float(EPS))
                rs = rsr.next()[:, 0:n]
                act(rs, ln, AF.Exp, scale=-0.5)
                for c in range(2):
                    stt(cqn[:, c, 0:n], raws[c], pvc('qa_g', c), rs, ALU.mult, ALU.mult)
                for h in range(4):
                    rp = raw_ring.next()[0:96, 0:n]
                    for c in range(2):
                        mm(rp, wuq_sb[:, c, h * 96:(h + 1) * 96], cqn[:, c, 0:n], start=(c == 0), stop=(c == 1))
                    rope = (Rmla[0:96, 0:96], ropeT[0:96, 0, t0:t0 + n], ropeT[0:96, 1, t0:t0 + n]) if lat else None
                    norm_rope(rp, 96, n, allones[0:96, 0:96], 1.0 / 96, pvc('mq_g', 0, 96), QT[0:96, h, t0:t0 + n], rope)
                rp = raw_ring.next()[:, 0:n]
                proj_fm(rp, (256, 384), t0, n)
                norm_rope(rp, 128, n, allones, 1.0 / 128, pvc('kva_g'), ckvnT[:, t0:t0 + n], None)
                rpk = raw_ring.next()
                for kc in range(8):
                    mm(rpk[64:96, 0:n], wA[:, kc, 384:416], hT[:, kc, t0:t0 + n], start=(kc == 0), stop=(kc == 7))
                krs = t2r.next()
                act(krs[64:96, 0:n], rpk[64:96, 0:n], AF.Copy)
                for h in range(4):
                    rp = raw_ring.next()[0:64, 0:n]
                    mm(rp, wukv_sb[:, h * 128:h * 128 + 64], ckvnT[:, t0:t0 + n])
                    kr_ = kraw.next()
                    act(kr_[0:64, 0:n], rp, AF.Copy)
                    cp(kr_[64:96, 0:n], krs[64:96, 0:n])
                    rope = (Rmla[0:96, 0:96], ropeT[0:96, 0, t0:t0 + n], ropeT[0:96, 1, t0:t0 + n]) if lat else None
                    norm_rope(kr_[0:96, 0:n], 96, n, allones[0:96, 0:96], 1.0 / 96, pvc('mk_g', 0, 96), KT[0:96, h, t0:t0 + n], rope)
            ck('mlap')
            for kt in range(NKT):
                vp = vp_ring.next()
                for h in range(4):
                    mm(vp[:, h * 64:(h + 1) * 64], ckvnT[:, kt * 128:(kt + 1) * 128], wukv_sb[:, h * 128 + 64:h * 128 + 128])
                for h in range(4):
                    if _os.environ.get('NOVCP'):
                        continue
                    vcopy(v_dst(kt, h), vp[:, h * 64:(h + 1) * 64], vp, True)
            ck('mlav')
            run_attention(0, None, lambda h: h, 96, 96 ** -0.5)
            ck('mla')

            dma('pool', wA[:, :, 0:768], win_d[l, :, 416:1184].rearrange("(a p) n -> p a n", p=128))
            dma('sp', ropeT, tabs_d[1])
            for tbi, (t0, n) in enumerate(TBS):
                lat = tbi < 4
                rope = (Rdiff[0:64, 0:64], ropeT[0:64, 0, t0:t0 + n], ropeT[0:64, 1, t0:t0 + n]) if lat else None
                for h in range(4):
                    rp = raw_ring.next()[0:64, 0:n]
                    proj_fm(rp, (h * 64, h * 64 + 64), t0, n)
                    norm_rope(rp, 64, n, blk32[0:64, 0:64], 1.0 / 32, pvc('dq_g', 0, 64), QT[0:64, h, t0:t0 + n], rope)
                    rp = raw_ring.next()[0:64, 0:n]
                    proj_fm(rp, (256 + h * 64, 256 + h * 64 + 64), t0, n)
                    norm_rope(rp, 64, n, blk32[0:64, 0:64], 1.0 / 32, pvc('dk_g', 0, 64), KT[0:64, h, t0:t0 + n], rope)
            proj_v(512, [[0], [1], [2], [3]])
            run_attention(1, None, lambda h: h, 32, 32 ** -0.5, diff=True)
            ck('diff')

            dma('pool', wA[:, :, 0:768], win_d[l, :, 1184:1952].rearrange("(a p) n -> p a n", p=128))
            for h in range(4):
                dma('sp', nastg, nab_d[l, h])
                act(natab[:, h, :], nastg, AF.Exp)
                for (c0_, c1_) in ((0, TL), (TL, T)):
                    dma('pool', QT[64:96, h, c0_:c1_], aug_d[0][:, c0_:c1_])
                    dma('pool', KT[64:96, h, c0_:c1_], aug_d[1][:, c0_:c1_])
            for tbi, (t0, n) in enumerate(TBS):
                for h in range(4):
                    rp = raw_ring.next()[0:64, 0:n]
                    proj_fm(rp, (h * 64, h * 64 + 64), t0, n)
                    norm_rope(rp, 64, n, allones[0:64, 0:64], 1.0 / 64, pvc('nq_g', 0, 64), QT[0:64, h, t0:t0 + n], None)
                    rp = raw_ring.next()[0:64, 0:n]
                    proj_fm(rp, (256 + h * 64, 256 + h * 64 + 64), t0, n)
                    norm_rope(rp, 64, n, allones[0:64, 0:64], 1.0 / 64, pvc('nk_g', 0, 64), KT[0:64, h, t0:t0 + n], None)
            proj_v(512, [[0], [1], [2], [3]])
            run_attention(2, None, lambda h: h, 96, 64 ** -0.5, na=True)
            ck('na')

            dma('pool', wA[:, :, 0:512], win_d[l, :, 1952:2464].rearrange("(a p) n -> p a n", p=128))
            dma('sp', ropeT, tabs_d[2])
            for tbi, (t0, n) in enumerate(TBS):
                lat = tbi < 4
                rope = (Rgqa[0:64, 0:64], ropeT[0:64, 0, t0:t0 + n], ropeT[0:64, 1, t0:t0 + n]) if lat else None
                for h in range(4):
                    rp = raw_ring.next()[0:64, 0:n]
                    proj_fm(rp, (h * 64, h * 64 + 64), t0, n)
                    norm_rope(rp, 64, n, allones[0:64, 0:64], 1.0 / 64, pvc('gq_g', 0, 64), QT[0:64, h, t0:t0 + n], rope)
                for g in range(2):
                    rp = raw_ring.next()[0:64, 0:n]
                    proj_fm(rp, (256 + g * 64, 256 + g * 64 + 64), t0, n)
                    norm_rope(rp, 64, n, allones[0:64, 0:64], 1.0 / 64, pvc('gk_g', 0, 64), KT[0:64, g, t0:t0 + n], rope)
            ck('gqap')
            proj_v(384, [[0, 1], [2, 3]])
            ck('gqav')
            run_attention(3, None, lambda h: h // 2, 64, 64 ** -0.5)
            ck('gqa')

            dma('pool', wout_sb, wout_d[l].rearrange("(a p) n -> p a n", p=128))
            all_ps = Ring([PB(i) for i in range(7)])
            ss_ring = Ring([PB(5), PB(6)])
            op_ring = Ring([PB(i) for i in range(5)])

            def outproj_loader(tbi, t0, n):
                dma('sp', xblk[:, :, 0:n], xT_d[:, :, t0:t0 + n], reads=xkeys(tbi))
                w = 0 if tbi < 4 else 1
                for j in range(8):
                    pb = op_ring.next()[:, 0:n]
                    for c in range(8):
                        mm(pb, wout_sb[:, c, j * 128:(j + 1) * 128], mixT[:, c, t0:t0 + n], start=(c == 0), stop=(c == 7))
                    stt(xblk[:, j, 0:n], pb, modT[:, l, 2 * 8 + j, w:w + 1], xblk[:, j, 0:n], ALU.mult, ALU.add)
                dma('sp', xT_d[:, :, t0:t0 + n], xblk[:, :, 0:n], writes=xkeys(tbi))

            def outproj_loader_lastctx(tbi, t0, n):
                if tbi == 4:
                    load_x_block(tbi, t0, n)
                else:
                    outproj_loader(tbi, t0, n)

            modulate(l, 1, outproj_loader if with_ctx else outproj_loader_lastctx)

            ck('outproj')
            f_tbs = TBS if with_ctx else TBS[:4]
            for ub in ubuf:
                mset(ub[:, 0:1], 0.0, eng='pool')
                mset(ub[:, 2049:2051], 0.0, eng='pool')
                mset(ub[:, 2307:2308], 0.0, eng='pool')
            up_ring = Ring([PB(0), PB(1), PB(2), PB(3)])
            dn_ring = Ring([PB(4), PB(5), PB(6)])
            cring = Ring(ctile)

            def ucol(t0):
                return (1 + t0) if t0 < TL else (2051 + t0 - TL)

            for (g0, gn) in FFG:
                wn = gn * 128
                dma('pool', wup_sb[:, :, 0, 0:wn], wup_d[l, :, g0 * 128:g0 * 128 + wn].rearrange("(a p) n -> p a n", p=128))
                dma('pool', wup_sb[:, :, 1, 0:wn], wup_d[l, :, DFF + g0 * 128:DFF + g0 * 128 + wn].rearrange("(a p) n -> p a n", p=128))
                dma('pool', wdn_sb[:, 0:gn, :], wdn_d[l, g0 * 128:(g0 + gn) * 128, :].rearrange("(a p) n -> p a n", p=128))
                for cl in range(gn):
                    c = g0 + cl
                    for (t0, n) in f_tbs:
                        u0 = ucol(t0)
                        for ag in range(2):
                            pb = up_ring.next()[:, 0:n]
                            for kc in range(8):
                                mm(pb, wup_sb[:, kc, ag, cl * 128:(cl + 1) * 128], hT[:, kc, t0:t0 + n], start=(kc == 0), stop=(kc == 7))
                            if ag == 0:
                                act(ubuf[ag][:, u0:u0 + n], pb, AF.Copy)
                            else:
                                cp(ubuf[ag][:, u0:u0 + n], pb)
                    for (t0, n) in f_tbs:
                        u0 = ucol(t0)
                        outs = []
                        for ag in range(2):
                            col = c if ag == 0 else NCH + c
                            ct = cring.next()[:, 0:n]
                            act(ct, ubuf[ag][:, u0:u0 + n], AF.Identity, scale=pvc('cw1', col), bias=pvc('cb', col))
                            stt(ct, ubuf[ag][:, u0 - 1:u0 - 1 + n], pvc('cw0', col), ct, ALU.mult, ALU.add)
                            stt(ct, ubuf[ag][:, u0 + 1:u0 + 1 + n], pvc('cw2', col), ct, ALU.mult, ALU.add)
                            outs.append(ct)
                        act(outs[1], outs[1], AF.Silu)
                        tt(actT[:, cl, t0:t0 + n], outs[1], outs[0], ALU.mult)
                for tbi, (t0, n) in enumerate(f_tbs):
                    w = 0 if tbi < 4 else 1
                    for j in range(8):
                        pb = dn_ring.next()[:, 0:n]
                        for cl in range(gn):
                            mm(pb, wdn_sb[:, cl, j * 128:(j + 1) * 128], actT[:, cl, t0:t0 + n], start=(cl == 0), stop=(cl == gn - 1))
                        xt = xring.next()[:, 0:n]
                        key = "xf:%d:%d" % (tbi, j)
                        dma('sp', xt, xT_d[:, j, t0:t0 + n], reads=[xkey(tbi), key])
                        stt(xt, pb, modT[:, l, 5 * 8 + j, w:w + 1], xt, ALU.mult, ALU.add)
                        dma('sp', xT_d[:, j, t0:t0 + n], xt, writes=[key])

    except StopBuild:
        pass
    import os as _os
    for tbi, (t0, n) in enumerate(TBS[:4]):
        dma('sp', xblk[:, :, 0:n], xT_d[:, :, t0:t0 + n], reads=xkeys(tbi))
        if _os.environ.get('SKIP_FINAL'):
            dma('sp', y_d[t0:t0 + n, :].rearrange("(a p) d -> p a d", p=128), xblk[:, 0:4, :].rearrange("p a (b c) -> p (a b) c", b=1)[:, :, :].rearrange("p a c -> p a c") if False else xld[:, 0:4, :])
            continue
        for a in range(4):
            for half in range(2):
                pb = PB((a * 2 + half) % int(_os.environ.get("NB", "6")))
                for jj in range(4):
                    j = half * 4 + jj
                    tr(pb[:, jj * 128:(jj + 1) * 128], xblk[:, j, a * 128:(a + 1) * 128], ident)
                if half == 0:
                    act(xld[:, a, 0:512], pb, AF.Copy)
                else:
                    cp(xld[:, a, 512:1024], pb)
        dma('sp', y_d[t0:t0 + n, :].rearrange("(a p) d -> p a d", p=128), xld[:, 0:4, :])
    if dbg:
        for tbi, (t0, n) in enumerate(TBS):
            dma('sp', xblk[:, :, 0:n], xT_d[:, :, t0:t0 + n], reads=xkeys(tbi))
            dma('sp', dbg_d[:, :, t0:t0 + n], xblk[:, :, 0:n])
    S.emit()
    return nc, S


def _rope_tab(rot):
    nf = rot // 4
    inv = np.power(np.float32(10000.0), -np.arange(nf, dtype=np.float32) / np.float32(nf)).astype(np.float32)
    t = np.arange(TL)
    row = (t // 64).astype(np.float32)
    col = (t % 64).astype(np.float32)
    ar = row[:, None] * inv
    ac = col[:, None] * inv
    ang = np.concatenate([ar, ar, ac, ac], axis=-1).astype(np.float32)
    return np.cos(ang).T.astype(np.float32), np.sin(ang).T.astype(np.float32)


def _constants():
    cf = np.zeros((128, 256), np.float32)
    cf[:, 0:128] = np.eye(128, dtype=np.float32)
    cf[:, 128:256] = 1.0
    cb = np.zeros((128, 6, 128), np.float32)
    cb[:, 0, :] = 1.0
    for b in range(4):
        cb[b * 32:(b + 1) * 32, 1, b * 32:(b + 1) * 32] = 1.0
    for b in range(2):
        cb[b * 64:(b + 1) * 64, 2, b * 64:(b + 1) * 64] = 1.0

    def add_rot(Rm, base, n):
        for i in range(n):
            Rm[base + i + n, base + i] = -1.0
            Rm[base + i, base + i + n] = 1.0
            Rm[base + 3 * n + i, base + 2 * n + i] = -1.0
            Rm[base + 2 * n + i, base + 3 * n + i] = 1.0
    add_rot(cb[:, 3, :], 64, 8)
    add_rot(cb[:, 4, :], 0, 8)
    add_rot(cb[:, 4, :], 32, 8)
    add_rot(cb[:, 5, :], 0, 16)
    aug = np.zeros((2, 32, T), np.float32)
    for q in range(TL):
        qr = q // 64
        rs = min(max(qr - 4, 0), 24)
        aug[0, :, q] = -BIG
        aug[0, rs:rs + 8, q] = 0.0
        aug[1, qr, q] = 1.0
    tabs = np.zeros((3, 128, 2, TL), np.float32)
    c32, s32 = _rope_tab(32)
    c64, s64 = _rope_tab(64)
    tabs[0, 0:64, 0, :] = 1.0
    tabs[0, 64:96, 0, :] = c32
    tabs[0, 64:96, 1, :] = s32
    tabs[1, 0:32, 0, :] = c32
    tabs[1, 32:64, 0, :] = c32
    tabs[1, 0:32, 1, :] = s32
    tabs[1, 32:64, 1, :] = s32
    tabs[2, 0:64, 0, :] = c64
    tabs[2, 0:64, 1, :] = s64
    return cf, cb.reshape(128, 768), aug, tabs


def _na_index():
    idx = np.full((128, 22, 64), 15 * 31, np.int64)
    for p in range(128):
        half, kc = p // 64, p % 64
        for pos in range(22):
            i = pos - 3 - half
            if i < 0 or i > 14:
                continue
            for qc in range(64):
                ws = min(max(qc - 8, 0), 48)
                if ws <= kc < ws + 16:
                    co = min(max(kc - qc + 15, 0), 30)
                    idx[p, pos, qc] = (14 - i) * 31 + co
    return idx.reshape(128, NAW)


def _tile_rows(v, reps, rows=128):
    out = np.zeros((rows,), np.float32)
    t = np.tile(np.asarray(v, np.float32), reps)
    out[:t.shape[0]] = t
    return out


def _prep_shared(inp):
    cf, cb, aug, tabs = _constants()
    pvA = np.zeros((128, L, NV), np.float32)

    def put(name, l, arr2d):
        c0, w = PV[name]
        pvA[:, l, c0:c0 + w] = arr2d

    for l in range(L):
        put('g_mix', l, inp['g_mix'][l].reshape(8, 128).T)
        put('g_ffn', l, inp['g_ffn'][l].reshape(8, 128).T)
        put('qa_g', l, inp['mla_q_a_g'][l].reshape(2, 128).T)
        put('kva_g', l, inp['mla_kv_a_g'][l].reshape(1, 128).T)
        put('mq_g', l, _tile_rows(inp['mla_q_g'][l], 1)[:, None])
        put('mk_g', l, _tile_rows(inp['mla_k_g'][l], 1)[:, None])
        put('dq_g', l, _tile_rows(inp['diff_q_g'][l], 4)[:, None])
        put('dk_g', l, _tile_rows(inp['diff_k_g'][l], 4)[:, None])
        put('dsub_g', l, _tile_rows(inp['diff_subln_g'][l], 2)[:, None])
        put('nq_g', l, _tile_rows(inp['na_q_g'][l], 2)[:, None])
        put('nk_g', l, _tile_rows(inp['na_k_g'][l], 2)[:, None])
        put('gq_g', l, _tile_rows(inp['gqa_q_g'][l], 2)[:, None])
        put('gk_g', l, _tile_rows(inp['gqa_k_g'][l], 2)[:, None])
        for i in range(3):
            put('cw%d' % i, l, inp['conv_w'][l, i].reshape(44, 128).T)
        put('cb', l, inp['conv_b'][l].reshape(44, 128).T)
        lv = np.concatenate([inp['diff_lq1'][l], inp['diff_lk1'][l], inp['diff_lq2'][l], inp['diff_lk2'][l]]).astype(np.float32)
        put('lvec', l, np.broadcast_to(lv[None, :], (128, 128)))
    idx = _na_index()
    nab = np.zeros((L, 4, 128, NAW), np.float32)
    for l in range(L):
        for h in range(4):
            src = np.concatenate([np.asarray(inp['na_rpb'][l, h], np.float32).ravel(), np.array([-10000.0], np.float32)])
            nab[l, h] = src[idx]
    f = lambda a: np.ascontiguousarray(np.asarray(a, np.float32))
    return {
        "w_mod": f(inp['w_mod']), "b_mod": f(inp['b_mod']), "w_in": f(inp['w_in']), "w_out": f(inp['w_out']),
        "w_uq": f(inp['mla_w_uq']), "w_ukv": f(inp['mla_w_ukv']), "w_up": f(inp['w_up']), "w_down": f(inp['w_down']),
        "pv": np.ascontiguousarray(pvA.reshape(128, L * NV)), "cf32": cf, "cbf": np.ascontiguousarray(cb),
        "aug": aug, "tabs": tabs, "nab": nab,
    }


_CACHE = {}


def kernel(**inp):
    n_layers = inp.pop('_n_layers', L)
    dbg = inp.pop('_dbg', False)
    stop = inp.pop('_stop', None)
    ncores = inp.pop('_ncores', 8)
    key = (n_layers, dbg, stop)
    if key not in _CACHE:
        _CACHE[key] = build_program(n_layers, dbg, stop)[0]
    nc = _CACHE[key]
    shared = _prep_shared(inp)
    x = np.asarray(inp['x'], np.float32)
    ctx = np.asarray(inp['ctx'], np.float32)
    c = np.asarray(inp['c'], np.float32)
    cc = np.asarray(inp['c_ctx'], np.float32)
    in_maps = []
    for b in range(ncores):
        m = dict(shared)
        m["xc"] = np.ascontiguousarray(np.concatenate([x[b], ctx[b]], axis=0))
        cT = np.zeros((128, 8, 2), np.float32)
        cT[:, :, 0] = c[b].reshape(8, 128).T
        cT[:, :, 1] = cc.reshape(8, 128).T
        m["cT"] = np.ascontiguousarray(cT.reshape(128, 16))
        in_maps.append(m)
    res = run_bass_kernel_spmd(nc, in_maps, core_ids=list(range(ncores)))
    out = np.stack([np.asarray(r["y"], np.float32) for r in res.results], axis=0)
    if dbg:
        kernel.dbg = [np.asarray(r["dbg"], np.float32) for r in res.results]
    return out
```

```python
import math
import numpy as np
import concourse.bass as bass
import concourse.mybir as mybir
from concourse.bass_utils import run_bass_kernel_spmd

F32 = mybir.dt.float32
BF16 = mybir.dt.bfloat16
AF = mybir.ActivationFunctionType
ALU = mybir.AluOpType
AX = mybir.AxisListType

ENGS = ['pe', 'act', 'dve', 'pool', 'sp']
CELL = 256
_ESZ = {}


def esz(dt):
    if dt not in _ESZ:
        _ESZ[dt] = mybir.dt.size(dt)
    return _ESZ[dt]


def ap_cells(ap):
    space = str(ap.space)
    sp = 0 if space == 'SB' else 1
    dims = ap.ap
    pstep, pcount = dims[0]
    e = esz(ap.dtype)
    off = ap.offset
    p0 = off // pstep
    foff = off % pstep
    ranges = [(foff, foff + 1)]
    for (st, cnt) in dims[1:]:
        if cnt <= 1:
            continue
        if len(ranges) * cnt <= 512 and abs(st) * e >= CELL:
            ranges = [(lo + i * st, hi + i * st) for (lo, hi) in ranges for i in range(cnt)]
        else:
            ext = (cnt - 1) * st
            if ext >= 0:
                ranges = [(lo, hi + ext) for (lo, hi) in ranges]
            else:
                ranges = [(lo + ext, hi) for (lo, hi) in ranges]
    cs = set()
    for (lo, hi) in ranges:
        c0 = (lo * e) // CELL
        c1 = (hi * e - 1) // CELL
        for c in range(c0, c1 + 1):
            cs.add(c)
    if sp == 1:
        return sorted(set(4 * 4096 + (c * CELL) // 2048 for c in cs))
    q0 = p0 // 32
    q1 = (p0 + pcount - 1) // 32
    out = []
    for q in range(q0, q1 + 1):
        base = (sp * 4 + q) * 4096
        for c in cs:
            out.append(base + c)
    return out


class Sched:
    def __init__(self, nc, n_lanes=48, same_eng_sync=True):
        self.nc = nc
        self.ops = {e: [] for e in ENGS}
        self.cw = {}
        self.cr = {}
        self.known = {e: {} for e in ENGS}
        self.snap = {}
        self.n_lanes = n_lanes
        self.lane_count = [0] * n_lanes
        self.next_lane = 0
        self.same_eng_sync = same_eng_sync
        self.eng_sem = {e: nc.alloc_semaphore("sem_" + e) for e in ENGS}
        self.lane_sem = [nc.alloc_semaphore("lane%d" % i) for i in range(n_lanes)]

    def _cells(self, items):
        cs = []
        for it in items:
            if it is None:
                continue
            if isinstance(it, (str, tuple)):
                cs.append(it)
            else:
                cs.extend(ap_cells(it))
        return cs

    def add(self, eng, fn, reads=(), writes=(), dma=False):
        ops = self.ops[eng]
        idx = len(ops)
        rc = self._cells(reads)
        wc = self._cells(writes)
        need = {}

        def want(tok, war=False):
            key, seq = tok
            if key == eng:
                if eng == 'pe' or war or not self.same_eng_sync:
                    return
            if need.get(key, -1) < seq:
                need[key] = seq

        for c in rc:
            t = self.cw.get(c)
            if t is not None:
                want(t)
        for c in wc:
            t = self.cw.get(c)
            if t is not None:
                want(t)
            rs = self.cr.get(c)
            if rs:
                for k, s in rs.items():
                    want((k, s), war=True)
        if dma:
            lane = self.next_lane
            self.next_lane = (lane + 1) % self.n_lanes
            cnt = self.lane_count[lane] + 1
            self.lane_count[lane] = cnt
            tok = (('L', lane), cnt)
            if cnt > 1:
                want((('L', lane), cnt - 1))
        else:
            tok = (eng, idx)
        kn = self.known[eng]
        waits = []
        for key, seq in need.items():
            if kn.get(key, -1) >= seq:
                continue
            waits.append((key, seq))
            kn[key] = seq
            if isinstance(key, str):
                self.ops[key][seq]['signal'] = True
                sn = self.snap.get((key, seq))
                if sn:
                    for k2, s2 in sn.items():
                        if kn.get(k2, -1) < s2:
                            kn[k2] = s2
        if not dma:
            self.snap[(eng, idx)] = dict(kn)
        ops.append(dict(fn=fn, waits=waits, tok=tok, dma=dma, signal=False))
        for c in wc:
            self.cw[c] = tok
            self.cr[c] = {}
        for c in rc:
            d = self.cr.get(c)
            if d is None:
                d = {}
                self.cr[c] = d
            if d.get(tok[0], -1) < tok[1]:
                d[tok[0]] = tok[1]
        return tok

    def emit(self):
        nc = self.nc
        for e in ENGS:
            c = 0
            for op in self.ops[e]:
                if op['signal'] and not op['dma']:
                    c += 1
                    op['sigval'] = c
        emap = {'pe': 'tensor', 'act': 'scalar', 'dve': 'vector', 'pool': 'gpsimd', 'sp': 'sync'}

        def run(e, E):
            for op in self.ops[e]:
                for key, seq in op['waits']:
                    if isinstance(key, str):
                        E.wait_ge(self.eng_sem[key], self.ops[key][seq]['sigval'])
                    else:
                        E.wait_ge(self.lane_sem[key[1]], 16 * seq)
                inst = op['fn'](E)
                if op['dma']:
                    inst.then_inc(self.lane_sem[op['tok'][0][1]], 16)
                elif op['signal']:
                    inst.then_inc(self.eng_sem[e], 1)
            if e == 'sp':
                for i, cnt in enumerate(self.lane_count):
                    if cnt:
                        E.wait_ge(self.lane_sem[i], 16 * cnt)

        with nc.Block() as block:
            for e in ENGS:
                getattr(block, emap[e])(lambda E, e=e: run(e, E))

    def stats(self):
        return {e: (len(self.ops[e]), sum(len(o['waits']) for o in self.ops[e])) for e in ENGS}


class Ring:
    def __init__(self, items):
        self.items = list(items)
        self.i = 0

    def next(self):
        r = self.items[self.i]
        self.i = (self.i + 1) % len(self.items)
        return r


D = 1024
L = 4
TL = 2048
TC = 256
T = TL + TC
NKT = T // 128
TBS = [(0, 512), (512, 512), (1024, 512), (1536, 512), (2048, 256)]
IN_COLS = 2464
DFF = 2816
NCH = DFF // 128
EPS = 1e-6
BIG = 30000.0
NA_KT = {0: range(0, 6), 1: range(2, 10), 2: range(6, 14), 3: range(10, 16)}
NAW = 22 * 64
FFG = [(0, 6), (6, 6), (12, 6), (18, 4)]

PV = {}
_c = 0
for _n, _w in [('g_mix', 8), ('g_ffn', 8), ('qa_g', 2), ('kva_g', 1), ('mq_g', 1), ('mk_g', 1),
               ('dq_g', 1), ('dk_g', 1), ('dsub_g', 1), ('nq_g', 1), ('nk_g', 1), ('gq_g', 1), ('gk_g', 1),
               ('cw0', 44), ('cw1', 44), ('cw2', 44), ('cb', 44), ('lvec', 128)]:
    PV[_n] = (_c, _w)
    _c += _w
NV = _c


def lambda_init(l):
    return 0.8 - 0.6 * math.exp(-0.3 * l)


class StopBuild(Exception):
    pass


def build_program(n_layers=L, dbg=False, stop_after=None):
    nc = bass.Bass("TRN2", target_bir_lowering=False)

    def ck(name):
        if stop_after == name:
            raise StopBuild()

    def din(name, shape):
        return nc.dram_tensor(name, list(shape), F32, kind="ExternalInput").ap()

    xc_d = din("xc", [T, D])
    cT_d = din("cT", [128, 16])
    wmod_d = din("w_mod", [L, D, 6 * D])
    bmod_d = din("b_mod", [L, 6 * D])
    win_d = din("w_in", [L, D, IN_COLS])
    wout_d = din("w_out", [L, D, D])
    wuq_d = din("w_uq", [L, 256, 384])
    wukv_d = din("w_ukv", [L, 128, 512])
    wup_d = din("w_up", [L, D, 2 * DFF])
    wdn_d = din("w_down", [L, DFF, D])
    pv_d = din("pv", [128, L * NV])
    cf_d = din("cf32", [128, 256])
    cb_d = din("cbf", [128, 6 * 128])
    aug_d = din("aug", [2, 32, T])
    tabs_d = din("tabs", [3, 128, 2, TL])
    nab_d = din("nab", [L, 4, 128, NAW])
    y_d = nc.dram_tensor("y", [TL, D], F32, kind="ExternalOutput").ap()
    xT_d = nc.dram_tensor("xT_scr", [128, 8, T], F32).ap()
    if dbg:
        dbg_d = nc.dram_tensor("dbg", [128, 8, T], F32, kind="ExternalOutput").ap()

    ARENA_F32 = 53000
    arena = nc.alloc_sbuf_tensor("arena", [128, ARENA_F32], F32)
    psum = nc.alloc_psum_tensor("psum", [128, 4096], F32)
    S = Sched(nc)

    pos = [0]

    def alloc_b(nbytes):
        a = pos[0]
        n = (nbytes + 255) // 256 * 256
        pos[0] += n
        assert pos[0] <= ARENA_F32 * 4, pos[0]
        return a

    def f32v(boff, n):
        return arena[:, boff // 4: boff // 4 + n]

    def bf16v(boff, n):
        return arena[:, boff // 4: boff // 4 + (n + 1) // 2].bitcast(BF16)

    def PB(i):
        return psum[:, i * 512:(i + 1) * 512]

    ident = f32v(alloc_b(512), 128)
    ones_f = f32v(alloc_b(512), 128)
    cbf = bf16v(alloc_b(6 * 256), 6 * 128).rearrange("p (a b) -> p a b", a=6)
    allones, blk32, blk64, Rmla, Rdiff, Rgqa = [cbf[:, i, :] for i in range(6)]
    pv = f32v(alloc_b(L * NV * 4), L * NV).rearrange("p (l n) -> p l n", l=L)
    modT = f32v(alloc_b(L * 96 * 4), L * 96).rearrange("p (l s w) -> p l s w", l=L, s=48)
    cT = f32v(alloc_b(64), 16)
    scT_f = f32v(alloc_b(64), 16)
    scT = bf16v(alloc_b(32), 16).rearrange("p (a b) -> p a b", a=8)
    Gv = f32v(alloc_b(4 * 16 * 4), 64).rearrange("p (a j w) -> p a j w", a=2, j=8)
    misc = f32v(alloc_b(64 * 4), 64)
    hT = bf16v(alloc_b(8 * T * 2), 8 * T).rearrange("p (a t) -> p a t", a=8)
    mix_off = alloc_b(8 * T * 2)
    mixT = bf16v(mix_off, 8 * T).rearrange("p (a t) -> p a t", a=8)
    qkv_off = alloc_b(3 * 4 * T * 2)
    QT = bf16v(qkv_off, 4 * T).rearrange("p (a t) -> p a t", a=4)
    KT = bf16v(qkv_off + 4 * T * 2, 4 * T).rearrange("p (a t) -> p a t", a=4)
    VA = bf16v(qkv_off + 8 * T * 2, NKT * 4 * 128).rearrange("p (k h c) -> p k h c", k=NKT, h=4)
    wout_sb = bf16v(qkv_off, 8 * 1024).rearrange("p (a n) -> p a n", a=8)
    xblk = f32v(qkv_off + 16384, 8 * 512).rearrange("p (a n) -> p a n", a=8)
    xld = f32v(qkv_off + 32768, 4 * 1024).rearrange("p (a n) -> p a n", a=4)
    fo = mix_off
    actT = bf16v(fo, 6 * T).rearrange("p (a t) -> p a t", a=6); fo += 6 * T * 2
    wup_sb = bf16v(fo, 8 * 2 * 768).rearrange("p (k g n) -> p k g n", k=8, g=2); fo += 8 * 2 * 768 * 2
    wdn_sb = bf16v(fo, 6 * 1024).rearrange("p (a n) -> p a n", a=6); fo += 6 * 1024 * 2
    UW = T + 4
    UWP = (UW * 4 + 255) // 256 * 256
    ubuf = [f32v(fo + i * UWP, UW) for i in range(2)]; fo += 2 * UWP
    ctile = [f32v(fo + i * 2048, 512) for i in range(4)]; fo += 4 * 2048
    assert fo <= qkv_off + 3 * 4 * T * 2, (fo, qkv_off + 3 * 4 * T * 2)
    tab_off = alloc_b(2 * TL * 4)
    ropeT = f32v(tab_off, 2 * TL).rearrange("p (a t) -> p a t", a=2)
    natab = bf16v(tab_off, 4 * NAW).rearrange("p (h n) -> p h n", h=4)
    wA = bf16v(alloc_b(8 * 768 * 2), 8 * 768).rearrange("p (a n) -> p a n", a=8)
    ubuf2 = [f32v(tab_off + i * UWP, UW) for i in range(2)]
    assert 2 * UWP <= 2 * TL * 4 + 8 * 768 * 2
    wuq_sb = bf16v(alloc_b(2 * 384 * 2), 2 * 384).rearrange("p (a n) -> p a n", a=2)
    wukv_sb = bf16v(alloc_b(512 * 2), 512)
    ckvnT = bf16v(alloc_b(T * 2), T)
    cqn = bf16v(alloc_b(2 * 512 * 2), 2 * 512).rearrange("p (a n) -> p a n", a=2)
    tmp_off = alloc_b(24 * 1024)

    def tf(i):
        return f32v(tmp_off + i * 2048, 512)

    def tb16(i, half=0):
        return bf16v(tmp_off + i * 2048 + half * 1024, 512)

    kraw = Ring([tf(0), tf(1)])
    sqr = Ring([tb16(2, 0), tb16(2, 1)])
    lnr = Ring([tf(3), tf(4)])
    rsr = Ring([tf(5), tf(6)])
    qnr = Ring([tb16(7, 0), tb16(7, 1)])
    t1r = Ring([tf(8), tf(9)])
    t2r = Ring([tf(10), tf(11)])
    Pr = Ring([tb16(0, 0), tb16(0, 1), tb16(1, 0), tb16(1, 1)])
    P2r = Ring([tb16(2, 0), tb16(2, 1)])
    Osb = Ring([tf(3), tf(4)])
    zrow = Ring([tf(5), tf(6)])
    ABo = [tf(7), tf(8), tf(9)]
    xring = Ring([tf(8), tf(9), tf(10), tf(11)])
    nastg = f32v(tmp_off, NAW)
    wmr = Ring([bf16v(tmp_off + i * 4096, 2048) for i in range(3)])
    m_sb = f32v(qkv_off, 6 * D)
    bm_sb = f32v(qkv_off + 24576, 6 * D)

    def mm(out, lhsT, rhs, start=True, stop=True):
        S.add('pe', lambda E: E.matmul(out, lhsT, rhs, start=start, stop=stop), reads=[lhsT, rhs], writes=[out])

    def tr(out, in_, idn):
        S.add('pe', lambda E: E.transpose(out, in_, idn), reads=[in_, idn], writes=[out])

    def act(out, in_, func, scale=None, bias=None, eng='act'):
        kw = {}
        rd = [in_]
        if scale is not None:
            kw['scale'] = scale
            if not isinstance(scale, float):
                rd.append(scale)
        if bias is not None:
            kw['bias'] = bias
            if not isinstance(bias, float):
                rd.append(bias)
        S.add('act', lambda E: E.activation(out, in_, func, **kw), reads=rd, writes=[out])

    def stt(out, in0, scalar, in1, op0, op1):
        rd = [in0, in1] + ([] if isinstance(scalar, float) else [scalar])
        S.add('dve', lambda E: E.scalar_tensor_tensor(out=out, in0=in0, scalar=scalar, in1=in1, op0=op0, op1=op1),
              reads=rd, writes=[out])

    def tt(out, in0, in1, op, eng='dve'):
        S.add(eng, lambda E: E.tensor_tensor(out=out, in0=in0, in1=in1, op=op), reads=[in0, in1], writes=[out])

    def ts(out, in0, s1, s2, op0, op1=None, eng='dve'):
        rd = [in0] + [s for s in (s1, s2) if s is not None and not isinstance(s, float)]
        if op1 is None:
            S.add(eng, lambda E: E.tensor_scalar(out=out, in0=in0, scalar1=s1, scalar2=None, op0=op0), reads=rd, writes=[out])
        else:
            S.add(eng, lambda E: E.tensor_scalar(out=out, in0=in0, scalar1=s1, scalar2=s2, op0=op0, op1=op1), reads=rd, writes=[out])

    def cp(out, in_, eng='dve'):
        S.add(eng, lambda E: E.tensor_copy(out=out, in_=in_), reads=[in_], writes=[out])

    def vcopy(out, in_, bank, use_act):
        if use_act:
            S.add('act', lambda E: E.activation(out, in_, AF.Copy), reads=[bank], writes=[out])
        else:
            S.add('dve', lambda E: E.tensor_copy(out=out, in_=in_), reads=[bank], writes=[out])

    def mset(ap, val, eng='dve'):
        S.add(eng, lambda E: E.memset(ap, val), writes=[ap])

    def dma(q, out, in_, reads=None, writes=None):
        r = [in_] if reads is None else reads
        w = [out] if writes is None else writes
        r = [a for a in r if isinstance(a, (str, tuple)) or str(a.space) != 'DRAM']
        w = [a for a in w if isinstance(a, (str, tuple)) or str(a.space) != 'DRAM']
        S.add(q, lambda E: E.dma_start(out=out, in_=in_), reads=r, writes=w, dma=True)

    def xkey(tbi):
        return "x:%d" % tbi

    def xkeys(tbi):
        return ["x:%d" % tbi] + ["xf:%d:%d" % (tbi, j) for j in range(8)]

    try:
        import os as _os
        if _os.environ.get('SKIP_CONST'):
            raise StopBuild()
        dma('sp', ident, cf_d[:, 0:128])
        dma('sp', ones_f, cf_d[:, 128:256])
        dma('pool', cbf, cb_d.rearrange("p (a b) -> p a b", a=6))
        dma('sp', pv, pv_d.rearrange("p (l n) -> p l n", l=L))
        dma('sp', cT, cT_d)
        act(scT_f, cT, AF.Silu)
        cp(scT, scT_f.rearrange("p (a b) -> p a b", a=8))

        ck('c0')
        for tbi, (t0, n) in enumerate(TBS):
            ntt = n // 128
            dma('sp', xld[:, 0:ntt, :], xc_d[t0:t0 + n, :].rearrange("(a p) d -> p a d", p=128))
            for j in range(8):
                pb = PB(j % 2)
                for a in range(ntt):
                    tr(pb[:, a * 128:(a + 1) * 128], xld[:, a, j * 128:(j + 1) * 128], ident)
                if j % 2 == 0:
                    act(xblk[:, j, 0:n], pb[:, 0:n], AF.Copy)
                else:
                    cp(xblk[:, j, 0:n], pb[:, 0:n])
            dma('sp', xT_d[:, :, t0:t0 + n], xblk[:, :, 0:n], writes=xkeys(tbi))

        ck('xt')
        for l in range(n_layers):
            dma('sp', bm_sb[0:1, :], bmod_d[l:l + 1, :])
            dma('sp', bm_sb[1:2, :], bmod_d[l:l + 1, :])
            for cg in range(3):
                for kc in range(8):
                    piece = wmr.next()
                    dma('pool', piece, wmod_d[l, kc * 128:(kc + 1) * 128, cg * 2048:(cg + 1) * 2048])
                    for i in range(4):
                        mm(PB(i)[0:2, :], scT[:, kc, :], piece[:, i * 512:(i + 1) * 512], start=(kc == 0), stop=(kc == 7))
                for i in range(4):
                    c0 = cg * 2048 + i * 512
                    tt(m_sb[0:2, c0:c0 + 512], PB(i)[0:2, :], bm_sb[0:2, c0:c0 + 512], ALU.add)
            pt = PB(4)
            for s in range(48):
                tr(pt[:, 2 * s:2 * s + 2], m_sb[0:2, s * 128:(s + 1) * 128], ident[0:2, 0:2])
            cp(modT[:, l].rearrange("p s w -> p (s w)"), pt[:, 0:96])

        ck('mod')
        def rstd_from(src, R, N, ones_m, inv_d):
            sq = sqr.next()[0:R, 0:N]
            act(sq, src, AF.Square)
            ssp = ss_ring.next()[0:R, 0:N]
            mm(ssp, ones_m, sq)
            ln = lnr.next()[0:R, 0:N]
            act(ln, ssp, AF.Ln, scale=float(inv_d), bias=float(EPS))
            rs = rsr.next()[0:R, 0:N]
            act(rs, ln, AF.Exp, scale=-0.5)
            return rs

        def norm_rope(src, R, N, ones_m, inv_d, gain, out, rope=None):
            rs = rstd_from(src, R, N, ones_m, inv_d)
            if rope is None:
                stt(out, src, gain, rs, ALU.mult, ALU.mult)
                return
            Rm, cosT, sinT = rope
            qn = qnr.next()[0:R, 0:N]
            stt(qn, src, gain, rs, ALU.mult, ALU.mult)
            rp = rot_ring.next()[0:R, 0:N]
            mm(rp, Rm, qn)
            t1 = t1r.next()[0:R, 0:N]
            tt(t1, qn, cosT, ALU.mult, eng='pool')
            t2 = t2r.next()[0:R, 0:N]
            tt(t2, rp, sinT, ALU.mult)
            tt(out, t1, t2, ALU.add, eng='pool')

        def modulate(l, which, xsrc_loader):
            for tbi, (t0, n) in enumerate(TBS):
                w = 0 if tbi < 4 else 1
                xsrc_loader(tbi, t0, n)
                ssp = ss_ring.next()[:, 0:n]
                for j in range(8):
                    sq = sqr.next()[:, 0:n]
                    act(sq, xblk[:, j, 0:n], AF.Square)
                    mm(ssp, allones, sq, start=(j == 0), stop=(j == 7))
                ln = lnr.next()[:, 0:n]
                act(ln, ssp, AF.Ln, scale=1.0 / D, bias=float(EPS))
                rs = rsr.next()[:, 0:n]
                act(rs, ln, AF.Exp, scale=-0.5)
                for j in range(8):
                    t1 = t1r.next()[:, 0:n]
                    stt(t1, xblk[:, j, 0:n], Gv[:, which, j, w:w + 1], rs, ALU.mult, ALU.mult)
                    shift = modT[:, l, (3 * which) * 8 + j, w:w + 1]
                    act(hT[:, j, t0:t0 + n], t1, AF.Identity, bias=shift)

        def load_x_block(tbi, t0, n):
            dma('sp', xblk[:, :, 0:n], xT_d[:, :, t0:t0 + n], reads=xkeys(tbi))

        pending = []

        def flush_pending():
            while pending:
                pending.pop(0)()

        def attention(q_of, keytiles, N, scale, vrows, LOOK=2):
            Op = o_ring.next()
            nk = len(keytiles)
            Sps = {}

            def qk(i):
                Sps[i] = s_ring.next()[:, 0:N]
                mm(Sps[i], keytiles[i][0], q_of)

            for i in range(min(LOOK, nk)):
                qk(i)
            flush_pending()
            for i, (k_ap, v_ap, tab) in enumerate(keytiles):
                if i + LOOK < nk:
                    qk(i + LOOK)
                Sp = Sps.pop(i)
                P = Pr.next()[:, 0:N]
                act(P, Sp, AF.Exp, scale=float(scale))
                if tab is not None:
                    P2 = P2r.next()[:, 0:N]
                    tt(P2, P, tab[:, 0:N], ALU.mult)
                    P = P2
                mm(Op[0:vrows, 0:N], v_ap, P, start=(i == 0), stop=(i == nk - 1))
            return Op

        def normalize(Op, odd, N, out_sb):
            zp = 0 if odd else 64
            r0 = 64 if odd else 0
            zr = zrow.next()
            act(zr[zp:zp + 1, 0:N], Op[zp:zp + 1, 0:N], AF.Ln)
            zr2 = zrow.next()
            act(zr2[zp:zp + 1, 0:N], zr[zp:zp + 1, 0:N], AF.Exp, scale=-1.0)
            bc = bc_ring.next()
            mm(bc[:, 0:N], ones_f[zp:zp + 1, :], zr2[zp:zp + 1, 0:N])
            osb = Osb.next()
            act(osb[r0:r0 + 64, 0:N], Op[r0:r0 + 64, 0:N], AF.Copy)
            tt(out_sb[r0:r0 + 64, 0:N], osb[r0:r0 + 64, 0:N], bc[r0:r0 + 64, 0:N], ALU.mult)

        def v_slot(kt, h):
            odd = h % 2
            return VA[:, kt, h, 0:128] if odd else VA[:, kt, h, 0:65]

        def v_dst(kt, h):
            odd = h % 2
            return VA[:, kt, h, 64:128] if odd else VA[:, kt, h, 0:64]

        def init_va():
            mset(VA.rearrange("p k h c -> p (k h c)"), 0.0, eng='dve')
            for h in range(4):
                col = 0 if h % 2 else 64
                mset(VA[:, :, h, col:col + 1], 1.0, eng='dve')

        for l in range(n_layers):
            with_ctx = l < L - 1
            li = lambda_init(l)
            q_tbs = TBS if with_ctx else TBS[:4]
            ss_ring = Ring([PB(3), PB(4)])
            rot_ring = Ring([PB(5)])
            raw_ring = Ring([PB(0), PB(1), PB(2)])
            vp_ring = Ring([PB(int(_os.environ.get("VPB", "6")))])
            s_ring = Ring([PB(0), PB(1), PB(2)])
            o_ring = Ring([PB(3), PB(4)])
            bc_ring = Ring([PB(5)])

            def pvc(name, j=0, rows=128):
                c0, w = PV[name]
                return pv[0:rows, l, c0 + j:c0 + j + 1]

            for which, gname in ((0, 'g_mix'), (1, 'g_ffn')):
                c0, _ = PV[gname]
                for w in range(2):
                    sc = modT[:, l, (3 * which + 1) * 8:(3 * which + 2) * 8, w]
                    stt(Gv[:, which, :, w], sc, 1.0, pv[:, l, c0:c0 + 8], ALU.add, ALU.mult)
            c0, _ = PV['lvec']
            lv = pv[:, l, c0:c0 + 128].rearrange("p (a d) -> p a d", a=4)
            prod = tf(0)[:, 0:64].rearrange("p (a d) -> p a d", a=2)
            tt(prod[:, 0, :], lv[:, 0, :], lv[:, 1, :], ALU.mult)
            tt(prod[:, 1, :], lv[:, 2, :], lv[:, 3, :], ALU.mult)
            S.add('dve', lambda E, prod=prod: E.tensor_reduce(out=misc[:, 0:2], in_=prod, axis=AX.X, op=ALU.add),
                  reads=[prod], writes=[misc[:, 0:2]])
            act(misc[:, 2:4], misc[:, 0:2], AF.Exp)
            tt(misc[:, 4:5], misc[:, 3:4], misc[:, 2:3], ALU.subtract)
            ts(misc[:, 5:6], misc[:, 4:5], float(-li), None, ALU.add)
            ts(misc[:, 6:7], pvc('dsub_g'), float(1.0 - li), None, ALU.mult)
            nlam = misc[:, 5:6]
            gsub = misc[:, 6:7]

            modulate(l, 0, load_x_block)
            ck('m1')
            init_va()
            ck('va')

            def proj_fm(out_ps, wcols, t0, n):
                for kc in range(8):
                    mm(out_ps, wA[:, kc, wcols[0]:wcols[1]], hT[:, kc, t0:t0 + n], start=(kc == 0), stop=(kc == 7))

            def proj_v(col0, heads, ncols_per_head=64, slot_of=None):
                nh = len(heads)
                for kt in range(NKT):
                    vp = vp_ring.next()
                    for kc in range(8):
                        mm(vp[:, 0:nh * 64], hT[:, kc, kt * 128:(kt + 1) * 128], wA[:, kc, col0:col0 + nh * 64],
                           start=(kc == 0), stop=(kc == 7))
                    for i, hs in enumerate(heads):
                        for h in hs:
                            vcopy(v_dst(kt, h), vp[:, i * 64:(i + 1) * 64], vp, True)

            def run_attention(mixer, head_q, head_k, krows, scale, tab_of=None, na=False, diff=False):
                for h in range(4):
                    odd = h % 2
                    chunk = 2 * mixer + h // 2
                    for tbi, (t0, n) in enumerate(q_tbs):
                        if tbi < 4:
                            if na:
                                kts = list(NA_KT[tbi]) + [16, 17]
                            else:
                                kts = list(range(NKT))
                        else:
                            kts = [16, 17]
                        vrows = 128 if odd else 65
                        if not diff:
                            tiles = []
                            for kt in kts:
                                tab = None
                                if na and kt < 16:
                                    i0 = 8 * tbi - 2 * kt + 7
                                    tab = natab[:, h, (i0 + 3) * 64:(i0 + 3) * 64 + 512]
                                tiles.append((KT[0:krows, head_k(h), kt * 128:(kt + 1) * 128], v_slot(kt, h), tab))
                            Op = attention(QT[0:krows, h, t0:t0 + n], tiles, n, scale, vrows)
                            pending.append(lambda Op=Op, odd=odd, n=n, chunk=chunk, t0=t0: normalize(Op, odd, n, mixT[:, chunk, t0:t0 + n]))
                        else:
                            r0 = 64 if odd else 0
                            for pr in range(2):
                                tiles = [(KT[32 * pr:32 * pr + 32, h, kt * 128:(kt + 1) * 128], v_slot(kt, h), None) for kt in kts]
                                Op = attention(QT[32 * pr:32 * pr + 32, h, t0:t0 + n], tiles, n, scale, vrows)
                                if pr == 0:
                                    pending.append(lambda Op=Op, odd=odd, n=n: normalize(Op, odd, n, ABo[0]))
                                else:
                                    def fin(Op=Op, odd=odd, n=n, r0=r0, chunk=chunk, t0=t0):
                                        normalize(Op, odd, n, ABo[1])
                                        o = ABo[2]
                                        stt(o[r0:r0 + 64, 0:n], ABo[1][r0:r0 + 64, 0:n], nlam[r0:r0 + 64, :], ABo[0][r0:r0 + 64, 0:n], ALU.mult, ALU.add)
                                        sq = sqr.next()
                                        act(sq[r0:r0 + 64, 0:n], o[r0:r0 + 64, 0:n], AF.Square)
                                        ssp = PB(6)
                                        mm(ssp[r0:r0 + 64, 0:n], blk64[r0:r0 + 64, r0:r0 + 64], sq[r0:r0 + 64, 0:n])
                                        ln = lnr.next()
                                        act(ln[r0:r0 + 64, 0:n], ssp[r0:r0 + 64, 0:n], AF.Ln, scale=1.0 / 64, bias=float(EPS))
                                        rs = rsr.next()
                                        act(rs[r0:r0 + 64, 0:n], ln[r0:r0 + 64, 0:n], AF.Exp, scale=-0.5)
                                        stt(mixT[r0:r0 + 64, chunk, t0:t0 + n], o[r0:r0 + 64, 0:n], gsub[r0:r0 + 64, :], rs[r0:r0 + 64, 0:n],
                                            ALU.mult, ALU.mult)
                                    pending.append(fin)
                    flush_pending()

            dma('pool', wA[:, :, 0:416], win_d[l, :, 0:416].rearrange("(a p) n -> p a n", p=128))
            dma('pool', wuq_sb, wuq_d[l].rearrange("(a p) n -> p a n", p=128))
            dma('pool', wukv_sb, wukv_d[l])
            dma('sp', ropeT, tabs_d[0])
            for tbi, (t0, n) in enumerate(TBS):
                lat = tbi < 4
                raws = []
                for c in range(2):
                    rp = raw_ring.next()[:, 0:n]
                    proj_fm(rp, (c * 128, (c + 1) * 128), t0, n)
                    raws.append(rp)
                ssp = ss_ring.next()[:, 0:n]
                for c in range(2):
                    sq = sqr.next()[:, 0:n]
                    act(sq, raws[c], AF.Square)
                    mm(ssp, allones, sq, start=(c == 0), stop=(c == 1))
                ln = lnr.next()[:, 0:n]
                act(ln, ssp, AF.Ln, scale=1.0 / 256, bias=float(EPS))
                rs = rsr.next()[:, 0:n]
                act(rs, ln, AF.Exp, scale=-0.5)
                for c in range(2):
                    stt(cqn[:, c, 0:n], raws[c], pvc('qa_g', c), rs, ALU.mult, ALU.mult)
                for h in range(4):
                    rp = raw_ring.next()[0:96, 0:n]
                    for c in range(2):
                        mm(rp, wuq_sb[:, c, h * 96:(h + 1) * 96], cqn[:, c, 0:n], start=(c == 0), stop=(c == 1))
                    rope = (Rmla[0:96, 0:96], ropeT[0:96, 0, t0:t0 + n], ropeT[0:96, 1, t0:t0 + n]) if lat else None
                    norm_rope(rp, 96, n, allones[0:96, 0:96], 1.0 / 96, pvc('mq_g', 0, 96), QT[0:96, h, t0:t0 + n], rope)
                rp = raw_ring.next()[:, 0:n]
                proj_fm(rp, (256, 384), t0, n)
                norm_rope(rp, 128, n, allones, 1.0 / 128, pvc('kva_g'), ckvnT[:, t0:t0 + n], None)
                rpk = raw_ring.next()
                for kc in range(8):
                    mm(rpk[64:96, 0:n], wA[:, kc, 384:416], hT[:, kc, t0:t0 + n], start=(kc == 0), stop=(kc == 7))
                krs = t2r.next()
                act(krs[64:96, 0:n], rpk[64:96, 0:n], AF.Copy)
                for h in range(4):
                    rp = raw_ring.next()[0:64, 0:n]
                    mm(rp, wukv_sb[:, h * 128:h * 128 + 64], ckvnT[:, t0:t0 + n])
                    kr_ = kraw.next()
                    act(kr_[0:64, 0:n], rp, AF.Copy)
                    cp(kr_[64:96, 0:n], krs[64:96, 0:n])
                    rope = (Rmla[0:96, 0:96], ropeT[0:96, 0, t0:t0 + n], ropeT[0:96, 1, t0:t0 + n]) if lat else None
                    norm_rope(kr_[0:96, 0:n], 96, n, allones[0:96, 0:96], 1.0 / 96, pvc('mk_g', 0, 96), KT[0:96, h, t0:t0 + n], rope)
            ck('mlap')
            for kt in range(NKT):
                vp = vp_ring.next()
                for h in range(4):
                    mm(vp[:, h * 64:(h + 1) * 64], ckvnT[:, kt * 128:(kt + 1) * 128], wukv_sb[:, h * 128 + 64:h * 128 + 128])
                for h in range(4):
                    if _os.environ.get('NOVCP'):
                        continue
                    vcopy(v_dst(kt, h), vp[:, h * 64:(h + 1) * 64], vp, True)
            ck('mlav')
            run_attention(0, None, lambda h: h, 96, 96 ** -0.5)
            ck('mla')

            dma('pool', wA[:, :, 0:768], win_d[l, :, 416:1184].rearrange("(a p) n -> p a n", p=128))
            dma('sp', ropeT, tabs_d[1])
            for tbi, (t0, n) in enumerate(TBS):
                lat = tbi < 4
                rope = (Rdiff[0:64, 0:64], ropeT[0:64, 0, t0:t0 + n], ropeT[0:64, 1, t0:t0 + n]) if lat else None
                for h in range(4):
                    rp = raw_ring.next()[0:64, 0:n]
                    proj_fm(rp, (h * 64, h * 64 + 64), t0, n)
                    norm_rope(rp, 64, n, blk32[0:64, 0:64], 1.0 / 32, pvc('dq_g', 0, 64), QT[0:64, h, t0:t0 + n], rope)
                    rp = raw_ring.next()[0:64, 0:n]
                    proj_fm(rp, (256 + h * 64, 256 + h * 64 + 64), t0, n)
                    norm_rope(rp, 64, n, blk32[0:64, 0:64], 1.0 / 32, pvc('dk_g', 0, 64), KT[0:64, h, t0:t0 + n], rope)
            proj_v(512, [[0], [1], [2], [3]])
            run_attention(1, None, lambda h: h, 32, 32 ** -0.5, diff=True)
            ck('diff')

            dma('pool', wA[:, :, 0:768], win_d[l, :, 1184:1952].rearrange("(a p) n -> p a n", p=128))
            for h in range(4):
                dma('sp', nastg, nab_d[l, h])
                act(natab[:, h, :], nastg, AF.Exp)
                for (c0_, c1_) in ((0, TL), (TL, T)):
                    dma('pool', QT[64:96, h, c0_:c1_], aug_d[0][:, c0_:c1_])
                    dma('pool', KT[64:96, h, c0_:c1_], aug_d[1][:, c0_:c1_])
            for tbi, (t0, n) in enumerate(TBS):
                for h in range(4):
                    rp = raw_ring.next()[0:64, 0:n]
                    proj_fm(rp, (h * 64, h * 64 + 64), t0, n)
                    norm_rope(rp, 64, n, allones[0:64, 0:64], 1.0 / 64, pvc('nq_g', 0, 64), QT[0:64, h, t0:t0 + n], None)
                    rp = raw_ring.next()[0:64, 0:n]
                    proj_fm(rp, (256 + h * 64, 256 + h * 64 + 64), t0, n)
                    norm_rope(rp, 64, n, allones[0:64, 0:64], 1.0 / 64, pvc('nk_g', 0, 64), KT[0:64, h, t0:t0 + n], None)
            proj_v(512, [[0], [1], [2], [3]])
            run_attention(2, None, lambda h: h, 96, 64 ** -0.5, na=True)
            ck('na')

            dma('pool', wA[:, :, 0:512], win_d[l, :, 1952:2464].rearrange("(a p) n -> p a n", p=128))
            dma('sp', ropeT, tabs_d[2])
            for tbi, (t0, n) in enumerate(TBS):
                lat = tbi < 4
                rope = (Rgqa[0:64, 0:64], ropeT[0:64, 0, t0:t0 + n], ropeT[0:64, 1, t0:t0 + n]) if lat else None
                for h in range(4):
                    rp = raw_ring.next()[0:64, 0:n]
                    proj_fm(rp, (h * 64, h * 64 + 64), t0, n)
                    norm_rope(rp, 64, n, allones[0:64, 0:64], 1.0 / 64, pvc('gq_g', 0, 64), QT[0:64, h, t0:t0 + n], rope)
                for g in range(2):
                    rp = raw_ring.next()[0:64, 0:n]
                    proj_fm(rp, (256 + g * 64, 256 + g * 64 + 64), t0, n)
                    norm_rope(rp, 64, n, allones[0:64, 0:64], 1.0 / 64, pvc('gk_g', 0, 64), KT[0:64, g, t0:t0 + n], rope)
            ck('gqap')
            proj_v(384, [[0, 1], [2, 3]])
            ck('gqav')
            run_attention(3, None, lambda h: h // 2, 64, 64 ** -0.5)
            ck('gqa')

            dma('pool', wout_sb, wout_d[l].rearrange("(a p) n -> p a n", p=128))
            all_ps = Ring([PB(i) for i in range(7)])
            ss_ring = Ring([PB(5), PB(6)])
            op_ring = Ring([PB(i) for i in range(5)])

            def outproj_loader(tbi, t0, n):
                dma('sp', xblk[:, :, 0:n], xT_d[:, :, t0:t0 + n], reads=xkeys(tbi))
                w = 0 if tbi < 4 else 1
                for j in range(8):
                    pb = op_ring.next()[:, 0:n]
                    for c in range(8):
                        mm(pb, wout_sb[:, c, j * 128:(j + 1) * 128], mixT[:, c, t0:t0 + n], start=(c == 0), stop=(c == 7))
                    stt(xblk[:, j, 0:n], pb, modT[:, l, 2 * 8 + j, w:w + 1], xblk[:, j, 0:n], ALU.mult, ALU.add)
                dma('sp', xT_d[:, :, t0:t0 + n], xblk[:, :, 0:n], writes=xkeys(tbi))

            def outproj_loader_lastctx(tbi, t0, n):
                if tbi == 4:
                    load_x_block(tbi, t0, n)
                else:
                    outproj_loader(tbi, t0, n)

            modulate(l, 1, outproj_loader if with_ctx else outproj_loader_lastctx)

            ck('outproj')
            f_tbs = TBS if with_ctx else TBS[:4]
            for ub in ubuf + ubuf2:
                mset(ub[:, 0:1], 0.0, eng='pool')
                mset(ub[:, 2049:2051], 0.0, eng='pool')
                mset(ub[:, 2307:2308], 0.0, eng='pool')
            up_ring = Ring([PB(0), PB(1), PB(2), PB(3)])
            dn_ring = Ring([PB(4), PB(5), PB(6)])
            cring = Ring(ctile)

            def ucol(t0):
                return (1 + t0) if t0 < TL else (2051 + t0 - TL)

            for (g0, gn) in FFG:
                wn = gn * 128
                dma('pool', wup_sb[:, :, 0, 0:wn], wup_d[l, :, g0 * 128:g0 * 128 + wn].rearrange("(a p) n -> p a n", p=128))
                dma('pool', wup_sb[:, :, 1, 0:wn], wup_d[l, :, DFF + g0 * 128:DFF + g0 * 128 + wn].rearrange("(a p) n -> p a n", p=128))
                dma('pool', wdn_sb[:, 0:gn, :], wdn_d[l, g0 * 128:(g0 + gn) * 128, :].rearrange("(a p) n -> p a n", p=128))
                for cl in range(gn):
                    c = g0 + cl
                    ub = ubuf if c % 2 == 0 else ubuf2
                    for (t0, n) in f_tbs:
                        u0 = ucol(t0)
                        for ag in range(2):
                            pb = up_ring.next()[:, 0:n]
                            for kc in range(8):
                                mm(pb, wup_sb[:, kc, ag, cl * 128:(cl + 1) * 128], hT[:, kc, t0:t0 + n], start=(kc == 0), stop=(kc == 7))
                            act(ub[ag][:, u0:u0 + n], pb, AF.Copy)
                    for (t0, n) in f_tbs:
                        u0 = ucol(t0)
                        outs = []
                        for ag in range(2):
                            col = c if ag == 0 else NCH + c
                            ct = cring.next()[:, 0:n]
                            act(ct, ub[ag][:, u0:u0 + n], AF.Identity, scale=pvc('cw1', col), bias=pvc('cb', col))
                            stt(ct, ub[ag][:, u0 - 1:u0 - 1 + n], pvc('cw0', col), ct, ALU.mult, ALU.add)
                            stt(ct, ub[ag][:, u0 + 1:u0 + 1 + n], pvc('cw2', col), ct, ALU.mult, ALU.add)
                            outs.append(ct)
                        act(outs[1], outs[1], AF.Silu)
                        tt(actT[:, cl, t0:t0 + n], outs[1], outs[0], ALU.mult, eng='pool')
                for tbi, (t0, n) in enumerate(f_tbs):
                    w = 0 if tbi < 4 else 1
                    for j in range(8):
                        pb = dn_ring.next()[:, 0:n]
                        for cl in range(gn):
                            mm(pb, wdn_sb[:, cl, j * 128:(j + 1) * 128], actT[:, cl, t0:t0 + n], start=(cl == 0), stop=(cl == gn - 1))
                        xt = xring.next()[:, 0:n]
                        key = "xf:%d:%d" % (tbi, j)
                        dma('sp', xt, xT_d[:, j, t0:t0 + n], reads=[xkey(tbi), key])
                        stt(xt, pb, modT[:, l, 5 * 8 + j, w:w + 1], xt, ALU.mult, ALU.add)
                        dma('sp', xT_d[:, j, t0:t0 + n], xt, writes=[key])

    except StopBuild:
        pass
    import os as _os
    for tbi, (t0, n) in enumerate(TBS[:4]):
        dma('sp', xblk[:, :, 0:n], xT_d[:, :, t0:t0 + n], reads=xkeys(tbi))
        if _os.environ.get('SKIP_FINAL'):
            dma('sp', y_d[t0:t0 + n, :].rearrange("(a p) d -> p a d", p=128), xblk[:, 0:4, :].rearrange("p a (b c) -> p (a b) c", b=1)[:, :, :].rearrange("p a c -> p a c") if False else xld[:, 0:4, :])
            continue
        for a in range(4):
            for half in range(2):
                pb = PB((a * 2 + half) % int(_os.environ.get("NB", "6")))
                for jj in range(4):
                    j = half * 4 + jj
                    tr(pb[:, jj * 128:(jj + 1) * 128], xblk[:, j, a * 128:(a + 1) * 128], ident)
                if half == 0:
                    act(xld[:, a, 0:512], pb, AF.Copy)
                else:
                    cp(xld[:, a, 512:1024], pb)
        dma('sp', y_d[t0:t0 + n, :].rearrange("(a p) d -> p a d", p=128), xld[:, 0:4, :])
    if dbg:
        for tbi, (t0, n) in enumerate(TBS):
            dma('sp', xblk[:, :, 0:n], xT_d[:, :, t0:t0 + n], reads=xkeys(tbi))
            dma('sp', dbg_d[:, :, t0:t0 + n], xblk[:, :, 0:n])
    S.emit()
    return nc, S


def _rope_tab(rot):
    nf = rot // 4
    inv = np.power(np.float32(10000.0), -np.arange(nf, dtype=np.float32) / np.float32(nf)).astype(np.float32)
    t = np.arange(TL)
    row = (t // 64).astype(np.float32)
    col = (t % 64).astype(np.float32)
    ar = row[:, None] * inv
    ac = col[:, None] * inv
    ang = np.concatenate([ar, ar, ac, ac], axis=-1).astype(np.float32)
    return np.cos(ang).T.astype(np.float32), np.sin(ang).T.astype(np.float32)


def _constants():
    cf = np.zeros((128, 256), np.float32)
    cf[:, 0:128] = np.eye(128, dtype=np.float32)
    cf[:, 128:256] = 1.0
    cb = np.zeros((128, 6, 128), np.float32)
    cb[:, 0, :] = 1.0
    for b in range(4):
        cb[b * 32:(b + 1) * 32, 1, b * 32:(b + 1) * 32] = 1.0
    for b in range(2):
        cb[b * 64:(b + 1) * 64, 2, b * 64:(b + 1) * 64] = 1.0

    def add_rot(Rm, base, n):
        for i in range(n):
            Rm[base + i + n, base + i] = -1.0
            Rm[base + i, base + i + n] = 1.0
            Rm[base + 3 * n + i, base + 2 * n + i] = -1.0
            Rm[base + 2 * n + i, base + 3 * n + i] = 1.0
    add_rot(cb[:, 3, :], 64, 8)
    add_rot(cb[:, 4, :], 0, 8)
    add_rot(cb[:, 4, :], 32, 8)
    add_rot(cb[:, 5, :], 0, 16)
    aug = np.zeros((2, 32, T), np.float32)
    for q in range(TL):
        qr = q // 64
        rs = min(max(qr - 4, 0), 24)
        aug[0, :, q] = -BIG
        aug[0, rs:rs + 8, q] = 0.0
        aug[1, qr, q] = 1.0
    tabs = np.zeros((3, 128, 2, TL), np.float32)
    c32, s32 = _rope_tab(32)
    c64, s64 = _rope_tab(64)
    tabs[0, 0:64, 0, :] = 1.0
    tabs[0, 64:96, 0, :] = c32
    tabs[0, 64:96, 1, :] = s32
    tabs[1, 0:32, 0, :] = c32
    tabs[1, 32:64, 0, :] = c32
    tabs[1, 0:32, 1, :] = s32
    tabs[1, 32:64, 1, :] = s32
    tabs[2, 0:64, 0, :] = c64
    tabs[2, 0:64, 1, :] = s64
    return cf, cb.reshape(128, 768), aug, tabs


def _na_index():
    idx = np.full((128, 22, 64), 15 * 31, np.int64)
    for p in range(128):
        half, kc = p // 64, p % 64
        for pos in range(22):
            i = pos - 3 - half
            if i < 0 or i > 14:
                continue
            for qc in range(64):
                ws = min(max(qc - 8, 0), 48)
                if ws <= kc < ws + 16:
                    co = min(max(kc - qc + 15, 0), 30)
                    idx[p, pos, qc] = (14 - i) * 31 + co
    return idx.reshape(128, NAW)


def _tile_rows(v, reps, rows=128):
    out = np.zeros((rows,), np.float32)
    t = np.tile(np.asarray(v, np.float32), reps)
    out[:t.shape[0]] = t
    return out


def _prep_shared(inp):
    cf, cb, aug, tabs = _constants()
    pvA = np.zeros((128, L, NV), np.float32)

    def put(name, l, arr2d):
        c0, w = PV[name]
        pvA[:, l, c0:c0 + w] = arr2d

    for l in range(L):
        put('g_mix', l, inp['g_mix'][l].reshape(8, 128).T)
        put('g_ffn', l, inp['g_ffn'][l].reshape(8, 128).T)
        put('qa_g', l, inp['mla_q_a_g'][l].reshape(2, 128).T)
        put('kva_g', l, inp['mla_kv_a_g'][l].reshape(1, 128).T)
        put('mq_g', l, _tile_rows(inp['mla_q_g'][l], 1)[:, None])
        put('mk_g', l, _tile_rows(inp['mla_k_g'][l], 1)[:, None])
        put('dq_g', l, _tile_rows(inp['diff_q_g'][l], 4)[:, None])
        put('dk_g', l, _tile_rows(inp['diff_k_g'][l], 4)[:, None])
        put('dsub_g', l, _tile_rows(inp['diff_subln_g'][l], 2)[:, None])
        put('nq_g', l, _tile_rows(inp['na_q_g'][l], 2)[:, None])
        put('nk_g', l, _tile_rows(inp['na_k_g'][l], 2)[:, None])
        put('gq_g', l, _tile_rows(inp['gqa_q_g'][l], 2)[:, None])
        put('gk_g', l, _tile_rows(inp['gqa_k_g'][l], 2)[:, None])
        for i in range(3):
            put('cw%d' % i, l, inp['conv_w'][l, i].reshape(44, 128).T)
        put('cb', l, inp['conv_b'][l].reshape(44, 128).T)
        lv = np.concatenate([inp['diff_lq1'][l], inp['diff_lk1'][l], inp['diff_lq2'][l], inp['diff_lk2'][l]]).astype(np.float32)
        put('lvec', l, np.broadcast_to(lv[None, :], (128, 128)))
    idx = _na_index()
    nab = np.zeros((L, 4, 128, NAW), np.float32)
    for l in range(L):
        for h in range(4):
            src = np.concatenate([np.asarray(inp['na_rpb'][l, h], np.float32).ravel(), np.array([-10000.0], np.float32)])
            nab[l, h] = src[idx]
    f = lambda a: np.ascontiguousarray(np.asarray(a, np.float32))
    return {
        "w_mod": f(inp['w_mod']), "b_mod": f(inp['b_mod']), "w_in": f(inp['w_in']), "w_out": f(inp['w_out']),
        "w_uq": f(inp['mla_w_uq']), "w_ukv": f(inp['mla_w_ukv']), "w_up": f(inp['w_up']), "w_down": f(inp['w_down']),
        "pv": np.ascontiguousarray(pvA.reshape(128, L * NV)), "cf32": cf, "cbf": np.ascontiguousarray(cb),
        "aug": aug, "tabs": tabs, "nab": nab,
    }


_CACHE = {}


def kernel(**inp):
    n_layers = inp.pop('_n_layers', L)
    dbg = inp.pop('_dbg', False)
    stop = inp.pop('_stop', None)
    ncores = inp.pop('_ncores', 8)
    key = (n_layers, dbg, stop)
    if key not in _CACHE:
        _CACHE[key] = build_program(n_layers, dbg, stop)[0]
    nc = _CACHE[key]
    shared = _prep_shared(inp)
    x = np.asarray(inp['x'], np.float32)
    ctx = np.asarray(inp['ctx'], np.float32)
    c = np.asarray(inp['c'], np.float32)
    cc = np.asarray(inp['c_ctx'], np.float32)
    in_maps = []
    for b in range(ncores):
        m = dict(shared)
        m["xc"] = np.ascontiguousarray(np.concatenate([x[b], ctx[b]], axis=0))
        cT = np.zeros((128, 8, 2), np.float32)
        cT[:, :, 0] = c[b].reshape(8, 128).T
        cT[:, :, 1] = cc.reshape(8, 128).T
        m["cT"] = np.ascontiguousarray(cT.reshape(128, 16))
        in_maps.append(m)
    res = run_bass_kernel_spmd(nc, in_maps, core_ids=list(range(ncores)))
    out = np.stack([np.asarray(r["y"], np.float32) for r in res.results], axis=0)
    if dbg:
        kernel.dbg = [np.asarray(r["dbg"], np.float32) for r in res.results]
    return out
```

```python
import math
import numpy as np
import concourse.bass as bass
import concourse.mybir as mybir
from concourse.bass_utils import run_bass_kernel_spmd

F32 = mybir.dt.float32
BF16 = mybir.dt.bfloat16
AF = mybir.ActivationFunctionType
ALU = mybir.AluOpType
AX = mybir.AxisListType

ENGS = ['pe', 'act', 'dve', 'pool', 'sp']
CELL = 256
_ESZ = {}


def esz(dt):
    if dt not in _ESZ:
        _ESZ[dt] = mybir.dt.size(dt)
    return _ESZ[dt]


def ap_cells(ap):
    space = str(ap.space)
    sp = 0 if space == 'SB' else 1
    dims = ap.ap
    pstep, pcount = dims[0]
    e = esz(ap.dtype)
    off = ap.offset
    p0 = off // pstep
    foff = off % pstep
    ranges = [(foff, foff + 1)]
    for (st, cnt) in dims[1:]:
        if cnt <= 1:
            continue
        if len(ranges) * cnt <= 512 and abs(st) * e >= CELL:
            ranges = [(lo + i * st, hi + i * st) for (lo, hi) in ranges for i in range(cnt)]
        else:
            ext = (cnt - 1) * st
            if ext >= 0:
                ranges = [(lo, hi + ext) for (lo, hi) in ranges]
            else:
                ranges = [(lo + ext, hi) for (lo, hi) in ranges]
    cs = set()
    for (lo, hi) in ranges:
        c0 = (lo * e) // CELL
        c1 = (hi * e - 1) // CELL
        for c in range(c0, c1 + 1):
            cs.add(c)
    if sp == 1:
        return sorted(set(4 * 4096 + (c * CELL) // 2048 for c in cs))
    q0 = p0 // 32
    q1 = (p0 + pcount - 1) // 32
    out = []
    for q in range(q0, q1 + 1):
        base = (sp * 4 + q) * 4096
        for c in cs:
            out.append(base + c)
    return out


class Sched:
    def __init__(self, nc, n_lanes=48, same_eng_sync=True):
        self.nc = nc
        self.ops = {e: [] for e in ENGS}
        self.cw = {}
        self.cr = {}
        self.known = {e: {} for e in ENGS}
        self.snap = {}
        self.n_lanes = n_lanes
        self.lane_count = [0] * n_lanes
        self.next_lane = 0
        self.same_eng_sync = same_eng_sync
        self.eng_sem = {e: nc.alloc_semaphore("sem_" + e) for e in ENGS}
        self.lane_sem = [nc.alloc_semaphore("lane%d" % i) for i in range(n_lanes)]

    def _cells(self, items):
        cs = []
        for it in items:
            if it is None:
                continue
            if isinstance(it, (str, tuple)):
                cs.append(it)
            else:
                cs.extend(ap_cells(it))
        return cs

    def add(self, eng, fn, reads=(), writes=(), dma=False):
        ops = self.ops[eng]
        idx = len(ops)
        rc = self._cells(reads)
        wc = self._cells(writes)
        need = {}

        def want(tok, war=False):
            key, seq = tok
            if key == eng:
                if eng == 'pe' or war or not self.same_eng_sync:
                    return
            if need.get(key, -1) < seq:
                need[key] = seq

        for c in rc:
            t = self.cw.get(c)
            if t is not None:
                want(t)
        for c in wc:
            t = self.cw.get(c)
            if t is not None:
                want(t)
            rs = self.cr.get(c)
            if rs:
                for k, s in rs.items():
                    want((k, s), war=True)
        if dma:
            lane = self.next_lane
            self.next_lane = (lane + 1) % self.n_lanes
            cnt = self.lane_count[lane] + 1
            self.lane_count[lane] = cnt
            tok = (('L', lane), cnt)
            if cnt > 1:
                want((('L', lane), cnt - 1))
        else:
            tok = (eng, idx)
        kn = self.known[eng]
        waits = []
        for key, seq in need.items():
            if kn.get(key, -1) >= seq:
                continue
            waits.append((key, seq))
            kn[key] = seq
            if isinstance(key, str):
                self.ops[key][seq]['signal'] = True
                sn = self.snap.get((key, seq))
                if sn:
                    for k2, s2 in sn.items():
                        if kn.get(k2, -1) < s2:
                            kn[k2] = s2
        if not dma:
            self.snap[(eng, idx)] = dict(kn)
        ops.append(dict(fn=fn, waits=waits, tok=tok, dma=dma, signal=False))
        for c in wc:
            self.cw[c] = tok
            self.cr[c] = {}
        for c in rc:
            d = self.cr.get(c)
            if d is None:
                d = {}
                self.cr[c] = d
            if d.get(tok[0], -1) < tok[1]:
                d[tok[0]] = tok[1]
        return tok

    def emit(self):
        nc = self.nc
        for e in ENGS:
            c = 0
            for op in self.ops[e]:
                if op['signal'] and not op['dma']:
                    c += 1
                    op['sigval'] = c
        emap = {'pe': 'tensor', 'act': 'scalar', 'dve': 'vector', 'pool': 'gpsimd', 'sp': 'sync'}

        def run(e, E):
            for op in self.ops[e]:
                for key, seq in op['waits']:
                    if isinstance(key, str):
                        E.wait_ge(self.eng_sem[key], self.ops[key][seq]['sigval'])
                    else:
                        E.wait_ge(self.lane_sem[key[1]], 16 * seq)
                inst = op['fn'](E)
                if op['dma']:
                    inst.then_inc(self.lane_sem[op['tok'][0][1]], 16)
                elif op['signal']:
                    inst.then_inc(self.eng_sem[e], 1)
            if e == 'sp':
                for i, cnt in enumerate(self.lane_count):
                    if cnt:
                        E.wait_ge(self.lane_sem[i], 16 * cnt)

        with nc.Block() as block:
            for e in ENGS:
                getattr(block, emap[e])(lambda E, e=e: run(e, E))

    def stats(self):
        return {e: (len(self.ops[e]), sum(len(o['waits']) for o in self.ops[e])) for e in ENGS}


class Ring:
    def __init__(self, items):
        self.items = list(items)
        self.i = 0

    def next(self):
        r = self.items[self.i]
        self.i = (self.i + 1) % len(self.items)
        return r


D = 1024
L = 4
TL = 2048
TC = 256
T = TL + TC
NKT = T // 128
TBS = [(0, 512), (512, 512), (1024, 512), (1536, 512), (2048, 256)]
IN_COLS = 2464
DFF = 2816
NCH = DFF // 128
EPS = 1e-6
BIG = 30000.0
NA_KT = {0: range(0, 6), 1: range(2, 10), 2: range(6, 14), 3: range(10, 16)}
NAW = 22 * 64
FFG = [(0, 4), (4, 4), (8, 4), (12, 4), (16, 4), (20, 2)]

PV = {}
_c = 0
for _n, _w in [('g_mix', 8), ('g_ffn', 8), ('qa_g', 2), ('kva_g', 1), ('mq_g', 1), ('mk_g', 1),
               ('dq_g', 1), ('dk_g', 1), ('dsub_g', 1), ('nq_g', 1), ('nk_g', 1), ('gq_g', 1), ('gk_g', 1),
               ('cw0', 44), ('cw1', 44), ('cw2', 44), ('cb', 44), ('lvec', 128)]:
    PV[_n] = (_c, _w)
    _c += _w
NV = _c


def lambda_init(l):
    return 0.8 - 0.6 * math.exp(-0.3 * l)


class StopBuild(Exception):
    pass


def build_program(n_layers=L, dbg=False, stop_after=None):
    nc = bass.Bass("TRN2", target_bir_lowering=False)

    def ck(name):
        if stop_after == name:
            raise StopBuild()

    def din(name, shape):
        return nc.dram_tensor(name, list(shape), F32, kind="ExternalInput").ap()

    xc_d = din("xc", [T, D])
    cT_d = din("cT", [128, 16])
    wmod_d = din("w_mod", [L, D, 6 * D])
    bmod_d = din("b_mod", [L, 6 * D])
    win_d = din("w_in", [L, D, IN_COLS])
    wout_d = din("w_out", [L, D, D])
    wuq_d = din("w_uq", [L, 256, 384])
    wukv_d = din("w_ukv", [L, 128, 512])
    wup_d = din("w_up", [L, D, 2 * DFF])
    wdn_d = din("w_down", [L, DFF, D])
    pv_d = din("pv", [128, L * NV])
    cf_d = din("cf32", [128, 256])
    cb_d = din("cbf", [128, 6 * 128])
    aug_d = din("aug", [2, 32, T])
    tabs_d = din("tabs", [3, 128, 2, TL])
    nab_d = din("nab", [L, 4, 128, NAW])
    y_d = nc.dram_tensor("y", [TL, D], F32, kind="ExternalOutput").ap()
    xT_d = nc.dram_tensor("xT_scr", [128, 8, T], F32).ap()
    if dbg:
        dbg_d = nc.dram_tensor("dbg", [128, 8, T], F32, kind="ExternalOutput").ap()

    ARENA_F32 = 53000
    arena = nc.alloc_sbuf_tensor("arena", [128, ARENA_F32], F32)
    psum = nc.alloc_psum_tensor("psum", [128, 4096], F32)
    S = Sched(nc)

    pos = [0]

    def alloc_b(nbytes):
        a = pos[0]
        n = (nbytes + 255) // 256 * 256
        pos[0] += n
        assert pos[0] <= ARENA_F32 * 4, pos[0]
        return a

    def f32v(boff, n):
        return arena[:, boff // 4: boff // 4 + n]

    def bf16v(boff, n):
        return arena[:, boff // 4: boff // 4 + (n + 1) // 2].bitcast(BF16)

    def PB(i):
        return psum[:, i * 512:(i + 1) * 512]

    ident = f32v(alloc_b(512), 128)
    ones_f = f32v(alloc_b(512), 128)
    cbf = bf16v(alloc_b(6 * 256), 6 * 128).rearrange("p (a b) -> p a b", a=6)
    allones, blk32, blk64, Rmla, Rdiff, Rgqa = [cbf[:, i, :] for i in range(6)]
    pv = f32v(alloc_b(L * NV * 4), L * NV).rearrange("p (l n) -> p l n", l=L)
    modT = f32v(alloc_b(L * 96 * 4), L * 96).rearrange("p (l s w) -> p l s w", l=L, s=48)
    cT = f32v(alloc_b(64), 16)
    scT_f = f32v(alloc_b(64), 16)
    scT = bf16v(alloc_b(32), 16).rearrange("p (a b) -> p a b", a=8)
    Gv = f32v(alloc_b(4 * 16 * 4), 64).rearrange("p (a j w) -> p a j w", a=2, j=8)
    misc = f32v(alloc_b(64 * 4), 64)
    hT = bf16v(alloc_b(8 * T * 2), 8 * T).rearrange("p (a t) -> p a t", a=8)
    mix_off = alloc_b(8 * T * 2)
    mixT = bf16v(mix_off, 8 * T).rearrange("p (a t) -> p a t", a=8)
    qkv_off = alloc_b(3 * 4 * T * 2)
    QT = bf16v(qkv_off, 4 * T).rearrange("p (a t) -> p a t", a=4)
    KT = bf16v(qkv_off + 4 * T * 2, 4 * T).rearrange("p (a t) -> p a t", a=4)
    VA = bf16v(qkv_off + 8 * T * 2, NKT * 4 * 128).rearrange("p (k h c) -> p k h c", k=NKT, h=4)
    xblk = f32v(qkv_off + 16384, 8 * 512).rearrange("p (a n) -> p a n", a=8)
    xld = f32v(qkv_off + 32768, 4 * 1024).rearrange("p (a n) -> p a n", a=4)
    GN = 4
    fo = mix_off
    actT = bf16v(fo, GN * T).rearrange("p (a t) -> p a t", a=GN); fo += GN * T * 2
    UW = T + 4
    UWP = (UW * 4 + 255) // 256 * 256
    ubuf = [f32v(fo + i * UWP, UW) for i in range(2)]; fo += 2 * UWP
    assert fo <= qkv_off + 512, (fo, qkv_off)
    WUPB = 8 * 2 * GN * 128 * 2
    wup_bufs = [bf16v(qkv_off + 32768, 8 * 2 * GN * 128).rearrange("p (k g n) -> p k g n", k=8, g=2),
                bf16v(qkv_off + 512, 8 * 2 * GN * 128).rearrange("p (k g n) -> p k g n", k=8, g=2)]
    assert qkv_off + 32768 + WUPB <= qkv_off + 3 * 4 * T * 2 and 512 + WUPB <= 32768
    fo = qkv_off
    assert fo <= qkv_off + 3 * 4 * T * 2, (fo, qkv_off + 3 * 4 * T * 2)
    tab_off = alloc_b(2 * TL * 4)
    ropeT = f32v(tab_off, 2 * TL).rearrange("p (a t) -> p a t", a=2)
    natab = bf16v(tab_off, 4 * NAW).rearrange("p (h n) -> p h n", h=4)
    wA = bf16v(alloc_b(8 * 768 * 2), 8 * 768).rearrange("p (a n) -> p a n", a=8)
    wout_sb = bf16v(tab_off, 8 * 1024).rearrange("p (a n) -> p a n", a=8)
    ubuf2 = [f32v(tab_off + i * UWP, UW) for i in range(2)]
    assert 2 * UWP <= 2 * TL * 4 + 8 * 768 * 2
    wuq_sb = bf16v(alloc_b(2 * 384 * 2), 2 * 384).rearrange("p (a n) -> p a n", a=2)
    wukv_sb = bf16v(alloc_b(512 * 2), 512)
    ckvnT = bf16v(alloc_b(T * 2), T)
    cqn = bf16v(alloc_b(2 * 512 * 2), 2 * 512).rearrange("p (a n) -> p a n", a=2)
    r1_off = tab_off + 2 * UWP
    wdn_bufs = [bf16v(r1_off + i * GN * 1024 * 2, GN * 1024).rearrange("p (a n) -> p a n", a=GN) for i in range(2)]
    tmp_off = alloc_b(24 * 1024)
    assert r1_off + 2 * GN * 1024 * 2 <= tmp_off, (r1_off, tmp_off)
    krs_buf = f32v(alloc_b(2048), 512)
    sq_extra = alloc_b(2048)

    def tf(i):
        return f32v(tmp_off + i * 2048, 512)

    def tb16(i, half=0):
        return bf16v(tmp_off + i * 2048 + half * 1024, 512)

    kraw = Ring([tf(0), tf(1)])
    sqr = Ring([tb16(2, 0), tb16(2, 1), bf16v(sq_extra, 512), bf16v(sq_extra + 1024, 512)])
    lnr = Ring([tf(3), tf(4)])
    rsr = Ring([tf(5), tf(6)])
    qnr = Ring([tb16(7, 0), tb16(7, 1)])
    t1r = Ring([tf(8), tf(9)])
    t2r = Ring([tf(10), tf(11)])
    Pr = Ring([tb16(0, 0), tb16(0, 1), tb16(1, 0), tb16(1, 1)])
    P2r = Ring([tb16(2, 0), tb16(2, 1)])
    Osb = Ring([tf(3), tf(4)])
    zrow = Ring([tf(5), tf(6)])
    ABo = [tf(7), tf(8), tf(9)]
    xring = Ring([tf(8), tf(9), tf(10), tf(11)])
    ctile = [tf(0), tf(1), tf(2), tf(3)]
    nastg = f32v(tmp_off, NAW)
    wmr = Ring([bf16v(tmp_off + i * 4096, 2048) for i in range(3)])
    m_sb = f32v(qkv_off, 6 * D)
    bm_sb = f32v(qkv_off + 24576, 6 * D)

    def mm(out, lhsT, rhs, start=True, stop=True):
        S.add('pe', lambda E: E.matmul(out, lhsT, rhs, start=start, stop=stop), reads=[lhsT, rhs], writes=[out])

    def tr(out, in_, idn):
        S.add('pe', lambda E: E.transpose(out, in_, idn), reads=[in_, idn], writes=[out])

    def act(out, in_, func, scale=None, bias=None, eng='act'):
        kw = {}
        rd = [in_]
        if scale is not None:
            kw['scale'] = scale
            if not isinstance(scale, float):
                rd.append(scale)
        if bias is not None:
            kw['bias'] = bias
            if not isinstance(bias, float):
                rd.append(bias)
        S.add('act', lambda E: E.activation(out, in_, func, **kw), reads=rd, writes=[out])

    def stt(out, in0, scalar, in1, op0, op1):
        rd = [in0, in1] + ([] if isinstance(scalar, float) else [scalar])
        S.add('dve', lambda E: E.scalar_tensor_tensor(out=out, in0=in0, scalar=scalar, in1=in1, op0=op0, op1=op1),
              reads=rd, writes=[out])

    def tt(out, in0, in1, op, eng='dve'):
        S.add(eng, lambda E: E.tensor_tensor(out=out, in0=in0, in1=in1, op=op), reads=[in0, in1], writes=[out])

    def ts(out, in0, s1, s2, op0, op1=None, eng='dve'):
        rd = [in0] + [s for s in (s1, s2) if s is not None and not isinstance(s, float)]
        if op1 is None:
            S.add(eng, lambda E: E.tensor_scalar(out=out, in0=in0, scalar1=s1, scalar2=None, op0=op0), reads=rd, writes=[out])
        else:
            S.add(eng, lambda E: E.tensor_scalar(out=out, in0=in0, scalar1=s1, scalar2=s2, op0=op0, op1=op1), reads=rd, writes=[out])

    def cp(out, in_, eng='dve'):
        S.add(eng, lambda E: E.tensor_copy(out=out, in_=in_), reads=[in_], writes=[out])

    def vcopy(out, in_, bank, use_act):
        if use_act:
            S.add('act', lambda E: E.activation(out, in_, AF.Copy), reads=[bank], writes=[out])
        else:
            S.add('dve', lambda E: E.tensor_copy(out=out, in_=in_), reads=[bank], writes=[out])

    def mset(ap, val, eng='dve'):
        S.add(eng, lambda E: E.memset(ap, val), writes=[ap])

    def dma(q, out, in_, reads=None, writes=None):
        r = [in_] if reads is None else reads
        w = [out] if writes is None else writes
        r = [a for a in r if isinstance(a, (str, tuple)) or str(a.space) != 'DRAM']
        w = [a for a in w if isinstance(a, (str, tuple)) or str(a.space) != 'DRAM']
        S.add(q, lambda E: E.dma_start(out=out, in_=in_), reads=r, writes=w, dma=True)

    def xkey(tbi):
        return "x:%d" % tbi

    def xkeys(tbi):
        return ["x:%d" % tbi] + ["xf:%d:%d" % (tbi, j) for j in range(8)]

    try:
        import os as _os
        if _os.environ.get('SKIP_CONST'):
            raise StopBuild()
        dma('sp', ident, cf_d[:, 0:128])
        dma('sp', ones_f, cf_d[:, 128:256])
        dma('pool', cbf, cb_d.rearrange("p (a b) -> p a b", a=6))
        dma('sp', pv, pv_d.rearrange("p (l n) -> p l n", l=L))
        dma('sp', cT, cT_d)
        act(scT_f, cT, AF.Silu)
        cp(scT, scT_f.rearrange("p (a b) -> p a b", a=8))

        ck('c0')
        for tbi, (t0, n) in enumerate(TBS):
            ntt = n // 128
            dma('sp', xld[:, 0:ntt, :], xc_d[t0:t0 + n, :].rearrange("(a p) d -> p a d", p=128))
            for j in range(8):
                pb = PB(j % 2)
                for a in range(ntt):
                    tr(pb[:, a * 128:(a + 1) * 128], xld[:, a, j * 128:(j + 1) * 128], ident)
                if j % 2 == 0:
                    act(xblk[:, j, 0:n], pb[:, 0:n], AF.Copy)
                else:
                    cp(xblk[:, j, 0:n], pb[:, 0:n])
            dma('sp', xT_d[:, :, t0:t0 + n], xblk[:, :, 0:n], writes=xkeys(tbi))

        ck('xt')
        for l in range(n_layers):
            dma('sp', bm_sb[0:1, :], bmod_d[l:l + 1, :])
            dma('sp', bm_sb[1:2, :], bmod_d[l:l + 1, :])
            for cg in range(3):
                for kc in range(8):
                    piece = wmr.next()
                    dma('pool', piece, wmod_d[l, kc * 128:(kc + 1) * 128, cg * 2048:(cg + 1) * 2048])
                    for i in range(4):
                        mm(PB(i)[0:2, :], scT[:, kc, :], piece[:, i * 512:(i + 1) * 512], start=(kc == 0), stop=(kc == 7))
                for i in range(4):
                    c0 = cg * 2048 + i * 512
                    tt(m_sb[0:2, c0:c0 + 512], PB(i)[0:2, :], bm_sb[0:2, c0:c0 + 512], ALU.add)
            pt = PB(4)
            for s in range(48):
                tr(pt[:, 2 * s:2 * s + 2], m_sb[0:2, s * 128:(s + 1) * 128], ident[0:2, 0:2])
            cp(modT[:, l].rearrange("p s w -> p (s w)"), pt[:, 0:96])

        ck('mod')
        def rstd_from(src, R, N, ones_m, inv_d):
            sq = sqr.next()[0:R, 0:N]
            act(sq, src, AF.Square)
            ssp = ss_ring.next()[0:R, 0:N]
            mm(ssp, ones_m, sq)
            ln = lnr.next()[0:R, 0:N]
            act(ln, ssp, AF.Ln, scale=float(inv_d), bias=float(EPS))
            rs = rsr.next()[0:R, 0:N]
            act(rs, ln, AF.Exp, scale=-0.5)
            return rs

        def norm_rope(src, R, N, ones_m, inv_d, gain, out, rope=None):
            rs = rstd_from(src, R, N, ones_m, inv_d)
            if rope is None:
                stt(out, src, gain, rs, ALU.mult, ALU.mult)
                return
            Rm, cosT, sinT = rope
            qn = qnr.next()[0:R, 0:N]
            stt(qn, src, gain, rs, ALU.mult, ALU.mult)
            rp = rot_ring.next()[0:R, 0:N]
            mm(rp, Rm, qn)
            t1 = t1r.next()[0:R, 0:N]
            tt(t1, qn, cosT, ALU.mult, eng='pool')
            t2 = t2r.next()[0:R, 0:N]
            tt(t2, rp, sinT, ALU.mult)
            tt(out, t1, t2, ALU.add, eng='pool')

        def chain(src_fn, R, N, ones_m, inv_d, gain, out, rope=None):
            st = {}

            def A():
                st['src'] = src_fn()
                sq = sqr.next()[0:R, 0:N]
                act(sq, st['src'], AF.Square)
                st['sq'] = sq

            def B():
                ssp = ss_ring.next()[0:R, 0:N]
                mm(ssp, ones_m, st['sq'])
                ln = lnr.next()[0:R, 0:N]
                act(ln, ssp, AF.Ln, scale=float(inv_d), bias=float(EPS))
                rs = rsr.next()[0:R, 0:N]
                act(rs, ln, AF.Exp, scale=-0.5)
                if rope is None:
                    stt(out, st['src'], gain, rs, ALU.mult, ALU.mult)
                else:
                    qn = qnr.next()[0:R, 0:N]
                    stt(qn, st['src'], gain, rs, ALU.mult, ALU.mult)
                    st['qn'] = qn

            def C():
                Rm, cosT, sinT = rope
                qn = st['qn']
                rp = rot_ring.next()[0:R, 0:N]
                mm(rp, Rm, qn)
                t1 = t1r.next()[0:R, 0:N]
                tt(t1, qn, cosT, ALU.mult, eng='pool')
                t2 = t2r.next()[0:R, 0:N]
                tt(t2, rp, sinT, ALU.mult)
                tt(out, t1, t2, ALU.add, eng='pool')

            return [A, B, C if rope is not None else None]

        def run_pipe(tiles):
            n = len(tiles)
            for step in range(n + 2):
                if step < n and tiles[step][0]:
                    tiles[step][0]()
                if 0 <= step - 1 < n and tiles[step - 1][1]:
                    tiles[step - 1][1]()
                if 0 <= step - 2 < n and tiles[step - 2][2]:
                    tiles[step - 2][2]()

        def modulate(l, which, xsrc_loader):
            for tbi, (t0, n) in enumerate(TBS):
                w = 0 if tbi < 4 else 1
                xsrc_loader(tbi, t0, n)
                ssp = ss_ring.next()[:, 0:n]
                for j in range(8):
                    sq = sqr.next()[:, 0:n]
                    act(sq, xblk[:, j, 0:n], AF.Square)
                    mm(ssp, allones, sq, start=(j == 0), stop=(j == 7))
                ln = lnr.next()[:, 0:n]
                act(ln, ssp, AF.Ln, scale=1.0 / D, bias=float(EPS))
                rs = rsr.next()[:, 0:n]
                act(rs, ln, AF.Exp, scale=-0.5)
                for j in range(8):
                    t1 = t1r.next()[:, 0:n]
                    stt(t1, xblk[:, j, 0:n], Gv[:, which, j, w:w + 1], rs, ALU.mult, ALU.mult)
                    shift = modT[:, l, (3 * which) * 8 + j, w:w + 1]
                    act(hT[:, j, t0:t0 + n], t1, AF.Identity, bias=shift)

        def load_x_block(tbi, t0, n):
            dma('sp', xblk[:, :, 0:n], xT_d[:, :, t0:t0 + n], reads=xkeys(tbi))

        pending = []

        def flush_pending():
            while pending:
                pending.pop(0)()

        def attention(q_of, keytiles, N, scale, vrows, LOOK=2):
            Op = o_ring.next()
            nk = len(keytiles)
            Sps = {}

            def qk(i):
                Sps[i] = s_ring.next()[:, 0:N]
                mm(Sps[i], keytiles[i][0], q_of)

            for i in range(min(LOOK, nk)):
                qk(i)
            flush_pending()
            for i, (k_ap, v_ap, tab) in enumerate(keytiles):
                if i + LOOK < nk:
                    qk(i + LOOK)
                Sp = Sps.pop(i)
                P = Pr.next()[:, 0:N]
                act(P, Sp, AF.Exp, scale=float(scale))
                if tab is not None:
                    P2 = P2r.next()[:, 0:N]
                    tt(P2, P, tab[:, 0:N], ALU.mult)
                    P = P2
                mm(Op[0:vrows, 0:N], v_ap, P, start=(i == 0), stop=(i == nk - 1))
            return Op

        def normalize(Op, odd, N, out_sb):
            zp = 0 if odd else 64
            r0 = 64 if odd else 0
            zr = zrow.next()
            act(zr[zp:zp + 1, 0:N], Op[zp:zp + 1, 0:N], AF.Ln)
            zr2 = zrow.next()
            act(zr2[zp:zp + 1, 0:N], zr[zp:zp + 1, 0:N], AF.Exp, scale=-1.0)
            bc = bc_ring.next()
            mm(bc[:, 0:N], ones_f[zp:zp + 1, :], zr2[zp:zp + 1, 0:N])
            osb = Osb.next()
            act(osb[r0:r0 + 64, 0:N], Op[r0:r0 + 64, 0:N], AF.Copy)
            tt(out_sb[r0:r0 + 64, 0:N], osb[r0:r0 + 64, 0:N], bc[r0:r0 + 64, 0:N], ALU.mult)

        def v_slot(kt, h):
            odd = h % 2
            return VA[:, kt, h, 0:128] if odd else VA[:, kt, h, 0:65]

        def v_dst(kt, h):
            odd = h % 2
            return VA[:, kt, h, 64:128] if odd else VA[:, kt, h, 0:64]

        def init_va():
            mset(VA.rearrange("p k h c -> p (k h c)"), 0.0, eng='dve')
            for h in range(4):
                col = 0 if h % 2 else 64
                mset(VA[:, :, h, col:col + 1], 1.0, eng='dve')

        for l in range(n_layers):
            with_ctx = l < L - 1
            li = lambda_init(l)
            q_tbs = TBS if with_ctx else TBS[:4]
            ss_ring = Ring([PB(3), PB(4)])
            rot_ring = Ring([PB(5), PB(6)])
            raw_ring = Ring([PB(0), PB(1), PB(2)])
            vp_ring = Ring([PB(int(_os.environ.get("VPB", "6")))])
            s_ring = Ring([PB(0), PB(1), PB(2)])
            o_ring = Ring([PB(3), PB(4)])
            bc_ring = Ring([PB(5)])

            def pvc(name, j=0, rows=128):
                c0, w = PV[name]
                return pv[0:rows, l, c0 + j:c0 + j + 1]

            for which, gname in ((0, 'g_mix'), (1, 'g_ffn')):
                c0, _ = PV[gname]
                for w in range(2):
                    sc = modT[:, l, (3 * which + 1) * 8:(3 * which + 2) * 8, w]
                    stt(Gv[:, which, :, w], sc, 1.0, pv[:, l, c0:c0 + 8], ALU.add, ALU.mult)
            c0, _ = PV['lvec']
            lv = pv[:, l, c0:c0 + 128].rearrange("p (a d) -> p a d", a=4)
            prod = tf(0)[:, 0:64].rearrange("p (a d) -> p a d", a=2)
            tt(prod[:, 0, :], lv[:, 0, :], lv[:, 1, :], ALU.mult)
            tt(prod[:, 1, :], lv[:, 2, :], lv[:, 3, :], ALU.mult)
            S.add('dve', lambda E, prod=prod: E.tensor_reduce(out=misc[:, 0:2], in_=prod, axis=AX.X, op=ALU.add),
                  reads=[prod], writes=[misc[:, 0:2]])
            act(misc[:, 2:4], misc[:, 0:2], AF.Exp)
            tt(misc[:, 4:5], misc[:, 3:4], misc[:, 2:3], ALU.subtract)
            ts(misc[:, 5:6], misc[:, 4:5], float(-li), None, ALU.add)
            ts(misc[:, 6:7], pvc('dsub_g'), float(1.0 - li), None, ALU.mult)
            nlam = misc[:, 5:6]
            gsub = misc[:, 6:7]

            modulate(l, 0, load_x_block)
            ck('m1')
            init_va()
            ck('va')

            def proj_fm(out_ps, wcols, t0, n):
                for kc in range(8):
                    mm(out_ps, wA[:, kc, wcols[0]:wcols[1]], hT[:, kc, t0:t0 + n], start=(kc == 0), stop=(kc == 7))

            def proj_v(col0, heads, ncols_per_head=64, slot_of=None):
                nh = len(heads)
                for kt in range(NKT):
                    vp = vp_ring.next()
                    for kc in range(8):
                        mm(vp[:, 0:nh * 64], hT[:, kc, kt * 128:(kt + 1) * 128], wA[:, kc, col0:col0 + nh * 64],
                           start=(kc == 0), stop=(kc == 7))
                    for i, hs in enumerate(heads):
                        for h in hs:
                            vcopy(v_dst(kt, h), vp[:, i * 64:(i + 1) * 64], vp, True)

            def run_attention(mixer, head_q, head_k, krows, scale, tab_of=None, na=False, diff=False):
                for h in range(4):
                    odd = h % 2
                    chunk = 2 * mixer + h // 2
                    for tbi, (t0, n) in enumerate(q_tbs):
                        if tbi < 4:
                            if na:
                                kts = list(NA_KT[tbi]) + [16, 17]
                            else:
                                kts = list(range(NKT))
                        else:
                            kts = [16, 17]
                        vrows = 128 if odd else 65
                        if not diff:
                            tiles = []
                            for kt in kts:
                                tab = None
                                if na and kt < 16:
                                    i0 = 8 * tbi - 2 * kt + 7
                                    tab = natab[:, h, (i0 + 3) * 64:(i0 + 3) * 64 + 512]
                                tiles.append((KT[0:krows, head_k(h), kt * 128:(kt + 1) * 128], v_slot(kt, h), tab))
                            Op = attention(QT[0:krows, h, t0:t0 + n], tiles, n, scale, vrows)
                            pending.append(lambda Op=Op, odd=odd, n=n, chunk=chunk, t0=t0: normalize(Op, odd, n, mixT[:, chunk, t0:t0 + n]))
                        else:
                            r0 = 64 if odd else 0
                            for pr in range(2):
                                tiles = [(KT[32 * pr:32 * pr + 32, h, kt * 128:(kt + 1) * 128], v_slot(kt, h), None) for kt in kts]
                                Op = attention(QT[32 * pr:32 * pr + 32, h, t0:t0 + n], tiles, n, scale, vrows)
                                if pr == 0:
                                    pending.append(lambda Op=Op, odd=odd, n=n: normalize(Op, odd, n, ABo[0]))
                                else:
                                    def fin(Op=Op, odd=odd, n=n, r0=r0, chunk=chunk, t0=t0):
                                        normalize(Op, odd, n, ABo[1])
                                        o = ABo[2]
                                        stt(o[r0:r0 + 64, 0:n], ABo[1][r0:r0 + 64, 0:n], nlam[r0:r0 + 64, :], ABo[0][r0:r0 + 64, 0:n], ALU.mult, ALU.add)
                                        sq = sqr.next()
                                        act(sq[r0:r0 + 64, 0:n], o[r0:r0 + 64, 0:n], AF.Square)
                                        ssp = PB(6)
                                        mm(ssp[r0:r0 + 64, 0:n], blk64[r0:r0 + 64, r0:r0 + 64], sq[r0:r0 + 64, 0:n])
                                        ln = lnr.next()
                                        act(ln[r0:r0 + 64, 0:n], ssp[r0:r0 + 64, 0:n], AF.Ln, scale=1.0 / 64, bias=float(EPS))
                                        rs = rsr.next()
                                        act(rs[r0:r0 + 64, 0:n], ln[r0:r0 + 64, 0:n], AF.Exp, scale=-0.5)
                                        stt(mixT[r0:r0 + 64, chunk, t0:t0 + n], o[r0:r0 + 64, 0:n], gsub[r0:r0 + 64, :], rs[r0:r0 + 64, 0:n],
                                            ALU.mult, ALU.mult)
                                    pending.append(fin)
                    flush_pending()

            dma('pool', wA[:, :, 0:416], win_d[l, :, 0:416].rearrange("(a p) n -> p a n", p=128))
            dma('pool', wuq_sb, wuq_d[l].rearrange("(a p) n -> p a n", p=128))
            dma('pool', wukv_sb, wukv_d[l])
            dma('sp', ropeT, tabs_d[0])
            tiles = []
            for tbi, (t0, n) in enumerate(TBS):
                lat = tbi < 4
                rope = (Rmla[0:96, 0:96], ropeT[0:96, 0, t0:t0 + n], ropeT[0:96, 1, t0:t0 + n]) if lat else None
                st = {}

                def A_cq(t0=t0, n=n, st=st):
                    st['raws'] = []
                    for c in range(2):
                        rp = raw_ring.next()[:, 0:n]
                        proj_fm(rp, (c * 128, (c + 1) * 128), t0, n)
                        st['raws'].append(rp)

                def B_cq(t0=t0, n=n, st=st):
                    ssp = ss_ring.next()[:, 0:n]
                    for c in range(2):
                        sq = sqr.next()[:, 0:n]
                        act(sq, st['raws'][c], AF.Square)
                        mm(ssp, allones, sq, start=(c == 0), stop=(c == 1))
                    ln = lnr.next()[:, 0:n]
                    act(ln, ssp, AF.Ln, scale=1.0 / 256, bias=float(EPS))
                    rs = rsr.next()[:, 0:n]
                    act(rs, ln, AF.Exp, scale=-0.5)
                    for c in range(2):
                        stt(cqn[:, c, 0:n], st['raws'][c], pvc('qa_g', c), rs, ALU.mult, ALU.mult)

                tiles.append([A_cq, B_cq, None])

                def src_ckv(t0=t0, n=n):
                    rp = raw_ring.next()[:, 0:n]
                    proj_fm(rp, (256, 384), t0, n)
                    return rp
                tiles.append(chain(src_ckv, 128, n, allones, 1.0 / 128, pvc('kva_g'), ckvnT[:, t0:t0 + n], None))

                def A_kr(t0=t0, n=n):
                    rpk = raw_ring.next()
                    for kc in range(8):
                        mm(rpk[64:96, 0:n], wA[:, kc, 384:416], hT[:, kc, t0:t0 + n], start=(kc == 0), stop=(kc == 7))
                    act(krs_buf[64:96, 0:n], rpk[64:96, 0:n], AF.Copy)
                tiles.append([A_kr, None, None])

                for h in range(4):
                    def src_q(h=h, n=n):
                        rp = raw_ring.next()[0:96, 0:n]
                        for c in range(2):
                            mm(rp, wuq_sb[:, c, h * 96:(h + 1) * 96], cqn[:, c, 0:n], start=(c == 0), stop=(c == 1))
                        return rp
                    tiles.append(chain(src_q, 96, n, allones[0:96, 0:96], 1.0 / 96, pvc('mq_g', 0, 96), QT[0:96, h, t0:t0 + n], rope))
                for h in range(4):
                    def src_k(h=h, t0=t0, n=n):
                        rp = raw_ring.next()[0:64, 0:n]
                        mm(rp, wukv_sb[:, h * 128:h * 128 + 64], ckvnT[:, t0:t0 + n])
                        kr_ = kraw.next()
                        act(kr_[0:64, 0:n], rp, AF.Copy)
                        cp(kr_[64:96, 0:n], krs_buf[64:96, 0:n])
                        return kr_[0:96, 0:n]
                    tiles.append(chain(src_k, 96, n, allones[0:96, 0:96], 1.0 / 96, pvc('mk_g', 0, 96), KT[0:96, h, t0:t0 + n], rope))
            run_pipe(tiles)
            ck('mlap')
            for kt in range(NKT):
                vp = vp_ring.next()
                for h in range(4):
                    mm(vp[:, h * 64:(h + 1) * 64], ckvnT[:, kt * 128:(kt + 1) * 128], wukv_sb[:, h * 128 + 64:h * 128 + 128])
                for h in range(4):
                    if _os.environ.get('NOVCP'):
                        continue
                    vcopy(v_dst(kt, h), vp[:, h * 64:(h + 1) * 64], vp, True)
            ck('mlav')
            run_attention(0, None, lambda h: h, 96, 96 ** -0.5)
            ck('mla')

            dma('pool', wA[:, :, 0:768], win_d[l, :, 416:1184].rearrange("(a p) n -> p a n", p=128))
            dma('sp', ropeT, tabs_d[1])
            tiles = []
            for tbi, (t0, n) in enumerate(TBS):
                lat = tbi < 4
                rope = (Rdiff[0:64, 0:64], ropeT[0:64, 0, t0:t0 + n], ropeT[0:64, 1, t0:t0 + n]) if lat else None
                for h in range(4):
                    for (cb, gname, dst) in ((0, 'dq_g', QT), (256, 'dk_g', KT)):
                        def src(h=h, cb=cb, t0=t0, n=n):
                            rp = raw_ring.next()[0:64, 0:n]
                            proj_fm(rp, (cb + h * 64, cb + h * 64 + 64), t0, n)
                            return rp
                        tiles.append(chain(src, 64, n, blk32[0:64, 0:64], 1.0 / 32, pvc(gname, 0, 64), dst[0:64, h, t0:t0 + n], rope))
            run_pipe(tiles)
            proj_v(512, [[0], [1], [2], [3]])
            run_attention(1, None, lambda h: h, 32, 32 ** -0.5, diff=True)
            ck('diff')

            dma('pool', wA[:, :, 0:768], win_d[l, :, 1184:1952].rearrange("(a p) n -> p a n", p=128))
            for h in range(4):
                dma('sp', nastg, nab_d[l, h])
                act(natab[:, h, :], nastg, AF.Exp)
                for (c0_, c1_) in ((0, TL), (TL, T)):
                    dma('pool', QT[64:96, h, c0_:c1_], aug_d[0][:, c0_:c1_])
                    dma('pool', KT[64:96, h, c0_:c1_], aug_d[1][:, c0_:c1_])
            tiles = []
            for tbi, (t0, n) in enumerate(TBS):
                for h in range(4):
                    for (cb, gname, dst) in ((0, 'nq_g', QT), (256, 'nk_g', KT)):
                        def src(h=h, cb=cb, t0=t0, n=n):
                            rp = raw_ring.next()[0:64, 0:n]
                            proj_fm(rp, (cb + h * 64, cb + h * 64 + 64), t0, n)
                            return rp
                        tiles.append(chain(src, 64, n, allones[0:64, 0:64], 1.0 / 64, pvc(gname, 0, 64), dst[0:64, h, t0:t0 + n], None))
            run_pipe(tiles)
            proj_v(512, [[0], [1], [2], [3]])
            run_attention(2, None, lambda h: h, 96, 64 ** -0.5, na=True)
            ck('na')

            dma('pool', wA[:, :, 0:512], win_d[l, :, 1952:2464].rearrange("(a p) n -> p a n", p=128))
            dma('sp', ropeT, tabs_d[2])
            tiles = []
            for tbi, (t0, n) in enumerate(TBS):
                lat = tbi < 4
                rope = (Rgqa[0:64, 0:64], ropeT[0:64, 0, t0:t0 + n], ropeT[0:64, 1, t0:t0 + n]) if lat else None
                for (cb, gname, dst, nh) in ((0, 'gq_g', QT, 4), (256, 'gk_g', KT, 2)):
                    for h in range(nh):
                        def src(h=h, cb=cb, t0=t0, n=n):
                            rp = raw_ring.next()[0:64, 0:n]
                            proj_fm(rp, (cb + h * 64, cb + h * 64 + 64), t0, n)
                            return rp
                        tiles.append(chain(src, 64, n, allones[0:64, 0:64], 1.0 / 64, pvc(gname, 0, 64), dst[0:64, h, t0:t0 + n], rope))
            run_pipe(tiles)
            ck('gqap')
            proj_v(384, [[0, 1], [2, 3]])
            ck('gqav')
            run_attention(3, None, lambda h: h // 2, 64, 64 ** -0.5)
            ck('gqa')

            def load_group(gi):
                g0_, gn_ = FFG[gi]
                wn_ = gn_ * 128
                wb = wup_bufs[gi % 2]
                dma('pool', wb[:, :, 0, 0:wn_], wup_d[l, :, g0_ * 128:g0_ * 128 + wn_].rearrange("(a p) n -> p a n", p=128))
                dma('pool', wb[:, :, 1, 0:wn_], wup_d[l, :, DFF + g0_ * 128:DFF + g0_ * 128 + wn_].rearrange("(a p) n -> p a n", p=128))
                dma('pool', wdn_bufs[gi % 2][:, 0:gn_, :], wdn_d[l, g0_ * 128:(g0_ + gn_) * 128, :].rearrange("(a p) n -> p a n", p=128))

            load_group(0)

            dma('pool', wout_sb, wout_d[l].rearrange("(a p) n -> p a n", p=128))
            all_ps = Ring([PB(i) for i in range(7)])
            ss_ring = Ring([PB(5), PB(6)])
            op_ring = Ring([PB(i) for i in range(5)])

            def outproj_loader(tbi, t0, n):
                dma('sp', xblk[:, :, 0:n], xT_d[:, :, t0:t0 + n], reads=xkeys(tbi))
                w = 0 if tbi < 4 else 1
                for j in range(8):
                    pb = op_ring.next()[:, 0:n]
                    for c in range(8):
                        mm(pb, wout_sb[:, c, j * 128:(j + 1) * 128], mixT[:, c, t0:t0 + n], start=(c == 0), stop=(c == 7))
                    stt(xblk[:, j, 0:n], pb, modT[:, l, 2 * 8 + j, w:w + 1], xblk[:, j, 0:n], ALU.mult, ALU.add)
                dma('sp', xT_d[:, :, t0:t0 + n], xblk[:, :, 0:n], writes=xkeys(tbi))

            def outproj_loader_lastctx(tbi, t0, n):
                if tbi == 4:
                    load_x_block(tbi, t0, n)
                else:
                    outproj_loader(tbi, t0, n)

            modulate(l, 1, outproj_loader if with_ctx else outproj_loader_lastctx)

            ck('outproj')
            f_tbs = TBS if with_ctx else TBS[:4]
            for ub in ubuf + ubuf2:
                mset(ub[:, 0:1], 0.0, eng='pool')
                mset(ub[:, 2049:2051], 0.0, eng='pool')
                mset(ub[:, 2307:2308], 0.0, eng='pool')
            up_ring = Ring([PB(0), PB(1), PB(2), PB(3)])
            dn_ring = Ring([PB(4), PB(5), PB(6)])
            cring = Ring(ctile)

            def ucol(t0):
                return (1 + t0) if t0 < TL else (2051 + t0 - TL)

            for gi, (g0, gn) in enumerate(FFG):
                wup_sb = wup_bufs[gi % 2]
                wdn_sb = wdn_bufs[gi % 2]
                if gi + 1 < len(FFG):
                    load_group(gi + 1)
                for cl in range(gn):
                    c = g0 + cl
                    ub = ubuf if c % 2 == 0 else ubuf2
                    for (t0, n) in f_tbs:
                        u0 = ucol(t0)
                        for ag in range(2):
                            pb = up_ring.next()[:, 0:n]
                            for kc in range(8):
                                mm(pb, wup_sb[:, kc, ag, cl * 128:(cl + 1) * 128], hT[:, kc, t0:t0 + n], start=(kc == 0), stop=(kc == 7))
                            act(ub[ag][:, u0:u0 + n], pb, AF.Copy)
                    for (t0, n) in f_tbs:
                        u0 = ucol(t0)
                        outs = []
                        for ag in range(2):
                            col = c if ag == 0 else NCH + c
                            ct = cring.next()[:, 0:n]
                            act(ct, ub[ag][:, u0:u0 + n], AF.Identity, scale=pvc('cw1', col), bias=pvc('cb', col))
                            stt(ct, ub[ag][:, u0 - 1:u0 - 1 + n], pvc('cw0', col), ct, ALU.mult, ALU.add)
                            stt(ct, ub[ag][:, u0 + 1:u0 + 1 + n], pvc('cw2', col), ct, ALU.mult, ALU.add)
                            outs.append(ct)
                        act(outs[1], outs[1], AF.Silu)
                        tt(actT[:, cl, t0:t0 + n], outs[1], outs[0], ALU.mult, eng='pool')
                for tbi, (t0, n) in enumerate(f_tbs):
                    w = 0 if tbi < 4 else 1
                    for j in range(8):
                        pb = dn_ring.next()[:, 0:n]
                        for cl in range(gn):
                            mm(pb, wdn_sb[:, cl, j * 128:(j + 1) * 128], actT[:, cl, t0:t0 + n], start=(cl == 0), stop=(cl == gn - 1))
                        xt = xring.next()[:, 0:n]
                        key = "xf:%d:%d" % (tbi, j)
                        dma('sp', xt, xT_d[:, j, t0:t0 + n], reads=[xkey(tbi), key])
                        stt(xt, pb, modT[:, l, 5 * 8 + j, w:w + 1], xt, ALU.mult, ALU.add)
                        dma('sp', xT_d[:, j, t0:t0 + n], xt, writes=[key])

    except StopBuild:
        pass
    import os as _os
    for tbi, (t0, n) in enumerate(TBS[:4]):
        dma('sp', xblk[:, :, 0:n], xT_d[:, :, t0:t0 + n], reads=xkeys(tbi))
        if _os.environ.get('SKIP_FINAL'):
            dma('sp', y_d[t0:t0 + n, :].rearrange("(a p) d -> p a d", p=128), xblk[:, 0:4, :].rearrange("p a (b c) -> p (a b) c", b=1)[:, :, :].rearrange("p a c -> p a c") if False else xld[:, 0:4, :])
            continue
        for a in range(4):
            for half in range(2):
                pb = PB((a * 2 + half) % int(_os.environ.get("NB", "6")))
                for jj in range(4):
                    j = half * 4 + jj
                    tr(pb[:, jj * 128:(jj + 1) * 128], xblk[:, j, a * 128:(a + 1) * 128], ident)
                if half == 0:
                    act(xld[:, a, 0:512], pb, AF.Copy)
                else:
                    cp(xld[:, a, 512:1024], pb)
        dma('sp', y_d[t0:t0 + n, :].rearrange("(a p) d -> p a d", p=128), xld[:, 0:4, :])
    if dbg:
        for tbi, (t0, n) in enumerate(TBS):
            dma('sp', xblk[:, :, 0:n], xT_d[:, :, t0:t0 + n], reads=xkeys(tbi))
            dma('sp', dbg_d[:, :, t0:t0 + n], xblk[:, :, 0:n])
    S.emit()
    return nc, S


def _rope_tab(rot):
    nf = rot // 4
    inv = np.power(np.float32(10000.0), -np.arange(nf, dtype=np.float32) / np.float32(nf)).astype(np.float32)
    t = np.arange(TL)
    row = (t // 64).astype(np.float32)
    col = (t % 64).astype(np.float32)
    ar = row[:, None] * inv
    ac = col[:, None] * inv
    ang = np.concatenate([ar, ar, ac, ac], axis=-1).astype(np.float32)
    return np.cos(ang).T.astype(np.float32), np.sin(ang).T.astype(np.float32)


def _constants():
    cf = np.zeros((128, 256), np.float32)
    cf[:, 0:128] = np.eye(128, dtype=np.float32)
    cf[:, 128:256] = 1.0
    cb = np.zeros((128, 6, 128), np.float32)
    cb[:, 0, :] = 1.0
    for b in range(4):
        cb[b * 32:(b + 1) * 32, 1, b * 32:(b + 1) * 32] = 1.0
    for b in range(2):
        cb[b * 64:(b + 1) * 64, 2, b * 64:(b + 1) * 64] = 1.0

    def add_rot(Rm, base, n):
        for i in range(n):
            Rm[base + i + n, base + i] = -1.0
            Rm[base + i, base + i + n] = 1.0
            Rm[base + 3 * n + i, base + 2 * n + i] = -1.0
            Rm[base + 2 * n + i, base + 3 * n + i] = 1.0
    add_rot(cb[:, 3, :], 64, 8)
    add_rot(cb[:, 4, :], 0, 8)
    add_rot(cb[:, 4, :], 32, 8)
    add_rot(cb[:, 5, :], 0, 16)
    aug = np.zeros((2, 32, T), np.float32)
    for q in range(TL):
        qr = q // 64
        rs = min(max(qr - 4, 0), 24)
        aug[0, :, q] = -BIG
        aug[0, rs:rs + 8, q] = 0.0
        aug[1, qr, q] = 1.0
    tabs = np.zeros((3, 128, 2, TL), np.float32)
    c32, s32 = _rope_tab(32)
    c64, s64 = _rope_tab(64)
    tabs[0, 0:64, 0, :] = 1.0
    tabs[0, 64:96, 0, :] = c32
    tabs[0, 64:96, 1, :] = s32
    tabs[1, 0:32, 0, :] = c32
    tabs[1, 32:64, 0, :] = c32
    tabs[1, 0:32, 1, :] = s32
    tabs[1, 32:64, 1, :] = s32
    tabs[2, 0:64, 0, :] = c64
    tabs[2, 0:64, 1, :] = s64
    return cf, cb.reshape(128, 768), aug, tabs


def _na_index():
    idx = np.full((128, 22, 64), 15 * 31, np.int64)
    for p in range(128):
        half, kc = p // 64, p % 64
        for pos in range(22):
            i = pos - 3 - half
            if i < 0 or i > 14:
                continue
            for qc in range(64):
                ws = min(max(qc - 8, 0), 48)
                if ws <= kc < ws + 16:
                    co = min(max(kc - qc + 15, 0), 30)
                    idx[p, pos, qc] = (14 - i) * 31 + co
    return idx.reshape(128, NAW)


def _tile_rows(v, reps, rows=128):
    out = np.zeros((rows,), np.float32)
    t = np.tile(np.asarray(v, np.float32), reps)
    out[:t.shape[0]] = t
    return out


def _prep_shared(inp):
    cf, cb, aug, tabs = _constants()
    pvA = np.zeros((128, L, NV), np.float32)

    def put(name, l, arr2d):
        c0, w = PV[name]
        pvA[:, l, c0:c0 + w] = arr2d

    for l in range(L):
        put('g_mix', l, inp['g_mix'][l].reshape(8, 128).T)
        put('g_ffn', l, inp['g_ffn'][l].reshape(8, 128).T)
        put('qa_g', l, inp['mla_q_a_g'][l].reshape(2, 128).T)
        put('kva_g', l, inp['mla_kv_a_g'][l].reshape(1, 128).T)
        put('mq_g', l, _tile_rows(inp['mla_q_g'][l], 1)[:, None])
        put('mk_g', l, _tile_rows(inp['mla_k_g'][l], 1)[:, None])
        put('dq_g', l, _tile_rows(inp['diff_q_g'][l], 4)[:, None])
        put('dk_g', l, _tile_rows(inp['diff_k_g'][l], 4)[:, None])
        put('dsub_g', l, _tile_rows(inp['diff_subln_g'][l], 2)[:, None])
        put('nq_g', l, _tile_rows(inp['na_q_g'][l], 2)[:, None])
        put('nk_g', l, _tile_rows(inp['na_k_g'][l], 2)[:, None])
        put('gq_g', l, _tile_rows(inp['gqa_q_g'][l], 2)[:, None])
        put('gk_g', l, _tile_rows(inp['gqa_k_g'][l], 2)[:, None])
        for i in range(3):
            put('cw%d' % i, l, inp['conv_w'][l, i].reshape(44, 128).T)
        put('cb', l, inp['conv_b'][l].reshape(44, 128).T)
        lv = np.concatenate([inp['diff_lq1'][l], inp['diff_lk1'][l], inp['diff_lq2'][l], inp['diff_lk2'][l]]).astype(np.float32)
        put('lvec', l, np.broadcast_to(lv[None, :], (128, 128)))
    idx = _na_index()
    nab = np.zeros((L, 4, 128, NAW), np.float32)
    for l in range(L):
        for h in range(4):
            src = np.concatenate([np.asarray(inp['na_rpb'][l, h], np.float32).ravel(), np.array([-10000.0], np.float32)])
            nab[l, h] = src[idx]
    f = lambda a: np.ascontiguousarray(np.asarray(a, np.float32))
    return {
        "w_mod": f(inp['w_mod']), "b_mod": f(inp['b_mod']), "w_in": f(inp['w_in']), "w_out": f(inp['w_out']),
        "w_uq": f(inp['mla_w_uq']), "w_ukv": f(inp['mla_w_ukv']), "w_up": f(inp['w_up']), "w_down": f(inp['w_down']),
        "pv": np.ascontiguousarray(pvA.reshape(128, L * NV)), "cf32": cf, "cbf": np.ascontiguousarray(cb),
        "aug": aug, "tabs": tabs, "nab": nab,
    }


_CACHE = {}


def kernel(**inp):
    n_layers = inp.pop('_n_layers', L)
    dbg = inp.pop('_dbg', False)
    stop = inp.pop('_stop', None)
    ncores = inp.pop('_ncores', 8)
    key = (n_layers, dbg, stop)
    if key not in _CACHE:
        _CACHE[key] = build_program(n_layers, dbg, stop)[0]
    nc = _CACHE[key]
    shared = _prep_shared(inp)
    x = np.asarray(inp['x'], np.float32)
    ctx = np.asarray(inp['ctx'], np.float32)
    c = np.asarray(inp['c'], np.float32)
    cc = np.asarray(inp['c_ctx'], np.float32)
    in_maps = []
    for b in range(ncores):
        m = dict(shared)
        m["xc"] = np.ascontiguousarray(np.concatenate([x[b], ctx[b]], axis=0))
        cT = np.zeros((128, 8, 2), np.float32)
        cT[:, :, 0] = c[b].reshape(8, 128).T
        cT[:, :, 1] = cc.reshape(8, 128).T
        m["cT"] = np.ascontiguousarray(cT.reshape(128, 16))
        in_maps.append(m)
    res = run_bass_kernel_spmd(nc, in_maps, core_ids=list(range(ncores)))
    out = np.stack([np.asarray(r["y"], np.float32) for r in res.results], axis=0)
    if dbg:
        kernel.dbg = [np.asarray(r["dbg"], np.float32) for r in res.results]
    return out
```

```python
import math
import numpy as np
import concourse.bass as bass
import concourse.mybir as mybir
from concourse.bass_utils import run_bass_kernel_spmd

F32 = mybir.dt.float32
BF16 = mybir.dt.bfloat16
AF = mybir.ActivationFunctionType
ALU = mybir.AluOpType
AX = mybir.AxisListType

ENGS = ['pe', 'act', 'dve', 'pool', 'sp']
CELL = 256
_ESZ = {}


def esz(dt):
    if dt not in _ESZ:
        _ESZ[dt] = mybir.dt.size(dt)
    return _ESZ[dt]


def ap_cells(ap):
    space = str(ap.space)
    sp = 0 if space == 'SB' else 1
    dims = ap.ap
    pstep, pcount = dims[0]
    e = esz(ap.dtype)
    off = ap.offset
    p0 = off // pstep
    foff = off % pstep
    ranges = [(foff, foff + 1)]
    for (st, cnt) in dims[1:]:
        if cnt <= 1:
            continue
        if len(ranges) * cnt <= 512 and abs(st) * e >= CELL:
            ranges = [(lo + i * st, hi + i * st) for (lo, hi) in ranges for i in range(cnt)]
        else:
            ext = (cnt - 1) * st
            if ext >= 0:
                ranges = [(lo, hi + ext) for (lo, hi) in ranges]
            else:
                ranges = [(lo + ext, hi) for (lo, hi) in ranges]
    cs = set()
    for (lo, hi) in ranges:
        c0 = (lo * e) // CELL
        c1 = (hi * e - 1) // CELL
        for c in range(c0, c1 + 1):
            cs.add(c)
    if sp == 1:
        return sorted(set(4 * 4096 + (c * CELL) // 2048 for c in cs))
    q0 = p0 // 32
    q1 = (p0 + pcount - 1) // 32
    out = []
    for q in range(q0, q1 + 1):
        base = (sp * 4 + q) * 4096
        for c in cs:
            out.append(base + c)
    return out


class Sched:
    def __init__(self, nc, n_lanes=48, same_eng_sync=True):
        self.nc = nc
        self.ops = {e: [] for e in ENGS}
        self.cw = {}
        self.cr = {}
        self.known = {e: {} for e in ENGS}
        self.snap = {}
        self.n_lanes = n_lanes
        self.lane_count = [0] * n_lanes
        self.next_lane = 0
        self.same_eng_sync = same_eng_sync
        self.eng_sem = {e: nc.alloc_semaphore("sem_" + e) for e in ENGS}
        self.lane_sem = [nc.alloc_semaphore("lane%d" % i) for i in range(n_lanes)]

    def _cells(self, items):
        cs = []
        for it in items:
            if it is None:
                continue
            if isinstance(it, (str, tuple)):
                cs.append(it)
            else:
                cs.extend(ap_cells(it))
        return cs

    def add(self, eng, fn, reads=(), writes=(), dma=False):
        ops = self.ops[eng]
        idx = len(ops)
        rc = self._cells(reads)
        wc = self._cells(writes)
        need = {}

        def want(tok, war=False):
            key, seq = tok
            if key == eng:
                if eng == 'pe' or war or not self.same_eng_sync:
                    return
            if need.get(key, -1) < seq:
                need[key] = seq

        for c in rc:
            t = self.cw.get(c)
            if t is not None:
                want(t)
        for c in wc:
            t = self.cw.get(c)
            if t is not None:
                want(t)
            rs = self.cr.get(c)
            if rs:
                for k, s in rs.items():
                    want((k, s), war=True)
        if dma:
            lane = self.next_lane
            self.next_lane = (lane + 1) % self.n_lanes
            cnt = self.lane_count[lane] + 1
            self.lane_count[lane] = cnt
            tok = (('L', lane), cnt)
            if cnt > 1:
                want((('L', lane), cnt - 1))
        else:
            tok = (eng, idx)
        kn = self.known[eng]
        waits = []
        for key, seq in need.items():
            if kn.get(key, -1) >= seq:
                continue
            waits.append((key, seq))
            kn[key] = seq
            if isinstance(key, str):
                self.ops[key][seq]['signal'] = True
                sn = self.snap.get((key, seq))
                if sn:
                    for k2, s2 in sn.items():
                        if kn.get(k2, -1) < s2:
                            kn[k2] = s2
        if not dma:
            self.snap[(eng, idx)] = dict(kn)
        ops.append(dict(fn=fn, waits=waits, tok=tok, dma=dma, signal=False))
        for c in wc:
            self.cw[c] = tok
            self.cr[c] = {}
        for c in rc:
            d = self.cr.get(c)
            if d is None:
                d = {}
                self.cr[c] = d
            if d.get(tok[0], -1) < tok[1]:
                d[tok[0]] = tok[1]
        return tok

    def emit(self):
        nc = self.nc
        for e in ENGS:
            c = 0
            for op in self.ops[e]:
                if op['signal'] and not op['dma']:
                    c += 1
                    op['sigval'] = c
        emap = {'pe': 'tensor', 'act': 'scalar', 'dve': 'vector', 'pool': 'gpsimd', 'sp': 'sync'}

        def run(e, E):
            for op in self.ops[e]:
                for key, seq in op['waits']:
                    if isinstance(key, str):
                        E.wait_ge(self.eng_sem[key], self.ops[key][seq]['sigval'])
                    else:
                        E.wait_ge(self.lane_sem[key[1]], 16 * seq)
                inst = op['fn'](E)
                if op['dma']:
                    inst.then_inc(self.lane_sem[op['tok'][0][1]], 16)
                elif op['signal']:
                    inst.then_inc(self.eng_sem[e], 1)
            if e == 'sp':
                for i, cnt in enumerate(self.lane_count):
                    if cnt:
                        E.wait_ge(self.lane_sem[i], 16 * cnt)

        with nc.Block() as block:
            for e in ENGS:
                getattr(block, emap[e])(lambda E, e=e: run(e, E))

    def stats(self):
        return {e: (len(self.ops[e]), sum(len(o['waits']) for o in self.ops[e])) for e in ENGS}


class Ring:
    def __init__(self, items):
        self.items = list(items)
        self.i = 0

    def next(self):
        r = self.items[self.i]
        self.i = (self.i + 1) % len(self.items)
        return r


D = 1024
L = 4
TL = 2048
TC = 256
T = TL + TC
NKT = T // 128
TBS = [(0, 512), (512, 512), (1024, 512), (1536, 512), (2048, 256)]
IN_COLS = 2464
DFF = 2816
NCH = DFF // 128
EPS = 1e-6
BIG = 30000.0
NA_KT = {0: range(0, 6), 1: range(2, 10), 2: range(6, 14), 3: range(10, 16)}
NAW = 22 * 64
FFG = [(0, 4), (4, 4), (8, 4), (12, 4), (16, 4), (20, 2)]

PV = {}
_c = 0
for _n, _w in [('g_mix', 8), ('g_ffn', 8), ('qa_g', 2), ('kva_g', 1), ('mq_g', 1), ('mk_g', 1),
               ('dq_g', 1), ('dk_g', 1), ('dsub_g', 1), ('nq_g', 1), ('nk_g', 1), ('gq_g', 1), ('gk_g', 1),
               ('cw0', 44), ('cw1', 44), ('cw2', 44), ('cb', 44), ('lvec', 128)]:
    PV[_n] = (_c, _w)
    _c += _w
NV = _c


def lambda_init(l):
    return 0.8 - 0.6 * math.exp(-0.3 * l)


class StopBuild(Exception):
    pass


def build_program(n_layers=L, dbg=False, stop_after=None):
    nc = bass.Bass("TRN2", target_bir_lowering=False)

    def ck(name):
        if stop_after == name:
            raise StopBuild()

    def din(name, shape):
        return nc.dram_tensor(name, list(shape), F32, kind="ExternalInput").ap()

    xc_d = din("xc", [T, D])
    cT_d = din("cT", [128, 16])
    wmod_d = din("w_mod", [L, D, 6 * D])
    bmod_d = din("b_mod", [L, 6 * D])
    win_d = din("w_in", [L, D, IN_COLS])
    wout_d = din("w_out", [L, D, D])
    wuq_d = din("w_uq", [L, 256, 384])
    wukv_d = din("w_ukv", [L, 128, 512])
    wup_d = din("w_up", [L, D, 2 * DFF])
    wdn_d = din("w_down", [L, DFF, D])
    pv_d = din("pv", [128, L * NV])
    cf_d = din("cf32", [128, 256])
    cb_d = din("cbf", [128, 6 * 128])
    aug_d = din("aug", [2, 32, T])
    tabs_d = din("tabs", [3, 128, 2, TL])
    nab_d = din("nab", [L, 4, 128, NAW])
    y_d = nc.dram_tensor("y", [TL, D], F32, kind="ExternalOutput").ap()
    xT_d = nc.dram_tensor("xT_scr", [128, 8, T], F32).ap()
    if dbg:
        dbg_d = nc.dram_tensor("dbg", [128, 8, T], F32, kind="ExternalOutput").ap()

    ARENA_F32 = 53000
    arena = nc.alloc_sbuf_tensor("arena", [128, ARENA_F32], F32)
    psum = nc.alloc_psum_tensor("psum", [128, 4096], F32)
    S = Sched(nc)

    pos = [0]

    def alloc_b(nbytes):
        a = pos[0]
        n = (nbytes + 255) // 256 * 256
        pos[0] += n
        assert pos[0] <= ARENA_F32 * 4, pos[0]
        return a

    def f32v(boff, n):
        return arena[:, boff // 4: boff // 4 + n]

    def bf16v(boff, n):
        return arena[:, boff // 4: boff // 4 + (n + 1) // 2].bitcast(BF16)

    def PB(i):
        return psum[:, i * 512:(i + 1) * 512]

    ident = f32v(alloc_b(512), 128)
    ones_f = f32v(alloc_b(512), 128)
    cbf = bf16v(alloc_b(6 * 256), 6 * 128).rearrange("p (a b) -> p a b", a=6)
    allones, blk32, blk64, Rmla, Rdiff, Rgqa = [cbf[:, i, :] for i in range(6)]
    pv = f32v(alloc_b(L * NV * 4), L * NV).rearrange("p (l n) -> p l n", l=L)
    modT = f32v(alloc_b(L * 96 * 4), L * 96).rearrange("p (l s w) -> p l s w", l=L, s=48)
    cT = f32v(alloc_b(64), 16)
    scT_f = f32v(alloc_b(64), 16)
    scT = bf16v(alloc_b(32), 16).rearrange("p (a b) -> p a b", a=8)
    Gv = f32v(alloc_b(4 * 16 * 4), 64).rearrange("p (a j w) -> p a j w", a=2, j=8)
    misc = f32v(alloc_b(64 * 4), 64)
    hT = bf16v(alloc_b(8 * T * 2), 8 * T).rearrange("p (a t) -> p a t", a=8)
    mix_off = alloc_b(8 * T * 2)
    mixT = bf16v(mix_off, 8 * T).rearrange("p (a t) -> p a t", a=8)
    qkv_off = alloc_b(3 * 4 * T * 2)
    QT = bf16v(qkv_off, 4 * T).rearrange("p (a t) -> p a t", a=4)
    KT = bf16v(qkv_off + 4 * T * 2, 4 * T).rearrange("p (a t) -> p a t", a=4)
    VA = bf16v(qkv_off + 8 * T * 2, NKT * 4 * 128).rearrange("p (k h c) -> p k h c", k=NKT, h=4)
    xblk = f32v(qkv_off + 16384, 8 * 512).rearrange("p (a n) -> p a n", a=8)
    xld = f32v(qkv_off + 32768, 4 * 1024).rearrange("p (a n) -> p a n", a=4)
    GN = 4
    fo = mix_off
    actT = bf16v(fo, GN * T).rearrange("p (a t) -> p a t", a=GN); fo += GN * T * 2
    UW = T + 4
    UWP = (UW * 4 + 255) // 256 * 256
    ubuf = [f32v(fo + i * UWP, UW) for i in range(2)]; fo += 2 * UWP
    assert fo <= qkv_off + 512, (fo, qkv_off)
    WUPB = 8 * 2 * GN * 128 * 2
    wup_bufs = [bf16v(qkv_off + 32768, 8 * 2 * GN * 128).rearrange("p (k g n) -> p k g n", k=8, g=2),
                bf16v(qkv_off + 512, 8 * 2 * GN * 128).rearrange("p (k g n) -> p k g n", k=8, g=2)]
    assert qkv_off + 32768 + WUPB <= qkv_off + 3 * 4 * T * 2 and 512 + WUPB <= 32768
    fo = qkv_off
    assert fo <= qkv_off + 3 * 4 * T * 2, (fo, qkv_off + 3 * 4 * T * 2)
    tab_off = alloc_b(2 * TL * 4)
    ropeT = f32v(tab_off, 2 * TL).rearrange("p (a t) -> p a t", a=2)
    natab = bf16v(tab_off, 4 * NAW).rearrange("p (h n) -> p h n", h=4)
    wA = bf16v(alloc_b(8 * 768 * 2), 8 * 768).rearrange("p (a n) -> p a n", a=8)
    wout_sb = bf16v(tab_off, 8 * 1024).rearrange("p (a n) -> p a n", a=8)
    ubuf2 = [f32v(tab_off + i * UWP, UW) for i in range(2)]
    assert 2 * UWP <= 2 * TL * 4 + 8 * 768 * 2
    wuq_sb = bf16v(alloc_b(2 * 384 * 2), 2 * 384).rearrange("p (a n) -> p a n", a=2)
    wukv_sb = bf16v(alloc_b(512 * 2), 512)
    ckvnT = bf16v(alloc_b(T * 2), T)
    cqn = bf16v(alloc_b(2 * 512 * 2), 2 * 512).rearrange("p (a n) -> p a n", a=2)
    r1_off = tab_off + 2 * UWP
    wdn_bufs = [bf16v(r1_off + i * GN * 1024 * 2, GN * 1024).rearrange("p (a n) -> p a n", a=GN) for i in range(2)]
    tmp_off = alloc_b(24 * 1024)
    assert r1_off + 2 * GN * 1024 * 2 <= tmp_off, (r1_off, tmp_off)
    krs_buf = f32v(alloc_b(2048), 512)
    sq_extra = alloc_b(2048)

    def tf(i):
        return f32v(tmp_off + i * 2048, 512)

    def tb16(i, half=0):
        return bf16v(tmp_off + i * 2048 + half * 1024, 512)

    kraw = Ring([tf(0), tf(1)])
    sqr = Ring([tb16(2, 0), tb16(2, 1), bf16v(sq_extra, 512), bf16v(sq_extra + 1024, 512)])
    lnr = Ring([tf(3), tf(4)])
    rsr = Ring([tf(5), tf(6)])
    qnr = Ring([tb16(7, 0), tb16(7, 1)])
    t1r = Ring([tf(8), tf(9)])
    t2r = Ring([tf(10), tf(11)])
    Pr = Ring([tb16(0, 0), tb16(0, 1), tb16(1, 0), tb16(1, 1)])
    P2r = Ring([tb16(2, 0), tb16(2, 1)])
    Osb = Ring([tf(3), tf(4)])
    zrow = Ring([tf(5), tf(6)])
    ABo = [tf(7), tf(8), tf(9)]
    xring = Ring([tf(6), tf(7), tf(8), tf(9), tf(10), tf(11)])
    ctile = [tf(0), tf(1), tf(2), tf(3), tf(4), tf(5)]
    nastg = f32v(tmp_off, NAW)
    wmr = Ring([bf16v(tmp_off + i * 4096, 2048) for i in range(3)])
    m_sb = f32v(qkv_off, 6 * D)
    bm_sb = f32v(qkv_off + 24576, 6 * D)

    def mm(out, lhsT, rhs, start=True, stop=True):
        S.add('pe', lambda E: E.matmul(out, lhsT, rhs, start=start, stop=stop), reads=[lhsT, rhs], writes=[out])

    def tr(out, in_, idn):
        S.add('pe', lambda E: E.transpose(out, in_, idn), reads=[in_, idn], writes=[out])

    def act(out, in_, func, scale=None, bias=None, eng='act'):
        kw = {}
        rd = [in_]
        if scale is not None:
            kw['scale'] = scale
            if not isinstance(scale, float):
                rd.append(scale)
        if bias is not None:
            kw['bias'] = bias
            if not isinstance(bias, float):
                rd.append(bias)
        S.add('act', lambda E: E.activation(out, in_, func, **kw), reads=rd, writes=[out])

    def stt(out, in0, scalar, in1, op0, op1):
        rd = [in0, in1] + ([] if isinstance(scalar, float) else [scalar])
        S.add('dve', lambda E: E.scalar_tensor_tensor(out=out, in0=in0, scalar=scalar, in1=in1, op0=op0, op1=op1),
              reads=rd, writes=[out])

    def tt(out, in0, in1, op, eng='dve'):
        S.add(eng, lambda E: E.tensor_tensor(out=out, in0=in0, in1=in1, op=op), reads=[in0, in1], writes=[out])

    def ts(out, in0, s1, s2, op0, op1=None, eng='dve'):
        rd = [in0] + [s for s in (s1, s2) if s is not None and not isinstance(s, float)]
        if op1 is None:
            S.add(eng, lambda E: E.tensor_scalar(out=out, in0=in0, scalar1=s1, scalar2=None, op0=op0), reads=rd, writes=[out])
        else:
            S.add(eng, lambda E: E.tensor_scalar(out=out, in0=in0, scalar1=s1, scalar2=s2, op0=op0, op1=op1), reads=rd, writes=[out])

    def cp(out, in_, eng='dve'):
        S.add(eng, lambda E: E.tensor_copy(out=out, in_=in_), reads=[in_], writes=[out])

    def vcopy(out, in_, bank, use_act):
        if use_act:
            S.add('act', lambda E: E.activation(out, in_, AF.Copy), reads=[bank], writes=[out])
        else:
            S.add('dve', lambda E: E.tensor_copy(out=out, in_=in_), reads=[bank], writes=[out])

    def mset(ap, val, eng='dve'):
        S.add(eng, lambda E: E.memset(ap, val), writes=[ap])

    def dma(q, out, in_, reads=None, writes=None):
        r = [in_] if reads is None else reads
        w = [out] if writes is None else writes
        r = [a for a in r if isinstance(a, (str, tuple)) or str(a.space) != 'DRAM']
        w = [a for a in w if isinstance(a, (str, tuple)) or str(a.space) != 'DRAM']
        S.add(q, lambda E: E.dma_start(out=out, in_=in_), reads=r, writes=w, dma=True)

    def xkey(tbi):
        return "x:%d" % tbi

    def xkeys(tbi):
        return ["x:%d" % tbi] + ["xf:%d:%d" % (tbi, j) for j in range(8)]

    try:
        import os as _os
        if _os.environ.get('SKIP_CONST'):
            raise StopBuild()
        dma('sp', ident, cf_d[:, 0:128])
        dma('sp', ones_f, cf_d[:, 128:256])
        dma('pool', cbf, cb_d.rearrange("p (a b) -> p a b", a=6))
        dma('sp', pv, pv_d.rearrange("p (l n) -> p l n", l=L))
        dma('sp', cT, cT_d)
        act(scT_f, cT, AF.Silu)
        cp(scT, scT_f.rearrange("p (a b) -> p a b", a=8))

        ck('c0')
        for tbi, (t0, n) in enumerate(TBS):
            ntt = n // 128
            dma('sp', xld[:, 0:ntt, :], xc_d[t0:t0 + n, :].rearrange("(a p) d -> p a d", p=128))
            for j in range(8):
                pb = PB(j % 2)
                for a in range(ntt):
                    tr(pb[:, a * 128:(a + 1) * 128], xld[:, a, j * 128:(j + 1) * 128], ident)
                if j % 2 == 0:
                    act(xblk[:, j, 0:n], pb[:, 0:n], AF.Copy)
                else:
                    cp(xblk[:, j, 0:n], pb[:, 0:n])
            dma('sp', xT_d[:, :, t0:t0 + n], xblk[:, :, 0:n], writes=xkeys(tbi))

        ck('xt')
        for l in range(n_layers):
            dma('sp', bm_sb[0:1, :], bmod_d[l:l + 1, :])
            dma('sp', bm_sb[1:2, :], bmod_d[l:l + 1, :])
            for cg in range(3):
                for kc in range(8):
                    piece = wmr.next()
                    dma('pool', piece, wmod_d[l, kc * 128:(kc + 1) * 128, cg * 2048:(cg + 1) * 2048])
                    for i in range(4):
                        mm(PB(i)[0:2, :], scT[:, kc, :], piece[:, i * 512:(i + 1) * 512], start=(kc == 0), stop=(kc == 7))
                for i in range(4):
                    c0 = cg * 2048 + i * 512
                    tt(m_sb[0:2, c0:c0 + 512], PB(i)[0:2, :], bm_sb[0:2, c0:c0 + 512], ALU.add)
            pt = PB(4)
            for s in range(48):
                tr(pt[:, 2 * s:2 * s + 2], m_sb[0:2, s * 128:(s + 1) * 128], ident[0:2, 0:2])
            cp(modT[:, l].rearrange("p s w -> p (s w)"), pt[:, 0:96])

        ck('mod')
        def rstd_from(src, R, N, ones_m, inv_d):
            sq = sqr.next()[0:R, 0:N]
            act(sq, src, AF.Square)
            ssp = ss_ring.next()[0:R, 0:N]
            mm(ssp, ones_m, sq)
            ln = lnr.next()[0:R, 0:N]
            act(ln, ssp, AF.Ln, scale=float(inv_d), bias=float(EPS))
            rs = rsr.next()[0:R, 0:N]
            act(rs, ln, AF.Exp, scale=-0.5)
            return rs

        def norm_rope(src, R, N, ones_m, inv_d, gain, out, rope=None):
            rs = rstd_from(src, R, N, ones_m, inv_d)
            if rope is None:
                stt(out, src, gain, rs, ALU.mult, ALU.mult)
                return
            Rm, cosT, sinT = rope
            qn = qnr.next()[0:R, 0:N]
            stt(qn, src, gain, rs, ALU.mult, ALU.mult)
            rp = rot_ring.next()[0:R, 0:N]
            mm(rp, Rm, qn)
            t1 = t1r.next()[0:R, 0:N]
            tt(t1, qn, cosT, ALU.mult, eng='pool')
            t2 = t2r.next()[0:R, 0:N]
            tt(t2, rp, sinT, ALU.mult)
            tt(out, t1, t2, ALU.add, eng='pool')

        def chain(src_fn, R, N, ones_m, inv_d, gain, out, rope=None):
            st = {}

            def A():
                st['src'] = src_fn()
                sq = sqr.next()[0:R, 0:N]
                act(sq, st['src'], AF.Square)
                st['sq'] = sq

            def B():
                ssp = ss_ring.next()[0:R, 0:N]
                mm(ssp, ones_m, st['sq'])
                ln = lnr.next()[0:R, 0:N]
                act(ln, ssp, AF.Ln, scale=float(inv_d), bias=float(EPS))
                rs = rsr.next()[0:R, 0:N]
                act(rs, ln, AF.Exp, scale=-0.5)
                if rope is None:
                    stt(out, st['src'], gain, rs, ALU.mult, ALU.mult)
                else:
                    qn = qnr.next()[0:R, 0:N]
                    stt(qn, st['src'], gain, rs, ALU.mult, ALU.mult)
                    st['qn'] = qn

            def C():
                Rm, cosT, sinT = rope
                qn = st['qn']
                rp = rot_ring.next()[0:R, 0:N]
                mm(rp, Rm, qn)
                t1 = t1r.next()[0:R, 0:N]
                tt(t1, qn, cosT, ALU.mult, eng='pool')
                t2 = t2r.next()[0:R, 0:N]
                tt(t2, rp, sinT, ALU.mult)
                tt(out, t1, t2, ALU.add, eng='pool')

            return [A, B, C if rope is not None else None]

        def run_pipe(tiles):
            n = len(tiles)
            for step in range(n + 2):
                if step < n and tiles[step][0]:
                    tiles[step][0]()
                if 0 <= step - 1 < n and tiles[step - 1][1]:
                    tiles[step - 1][1]()
                if 0 <= step - 2 < n and tiles[step - 2][2]:
                    tiles[step - 2][2]()

        def modulate(l, which, xsrc_loader):
            for tbi, (t0, n) in enumerate(TBS):
                w = 0 if tbi < 4 else 1
                xsrc_loader(tbi, t0, n)
                ssp = ss_ring.next()[:, 0:n]
                for j in range(8):
                    sq = sqr.next()[:, 0:n]
                    act(sq, xblk[:, j, 0:n], AF.Square)
                    mm(ssp, allones, sq, start=(j == 0), stop=(j == 7))
                ln = lnr.next()[:, 0:n]
                act(ln, ssp, AF.Ln, scale=1.0 / D, bias=float(EPS))
                rs = rsr.next()[:, 0:n]
                act(rs, ln, AF.Exp, scale=-0.5)
                for j in range(8):
                    t1 = t1r.next()[:, 0:n]
                    stt(t1, xblk[:, j, 0:n], Gv[:, which, j, w:w + 1], rs, ALU.mult, ALU.mult)
                    shift = modT[:, l, (3 * which) * 8 + j, w:w + 1]
                    act(hT[:, j, t0:t0 + n], t1, AF.Identity, bias=shift)

        def load_x_block(tbi, t0, n):
            dma('sp', xblk[:, :, 0:n], xT_d[:, :, t0:t0 + n], reads=xkeys(tbi))

        pending = []

        def flush_pending():
            while pending:
                pending.pop(0)()

        def attention(q_of, keytiles, N, scale, vrows, LOOK=2):
            Op = o_ring.next()
            nk = len(keytiles)
            Sps = {}

            def qk(i):
                Sps[i] = s_ring.next()[:, 0:N]
                mm(Sps[i], keytiles[i][0], q_of)

            for i in range(min(LOOK, nk)):
                qk(i)
            flush_pending()
            for i, (k_ap, v_ap, tab) in enumerate(keytiles):
                if i + LOOK < nk:
                    qk(i + LOOK)
                Sp = Sps.pop(i)
                P = Pr.next()[:, 0:N]
                act(P, Sp, AF.Exp, scale=float(scale))
                if tab is not None:
                    P2 = P2r.next()[:, 0:N]
                    tt(P2, P, tab[:, 0:N], ALU.mult)
                    P = P2
                mm(Op[0:vrows, 0:N], v_ap, P, start=(i == 0), stop=(i == nk - 1))
            return Op

        def normalize(Op, odd, N, out_sb):
            zp = 0 if odd else 64
            r0 = 64 if odd else 0
            zr = zrow.next()
            act(zr[zp:zp + 1, 0:N], Op[zp:zp + 1, 0:N], AF.Ln)
            zr2 = zrow.next()
            act(zr2[zp:zp + 1, 0:N], zr[zp:zp + 1, 0:N], AF.Exp, scale=-1.0)
            bc = bc_ring.next()
            mm(bc[:, 0:N], ones_f[zp:zp + 1, :], zr2[zp:zp + 1, 0:N])
            osb = Osb.next()
            act(osb[r0:r0 + 64, 0:N], Op[r0:r0 + 64, 0:N], AF.Copy)
            tt(out_sb[r0:r0 + 64, 0:N], osb[r0:r0 + 64, 0:N], bc[r0:r0 + 64, 0:N], ALU.mult)

        def v_slot(kt, h):
            odd = h % 2
            return VA[:, kt, h, 0:128] if odd else VA[:, kt, h, 0:65]

        def v_dst(kt, h):
            odd = h % 2
            return VA[:, kt, h, 64:128] if odd else VA[:, kt, h, 0:64]

        def init_va():
            mset(VA.rearrange("p k h c -> p (k h c)"), 0.0, eng='dve')
            for h in range(4):
                col = 0 if h % 2 else 64
                mset(VA[:, :, h, col:col + 1], 1.0, eng='dve')

        for l in range(n_layers):
            with_ctx = l < L - 1
            li = lambda_init(l)
            q_tbs = TBS if with_ctx else TBS[:4]
            ss_ring = Ring([PB(3), PB(4)])
            rot_ring = Ring([PB(5), PB(6)])
            raw_ring = Ring([PB(0), PB(1), PB(2)])
            vp_ring = Ring([PB(int(_os.environ.get("VPB", "6")))])
            s_ring = Ring([PB(0), PB(1), PB(2)])
            o_ring = Ring([PB(3), PB(4)])
            bc_ring = Ring([PB(5)])

            def pvc(name, j=0, rows=128):
                c0, w = PV[name]
                return pv[0:rows, l, c0 + j:c0 + j + 1]

            for which, gname in ((0, 'g_mix'), (1, 'g_ffn')):
                c0, _ = PV[gname]
                for w in range(2):
                    sc = modT[:, l, (3 * which + 1) * 8:(3 * which + 2) * 8, w]
                    stt(Gv[:, which, :, w], sc, 1.0, pv[:, l, c0:c0 + 8], ALU.add, ALU.mult)
            c0, _ = PV['lvec']
            lv = pv[:, l, c0:c0 + 128].rearrange("p (a d) -> p a d", a=4)
            prod = tf(0)[:, 0:64].rearrange("p (a d) -> p a d", a=2)
            tt(prod[:, 0, :], lv[:, 0, :], lv[:, 1, :], ALU.mult)
            tt(prod[:, 1, :], lv[:, 2, :], lv[:, 3, :], ALU.mult)
            S.add('dve', lambda E, prod=prod: E.tensor_reduce(out=misc[:, 0:2], in_=prod, axis=AX.X, op=ALU.add),
                  reads=[prod], writes=[misc[:, 0:2]])
            act(misc[:, 2:4], misc[:, 0:2], AF.Exp)
            tt(misc[:, 4:5], misc[:, 3:4], misc[:, 2:3], ALU.subtract)
            ts(misc[:, 5:6], misc[:, 4:5], float(-li), None, ALU.add)
            ts(misc[:, 6:7], pvc('dsub_g'), float(1.0 - li), None, ALU.mult)
            nlam = misc[:, 5:6]
            gsub = misc[:, 6:7]

            modulate(l, 0, load_x_block)
            ck('m1')
            init_va()
            ck('va')

            def proj_fm(out_ps, wcols, t0, n):
                for kc in range(8):
                    mm(out_ps, wA[:, kc, wcols[0]:wcols[1]], hT[:, kc, t0:t0 + n], start=(kc == 0), stop=(kc == 7))

            def proj_v(col0, heads, ncols_per_head=64, slot_of=None):
                nh = len(heads)
                for kt in range(NKT):
                    vp = vp_ring.next()
                    for kc in range(8):
                        mm(vp[:, 0:nh * 64], hT[:, kc, kt * 128:(kt + 1) * 128], wA[:, kc, col0:col0 + nh * 64],
                           start=(kc == 0), stop=(kc == 7))
                    for i, hs in enumerate(heads):
                        for h in hs:
                            vcopy(v_dst(kt, h), vp[:, i * 64:(i + 1) * 64], vp, True)

            def run_attention(mixer, head_q, head_k, krows, scale, tab_of=None, na=False, diff=False):
                for h in range(4):
                    odd = h % 2
                    chunk = 2 * mixer + h // 2
                    for tbi, (t0, n) in enumerate(q_tbs):
                        if tbi < 4:
                            if na:
                                kts = list(NA_KT[tbi]) + [16, 17]
                            else:
                                kts = list(range(NKT))
                        else:
                            kts = [16, 17]
                        vrows = 128 if odd else 65
                        if not diff:
                            tiles = []
                            for kt in kts:
                                tab = None
                                if na and kt < 16:
                                    i0 = 8 * tbi - 2 * kt + 7
                                    tab = natab[:, h, (i0 + 3) * 64:(i0 + 3) * 64 + 512]
                                tiles.append((KT[0:krows, head_k(h), kt * 128:(kt + 1) * 128], v_slot(kt, h), tab))
                            Op = attention(QT[0:krows, h, t0:t0 + n], tiles, n, scale, vrows)
                            pending.append(lambda Op=Op, odd=odd, n=n, chunk=chunk, t0=t0: normalize(Op, odd, n, mixT[:, chunk, t0:t0 + n]))
                        else:
                            r0 = 64 if odd else 0
                            for pr in range(2):
                                tiles = [(KT[32 * pr:32 * pr + 32, h, kt * 128:(kt + 1) * 128], v_slot(kt, h), None) for kt in kts]
                                Op = attention(QT[32 * pr:32 * pr + 32, h, t0:t0 + n], tiles, n, scale, vrows)
                                if pr == 0:
                                    pending.append(lambda Op=Op, odd=odd, n=n: normalize(Op, odd, n, ABo[0]))
                                else:
                                    def fin(Op=Op, odd=odd, n=n, r0=r0, chunk=chunk, t0=t0):
                                        normalize(Op, odd, n, ABo[1])
                                        o = ABo[2]
                                        stt(o[r0:r0 + 64, 0:n], ABo[1][r0:r0 + 64, 0:n], nlam[r0:r0 + 64, :], ABo[0][r0:r0 + 64, 0:n], ALU.mult, ALU.add)
                                        sq = sqr.next()
                                        act(sq[r0:r0 + 64, 0:n], o[r0:r0 + 64, 0:n], AF.Square)
                                        ssp = PB(6)
                                        mm(ssp[r0:r0 + 64, 0:n], blk64[r0:r0 + 64, r0:r0 + 64], sq[r0:r0 + 64, 0:n])
                                        ln = lnr.next()
                                        act(ln[r0:r0 + 64, 0:n], ssp[r0:r0 + 64, 0:n], AF.Ln, scale=1.0 / 64, bias=float(EPS))
                                        rs = rsr.next()
                                        act(rs[r0:r0 + 64, 0:n], ln[r0:r0 + 64, 0:n], AF.Exp, scale=-0.5)
                                        stt(mixT[r0:r0 + 64, chunk, t0:t0 + n], o[r0:r0 + 64, 0:n], gsub[r0:r0 + 64, :], rs[r0:r0 + 64, 0:n],
                                            ALU.mult, ALU.mult)
                                    pending.append(fin)
                    flush_pending()

            dma('pool', wA[:, :, 0:416], win_d[l, :, 0:416].rearrange("(a p) n -> p a n", p=128))
            dma('pool', wuq_sb, wuq_d[l].rearrange("(a p) n -> p a n", p=128))
            dma('pool', wukv_sb, wukv_d[l])
            dma('sp', ropeT, tabs_d[0])
            tiles = []
            for tbi, (t0, n) in enumerate(TBS):
                lat = tbi < 4
                rope = (Rmla[0:96, 0:96], ropeT[0:96, 0, t0:t0 + n], ropeT[0:96, 1, t0:t0 + n]) if lat else None
                st = {}

                def A_cq(t0=t0, n=n, st=st):
                    st['raws'] = []
                    for c in range(2):
                        rp = raw_ring.next()[:, 0:n]
                        proj_fm(rp, (c * 128, (c + 1) * 128), t0, n)
                        st['raws'].append(rp)

                def B_cq(t0=t0, n=n, st=st):
                    ssp = ss_ring.next()[:, 0:n]
                    for c in range(2):
                        sq = sqr.next()[:, 0:n]
                        act(sq, st['raws'][c], AF.Square)
                        mm(ssp, allones, sq, start=(c == 0), stop=(c == 1))
                    ln = lnr.next()[:, 0:n]
                    act(ln, ssp, AF.Ln, scale=1.0 / 256, bias=float(EPS))
                    rs = rsr.next()[:, 0:n]
                    act(rs, ln, AF.Exp, scale=-0.5)
                    for c in range(2):
                        stt(cqn[:, c, 0:n], st['raws'][c], pvc('qa_g', c), rs, ALU.mult, ALU.mult)

                tiles.append([A_cq, B_cq, None])

                def src_ckv(t0=t0, n=n):
                    rp = raw_ring.next()[:, 0:n]
                    proj_fm(rp, (256, 384), t0, n)
                    return rp
                tiles.append(chain(src_ckv, 128, n, allones, 1.0 / 128, pvc('kva_g'), ckvnT[:, t0:t0 + n], None))

                def A_kr(t0=t0, n=n):
                    rpk = raw_ring.next()
                    for kc in range(8):
                        mm(rpk[64:96, 0:n], wA[:, kc, 384:416], hT[:, kc, t0:t0 + n], start=(kc == 0), stop=(kc == 7))
                    act(krs_buf[64:96, 0:n], rpk[64:96, 0:n], AF.Copy)
                tiles.append([A_kr, None, None])

                for h in range(4):
                    def src_q(h=h, n=n):
                        rp = raw_ring.next()[0:96, 0:n]
                        for c in range(2):
                            mm(rp, wuq_sb[:, c, h * 96:(h + 1) * 96], cqn[:, c, 0:n], start=(c == 0), stop=(c == 1))
                        return rp
                    tiles.append(chain(src_q, 96, n, allones[0:96, 0:96], 1.0 / 96, pvc('mq_g', 0, 96), QT[0:96, h, t0:t0 + n], rope))
                for h in range(4):
                    def src_k(h=h, t0=t0, n=n):
                        rp = raw_ring.next()[0:64, 0:n]
                        mm(rp, wukv_sb[:, h * 128:h * 128 + 64], ckvnT[:, t0:t0 + n])
                        kr_ = kraw.next()
                        act(kr_[0:64, 0:n], rp, AF.Copy)
                        cp(kr_[64:96, 0:n], krs_buf[64:96, 0:n])
                        return kr_[0:96, 0:n]
                    tiles.append(chain(src_k, 96, n, allones[0:96, 0:96], 1.0 / 96, pvc('mk_g', 0, 96), KT[0:96, h, t0:t0 + n], rope))
            run_pipe(tiles)
            ck('mlap')
            for kt in range(NKT):
                vp = vp_ring.next()
                for h in range(4):
                    mm(vp[:, h * 64:(h + 1) * 64], ckvnT[:, kt * 128:(kt + 1) * 128], wukv_sb[:, h * 128 + 64:h * 128 + 128])
                for h in range(4):
                    if _os.environ.get('NOVCP'):
                        continue
                    vcopy(v_dst(kt, h), vp[:, h * 64:(h + 1) * 64], vp, True)
            ck('mlav')
            run_attention(0, None, lambda h: h, 96, 96 ** -0.5)
            ck('mla')

            dma('pool', wA[:, :, 0:768], win_d[l, :, 416:1184].rearrange("(a p) n -> p a n", p=128))
            dma('sp', ropeT, tabs_d[1])
            tiles = []
            for tbi, (t0, n) in enumerate(TBS):
                lat = tbi < 4
                rope = (Rdiff[0:64, 0:64], ropeT[0:64, 0, t0:t0 + n], ropeT[0:64, 1, t0:t0 + n]) if lat else None
                for h in range(4):
                    for (cb, gname, dst) in ((0, 'dq_g', QT), (256, 'dk_g', KT)):
                        def src(h=h, cb=cb, t0=t0, n=n):
                            rp = raw_ring.next()[0:64, 0:n]
                            proj_fm(rp, (cb + h * 64, cb + h * 64 + 64), t0, n)
                            return rp
                        tiles.append(chain(src, 64, n, blk32[0:64, 0:64], 1.0 / 32, pvc(gname, 0, 64), dst[0:64, h, t0:t0 + n], rope))
            run_pipe(tiles)
            proj_v(512, [[0], [1], [2], [3]])
            run_attention(1, None, lambda h: h, 32, 32 ** -0.5, diff=True)
            ck('diff')

            dma('pool', wA[:, :, 0:768], win_d[l, :, 1184:1952].rearrange("(a p) n -> p a n", p=128))
            for h in range(4):
                dma('sp', nastg, nab_d[l, h])
                act(natab[:, h, :], nastg, AF.Exp)
                for (c0_, c1_) in ((0, TL), (TL, T)):
                    dma('pool', QT[64:96, h, c0_:c1_], aug_d[0][:, c0_:c1_])
                    dma('pool', KT[64:96, h, c0_:c1_], aug_d[1][:, c0_:c1_])
            tiles = []
            for tbi, (t0, n) in enumerate(TBS):
                for h in range(4):
                    for (cb, gname, dst) in ((0, 'nq_g', QT), (256, 'nk_g', KT)):
                        def src(h=h, cb=cb, t0=t0, n=n):
                            rp = raw_ring.next()[0:64, 0:n]
                            proj_fm(rp, (cb + h * 64, cb + h * 64 + 64), t0, n)
                            return rp
                        tiles.append(chain(src, 64, n, allones[0:64, 0:64], 1.0 / 64, pvc(gname, 0, 64), dst[0:64, h, t0:t0 + n], None))
            run_pipe(tiles)
            proj_v(512, [[0], [1], [2], [3]])
            run_attention(2, None, lambda h: h, 96, 64 ** -0.5, na=True)
            ck('na')

            dma('pool', wA[:, :, 0:512], win_d[l, :, 1952:2464].rearrange("(a p) n -> p a n", p=128))
            dma('sp', ropeT, tabs_d[2])
            tiles = []
            for tbi, (t0, n) in enumerate(TBS):
                lat = tbi < 4
                rope = (Rgqa[0:64, 0:64], ropeT[0:64, 0, t0:t0 + n], ropeT[0:64, 1, t0:t0 + n]) if lat else None
                for (cb, gname, dst, nh) in ((0, 'gq_g', QT, 4), (256, 'gk_g', KT, 2)):
                    for h in range(nh):
                        def src(h=h, cb=cb, t0=t0, n=n):
                            rp = raw_ring.next()[0:64, 0:n]
                            proj_fm(rp, (cb + h * 64, cb + h * 64 + 64), t0, n)
                            return rp
                        tiles.append(chain(src, 64, n, allones[0:64, 0:64], 1.0 / 64, pvc(gname, 0, 64), dst[0:64, h, t0:t0 + n], rope))
            run_pipe(tiles)
            ck('gqap')
            proj_v(384, [[0, 1], [2, 3]])
            ck('gqav')
            run_attention(3, None, lambda h: h // 2, 64, 64 ** -0.5)
            ck('gqa')

            def load_group(gi):
                g0_, gn_ = FFG[gi]
                wn_ = gn_ * 128
                wb = wup_bufs[gi % 2]
                dma('pool', wb[:, :, 0, 0:wn_], wup_d[l, :, g0_ * 128:g0_ * 128 + wn_].rearrange("(a p) n -> p a n", p=128))
                dma('pool', wb[:, :, 1, 0:wn_], wup_d[l, :, DFF + g0_ * 128:DFF + g0_ * 128 + wn_].rearrange("(a p) n -> p a n", p=128))
                dma('pool', wdn_bufs[gi % 2][:, 0:gn_, :], wdn_d[l, g0_ * 128:(g0_ + gn_) * 128, :].rearrange("(a p) n -> p a n", p=128))

            load_group(0)

            dma('pool', wout_sb, wout_d[l].rearrange("(a p) n -> p a n", p=128))
            all_ps = Ring([PB(i) for i in range(7)])
            ss_ring = Ring([PB(5), PB(6)])
            op_ring = Ring([PB(i) for i in range(5)])

            def outproj_loader(tbi, t0, n):
                dma('sp', xblk[:, :, 0:n], xT_d[:, :, t0:t0 + n], reads=xkeys(tbi))
                w = 0 if tbi < 4 else 1
                for j in range(8):
                    pb = op_ring.next()[:, 0:n]
                    for c in range(8):
                        mm(pb, wout_sb[:, c, j * 128:(j + 1) * 128], mixT[:, c, t0:t0 + n], start=(c == 0), stop=(c == 7))
                    stt(xblk[:, j, 0:n], pb, modT[:, l, 2 * 8 + j, w:w + 1], xblk[:, j, 0:n], ALU.mult, ALU.add)
                dma('sp', xT_d[:, :, t0:t0 + n], xblk[:, :, 0:n], writes=xkeys(tbi))

            def outproj_loader_lastctx(tbi, t0, n):
                if tbi == 4:
                    load_x_block(tbi, t0, n)
                else:
                    outproj_loader(tbi, t0, n)

            modulate(l, 1, outproj_loader if with_ctx else outproj_loader_lastctx)

            ck('outproj')
            f_tbs = TBS if with_ctx else TBS[:4]
            for ub in ubuf + ubuf2:
                mset(ub[:, 0:1], 0.0, eng='pool')
                mset(ub[:, 2049:2051], 0.0, eng='pool')
                mset(ub[:, 2307:2308], 0.0, eng='pool')
            up_ring = Ring([PB(0), PB(1), PB(2), PB(3)])
            dn_ring = Ring([PB(4), PB(5), PB(6)])
            cring = Ring(ctile)

            def ucol(t0):
                return (1 + t0) if t0 < TL else (2051 + t0 - TL)

            for gi, (g0, gn) in enumerate(FFG):
                wup_sb = wup_bufs[gi % 2]
                wdn_sb = wdn_bufs[gi % 2]
                if gi + 1 < len(FFG):
                    load_group(gi + 1)
                for cl in range(gn):
                    c = g0 + cl
                    ub = ubuf if c % 2 == 0 else ubuf2
                    for (t0, n) in f_tbs:
                        u0 = ucol(t0)
                        for ag in range(2):
                            pb = up_ring.next()[:, 0:n]
                            for kc in range(8):
                                mm(pb, wup_sb[:, kc, ag, cl * 128:(cl + 1) * 128], hT[:, kc, t0:t0 + n], start=(kc == 0), stop=(kc == 7))
                            act(ub[ag][:, u0:u0 + n], pb, AF.Copy)
                    for (t0, n) in f_tbs:
                        u0 = ucol(t0)
                        outs = []
                        for ag in range(2):
                            col = c if ag == 0 else NCH + c
                            ct = cring.next()[:, 0:n]
                            act(ct, ub[ag][:, u0:u0 + n], AF.Identity, scale=pvc('cw1', col), bias=pvc('cb', col))
                            stt(ct, ub[ag][:, u0 - 1:u0 - 1 + n], pvc('cw0', col), ct, ALU.mult, ALU.add)
                            stt(ct, ub[ag][:, u0 + 1:u0 + 1 + n], pvc('cw2', col), ct, ALU.mult, ALU.add)
                            outs.append(ct)
                        act(outs[1], outs[1], AF.Silu)
                        tt(actT[:, cl, t0:t0 + n], outs[1], outs[0], ALU.mult, eng='pool')
                dtiles = [(tbi, t0, n, j) for tbi, (t0, n) in enumerate(f_tbs) for j in range(8)]
                LA = 4
                xts = {}

                def xload(i):
                    tbi_, t0_, n_, j_ = dtiles[i]
                    xt_ = xring.next()[:, 0:n_]
                    dma('sp', xt_, xT_d[:, j_, t0_:t0_ + n_], reads=[xkey(tbi_), "xf:%d:%d" % (tbi_, j_)])
                    xts[i] = xt_

                for i in range(min(LA, len(dtiles))):
                    xload(i)
                for i, (tbi, t0, n, j) in enumerate(dtiles):
                    if i + LA < len(dtiles):
                        xload(i + LA)
                    w = 0 if tbi < 4 else 1
                    pb = dn_ring.next()[:, 0:n]
                    for cl in range(gn):
                        mm(pb, wdn_sb[:, cl, j * 128:(j + 1) * 128], actT[:, cl, t0:t0 + n], start=(cl == 0), stop=(cl == gn - 1))
                    xt = xts.pop(i)
                    stt(xt, pb, modT[:, l, 5 * 8 + j, w:w + 1], xt, ALU.mult, ALU.add)
                    dma('sp', xT_d[:, j, t0:t0 + n], xt, writes=["xf:%d:%d" % (tbi, j)])

    except StopBuild:
        pass
    import os as _os
    for tbi, (t0, n) in enumerate(TBS[:4]):
        dma('sp', xblk[:, :, 0:n], xT_d[:, :, t0:t0 + n], reads=xkeys(tbi))
        if _os.environ.get('SKIP_FINAL'):
            dma('sp', y_d[t0:t0 + n, :].rearrange("(a p) d -> p a d", p=128), xblk[:, 0:4, :].rearrange("p a (b c) -> p (a b) c", b=1)[:, :, :].rearrange("p a c -> p a c") if False else xld[:, 0:4, :])
            continue
        for a in range(4):
            for half in range(2):
                pb = PB((a * 2 + half) % int(_os.environ.get("NB", "6")))
                for jj in range(4):
                    j = half * 4 + jj
                    tr(pb[:, jj * 128:(jj + 1) * 128], xblk[:, j, a * 128:(a + 1) * 128], ident)
                if half == 0:
                    act(xld[:, a, 0:512], pb, AF.Copy)
                else:
                    cp(xld[:, a, 512:1024], pb)
        dma('sp', y_d[t0:t0 + n, :].rearrange("(a p) d -> p a d", p=128), xld[:, 0:4, :])
    if dbg:
        for tbi, (t0, n) in enumerate(TBS):
            dma('sp', xblk[:, :, 0:n], xT_d[:, :, t0:t0 + n], reads=xkeys(tbi))
            dma('sp', dbg_d[:, :, t0:t0 + n], xblk[:, :, 0:n])
    S.emit()
    return nc, S


def _rope_tab(rot):
    nf = rot // 4
    inv = np.power(np.float32(10000.0), -np.arange(nf, dtype=np.float32) / np.float32(nf)).astype(np.float32)
    t = np.arange(TL)
    row = (t // 64).astype(np.float32)
    col = (t % 64).astype(np.float32)
    ar = row[:, None] * inv
    ac = col[:, None] * inv
    ang = np.concatenate([ar, ar, ac, ac], axis=-1).astype(np.float32)
    return np.cos(ang).T.astype(np.float32), np.sin(ang).T.astype(np.float32)


def _constants():
    cf = np.zeros((128, 256), np.float32)
    cf[:, 0:128] = np.eye(128, dtype=np.float32)
    cf[:, 128:256] = 1.0
    cb = np.zeros((128, 6, 128), np.float32)
    cb[:, 0, :] = 1.0
    for b in range(4):
        cb[b * 32:(b + 1) * 32, 1, b * 32:(b + 1) * 32] = 1.0
    for b in range(2):
        cb[b * 64:(b + 1) * 64, 2, b * 64:(b + 1) * 64] = 1.0

    def add_rot(Rm, base, n):
        for i in range(n):
            Rm[base + i + n, base + i] = -1.0
            Rm[base + i, base + i + n] = 1.0
            Rm[base + 3 * n + i, base + 2 * n + i] = -1.0
            Rm[base + 2 * n + i, base + 3 * n + i] = 1.0
    add_rot(cb[:, 3, :], 64, 8)
    add_rot(cb[:, 4, :], 0, 8)
    add_rot(cb[:, 4, :], 32, 8)
    add_rot(cb[:, 5, :], 0, 16)
    aug = np.zeros((2, 32, T), np.float32)
    for q in range(TL):
        qr = q // 64
        rs = min(max(qr - 4, 0), 24)
        aug[0, :, q] = -BIG
        aug[0, rs:rs + 8, q] = 0.0
        aug[1, qr, q] = 1.0
    tabs = np.zeros((3, 128, 2, TL), np.float32)
    c32, s32 = _rope_tab(32)
    c64, s64 = _rope_tab(64)
    tabs[0, 0:64, 0, :] = 1.0
    tabs[0, 64:96, 0, :] = c32
    tabs[0, 64:96, 1, :] = s32
    tabs[1, 0:32, 0, :] = c32
    tabs[1, 32:64, 0, :] = c32
    tabs[1, 0:32, 1, :] = s32
    tabs[1, 32:64, 1, :] = s32
    tabs[2, 0:64, 0, :] = c64
    tabs[2, 0:64, 1, :] = s64
    return cf, cb.reshape(128, 768), aug, tabs


def _na_index():
    idx = np.full((128, 22, 64), 15 * 31, np.int64)
    for p in range(128):
        half, kc = p // 64, p % 64
        for pos in range(22):
            i = pos - 3 - half
            if i < 0 or i > 14:
                continue
            for qc in range(64):
                ws = min(max(qc - 8, 0), 48)
                if ws <= kc < ws + 16:
                    co = min(max(kc - qc + 15, 0), 30)
                    idx[p, pos, qc] = (14 - i) * 31 + co
    return idx.reshape(128, NAW)


def _tile_rows(v, reps, rows=128):
    out = np.zeros((rows,), np.float32)
    t = np.tile(np.asarray(v, np.float32), reps)
    out[:t.shape[0]] = t
    return out


def _prep_shared(inp):
    cf, cb, aug, tabs = _constants()
    pvA = np.zeros((128, L, NV), np.float32)

    def put(name, l, arr2d):
        c0, w = PV[name]
        pvA[:, l, c0:c0 + w] = arr2d

    for l in range(L):
        put('g_mix', l, inp['g_mix'][l].reshape(8, 128).T)
        put('g_ffn', l, inp['g_ffn'][l].reshape(8, 128).T)
        put('qa_g', l, inp['mla_q_a_g'][l].reshape(2, 128).T)
        put('kva_g', l, inp['mla_kv_a_g'][l].reshape(1, 128).T)
        put('mq_g', l, _tile_rows(inp['mla_q_g'][l], 1)[:, None])
        put('mk_g', l, _tile_rows(inp['mla_k_g'][l], 1)[:, None])
        put('dq_g', l, _tile_rows(inp['diff_q_g'][l], 4)[:, None])
        put('dk_g', l, _tile_rows(inp['diff_k_g'][l], 4)[:, None])
        put('dsub_g', l, _tile_rows(inp['diff_subln_g'][l], 2)[:, None])
        put('nq_g', l, _tile_rows(inp['na_q_g'][l], 2)[:, None])
        put('nk_g', l, _tile_rows(inp['na_k_g'][l], 2)[:, None])
        put('gq_g', l, _tile_rows(inp['gqa_q_g'][l], 2)[:, None])
        put('gk_g', l, _tile_rows(inp['gqa_k_g'][l], 2)[:, None])
        for i in range(3):
            put('cw%d' % i, l, inp['conv_w'][l, i].reshape(44, 128).T)
        put('cb', l, inp['conv_b'][l].reshape(44, 128).T)
        lv = np.concatenate([inp['diff_lq1'][l], inp['diff_lk1'][l], inp['diff_lq2'][l], inp['diff_lk2'][l]]).astype(np.float32)
        put('lvec', l, np.broadcast_to(lv[None, :], (128, 128)))
    idx = _na_index()
    nab = np.zeros((L, 4, 128, NAW), np.float32)
    for l in range(L):
        for h in range(4):
            src = np.concatenate([np.asarray(inp['na_rpb'][l, h], np.float32).ravel(), np.array([-10000.0], np.float32)])
            nab[l, h] = src[idx]
    f = lambda a: np.ascontiguousarray(np.asarray(a, np.float32))
    return {
        "w_mod": f(inp['w_mod']), "b_mod": f(inp['b_mod']), "w_in": f(inp['w_in']), "w_out": f(inp['w_out']),
        "w_uq": f(inp['mla_w_uq']), "w_ukv": f(inp['mla_w_ukv']), "w_up": f(inp['w_up']), "w_down": f(inp['w_down']),
        "pv": np.ascontiguousarray(pvA.reshape(128, L * NV)), "cf32": cf, "cbf": np.ascontiguousarray(cb),
        "aug": aug, "tabs": tabs, "nab": nab,
    }


_CACHE = {}


def kernel(**inp):
    n_layers = inp.pop('_n_layers', L)
    dbg = inp.pop('_dbg', False)
    stop = inp.pop('_stop', None)
    ncores = inp.pop('_ncores', 8)
    key = (n_layers, dbg, stop)
    if key not in _CACHE:
        _CACHE[key] = build_program(n_layers, dbg, stop)[0]
    nc = _CACHE[key]
    shared = _prep_shared(inp)
    x = np.asarray(inp['x'], np.float32)
    ctx = np.asarray(inp['ctx'], np.float32)
    c = np.asarray(inp['c'], np.float32)
    cc = np.asarray(inp['c_ctx'], np.float32)
    in_maps = []
    for b in range(ncores):
        m = dict(shared)
        m["xc"] = np.ascontiguousarray(np.concatenate([x[b], ctx[b]], axis=0))
        cT = np.zeros((128, 8, 2), np.float32)
        cT[:, :, 0] = c[b].reshape(8, 128).T
        cT[:, :, 1] = cc.reshape(8, 128).T
        m["cT"] = np.ascontiguousarray(cT.reshape(128, 16))
        in_maps.append(m)
    res = run_bass_kernel_spmd(nc, in_maps, core_ids=list(range(ncores)))
    out = np.stack([np.asarray(r["y"], np.float32) for r in res.results], axis=0)
    if dbg:
        kernel.dbg = [np.asarray(r["dbg"], np.float32) for r in res.results]
    return out
```

```python
import math
import numpy as np
import concourse.bass as bass
import concourse.mybir as mybir
from concourse.bass_utils import run_bass_kernel_spmd

F32 = mybir.dt.float32
BF16 = mybir.dt.bfloat16
AF = mybir.ActivationFunctionType
ALU = mybir.AluOpType
AX = mybir.AxisListType

ENGS = ['pe', 'act', 'dve', 'pool', 'sp']
CELL = 256
_ESZ = {}


def esz(dt):
    if dt not in _ESZ:
        _ESZ[dt] = mybir.dt.size(dt)
    return _ESZ[dt]


def ap_cells(ap):
    space = str(ap.space)
    sp = 0 if space == 'SB' else 1
    dims = ap.ap
    pstep, pcount = dims[0]
    e = esz(ap.dtype)
    off = ap.offset
    p0 = off // pstep
    foff = off % pstep
    ranges = [(foff, foff + 1)]
    for (st, cnt) in dims[1:]:
        if cnt <= 1:
            continue
        if len(ranges) * cnt <= 512 and abs(st) * e >= CELL:
            ranges = [(lo + i * st, hi + i * st) for (lo, hi) in ranges for i in range(cnt)]
        else:
            ext = (cnt - 1) * st
            if ext >= 0:
                ranges = [(lo, hi + ext) for (lo, hi) in ranges]
            else:
                ranges = [(lo + ext, hi) for (lo, hi) in ranges]
    cs = set()
    for (lo, hi) in ranges:
        c0 = (lo * e) // CELL
        c1 = (hi * e - 1) // CELL
        for c in range(c0, c1 + 1):
            cs.add(c)
    if sp == 1:
        return sorted(set(4 * 4096 + (c * CELL) // 2048 for c in cs))
    q0 = p0 // 32
    q1 = (p0 + pcount - 1) // 32
    out = []
    for q in range(q0, q1 + 1):
        base = (sp * 4 + q) * 4096
        for c in cs:
            out.append(base + c)
    return out


class Sched:
    def __init__(self, nc, n_lanes=48, same_eng_sync=True):
        self.nc = nc
        self.ops = {e: [] for e in ENGS}
        self.cw = {}
        self.cr = {}
        self.known = {e: {} for e in ENGS}
        self.snap = {}
        self.n_lanes = n_lanes
        self.lane_count = [0] * n_lanes
        self.next_lane = 0
        self.same_eng_sync = same_eng_sync
        self.eng_sem = {e: nc.alloc_semaphore("sem_" + e) for e in ENGS}
        self.lane_sem = [nc.alloc_semaphore("lane%d" % i) for i in range(n_lanes)]

    def _cells(self, items):
        cs = []
        for it in items:
            if it is None:
                continue
            if isinstance(it, (str, tuple)):
                cs.append(it)
            else:
                cs.extend(ap_cells(it))
        return cs

    def add(self, eng, fn, reads=(), writes=(), dma=False):
        ops = self.ops[eng]
        idx = len(ops)
        rc = self._cells(reads)
        wc = self._cells(writes)
        need = {}

        def want(tok, war=False):
            key, seq = tok
            if key == eng:
                if eng == 'pe' or war or not self.same_eng_sync:
                    return
            if need.get(key, -1) < seq:
                need[key] = seq

        for c in rc:
            t = self.cw.get(c)
            if t is not None:
                want(t)
        for c in wc:
            t = self.cw.get(c)
            if t is not None:
                want(t)
            rs = self.cr.get(c)
            if rs:
                for k, s in rs.items():
                    want((k, s), war=True)
        if dma:
            lane = self.next_lane
            self.next_lane = (lane + 1) % self.n_lanes
            cnt = self.lane_count[lane] + 1
            self.lane_count[lane] = cnt
            tok = (('L', lane), cnt)
            if cnt > 1:
                want((('L', lane), cnt - 1))
        else:
            tok = (eng, idx)
        kn = self.known[eng]
        waits = []
        for key, seq in need.items():
            if kn.get(key, -1) >= seq:
                continue
            waits.append((key, seq))
            kn[key] = seq
            if isinstance(key, str):
                self.ops[key][seq]['signal'] = True
                sn = self.snap.get((key, seq))
                if sn:
                    for k2, s2 in sn.items():
                        if kn.get(k2, -1) < s2:
                            kn[k2] = s2
        if not dma:
            self.snap[(eng, idx)] = dict(kn)
        ops.append(dict(fn=fn, waits=waits, tok=tok, dma=dma, signal=False))
        for c in wc:
            self.cw[c] = tok
            self.cr[c] = {}
        for c in rc:
            d = self.cr.get(c)
            if d is None:
                d = {}
                self.cr[c] = d
            if d.get(tok[0], -1) < tok[1]:
                d[tok[0]] = tok[1]
        return tok

    def emit(self):
        nc = self.nc
        for e in ENGS:
            c = 0
            for op in self.ops[e]:
                if op['signal'] and not op['dma']:
                    c += 1
                    op['sigval'] = c
        emap = {'pe': 'tensor', 'act': 'scalar', 'dve': 'vector', 'pool': 'gpsimd', 'sp': 'sync'}

        def run(e, E):
            for op in self.ops[e]:
                for key, seq in op['waits']:
                    if isinstance(key, str):
                        E.wait_ge(self.eng_sem[key], self.ops[key][seq]['sigval'])
                    else:
                        E.wait_ge(self.lane_sem[key[1]], 16 * seq)
                inst = op['fn'](E)
                if op['dma']:
                    inst.then_inc(self.lane_sem[op['tok'][0][1]], 16)
                elif op['signal']:
                    inst.then_inc(self.eng_sem[e], 1)
            if e == 'sp':
                for i, cnt in enumerate(self.lane_count):
                    if cnt:
                        E.wait_ge(self.lane_sem[i], 16 * cnt)

        with nc.Block() as block:
            for e in ENGS:
                getattr(block, emap[e])(lambda E, e=e: run(e, E))

    def stats(self):
        return {e: (len(self.ops[e]), sum(len(o['waits']) for o in self.ops[e])) for e in ENGS}


class Ring:
    def __init__(self, items):
        self.items = list(items)
        self.i = 0

    def next(self):
        r = self.items[self.i]
        self.i = (self.i + 1) % len(self.items)
        return r


D = 1024
L = 4
TL = 2048
TC = 256
T = TL + TC
NKT = T // 128
TBS = [(0, 512), (512, 512), (1024, 512), (1536, 512), (2048, 256)]
IN_COLS = 2464
DFF = 2816
NCH = DFF // 128
EPS = 1e-6
BIG = 30000.0
NA_KT = {0: range(0, 6), 1: range(2, 10), 2: range(6, 14), 3: range(10, 16)}
NAW = 22 * 64
FFG = [(0, 4), (4, 4), (8, 4), (12, 4), (16, 4), (20, 2)]

PV = {}
_c = 0
for _n, _w in [('g_mix', 8), ('g_ffn', 8), ('qa_g', 2), ('kva_g', 1), ('mq_g', 1), ('mk_g', 1),
               ('dq_g', 1), ('dk_g', 1), ('dsub_g', 1), ('nq_g', 1), ('nk_g', 1), ('gq_g', 1), ('gk_g', 1),
               ('cw0', 44), ('cw1', 44), ('cw2', 44), ('cb', 44), ('lvec', 128)]:
    PV[_n] = (_c, _w)
    _c += _w
NV = _c


def lambda_init(l):
    return 0.8 - 0.6 * math.exp(-0.3 * l)


class StopBuild(Exception):
    pass


def build_program(n_layers=L, dbg=False, stop_after=None):
    nc = bass.Bass("TRN2", target_bir_lowering=False)

    def ck(name):
        if stop_after == name:
            raise StopBuild()

    def din(name, shape):
        return nc.dram_tensor(name, list(shape), F32, kind="ExternalInput").ap()

    xc_d = din("xc", [T, D])
    cT_d = din("cT", [128, 16])
    wmod_d = din("w_mod", [L, D, 6 * D])
    bmod_d = din("b_mod", [L, 6 * D])
    win_d = din("w_in", [L, D, IN_COLS])
    wout_d = din("w_out", [L, D, D])
    wuq_d = din("w_uq", [L, 256, 384])
    wukv_d = din("w_ukv", [L, 128, 512])
    wup_d = din("w_up", [L, D, 2 * DFF])
    wdn_d = din("w_down", [L, DFF, D])
    pv_d = din("pv", [128, L * NV])
    cf_d = din("cf32", [128, 256])
    cb_d = din("cbf", [128, 6 * 128])
    aug_d = din("aug", [2, 32, T])
    tabs_d = din("tabs", [3, 128, 2, TL])
    nab_d = din("nab", [L, 4, 128, NAW])
    y_d = nc.dram_tensor("y", [TL, D], F32, kind="ExternalOutput").ap()
    xT_d = nc.dram_tensor("xT_scr", [128, 8, T], F32).ap()
    if dbg:
        dbg_d = nc.dram_tensor("dbg", [128, 8, T], F32, kind="ExternalOutput").ap()

    ARENA_F32 = 53000
    arena = nc.alloc_sbuf_tensor("arena", [128, ARENA_F32], F32)
    psum = nc.alloc_psum_tensor("psum", [128, 4096], F32)
    S = Sched(nc)

    pos = [0]

    def alloc_b(nbytes):
        a = pos[0]
        n = (nbytes + 255) // 256 * 256
        pos[0] += n
        assert pos[0] <= ARENA_F32 * 4, pos[0]
        return a

    def f32v(boff, n):
        return arena[:, boff // 4: boff // 4 + n]

    def bf16v(boff, n):
        return arena[:, boff // 4: boff // 4 + (n + 1) // 2].bitcast(BF16)

    def PB(i):
        return psum[:, i * 512:(i + 1) * 512]

    ident = f32v(alloc_b(512), 128)
    ones_f = f32v(alloc_b(512), 128)
    cbf = bf16v(alloc_b(6 * 256), 6 * 128).rearrange("p (a b) -> p a b", a=6)
    allones, blk32, blk64, Rmla, Rdiff, Rgqa = [cbf[:, i, :] for i in range(6)]
    pv = f32v(alloc_b(L * NV * 4), L * NV).rearrange("p (l n) -> p l n", l=L)
    modT = f32v(alloc_b(L * 96 * 4), L * 96).rearrange("p (l s w) -> p l s w", l=L, s=48)
    cT = f32v(alloc_b(64), 16)
    scT_f = f32v(alloc_b(64), 16)
    scT = bf16v(alloc_b(32), 16).rearrange("p (a b) -> p a b", a=8)
    Gv = f32v(alloc_b(4 * 16 * 4), 64).rearrange("p (a j w) -> p a j w", a=2, j=8)
    misc = f32v(alloc_b(64 * 4), 64)
    hT = bf16v(alloc_b(8 * T * 2), 8 * T).rearrange("p (a t) -> p a t", a=8)
    mix_off = alloc_b(8 * T * 2)
    mixT = bf16v(mix_off, 8 * T).rearrange("p (a t) -> p a t", a=8)
    qkv_off = alloc_b(3 * 4 * T * 2)
    QT = bf16v(qkv_off, 4 * T).rearrange("p (a t) -> p a t", a=4)
    KT = bf16v(qkv_off + 4 * T * 2, 4 * T).rearrange("p (a t) -> p a t", a=4)
    VA = bf16v(qkv_off + 8 * T * 2, NKT * 4 * 128).rearrange("p (k h c) -> p k h c", k=NKT, h=4)
    xblk = f32v(qkv_off + 16384, 8 * 512).rearrange("p (a n) -> p a n", a=8)
    xld = f32v(qkv_off + 32768, 4 * 1024).rearrange("p (a n) -> p a n", a=4)
    GN = 4
    fo = mix_off
    actT = bf16v(fo, GN * T).rearrange("p (a t) -> p a t", a=GN); fo += GN * T * 2
    UW = T + 4
    UWP = (UW * 4 + 255) // 256 * 256
    ubuf = [f32v(fo + i * UWP, UW) for i in range(2)]; fo += 2 * UWP
    assert fo <= qkv_off + 512, (fo, qkv_off)
    WUPB = 8 * 2 * GN * 128 * 2
    wup_bufs = [bf16v(qkv_off + 32768, 8 * 2 * GN * 128).rearrange("p (k g n) -> p k g n", k=8, g=2),
                bf16v(qkv_off + 512, 8 * 2 * GN * 128).rearrange("p (k g n) -> p k g n", k=8, g=2)]
    assert qkv_off + 32768 + WUPB <= qkv_off + 3 * 4 * T * 2 and 512 + WUPB <= 32768
    fo = qkv_off
    assert fo <= qkv_off + 3 * 4 * T * 2, (fo, qkv_off + 3 * 4 * T * 2)
    tab_off = alloc_b(2 * TL * 4)
    ropeT = f32v(tab_off, 2 * TL).rearrange("p (a t) -> p a t", a=2)
    natab = bf16v(tab_off, 4 * NAW).rearrange("p (h n) -> p h n", h=4)
    wA = bf16v(alloc_b(8 * 768 * 2), 8 * 768).rearrange("p (a n) -> p a n", a=8)
    wout_sb = bf16v(tab_off, 8 * 1024).rearrange("p (a n) -> p a n", a=8)
    ubuf2 = [f32v(tab_off + i * UWP, UW) for i in range(2)]
    assert 2 * UWP <= 2 * TL * 4 + 8 * 768 * 2
    wuq_sb = bf16v(alloc_b(2 * 384 * 2), 2 * 384).rearrange("p (a n) -> p a n", a=2)
    wukv_sb = bf16v(alloc_b(512 * 2), 512)
    ckvnT = bf16v(alloc_b(T * 2), T)
    cqn = bf16v(alloc_b(2 * 512 * 2), 2 * 512).rearrange("p (a n) -> p a n", a=2)
    r1_off = tab_off + 2 * UWP
    wdn_bufs = [bf16v(r1_off + i * GN * 1024 * 2, GN * 1024).rearrange("p (a n) -> p a n", a=GN) for i in range(2)]
    tmp_off = alloc_b(24 * 1024)
    assert r1_off + 2 * GN * 1024 * 2 <= tmp_off, (r1_off, tmp_off)
    krs_buf = f32v(alloc_b(2048), 512)
    sq_extra = alloc_b(2048)

    def tf(i):
        return f32v(tmp_off + i * 2048, 512)

    def tb16(i, half=0):
        return bf16v(tmp_off + i * 2048 + half * 1024, 512)

    kraw = Ring([tf(0), tf(1)])
    sqr = Ring([tb16(2, 0), tb16(2, 1), bf16v(sq_extra, 512), bf16v(sq_extra + 1024, 512)])
    lnr = Ring([tf(3), tf(4)])
    rsr = Ring([tf(5), tf(6)])
    qnr = Ring([tb16(7, 0), tb16(7, 1)])
    t1r = Ring([tf(8), tf(9)])
    t2r = Ring([tf(10), tf(11)])
    Pr = Ring([tb16(0, 0), tb16(0, 1), tb16(1, 0), tb16(1, 1)])
    P2r = Ring([tb16(2, 0), tb16(2, 1)])
    Osb = Ring([tf(3), tf(4)])
    zrow = Ring([tf(5), tf(6)])
    ABo = [tf(7), tf(8), tf(9)]
    xring = Ring([tf(6), tf(7), tf(8), tf(9), tf(10), tf(11)])
    ctile = [tf(0), tf(1), tf(2), tf(3), tf(4), tf(5)]
    nastg = f32v(tmp_off, NAW)
    wmr = Ring([bf16v(tmp_off + i * 4096, 2048) for i in range(3)])
    m_sb = f32v(qkv_off, 6 * D)
    bm_sb = f32v(qkv_off + 24576, 6 * D)

    def mm(out, lhsT, rhs, start=True, stop=True):
        S.add('pe', lambda E: E.matmul(out, lhsT, rhs, start=start, stop=stop), reads=[lhsT, rhs], writes=[out])

    def tr(out, in_, idn):
        S.add('pe', lambda E: E.transpose(out, in_, idn), reads=[in_, idn], writes=[out])

    def act(out, in_, func, scale=None, bias=None, eng='act'):
        kw = {}
        rd = [in_]
        if scale is not None:
            kw['scale'] = scale
            if not isinstance(scale, float):
                rd.append(scale)
        if bias is not None:
            kw['bias'] = bias
            if not isinstance(bias, float):
                rd.append(bias)
        S.add('act', lambda E: E.activation(out, in_, func, **kw), reads=rd, writes=[out])

    def stt(out, in0, scalar, in1, op0, op1):
        rd = [in0, in1] + ([] if isinstance(scalar, float) else [scalar])
        S.add('dve', lambda E: E.scalar_tensor_tensor(out=out, in0=in0, scalar=scalar, in1=in1, op0=op0, op1=op1),
              reads=rd, writes=[out])

    def tt(out, in0, in1, op, eng='dve'):
        S.add(eng, lambda E: E.tensor_tensor(out=out, in0=in0, in1=in1, op=op), reads=[in0, in1], writes=[out])

    def ts(out, in0, s1, s2, op0, op1=None, eng='dve'):
        rd = [in0] + [s for s in (s1, s2) if s is not None and not isinstance(s, float)]
        if op1 is None:
            S.add(eng, lambda E: E.tensor_scalar(out=out, in0=in0, scalar1=s1, scalar2=None, op0=op0), reads=rd, writes=[out])
        else:
            S.add(eng, lambda E: E.tensor_scalar(out=out, in0=in0, scalar1=s1, scalar2=s2, op0=op0, op1=op1), reads=rd, writes=[out])

    def cp(out, in_, eng='dve'):
        S.add(eng, lambda E: E.tensor_copy(out=out, in_=in_), reads=[in_], writes=[out])

    def vcopy(out, in_, bank, use_act):
        if use_act:
            S.add('act', lambda E: E.activation(out, in_, AF.Copy), reads=[bank], writes=[out])
        else:
            S.add('dve', lambda E: E.tensor_copy(out=out, in_=in_), reads=[bank], writes=[out])

    def mset(ap, val, eng='dve'):
        S.add(eng, lambda E: E.memset(ap, val), writes=[ap])

    def dma(q, out, in_, reads=None, writes=None):
        r = [in_] if reads is None else reads
        w = [out] if writes is None else writes
        r = [a for a in r if isinstance(a, (str, tuple)) or str(a.space) != 'DRAM']
        w = [a for a in w if isinstance(a, (str, tuple)) or str(a.space) != 'DRAM']
        S.add(q, lambda E: E.dma_start(out=out, in_=in_), reads=r, writes=w, dma=True)

    def xkey(tbi):
        return "x:%d" % tbi

    def xkeys(tbi):
        return ["x:%d" % tbi] + ["xf:%d:%d" % (tbi, j) for j in range(8)]

    try:
        import os as _os
        if _os.environ.get('SKIP_CONST'):
            raise StopBuild()
        dma('sp', ident, cf_d[:, 0:128])
        dma('sp', ones_f, cf_d[:, 128:256])
        dma('pool', cbf, cb_d.rearrange("p (a b) -> p a b", a=6))
        dma('sp', pv, pv_d.rearrange("p (l n) -> p l n", l=L))
        dma('sp', cT, cT_d)
        act(scT_f, cT, AF.Silu)
        cp(scT, scT_f.rearrange("p (a b) -> p a b", a=8))

        ck('c0')
        for tbi, (t0, n) in enumerate(TBS):
            ntt = n // 128
            dma('sp', xld[:, 0:ntt, :], xc_d[t0:t0 + n, :].rearrange("(a p) d -> p a d", p=128))
            for j in range(8):
                pb = PB(j % 2)
                for a in range(ntt):
                    tr(pb[:, a * 128:(a + 1) * 128], xld[:, a, j * 128:(j + 1) * 128], ident)
                if j % 2 == 0:
                    act(xblk[:, j, 0:n], pb[:, 0:n], AF.Copy)
                else:
                    cp(xblk[:, j, 0:n], pb[:, 0:n])
            dma('sp', xT_d[:, :, t0:t0 + n], xblk[:, :, 0:n], writes=xkeys(tbi))

        ck('xt')
        for l in range(n_layers):
            dma('sp', bm_sb[0:1, :], bmod_d[l:l + 1, :])
            dma('sp', bm_sb[1:2, :], bmod_d[l:l + 1, :])
            for cg in range(3):
                for kc in range(8):
                    piece = wmr.next()
                    dma('pool', piece, wmod_d[l, kc * 128:(kc + 1) * 128, cg * 2048:(cg + 1) * 2048])
                    for i in range(4):
                        mm(PB(i)[0:2, :], scT[:, kc, :], piece[:, i * 512:(i + 1) * 512], start=(kc == 0), stop=(kc == 7))
                for i in range(4):
                    c0 = cg * 2048 + i * 512
                    tt(m_sb[0:2, c0:c0 + 512], PB(i)[0:2, :], bm_sb[0:2, c0:c0 + 512], ALU.add)
            pt = PB(4)
            for s in range(48):
                tr(pt[:, 2 * s:2 * s + 2], m_sb[0:2, s * 128:(s + 1) * 128], ident[0:2, 0:2])
            cp(modT[:, l].rearrange("p s w -> p (s w)"), pt[:, 0:96])

        ck('mod')
        def rstd_from(src, R, N, ones_m, inv_d):
            sq = sqr.next()[0:R, 0:N]
            act(sq, src, AF.Square)
            ssp = ss_ring.next()[0:R, 0:N]
            mm(ssp, ones_m, sq)
            ln = lnr.next()[0:R, 0:N]
            act(ln, ssp, AF.Ln, scale=float(inv_d), bias=float(EPS))
            rs = rsr.next()[0:R, 0:N]
            act(rs, ln, AF.Exp, scale=-0.5)
            return rs

        def norm_rope(src, R, N, ones_m, inv_d, gain, out, rope=None):
            rs = rstd_from(src, R, N, ones_m, inv_d)
            if rope is None:
                stt(out, src, gain, rs, ALU.mult, ALU.mult)
                return
            Rm, cosT, sinT = rope
            qn = qnr.next()[0:R, 0:N]
            stt(qn, src, gain, rs, ALU.mult, ALU.mult)
            rp = rot_ring.next()[0:R, 0:N]
            mm(rp, Rm, qn)
            t1 = t1r.next()[0:R, 0:N]
            tt(t1, qn, cosT, ALU.mult, eng='pool')
            t2 = t2r.next()[0:R, 0:N]
            tt(t2, rp, sinT, ALU.mult)
            tt(out, t1, t2, ALU.add, eng='pool')

        def chain(src_fn, R, N, ones_m, inv_d, gain, out, rope=None):
            st = {}

            def A():
                st['src'] = src_fn()
                sq = sqr.next()[0:R, 0:N]
                act(sq, st['src'], AF.Square)
                st['sq'] = sq

            def B():
                ssp = ss_ring.next()[0:R, 0:N]
                mm(ssp, ones_m, st['sq'])
                ln = lnr.next()[0:R, 0:N]
                act(ln, ssp, AF.Ln, scale=float(inv_d), bias=float(EPS))
                rs = rsr.next()[0:R, 0:N]
                act(rs, ln, AF.Exp, scale=-0.5)
                if rope is None:
                    stt(out, st['src'], gain, rs, ALU.mult, ALU.mult)
                else:
                    qn = qnr.next()[0:R, 0:N]
                    stt(qn, st['src'], gain, rs, ALU.mult, ALU.mult)
                    st['qn'] = qn

            def C():
                Rm, cosT, sinT = rope
                qn = st['qn']
                rp = rot_ring.next()[0:R, 0:N]
                mm(rp, Rm, qn)
                t1 = t1r.next()[0:R, 0:N]
                tt(t1, qn, cosT, ALU.mult, eng='pool')
                t2 = t2r.next()[0:R, 0:N]
                tt(t2, rp, sinT, ALU.mult)
                tt(out, t1, t2, ALU.add, eng='pool')

            return [A, B, C if rope is not None else None]

        def run_pipe(tiles):
            n = len(tiles)
            for step in range(n + 2):
                if step < n and tiles[step][0]:
                    tiles[step][0]()
                if 0 <= step - 1 < n and tiles[step - 1][1]:
                    tiles[step - 1][1]()
                if 0 <= step - 2 < n and tiles[step - 2][2]:
                    tiles[step - 2][2]()

        def modulate(l, which, xsrc_loader):
            for tbi, (t0, n) in enumerate(TBS):
                w = 0 if tbi < 4 else 1
                xsrc_loader(tbi, t0, n)
                ssp = ss_ring.next()[:, 0:n]
                for j in range(8):
                    sq = sqr.next()[:, 0:n]
                    act(sq, xblk[:, j, 0:n], AF.Square)
                    mm(ssp, allones, sq, start=(j == 0), stop=(j == 7))
                ln = lnr.next()[:, 0:n]
                act(ln, ssp, AF.Ln, scale=1.0 / D, bias=float(EPS))
                rs = rsr.next()[:, 0:n]
                act(rs, ln, AF.Exp, scale=-0.5)
                for j in range(8):
                    t1 = t1r.next()[:, 0:n]
                    stt(t1, xblk[:, j, 0:n], Gv[:, which, j, w:w + 1], rs, ALU.mult, ALU.mult)
                    shift = modT[:, l, (3 * which) * 8 + j, w:w + 1]
                    act(hT[:, j, t0:t0 + n], t1, AF.Identity, bias=shift)

        def load_x_block(tbi, t0, n):
            dma('sp', xblk[:, :, 0:n], xT_d[:, :, t0:t0 + n], reads=xkeys(tbi))

        pending = []

        def flush_pending():
            while pending:
                pending.pop(0)()

        def attention(q_of, keytiles, N, scale, vrows, LOOK=2):
            Op = o_ring.next()
            nk = len(keytiles)
            Sps = {}

            def qk(i):
                Sps[i] = s_ring.next()[:, 0:N]
                mm(Sps[i], keytiles[i][0], q_of)

            for i in range(min(LOOK, nk)):
                qk(i)
            flush_pending()
            for i, (k_ap, v_ap, tab) in enumerate(keytiles):
                if i + LOOK < nk:
                    qk(i + LOOK)
                Sp = Sps.pop(i)
                P = Pr.next()[:, 0:N]
                act(P, Sp, AF.Exp, scale=float(scale))
                if tab is not None:
                    P2 = P2r.next()[:, 0:N]
                    tt(P2, P, tab[:, 0:N], ALU.mult)
                    P = P2
                mm(Op[0:vrows, 0:N], v_ap, P, start=(i == 0), stop=(i == nk - 1))
            return Op

        def normalize(Op, odd, N, out_sb):
            zp = 0 if odd else 64
            r0 = 64 if odd else 0
            zr = zrow.next()
            act(zr[zp:zp + 1, 0:N], Op[zp:zp + 1, 0:N], AF.Ln)
            zr2 = zrow.next()
            act(zr2[zp:zp + 1, 0:N], zr[zp:zp + 1, 0:N], AF.Exp, scale=-1.0)
            bc = bc_ring.next()
            mm(bc[:, 0:N], ones_f[zp:zp + 1, :], zr2[zp:zp + 1, 0:N])
            osb = Osb.next()
            act(osb[r0:r0 + 64, 0:N], Op[r0:r0 + 64, 0:N], AF.Copy)
            tt(out_sb[r0:r0 + 64, 0:N], osb[r0:r0 + 64, 0:N], bc[r0:r0 + 64, 0:N], ALU.mult)

        def v_slot(kt, h):
            odd = h % 2
            return VA[:, kt, h, 0:128] if odd else VA[:, kt, h, 0:65]

        def v_dst(kt, h):
            odd = h % 2
            return VA[:, kt, h, 64:128] if odd else VA[:, kt, h, 0:64]

        def init_va():
            mset(VA.rearrange("p k h c -> p (k h c)"), 0.0, eng='dve')
            for h in range(4):
                col = 0 if h % 2 else 64
                mset(VA[:, :, h, col:col + 1], 1.0, eng='dve')

        for l in range(n_layers):
            with_ctx = l < L - 1
            li = lambda_init(l)
            q_tbs = TBS if with_ctx else TBS[:4]
            ss_ring = Ring([PB(3), PB(4)])
            rot_ring = Ring([PB(5), PB(6)])
            raw_ring = Ring([PB(0), PB(1), PB(2)])
            vp_ring = Ring([PB(int(_os.environ.get("VPB", "6")))])
            s_ring = Ring([PB(0), PB(1), PB(2)])
            o_ring = Ring([PB(3), PB(4)])
            bc_ring = Ring([PB(5)])

            def pvc(name, j=0, rows=128):
                c0, w = PV[name]
                return pv[0:rows, l, c0 + j:c0 + j + 1]

            for which, gname in ((0, 'g_mix'), (1, 'g_ffn')):
                c0, _ = PV[gname]
                for w in range(2):
                    sc = modT[:, l, (3 * which + 1) * 8:(3 * which + 2) * 8, w]
                    stt(Gv[:, which, :, w], sc, 1.0, pv[:, l, c0:c0 + 8], ALU.add, ALU.mult)
            c0, _ = PV['lvec']
            lv = pv[:, l, c0:c0 + 128].rearrange("p (a d) -> p a d", a=4)
            prod = tf(0)[:, 0:64].rearrange("p (a d) -> p a d", a=2)
            tt(prod[:, 0, :], lv[:, 0, :], lv[:, 1, :], ALU.mult)
            tt(prod[:, 1, :], lv[:, 2, :], lv[:, 3, :], ALU.mult)
            S.add('dve', lambda E, prod=prod: E.tensor_reduce(out=misc[:, 0:2], in_=prod, axis=AX.X, op=ALU.add),
                  reads=[prod], writes=[misc[:, 0:2]])
            act(misc[:, 2:4], misc[:, 0:2], AF.Exp)
            tt(misc[:, 4:5], misc[:, 3:4], misc[:, 2:3], ALU.subtract)
            ts(misc[:, 5:6], misc[:, 4:5], float(-li), None, ALU.add)
            ts(misc[:, 6:7], pvc('dsub_g'), float(1.0 - li), None, ALU.mult)
            nlam = misc[:, 5:6]
            gsub = misc[:, 6:7]

            modulate(l, 0, load_x_block)
            ck('m1')
            init_va()
            ck('va')

            def proj_fm(out_ps, wcols, t0, n):
                for kc in range(8):
                    mm(out_ps, wA[:, kc, wcols[0]:wcols[1]], hT[:, kc, t0:t0 + n], start=(kc == 0), stop=(kc == 7))

            def proj_v(col0, heads, ncols_per_head=64, slot_of=None):
                nh = len(heads)
                for kt in range(NKT):
                    vp = vp_ring.next()
                    for kc in range(8):
                        mm(vp[:, 0:nh * 64], hT[:, kc, kt * 128:(kt + 1) * 128], wA[:, kc, col0:col0 + nh * 64],
                           start=(kc == 0), stop=(kc == 7))
                    for i, hs in enumerate(heads):
                        for h in hs:
                            vcopy(v_dst(kt, h), vp[:, i * 64:(i + 1) * 64], vp, True)

            def run_attention(mixer, head_q, head_k, krows, scale, tab_of=None, na=False, diff=False):
                for h in range(4):
                    odd = h % 2
                    chunk = 2 * mixer + h // 2
                    for tbi, (t0, n) in enumerate(q_tbs):
                        if tbi < 4:
                            if na:
                                kts = list(NA_KT[tbi]) + [16, 17]
                            else:
                                kts = list(range(NKT))
                        else:
                            kts = [16, 17]
                        vrows = 128 if odd else 65
                        if not diff:
                            tiles = []
                            for kt in kts:
                                tab = None
                                if na and kt < 16:
                                    i0 = 8 * tbi - 2 * kt + 7
                                    tab = natab[:, h, (i0 + 3) * 64:(i0 + 3) * 64 + 512]
                                tiles.append((KT[0:krows, head_k(h), kt * 128:(kt + 1) * 128], v_slot(kt, h), tab))
                            Op = attention(QT[0:krows, h, t0:t0 + n], tiles, n, scale, vrows)
                            pending.append(lambda Op=Op, odd=odd, n=n, chunk=chunk, t0=t0: normalize(Op, odd, n, mixT[:, chunk, t0:t0 + n]))
                        else:
                            r0 = 64 if odd else 0
                            for pr in range(2):
                                tiles = [(KT[32 * pr:32 * pr + 32, h, kt * 128:(kt + 1) * 128], v_slot(kt, h), None) for kt in kts]
                                Op = attention(QT[32 * pr:32 * pr + 32, h, t0:t0 + n], tiles, n, scale, vrows)
                                if pr == 0:
                                    pending.append(lambda Op=Op, odd=odd, n=n: normalize(Op, odd, n, ABo[0]))
                                else:
                                    def fin(Op=Op, odd=odd, n=n, r0=r0, chunk=chunk, t0=t0):
                                        normalize(Op, odd, n, ABo[1])
                                        o = ABo[2]
                                        stt(o[r0:r0 + 64, 0:n], ABo[1][r0:r0 + 64, 0:n], nlam[r0:r0 + 64, :], ABo[0][r0:r0 + 64, 0:n], ALU.mult, ALU.add)
                                        sq = sqr.next()
                                        act(sq[r0:r0 + 64, 0:n], o[r0:r0 + 64, 0:n], AF.Square)
                                        ssp = PB(6)
                                        mm(ssp[r0:r0 + 64, 0:n], blk64[r0:r0 + 64, r0:r0 + 64], sq[r0:r0 + 64, 0:n])
                                        ln = lnr.next()
                                        act(ln[r0:r0 + 64, 0:n], ssp[r0:r0 + 64, 0:n], AF.Ln, scale=1.0 / 64, bias=float(EPS))
                                        rs = rsr.next()
                                        act(rs[r0:r0 + 64, 0:n], ln[r0:r0 + 64, 0:n], AF.Exp, scale=-0.5)
                                        stt(mixT[r0:r0 + 64, chunk, t0:t0 + n], o[r0:r0 + 64, 0:n], gsub[r0:r0 + 64, :], rs[r0:r0 + 64, 0:n],
                                            ALU.mult, ALU.mult)
                                    pending.append(fin)
                    flush_pending()

            dma('pool', wA[:, :, 0:416], win_d[l, :, 0:416].rearrange("(a p) n -> p a n", p=128))
            dma('pool', wuq_sb, wuq_d[l].rearrange("(a p) n -> p a n", p=128))
            dma('pool', wukv_sb, wukv_d[l])
            dma('sp', ropeT, tabs_d[0])
            tiles = []
            for tbi, (t0, n) in enumerate(TBS):
                lat = tbi < 4
                rope = (Rmla[0:96, 0:96], ropeT[0:96, 0, t0:t0 + n], ropeT[0:96, 1, t0:t0 + n]) if lat else None
                st = {}

                def A_cq(t0=t0, n=n, st=st):
                    st['raws'] = []
                    for c in range(2):
                        rp = raw_ring.next()[:, 0:n]
                        proj_fm(rp, (c * 128, (c + 1) * 128), t0, n)
                        st['raws'].append(rp)

                def B_cq(t0=t0, n=n, st=st):
                    ssp = ss_ring.next()[:, 0:n]
                    for c in range(2):
                        sq = sqr.next()[:, 0:n]
                        act(sq, st['raws'][c], AF.Square)
                        mm(ssp, allones, sq, start=(c == 0), stop=(c == 1))
                    ln = lnr.next()[:, 0:n]
                    act(ln, ssp, AF.Ln, scale=1.0 / 256, bias=float(EPS))
                    rs = rsr.next()[:, 0:n]
                    act(rs, ln, AF.Exp, scale=-0.5)
                    for c in range(2):
                        stt(cqn[:, c, 0:n], st['raws'][c], pvc('qa_g', c), rs, ALU.mult, ALU.mult)

                tiles.append([A_cq, B_cq, None])

                def src_ckv(t0=t0, n=n):
                    rp = raw_ring.next()[:, 0:n]
                    proj_fm(rp, (256, 384), t0, n)
                    return rp
                tiles.append(chain(src_ckv, 128, n, allones, 1.0 / 128, pvc('kva_g'), ckvnT[:, t0:t0 + n], None))

                def A_kr(t0=t0, n=n):
                    rpk = raw_ring.next()
                    for kc in range(8):
                        mm(rpk[64:96, 0:n], wA[:, kc, 384:416], hT[:, kc, t0:t0 + n], start=(kc == 0), stop=(kc == 7))
                    act(krs_buf[64:96, 0:n], rpk[64:96, 0:n], AF.Copy)
                tiles.append([A_kr, None, None])

                for h in range(4):
                    def src_q(h=h, n=n):
                        rp = raw_ring.next()[0:96, 0:n]
                        for c in range(2):
                            mm(rp, wuq_sb[:, c, h * 96:(h + 1) * 96], cqn[:, c, 0:n], start=(c == 0), stop=(c == 1))
                        return rp
                    tiles.append(chain(src_q, 96, n, allones[0:96, 0:96], 1.0 / 96, pvc('mq_g', 0, 96), QT[0:96, h, t0:t0 + n], rope))
                for h in range(4):
                    def src_k(h=h, t0=t0, n=n):
                        rp = raw_ring.next()[0:64, 0:n]
                        mm(rp, wukv_sb[:, h * 128:h * 128 + 64], ckvnT[:, t0:t0 + n])
                        kr_ = kraw.next()
                        act(kr_[0:64, 0:n], rp, AF.Copy)
                        cp(kr_[64:96, 0:n], krs_buf[64:96, 0:n])
                        return kr_[0:96, 0:n]
                    tiles.append(chain(src_k, 96, n, allones[0:96, 0:96], 1.0 / 96, pvc('mk_g', 0, 96), KT[0:96, h, t0:t0 + n], rope))
            run_pipe(tiles)
            ck('mlap')
            for kt in range(NKT):
                vp = vp_ring.next()
                for h in range(4):
                    mm(vp[:, h * 64:(h + 1) * 64], ckvnT[:, kt * 128:(kt + 1) * 128], wukv_sb[:, h * 128 + 64:h * 128 + 128])
                for h in range(4):
                    if _os.environ.get('NOVCP'):
                        continue
                    vcopy(v_dst(kt, h), vp[:, h * 64:(h + 1) * 64], vp, True)
            ck('mlav')
            run_attention(0, None, lambda h: h, 96, 96 ** -0.5)
            ck('mla')

            dma('pool', wA[:, :, 0:768], win_d[l, :, 416:1184].rearrange("(a p) n -> p a n", p=128))
            dma('sp', ropeT, tabs_d[1])
            tiles = []
            for tbi, (t0, n) in enumerate(TBS):
                lat = tbi < 4
                rope = (Rdiff[0:64, 0:64], ropeT[0:64, 0, t0:t0 + n], ropeT[0:64, 1, t0:t0 + n]) if lat else None
                for h in range(4):
                    for (cb, gname, dst) in ((0, 'dq_g', QT), (256, 'dk_g', KT)):
                        def src(h=h, cb=cb, t0=t0, n=n):
                            rp = raw_ring.next()[0:64, 0:n]
                            proj_fm(rp, (cb + h * 64, cb + h * 64 + 64), t0, n)
                            return rp
                        tiles.append(chain(src, 64, n, blk32[0:64, 0:64], 1.0 / 32, pvc(gname, 0, 64), dst[0:64, h, t0:t0 + n], rope))
            run_pipe(tiles)
            proj_v(512, [[0], [1], [2], [3]])
            run_attention(1, None, lambda h: h, 32, 32 ** -0.5, diff=True)
            ck('diff')

            dma('pool', wA[:, :, 0:768], win_d[l, :, 1184:1952].rearrange("(a p) n -> p a n", p=128))
            for h in range(4):
                dma('sp', nastg, nab_d[l, h])
                act(natab[:, h, :], nastg, AF.Exp)
                for (c0_, c1_) in ((0, TL), (TL, T)):
                    dma('pool', QT[64:96, h, c0_:c1_], aug_d[0][:, c0_:c1_])
                    dma('pool', KT[64:96, h, c0_:c1_], aug_d[1][:, c0_:c1_])
            tiles = []
            for tbi, (t0, n) in enumerate(TBS):
                for h in range(4):
                    for (cb, gname, dst) in ((0, 'nq_g', QT), (256, 'nk_g', KT)):
                        def src(h=h, cb=cb, t0=t0, n=n):
                            rp = raw_ring.next()[0:64, 0:n]
                            proj_fm(rp, (cb + h * 64, cb + h * 64 + 64), t0, n)
                            return rp
                        tiles.append(chain(src, 64, n, allones[0:64, 0:64], 1.0 / 64, pvc(gname, 0, 64), dst[0:64, h, t0:t0 + n], None))
            run_pipe(tiles)
            proj_v(512, [[0], [1], [2], [3]])
            run_attention(2, None, lambda h: h, 96, 64 ** -0.5, na=True)
            ck('na')

            dma('pool', wA[:, :, 0:512], win_d[l, :, 1952:2464].rearrange("(a p) n -> p a n", p=128))
            dma('sp', ropeT, tabs_d[2])
            tiles = []
            for tbi, (t0, n) in enumerate(TBS):
                lat = tbi < 4
                rope = (Rgqa[0:64, 0:64], ropeT[0:64, 0, t0:t0 + n], ropeT[0:64, 1, t0:t0 + n]) if lat else None
                for (cb, gname, dst, nh) in ((0, 'gq_g', QT, 4), (256, 'gk_g', KT, 2)):
                    for h in range(nh):
                        def src(h=h, cb=cb, t0=t0, n=n):
                            rp = raw_ring.next()[0:64, 0:n]
                            proj_fm(rp, (cb + h * 64, cb + h * 64 + 64), t0, n)
                            return rp
                        tiles.append(chain(src, 64, n, allones[0:64, 0:64], 1.0 / 64, pvc(gname, 0, 64), dst[0:64, h, t0:t0 + n], rope))
            run_pipe(tiles)
            ck('gqap')
            proj_v(384, [[0, 1], [2, 3]])
            ck('gqav')
            run_attention(3, None, lambda h: h // 2, 64, 64 ** -0.5)
            ck('gqa')

            def load_group(gi):
                g0_, gn_ = FFG[gi]
                wn_ = gn_ * 128
                wb = wup_bufs[gi % 2]
                dma('pool', wb[:, :, 0, 0:wn_], wup_d[l, :, g0_ * 128:g0_ * 128 + wn_].rearrange("(a p) n -> p a n", p=128))
                dma('pool', wb[:, :, 1, 0:wn_], wup_d[l, :, DFF + g0_ * 128:DFF + g0_ * 128 + wn_].rearrange("(a p) n -> p a n", p=128))
                dma('pool', wdn_bufs[gi % 2][:, 0:gn_, :], wdn_d[l, g0_ * 128:(g0_ + gn_) * 128, :].rearrange("(a p) n -> p a n", p=128))

            load_group(0)

            dma('pool', wout_sb, wout_d[l].rearrange("(a p) n -> p a n", p=128))
            all_ps = Ring([PB(i) for i in range(7)])
            ss_ring = Ring([PB(5), PB(6)])
            op_ring = Ring([PB(i) for i in range(5)])

            def outproj_loader(tbi, t0, n):
                dma('sp', xblk[:, :, 0:n], xT_d[:, :, t0:t0 + n], reads=xkeys(tbi))
                w = 0 if tbi < 4 else 1
                for j in range(8):
                    pb = op_ring.next()[:, 0:n]
                    for c in range(8):
                        mm(pb, wout_sb[:, c, j * 128:(j + 1) * 128], mixT[:, c, t0:t0 + n], start=(c == 0), stop=(c == 7))
                    stt(xblk[:, j, 0:n], pb, modT[:, l, 2 * 8 + j, w:w + 1], xblk[:, j, 0:n], ALU.mult, ALU.add)
                dma('sp', xT_d[:, :, t0:t0 + n], xblk[:, :, 0:n], writes=xkeys(tbi))

            def outproj_loader_lastctx(tbi, t0, n):
                if tbi == 4:
                    load_x_block(tbi, t0, n)
                else:
                    outproj_loader(tbi, t0, n)

            modulate(l, 1, outproj_loader if with_ctx else outproj_loader_lastctx)

            ck('outproj')
            f_tbs = TBS if with_ctx else TBS[:4]
            for ub in ubuf + ubuf2:
                mset(ub[:, 0:1], 0.0, eng='pool')
                mset(ub[:, 2049:2051], 0.0, eng='pool')
                mset(ub[:, 2307:2308], 0.0, eng='pool')
            up_ring = Ring([PB(0), PB(1), PB(2), PB(3)])
            dn_ring = Ring([PB(4), PB(5), PB(6)])
            cring = Ring(ctile)

            def ucol(t0):
                return (1 + t0) if t0 < TL else (2051 + t0 - TL)

            for gi, (g0, gn) in enumerate(FFG):
                wup_sb = wup_bufs[gi % 2]
                wdn_sb = wdn_bufs[gi % 2]
                if gi + 1 < len(FFG):
                    load_group(gi + 1)
                for cl in range(gn):
                    c = g0 + cl
                    ub = ubuf if c % 2 == 0 else ubuf2
                    for (t0, n) in f_tbs:
                        u0 = ucol(t0)
                        for ag in range(2):
                            pb = up_ring.next()[:, 0:n]
                            for kc in range(8):
                                mm(pb, wup_sb[:, kc, ag, cl * 128:(cl + 1) * 128], hT[:, kc, t0:t0 + n], start=(kc == 0), stop=(kc == 7))
                            act(ub[ag][:, u0:u0 + n], pb, AF.Copy)
                    for (t0, n) in f_tbs:
                        u0 = ucol(t0)
                        outs = []
                        for ag in range(2):
                            col = c if ag == 0 else NCH + c
                            ct = cring.next()[:, 0:n]
                            act(ct, ub[ag][:, u0:u0 + n], AF.Identity, scale=pvc('cw1', col), bias=pvc('cb', col))
                            stt(ct, ub[ag][:, u0 - 1:u0 - 1 + n], pvc('cw0', col), ct, ALU.mult, ALU.add)
                            stt(ct, ub[ag][:, u0 + 1:u0 + 1 + n], pvc('cw2', col), ct, ALU.mult, ALU.add)
                            outs.append(ct)
                        act(outs[1], outs[1], AF.Silu)
                        tt(actT[:, cl, t0:t0 + n], outs[1], outs[0], ALU.mult, eng='pool')
                dtiles = [(tbi, t0, n, j) for tbi, (t0, n) in enumerate(f_tbs) for j in range(8)]
                LA = 4
                xts = {}

                def xload(i):
                    tbi_, t0_, n_, j_ = dtiles[i]
                    xt_ = xring.next()[:, 0:n_]
                    dma('sp', xt_, xT_d[:, j_, t0_:t0_ + n_], reads=[xkey(tbi_), "xf:%d:%d" % (tbi_, j_)])
                    xts[i] = xt_

                for i in range(min(LA, len(dtiles))):
                    xload(i)
                for i, (tbi, t0, n, j) in enumerate(dtiles):
                    if i + LA < len(dtiles):
                        xload(i + LA)
                    w = 0 if tbi < 4 else 1
                    pb = dn_ring.next()[:, 0:n]
                    for cl in range(gn):
                        mm(pb, wdn_sb[:, cl, j * 128:(j + 1) * 128], actT[:, cl, t0:t0 + n], start=(cl == 0), stop=(cl == gn - 1))
                    xt = xts.pop(i)
                    stt(xt, pb, modT[:, l, 5 * 8 + j, w:w + 1], xt, ALU.mult, ALU.add)
                    dma('act', xT_d[:, j, t0:t0 + n], xt, writes=["xf:%d:%d" % (tbi, j)])

    except StopBuild:
        pass
    import os as _os
    for tbi, (t0, n) in enumerate(TBS[:4]):
        dma('sp', xblk[:, :, 0:n], xT_d[:, :, t0:t0 + n], reads=xkeys(tbi))
        if _os.environ.get('SKIP_FINAL'):
            dma('sp', y_d[t0:t0 + n, :].rearrange("(a p) d -> p a d", p=128), xblk[:, 0:4, :].rearrange("p a (b c) -> p (a b) c", b=1)[:, :, :].rearrange("p a c -> p a c") if False else xld[:, 0:4, :])
            continue
        for a in range(4):
            for half in range(2):
                pb = PB((a * 2 + half) % int(_os.environ.get("NB", "6")))
                for jj in range(4):
                    j = half * 4 + jj
                    tr(pb[:, jj * 128:(jj + 1) * 128], xblk[:, j, a * 128:(a + 1) * 128], ident)
                if half == 0:
                    act(xld[:, a, 0:512], pb, AF.Copy)
                else:
                    cp(xld[:, a, 512:1024], pb)
        dma('sp', y_d[t0:t0 + n, :].rearrange("(a p) d -> p a d", p=128), xld[:, 0:4, :])
    if dbg:
        for tbi, (t0, n) in enumerate(TBS):
            dma('sp', xblk[:, :, 0:n], xT_d[:, :, t0:t0 + n], reads=xkeys(tbi))
            dma('sp', dbg_d[:, :, t0:t0 + n], xblk[:, :, 0:n])
    S.emit()
    return nc, S


def _rope_tab(rot):
    nf = rot // 4
    inv = np.power(np.float32(10000.0), -np.arange(nf, dtype=np.float32) / np.float32(nf)).astype(np.float32)
    t = np.arange(TL)
    row = (t // 64).astype(np.float32)
    col = (t % 64).astype(np.float32)
    ar = row[:, None] * inv
    ac = col[:, None] * inv
    ang = np.concatenate([ar, ar, ac, ac], axis=-1).astype(np.float32)
    return np.cos(ang).T.astype(np.float32), np.sin(ang).T.astype(np.float32)


def _constants():
    cf = np.zeros((128, 256), np.float32)
    cf[:, 0:128] = np.eye(128, dtype=np.float32)
    cf[:, 128:256] = 1.0
    cb = np.zeros((128, 6, 128), np.float32)
    cb[:, 0, :] = 1.0
    for b in range(4):
        cb[b * 32:(b + 1) * 32, 1, b * 32:(b + 1) * 32] = 1.0
    for b in range(2):
        cb[b * 64:(b + 1) * 64, 2, b * 64:(b + 1) * 64] = 1.0

    def add_rot(Rm, base, n):
        for i in range(n):
            Rm[base + i + n, base + i] = -1.0
            Rm[base + i, base + i + n] = 1.0
            Rm[base + 3 * n + i, base + 2 * n + i] = -1.0
            Rm[base + 2 * n + i, base + 3 * n + i] = 1.0
    add_rot(cb[:, 3, :], 64, 8)
    add_rot(cb[:, 4, :], 0, 8)
    add_rot(cb[:, 4, :], 32, 8)
    add_rot(cb[:, 5, :], 0, 16)
    aug = np.zeros((2, 32, T), np.float32)
    for q in range(TL):
        qr = q // 64
        rs = min(max(qr - 4, 0), 24)
        aug[0, :, q] = -BIG
        aug[0, rs:rs + 8, q] = 0.0
        aug[1, qr, q] = 1.0
    tabs = np.zeros((3, 128, 2, TL), np.float32)
    c32, s32 = _rope_tab(32)
    c64, s64 = _rope_tab(64)
    tabs[0, 0:64, 0, :] = 1.0
    tabs[0, 64:96, 0, :] = c32
    tabs[0, 64:96, 1, :] = s32
    tabs[1, 0:32, 0, :] = c32
    tabs[1, 32:64, 0, :] = c32
    tabs[1, 0:32, 1, :] = s32
    tabs[1, 32:64, 1, :] = s32
    tabs[2, 0:64, 0, :] = c64
    tabs[2, 0:64, 1, :] = s64
    return cf, cb.reshape(128, 768), aug, tabs


def _na_index():
    idx = np.full((128, 22, 64), 15 * 31, np.int64)
    for p in range(128):
        half, kc = p // 64, p % 64
        for pos in range(22):
            i = pos - 3 - half
            if i < 0 or i > 14:
                continue
            for qc in range(64):
                ws = min(max(qc - 8, 0), 48)
                if ws <= kc < ws + 16:
                    co = min(max(kc - qc + 15, 0), 30)
                    idx[p, pos, qc] = (14 - i) * 31 + co
    return idx.reshape(128, NAW)


def _tile_rows(v, reps, rows=128):
    out = np.zeros((rows,), np.float32)
    t = np.tile(np.asarray(v, np.float32), reps)
    out[:t.shape[0]] = t
    return out


def _prep_shared(inp):
    cf, cb, aug, tabs = _constants()
    pvA = np.zeros((128, L, NV), np.float32)

    def put(name, l, arr2d):
        c0, w = PV[name]
        pvA[:, l, c0:c0 + w] = arr2d

    for l in range(L):
        put('g_mix', l, inp['g_mix'][l].reshape(8, 128).T)
        put('g_ffn', l, inp['g_ffn'][l].reshape(8, 128).T)
        put('qa_g', l, inp['mla_q_a_g'][l].reshape(2, 128).T)
        put('kva_g', l, inp['mla_kv_a_g'][l].reshape(1, 128).T)
        put('mq_g', l, _tile_rows(inp['mla_q_g'][l], 1)[:, None])
        put('mk_g', l, _tile_rows(inp['mla_k_g'][l], 1)[:, None])
        put('dq_g', l, _tile_rows(inp['diff_q_g'][l], 4)[:, None])
        put('dk_g', l, _tile_rows(inp['diff_k_g'][l], 4)[:, None])
        put('dsub_g', l, _tile_rows(inp['diff_subln_g'][l], 2)[:, None])
        put('nq_g', l, _tile_rows(inp['na_q_g'][l], 2)[:, None])
        put('nk_g', l, _tile_rows(inp['na_k_g'][l], 2)[:, None])
        put('gq_g', l, _tile_rows(inp['gqa_q_g'][l], 2)[:, None])
        put('gk_g', l, _tile_rows(inp['gqa_k_g'][l], 2)[:, None])
        for i in range(3):
            put('cw%d' % i, l, inp['conv_w'][l, i].reshape(44, 128).T)
        put('cb', l, inp['conv_b'][l].reshape(44, 128).T)
        lv = np.concatenate([inp['diff_lq1'][l], inp['diff_lk1'][l], inp['diff_lq2'][l], inp['diff_lk2'][l]]).astype(np.float32)
        put('lvec', l, np.broadcast_to(lv[None, :], (128, 128)))
    idx = _na_index()
    nab = np.zeros((L, 4, 128, NAW), np.float32)
    for l in range(L):
        for h in range(4):
            src = np.concatenate([np.asarray(inp['na_rpb'][l, h], np.float32).ravel(), np.array([-10000.0], np.float32)])
            nab[l, h] = src[idx]
    f = lambda a: np.ascontiguousarray(np.asarray(a, np.float32))
    return {
        "w_mod": f(inp['w_mod']), "b_mod": f(inp['b_mod']), "w_in": f(inp['w_in']), "w_out": f(inp['w_out']),
        "w_uq": f(inp['mla_w_uq']), "w_ukv": f(inp['mla_w_ukv']), "w_up": f(inp['w_up']), "w_down": f(inp['w_down']),
        "pv": np.ascontiguousarray(pvA.reshape(128, L * NV)), "cf32": cf, "cbf": np.ascontiguousarray(cb),
        "aug": aug, "tabs": tabs, "nab": nab,
    }


_CACHE = {}


def kernel(**inp):
    n_layers = inp.pop('_n_layers', L)
    dbg = inp.pop('_dbg', False)
    stop = inp.pop('_stop', None)
    ncores = inp.pop('_ncores', 8)
    key = (n_layers, dbg, stop)
    if key not in _CACHE:
        _CACHE[key] = build_program(n_layers, dbg, stop)[0]
    nc = _CACHE[key]
    shared = _prep_shared(inp)
    x = np.asarray(inp['x'], np.float32)
    ctx = np.asarray(inp['ctx'], np.float32)
    c = np.asarray(inp['c'], np.float32)
    cc = np.asarray(inp['c_ctx'], np.float32)
    in_maps = []
    for b in range(ncores):
        m = dict(shared)
        m["xc"] = np.ascontiguousarray(np.concatenate([x[b], ctx[b]], axis=0))
        cT = np.zeros((128, 8, 2), np.float32)
        cT[:, :, 0] = c[b].reshape(8, 128).T
        cT[:, :, 1] = cc.reshape(8, 128).T
        m["cT"] = np.ascontiguousarray(cT.reshape(128, 16))
        in_maps.append(m)
    res = run_bass_kernel_spmd(nc, in_maps, core_ids=list(range(ncores)))
    out = np.stack([np.asarray(r["y"], np.float32) for r in res.results], axis=0)
    if dbg:
        kernel.dbg = [np.asarray(r["dbg"], np.float32) for r in res.results]
    return out
```

```python
import math
import numpy as np
import concourse.bass as bass
import concourse.mybir as mybir
from concourse.bass_utils import run_bass_kernel_spmd

F32 = mybir.dt.float32
BF16 = mybir.dt.bfloat16
AF = mybir.ActivationFunctionType
ALU = mybir.AluOpType
AX = mybir.AxisListType

ENGS = ['pe', 'act', 'dve', 'pool', 'sp']
CELL = 256
_ESZ = {}


def esz(dt):
    if dt not in _ESZ:
        _ESZ[dt] = mybir.dt.size(dt)
    return _ESZ[dt]


def ap_cells(ap):
    space = str(ap.space)
    sp = 0 if space == 'SB' else 1
    dims = ap.ap
    pstep, pcount = dims[0]
    e = esz(ap.dtype)
    off = ap.offset
    p0 = off // pstep
    foff = off % pstep
    ranges = [(foff, foff + 1)]
    for (st, cnt) in dims[1:]:
        if cnt <= 1:
            continue
        if len(ranges) * cnt <= 512 and abs(st) * e >= CELL:
            ranges = [(lo + i * st, hi + i * st) for (lo, hi) in ranges for i in range(cnt)]
        else:
            ext = (cnt - 1) * st
            if ext >= 0:
                ranges = [(lo, hi + ext) for (lo, hi) in ranges]
            else:
                ranges = [(lo + ext, hi) for (lo, hi) in ranges]
    cs = set()
    for (lo, hi) in ranges:
        c0 = (lo * e) // CELL
        c1 = (hi * e - 1) // CELL
        for c in range(c0, c1 + 1):
            cs.add(c)
    if sp == 1:
        return sorted(set(4 * 4096 + (c * CELL) // 2048 for c in cs))
    q0 = p0 // 32
    q1 = (p0 + pcount - 1) // 32
    out = []
    for q in range(q0, q1 + 1):
        base = (sp * 4 + q) * 4096
        for c in cs:
            out.append(base + c)
    return out


class Sched:
    def __init__(self, nc, n_lanes=48, same_eng_sync=True):
        self.nc = nc
        self.ops = {e: [] for e in ENGS}
        self.cw = {}
        self.cr = {}
        self.known = {e: {} for e in ENGS}
        self.snap = {}
        self.n_lanes = n_lanes
        self.lane_count = [0] * n_lanes
        self.next_lane = 0
        self.same_eng_sync = same_eng_sync
        self.eng_sem = {e: nc.alloc_semaphore("sem_" + e) for e in ENGS}
        self.lane_sem = [nc.alloc_semaphore("lane%d" % i) for i in range(n_lanes)]

    def _cells(self, items):
        cs = []
        for it in items:
            if it is None:
                continue
            if isinstance(it, (str, tuple)):
                cs.append(it)
            else:
                cs.extend(ap_cells(it))
        return cs

    def add(self, eng, fn, reads=(), writes=(), dma=False):
        ops = self.ops[eng]
        idx = len(ops)
        rc = self._cells(reads)
        wc = self._cells(writes)
        need = {}

        def want(tok, war=False):
            key, seq = tok
            if key == eng:
                if eng == 'pe' or war or not self.same_eng_sync:
                    return
            if need.get(key, -1) < seq:
                need[key] = seq

        for c in rc:
            t = self.cw.get(c)
            if t is not None:
                want(t)
        for c in wc:
            t = self.cw.get(c)
            if t is not None:
                want(t)
            rs = self.cr.get(c)
            if rs:
                for k, s in rs.items():
                    want((k, s), war=True)
        if dma:
            lane = self.next_lane
            self.next_lane = (lane + 1) % self.n_lanes
            cnt = self.lane_count[lane] + 1
            self.lane_count[lane] = cnt
            tok = (('L', lane), cnt)
            if cnt > 1:
                want((('L', lane), cnt - 1))
        else:
            tok = (eng, idx)
        kn = self.known[eng]
        waits = []
        for key, seq in need.items():
            if kn.get(key, -1) >= seq:
                continue
            waits.append((key, seq))
            kn[key] = seq
            if isinstance(key, str):
                self.ops[key][seq]['signal'] = True
                sn = self.snap.get((key, seq))
                if sn:
                    for k2, s2 in sn.items():
                        if kn.get(k2, -1) < s2:
                            kn[k2] = s2
        if not dma:
            self.snap[(eng, idx)] = dict(kn)
        ops.append(dict(fn=fn, waits=waits, tok=tok, dma=dma, signal=False))
        for c in wc:
            self.cw[c] = tok
            self.cr[c] = {}
        for c in rc:
            d = self.cr.get(c)
            if d is None:
                d = {}
                self.cr[c] = d
            if d.get(tok[0], -1) < tok[1]:
                d[tok[0]] = tok[1]
        return tok

    def emit(self):
        nc = self.nc
        for e in ENGS:
            c = 0
            for op in self.ops[e]:
                if op['signal'] and not op['dma']:
                    c += 1
                    op['sigval'] = c
        emap = {'pe': 'tensor', 'act': 'scalar', 'dve': 'vector', 'pool': 'gpsimd', 'sp': 'sync'}

        def run(e, E):
            for op in self.ops[e]:
                for key, seq in op['waits']:
                    if isinstance(key, str):
                        E.wait_ge(self.eng_sem[key], self.ops[key][seq]['sigval'])
                    else:
                        E.wait_ge(self.lane_sem[key[1]], 16 * seq)
                inst = op['fn'](E)
                if op['dma']:
                    inst.then_inc(self.lane_sem[op['tok'][0][1]], 16)
                elif op['signal']:
                    inst.then_inc(self.eng_sem[e], 1)
            if e == 'sp':
                for i, cnt in enumerate(self.lane_count):
                    if cnt:
                        E.wait_ge(self.lane_sem[i], 16 * cnt)

        with nc.Block() as block:
            for e in ENGS:
                getattr(block, emap[e])(lambda E, e=e: run(e, E))

    def stats(self):
        return {e: (len(self.ops[e]), sum(len(o['waits']) for o in self.ops[e])) for e in ENGS}


class Ring:
    def __init__(self, items):
        self.items = list(items)
        self.i = 0

    def next(self):
        r = self.items[self.i]
        self.i = (self.i + 1) % len(self.items)
        return r


D = 1024
L = 4
TL = 2048
TC = 256
T = TL + TC
NKT = T // 128
TBS = [(0, 512), (512, 512), (1024, 512), (1536, 512), (2048, 256)]
IN_COLS = 2464
DFF = 2816
NCH = DFF // 128
EPS = 1e-6
BIG = 30000.0
NA_KT = {0: range(0, 6), 1: range(2, 10), 2: range(6, 14), 3: range(10, 16)}
NAW = 22 * 64
FFG = [(0, 4), (4, 4), (8, 4), (12, 4), (16, 4), (20, 2)]

PV = {}
_c = 0
for _n, _w in [('g_mix', 8), ('g_ffn', 8), ('qa_g', 2), ('kva_g', 1), ('mq_g', 1), ('mk_g', 1),
               ('dq_g', 1), ('dk_g', 1), ('dsub_g', 1), ('nq_g', 1), ('nk_g', 1), ('gq_g', 1), ('gk_g', 1),
               ('cw0', 44), ('cw1', 44), ('cw2', 44), ('cb', 44), ('lvec', 128)]:
    PV[_n] = (_c, _w)
    _c += _w
NV = _c


def lambda_init(l):
    return 0.8 - 0.6 * math.exp(-0.3 * l)


class StopBuild(Exception):
    pass


def build_program(n_layers=L, dbg=False, stop_after=None):
    nc = bass.Bass("TRN2", target_bir_lowering=False)

    def ck(name):
        if stop_after == name:
            raise StopBuild()

    def din(name, shape):
        return nc.dram_tensor(name, list(shape), F32, kind="ExternalInput").ap()

    xc_d = din("xc", [T, D])
    cT_d = din("cT", [128, 16])
    wmod_d = din("w_mod", [L, D, 6 * D])
    bmod_d = din("b_mod", [L, 6 * D])
    win_d = din("w_in", [L, D, IN_COLS])
    wout_d = din("w_out", [L, D, D])
    wuq_d = din("w_uq", [L, 256, 384])
    wukv_d = din("w_ukv", [L, 128, 512])
    wup_d = din("w_up", [L, D, 2 * DFF])
    wdn_d = din("w_down", [L, DFF, D])
    pv_d = din("pv", [128, L * NV])
    cf_d = din("cf32", [128, 256])
    cb_d = din("cbf", [128, 6 * 128])
    aug_d = din("aug", [2, 32, T])
    tabs_d = din("tabs", [3, 128, 2, TL])
    nab_d = din("nab", [L, 4, 128, NAW])
    y_d = nc.dram_tensor("y", [TL, D], F32, kind="ExternalOutput").ap()
    xT_d = nc.dram_tensor("xT_scr", [128, 8, T], F32).ap()
    if dbg:
        dbg_d = nc.dram_tensor("dbg", [128, 8, T], F32, kind="ExternalOutput").ap()

    ARENA_F32 = 53000
    arena = nc.alloc_sbuf_tensor("arena", [128, ARENA_F32], F32)
    psum = nc.alloc_psum_tensor("psum", [128, 4096], F32)
    S = Sched(nc)

    pos = [0]

    def alloc_b(nbytes):
        a = pos[0]
        n = (nbytes + 255) // 256 * 256
        pos[0] += n
        assert pos[0] <= ARENA_F32 * 4, pos[0]
        return a

    def f32v(boff, n):
        return arena[:, boff // 4: boff // 4 + n]

    def bf16v(boff, n):
        return arena[:, boff // 4: boff // 4 + (n + 1) // 2].bitcast(BF16)

    def PB(i):
        return psum[:, i * 512:(i + 1) * 512]

    ident = f32v(alloc_b(512), 128)
    ones_f = f32v(alloc_b(512), 128)
    cbf = bf16v(alloc_b(6 * 256), 6 * 128).rearrange("p (a b) -> p a b", a=6)
    allones, blk32, blk64, Rmla, Rdiff, Rgqa = [cbf[:, i, :] for i in range(6)]
    pv = f32v(alloc_b(L * NV * 4), L * NV).rearrange("p (l n) -> p l n", l=L)
    modT = f32v(alloc_b(L * 96 * 4), L * 96).rearrange("p (l s w) -> p l s w", l=L, s=48)
    cT = f32v(alloc_b(64), 16)
    scT_f = f32v(alloc_b(64), 16)
    scT = bf16v(alloc_b(32), 16).rearrange("p (a b) -> p a b", a=8)
    Gv = f32v(alloc_b(4 * 16 * 4), 64).rearrange("p (a j w) -> p a j w", a=2, j=8)
    misc = f32v(alloc_b(64 * 4), 64)
    hT = bf16v(alloc_b(8 * T * 2), 8 * T).rearrange("p (a t) -> p a t", a=8)
    mix_off = alloc_b(8 * T * 2)
    mixT = bf16v(mix_off, 8 * T).rearrange("p (a t) -> p a t", a=8)
    qkv_off = alloc_b(3 * 4 * T * 2)
    QT = bf16v(qkv_off, 4 * T).rearrange("p (a t) -> p a t", a=4)
    KT = bf16v(qkv_off + 4 * T * 2, 4 * T).rearrange("p (a t) -> p a t", a=4)
    VA = bf16v(qkv_off + 8 * T * 2, NKT * 4 * 128).rearrange("p (k h c) -> p k h c", k=NKT, h=4)
    xblk = f32v(qkv_off + 16384, 8 * 512).rearrange("p (a n) -> p a n", a=8)
    xld = f32v(qkv_off + 32768, 4 * 1024).rearrange("p (a n) -> p a n", a=4)
    GN = 4
    fo = mix_off
    actT = bf16v(fo, GN * T).rearrange("p (a t) -> p a t", a=GN); fo += GN * T * 2
    UW = T + 4
    UWP = (UW * 4 + 255) // 256 * 256
    ubuf = [f32v(fo + i * UWP, UW) for i in range(2)]; fo += 2 * UWP
    assert fo <= qkv_off + 512, (fo, qkv_off)
    WUPB = 8 * 2 * GN * 128 * 2
    wup_bufs = [bf16v(qkv_off + 32768, 8 * 2 * GN * 128).rearrange("p (k g n) -> p k g n", k=8, g=2),
                bf16v(qkv_off + 512, 8 * 2 * GN * 128).rearrange("p (k g n) -> p k g n", k=8, g=2)]
    assert qkv_off + 32768 + WUPB <= qkv_off + 3 * 4 * T * 2 and 512 + WUPB <= 32768
    fo = qkv_off
    assert fo <= qkv_off + 3 * 4 * T * 2, (fo, qkv_off + 3 * 4 * T * 2)
    tab_off = alloc_b(2 * TL * 4)
    ropeT = f32v(tab_off, 2 * TL).rearrange("p (a t) -> p a t", a=2)
    natab = bf16v(tab_off, 4 * NAW).rearrange("p (h n) -> p h n", h=4)
    wA = bf16v(alloc_b(8 * 768 * 2), 8 * 768).rearrange("p (a n) -> p a n", a=8)
    wout_sb = bf16v(tab_off, 8 * 1024).rearrange("p (a n) -> p a n", a=8)
    ubuf2 = [f32v(tab_off + i * UWP, UW) for i in range(2)]
    assert 2 * UWP <= 2 * TL * 4 + 8 * 768 * 2
    wuq_sb = bf16v(alloc_b(2 * 384 * 2), 2 * 384).rearrange("p (a n) -> p a n", a=2)
    wukv_sb = bf16v(alloc_b(512 * 2), 512)
    ckvnT = bf16v(alloc_b(T * 2), T)
    cqn = bf16v(alloc_b(2 * 512 * 2), 2 * 512).rearrange("p (a n) -> p a n", a=2)
    r1_off = tab_off + 2 * UWP
    wdn_bufs = [bf16v(r1_off + i * GN * 1024 * 2, GN * 1024).rearrange("p (a n) -> p a n", a=GN) for i in range(2)]
    tmp_off = alloc_b(24 * 1024)
    assert r1_off + 2 * GN * 1024 * 2 <= tmp_off, (r1_off, tmp_off)
    krs_buf = f32v(alloc_b(2048), 512)
    sq_extra = alloc_b(2048)

    def tf(i):
        return f32v(tmp_off + i * 2048, 512)

    def tb16(i, half=0):
        return bf16v(tmp_off + i * 2048 + half * 1024, 512)

    kraw = Ring([tf(0), tf(1)])
    sqr = Ring([tb16(2, 0), tb16(2, 1), bf16v(sq_extra, 512), bf16v(sq_extra + 1024, 512)])
    lnr = Ring([tf(3), tf(4)])
    rsr = Ring([tf(5), tf(6)])
    qnr = Ring([tb16(7, 0), tb16(7, 1)])
    t1r = Ring([tf(8), tf(9)])
    t2r = Ring([tf(10), tf(11)])
    Pr = Ring([tb16(0, 0), tb16(0, 1), tb16(1, 0), tb16(1, 1)])
    P2r = Ring([tb16(2, 0), tb16(2, 1)])
    Osb = Ring([tf(3), tf(4)])
    zrow = Ring([tf(5), tf(6)])
    ABo = [tf(7), tf(8), tf(9)]
    xring = Ring([tf(6), tf(7), tf(8), tf(9), tf(10), tf(11)])
    ctile = [tf(0), tf(1), tf(2), tf(3), tf(4), tf(5)]
    nastg = f32v(tmp_off, NAW)
    wmr = Ring([bf16v(tmp_off + i * 4096, 2048) for i in range(3)])
    m_sb = f32v(qkv_off, 6 * D)
    bm_sb = f32v(qkv_off + 24576, 6 * D)

    def mm(out, lhsT, rhs, start=True, stop=True):
        S.add('pe', lambda E: E.matmul(out, lhsT, rhs, start=start, stop=stop), reads=[lhsT, rhs], writes=[out])

    def tr(out, in_, idn):
        S.add('pe', lambda E: E.transpose(out, in_, idn), reads=[in_, idn], writes=[out])

    def act(out, in_, func, scale=None, bias=None, eng='act'):
        kw = {}
        rd = [in_]
        if scale is not None:
            kw['scale'] = scale
            if not isinstance(scale, float):
                rd.append(scale)
        if bias is not None:
            kw['bias'] = bias
            if not isinstance(bias, float):
                rd.append(bias)
        S.add('act', lambda E: E.activation(out, in_, func, **kw), reads=rd, writes=[out])

    def stt(out, in0, scalar, in1, op0, op1):
        rd = [in0, in1] + ([] if isinstance(scalar, float) else [scalar])
        S.add('dve', lambda E: E.scalar_tensor_tensor(out=out, in0=in0, scalar=scalar, in1=in1, op0=op0, op1=op1),
              reads=rd, writes=[out])

    def tt(out, in0, in1, op, eng='dve'):
        S.add(eng, lambda E: E.tensor_tensor(out=out, in0=in0, in1=in1, op=op), reads=[in0, in1], writes=[out])

    def ts(out, in0, s1, s2, op0, op1=None, eng='dve'):
        rd = [in0] + [s for s in (s1, s2) if s is not None and not isinstance(s, float)]
        if op1 is None:
            S.add(eng, lambda E: E.tensor_scalar(out=out, in0=in0, scalar1=s1, scalar2=None, op0=op0), reads=rd, writes=[out])
        else:
            S.add(eng, lambda E: E.tensor_scalar(out=out, in0=in0, scalar1=s1, scalar2=s2, op0=op0, op1=op1), reads=rd, writes=[out])

    def cp(out, in_, eng='dve'):
        S.add(eng, lambda E: E.tensor_copy(out=out, in_=in_), reads=[in_], writes=[out])

    def vcopy(out, in_, bank, use_act):
        if use_act:
            S.add('act', lambda E: E.activation(out, in_, AF.Copy), reads=[bank], writes=[out])
        else:
            S.add('dve', lambda E: E.tensor_copy(out=out, in_=in_), reads=[bank], writes=[out])

    def mset(ap, val, eng='dve'):
        S.add(eng, lambda E: E.memset(ap, val), writes=[ap])

    def dma(q, out, in_, reads=None, writes=None):
        r = [in_] if reads is None else reads
        w = [out] if writes is None else writes
        r = [a for a in r if isinstance(a, (str, tuple)) or str(a.space) != 'DRAM']
        w = [a for a in w if isinstance(a, (str, tuple)) or str(a.space) != 'DRAM']
        S.add(q, lambda E: E.dma_start(out=out, in_=in_), reads=r, writes=w, dma=True)

    def xkey(tbi):
        return "x:%d" % tbi

    def xkeys(tbi):
        return ["x:%d" % tbi] + ["xf:%d:%d" % (tbi, j) for j in range(8)]

    try:
        import os as _os
        if _os.environ.get('SKIP_CONST'):
            raise StopBuild()
        dma('sp', ident, cf_d[:, 0:128])
        dma('sp', ones_f, cf_d[:, 128:256])
        dma('pool', cbf, cb_d.rearrange("p (a b) -> p a b", a=6))
        dma('sp', pv, pv_d.rearrange("p (l n) -> p l n", l=L))
        dma('sp', cT, cT_d)
        act(scT_f, cT, AF.Silu)
        cp(scT, scT_f.rearrange("p (a b) -> p a b", a=8))

        ck('c0')
        for tbi, (t0, n) in enumerate(TBS):
            ntt = n // 128
            dma('sp', xld[:, 0:ntt, :], xc_d[t0:t0 + n, :].rearrange("(a p) d -> p a d", p=128))
            for j in range(8):
                pb = PB(j % 2)
                for a in range(ntt):
                    tr(pb[:, a * 128:(a + 1) * 128], xld[:, a, j * 128:(j + 1) * 128], ident)
                if j % 2 == 0:
                    act(xblk[:, j, 0:n], pb[:, 0:n], AF.Copy)
                else:
                    cp(xblk[:, j, 0:n], pb[:, 0:n])
            dma('sp', xT_d[:, :, t0:t0 + n], xblk[:, :, 0:n], writes=xkeys(tbi))

        ck('xt')
        for l in range(n_layers):
            dma('sp', bm_sb[0:1, :], bmod_d[l:l + 1, :])
            dma('sp', bm_sb[1:2, :], bmod_d[l:l + 1, :])
            for cg in range(3):
                for kc in range(8):
                    piece = wmr.next()
                    dma('pool', piece, wmod_d[l, kc * 128:(kc + 1) * 128, cg * 2048:(cg + 1) * 2048])
                    for i in range(4):
                        mm(PB(i)[0:2, :], scT[:, kc, :], piece[:, i * 512:(i + 1) * 512], start=(kc == 0), stop=(kc == 7))
                for i in range(4):
                    c0 = cg * 2048 + i * 512
                    tt(m_sb[0:2, c0:c0 + 512], PB(i)[0:2, :], bm_sb[0:2, c0:c0 + 512], ALU.add)
            pt = PB(4)
            for s in range(48):
                tr(pt[:, 2 * s:2 * s + 2], m_sb[0:2, s * 128:(s + 1) * 128], ident[0:2, 0:2])
            cp(modT[:, l].rearrange("p s w -> p (s w)"), pt[:, 0:96])

        ck('mod')
        def rstd_from(src, R, N, ones_m, inv_d):
            sq = sqr.next()[0:R, 0:N]
            act(sq, src, AF.Square)
            ssp = ss_ring.next()[0:R, 0:N]
            mm(ssp, ones_m, sq)
            ln = lnr.next()[0:R, 0:N]
            act(ln, ssp, AF.Ln, scale=float(inv_d), bias=float(EPS))
            rs = rsr.next()[0:R, 0:N]
            act(rs, ln, AF.Exp, scale=-0.5)
            return rs

        def norm_rope(src, R, N, ones_m, inv_d, gain, out, rope=None):
            rs = rstd_from(src, R, N, ones_m, inv_d)
            if rope is None:
                stt(out, src, gain, rs, ALU.mult, ALU.mult)
                return
            Rm, cosT, sinT = rope
            qn = qnr.next()[0:R, 0:N]
            stt(qn, src, gain, rs, ALU.mult, ALU.mult)
            rp = rot_ring.next()[0:R, 0:N]
            mm(rp, Rm, qn)
            t1 = t1r.next()[0:R, 0:N]
            tt(t1, qn, cosT, ALU.mult, eng='pool')
            t2 = t2r.next()[0:R, 0:N]
            tt(t2, rp, sinT, ALU.mult)
            tt(out, t1, t2, ALU.add, eng='pool')

        def chain(src_fn, R, N, ones_m, inv_d, gain, out, rope=None):
            st = {}

            def A():
                st['src'] = src_fn()
                sq = sqr.next()[0:R, 0:N]
                act(sq, st['src'], AF.Square)
                st['sq'] = sq

            def B():
                ssp = ss_ring.next()[0:R, 0:N]
                mm(ssp, ones_m, st['sq'])
                ln = lnr.next()[0:R, 0:N]
                act(ln, ssp, AF.Ln, scale=float(inv_d), bias=float(EPS))
                rs = rsr.next()[0:R, 0:N]
                act(rs, ln, AF.Exp, scale=-0.5)
                if rope is None:
                    stt(out, st['src'], gain, rs, ALU.mult, ALU.mult)
                else:
                    qn = qnr.next()[0:R, 0:N]
                    stt(qn, st['src'], gain, rs, ALU.mult, ALU.mult)
                    st['qn'] = qn

            def C():
                Rm, cosT, sinT = rope
                qn = st['qn']
                rp = rot_ring.next()[0:R, 0:N]
                mm(rp, Rm, qn)
                t1 = t1r.next()[0:R, 0:N]
                tt(t1, qn, cosT, ALU.mult, eng='pool')
                t2 = t2r.next()[0:R, 0:N]
                tt(t2, rp, sinT, ALU.mult)
                tt(out, t1, t2, ALU.add, eng='pool')

            return [A, B, C if rope is not None else None]

        def run_pipe(tiles):
            n = len(tiles)
            for step in range(n + 2):
                if step < n and tiles[step][0]:
                    tiles[step][0]()
                if 0 <= step - 1 < n and tiles[step - 1][1]:
                    tiles[step - 1][1]()
                if 0 <= step - 2 < n and tiles[step - 2][2]:
                    tiles[step - 2][2]()

        def modulate(l, which, xsrc_loader):
            for tbi, (t0, n) in enumerate(TBS):
                w = 0 if tbi < 4 else 1
                xsrc_loader(tbi, t0, n)
                ssp = ss_ring.next()[:, 0:n]
                for j in range(8):
                    sq = sqr.next()[:, 0:n]
                    act(sq, xblk[:, j, 0:n], AF.Square)
                    mm(ssp, allones, sq, start=(j == 0), stop=(j == 7))
                ln = lnr.next()[:, 0:n]
                act(ln, ssp, AF.Ln, scale=1.0 / D, bias=float(EPS))
                rs = rsr.next()[:, 0:n]
                act(rs, ln, AF.Exp, scale=-0.5)
                for j in range(8):
                    t1 = t1r.next()[:, 0:n]
                    stt(t1, xblk[:, j, 0:n], Gv[:, which, j, w:w + 1], rs, ALU.mult, ALU.mult)
                    shift = modT[:, l, (3 * which) * 8 + j, w:w + 1]
                    act(hT[:, j, t0:t0 + n], t1, AF.Identity, bias=shift)

        def load_x_block(tbi, t0, n):
            dma('sp', xblk[:, :, 0:n], xT_d[:, :, t0:t0 + n], reads=xkeys(tbi))

        pending = []

        def flush_pending():
            while pending:
                pending.pop(0)()

        def attention(q_of, keytiles, N, scale, vrows, LOOK=2):
            Op = o_ring.next()
            nk = len(keytiles)
            Sps = {}

            def qk(i):
                Sps[i] = s_ring.next()[:, 0:N]
                mm(Sps[i], keytiles[i][0], q_of)

            for i in range(min(LOOK, nk)):
                qk(i)
            flush_pending()
            for i, (k_ap, v_ap, tab) in enumerate(keytiles):
                if i + LOOK < nk:
                    qk(i + LOOK)
                Sp = Sps.pop(i)
                P = Pr.next()[:, 0:N]
                act(P, Sp, AF.Exp, scale=float(scale))
                if tab is not None:
                    P2 = P2r.next()[:, 0:N]
                    tt(P2, P, tab[:, 0:N], ALU.mult)
                    P = P2
                mm(Op[0:vrows, 0:N], v_ap, P, start=(i == 0), stop=(i == nk - 1))
            return Op

        def normalize(Op, odd, N, out_sb, bcr=None):
            zp = 0 if odd else 64
            r0 = 64 if odd else 0
            zr = zrow.next()
            act(zr[zp:zp + 1, 0:N], Op[zp:zp + 1, 0:N], AF.Ln)
            zr2 = zrow.next()
            act(zr2[zp:zp + 1, 0:N], zr[zp:zp + 1, 0:N], AF.Exp, scale=-1.0)
            bc = (bcr or bc_ring).next()
            mm(bc[:, 0:N], ones_f[zp:zp + 1, :], zr2[zp:zp + 1, 0:N])
            osb = Osb.next()
            act(osb[r0:r0 + 64, 0:N], Op[r0:r0 + 64, 0:N], AF.Copy)
            tt(out_sb[r0:r0 + 64, 0:N], osb[r0:r0 + 64, 0:N], bc[r0:r0 + 64, 0:N], ALU.mult)

        def v_slot(kt, h):
            odd = h % 2
            return VA[:, kt, h, 0:128] if odd else VA[:, kt, h, 0:65]

        def v_dst(kt, h):
            odd = h % 2
            return VA[:, kt, h, 64:128] if odd else VA[:, kt, h, 0:64]

        def init_va():
            mset(VA.rearrange("p k h c -> p (k h c)"), 0.0, eng='dve')
            for h in range(4):
                col = 0 if h % 2 else 64
                mset(VA[:, :, h, col:col + 1], 1.0, eng='dve')

        for l in range(n_layers):
            with_ctx = l < L - 1
            li = lambda_init(l)
            q_tbs = TBS if with_ctx else TBS[:4]
            ss_ring = Ring([PB(3), PB(4)])
            rot_ring = Ring([PB(5), PB(6)])
            raw_ring = Ring([PB(0), PB(1), PB(2)])
            vp_ring = Ring([PB(int(_os.environ.get("VPB", "6")))])
            s_ring = Ring([PB(0), PB(1), PB(2)])
            o_ring = Ring([PB(3), PB(4)])
            bc_ring = Ring([PB(5)])
            ds_ring = Ring([PB(0), PB(1), PB(2), PB(3)])
            do_ring = Ring([PB(4), PB(5), PB(6)])
            dscratch = [PB(6)]

            def pvc(name, j=0, rows=128):
                c0, w = PV[name]
                return pv[0:rows, l, c0 + j:c0 + j + 1]

            for which, gname in ((0, 'g_mix'), (1, 'g_ffn')):
                c0, _ = PV[gname]
                for w in range(2):
                    sc = modT[:, l, (3 * which + 1) * 8:(3 * which + 2) * 8, w]
                    stt(Gv[:, which, :, w], sc, 1.0, pv[:, l, c0:c0 + 8], ALU.add, ALU.mult)
            c0, _ = PV['lvec']
            lv = pv[:, l, c0:c0 + 128].rearrange("p (a d) -> p a d", a=4)
            prod = tf(0)[:, 0:64].rearrange("p (a d) -> p a d", a=2)
            tt(prod[:, 0, :], lv[:, 0, :], lv[:, 1, :], ALU.mult)
            tt(prod[:, 1, :], lv[:, 2, :], lv[:, 3, :], ALU.mult)
            S.add('dve', lambda E, prod=prod: E.tensor_reduce(out=misc[:, 0:2], in_=prod, axis=AX.X, op=ALU.add),
                  reads=[prod], writes=[misc[:, 0:2]])
            act(misc[:, 2:4], misc[:, 0:2], AF.Exp)
            tt(misc[:, 4:5], misc[:, 3:4], misc[:, 2:3], ALU.subtract)
            ts(misc[:, 5:6], misc[:, 4:5], float(-li), None, ALU.add)
            ts(misc[:, 6:7], pvc('dsub_g'), float(1.0 - li), None, ALU.mult)
            nlam = misc[:, 5:6]
            gsub = misc[:, 6:7]

            modulate(l, 0, load_x_block)
            ck('m1')
            init_va()
            ck('va')

            def proj_fm(out_ps, wcols, t0, n):
                for kc in range(8):
                    mm(out_ps, wA[:, kc, wcols[0]:wcols[1]], hT[:, kc, t0:t0 + n], start=(kc == 0), stop=(kc == 7))

            def proj_v(col0, heads, ncols_per_head=64, slot_of=None):
                nh = len(heads)
                for kt in range(NKT):
                    vp = vp_ring.next()
                    for kc in range(8):
                        mm(vp[:, 0:nh * 64], hT[:, kc, kt * 128:(kt + 1) * 128], wA[:, kc, col0:col0 + nh * 64],
                           start=(kc == 0), stop=(kc == 7))
                    for i, hs in enumerate(heads):
                        for h in hs:
                            vcopy(v_dst(kt, h), vp[:, i * 64:(i + 1) * 64], vp, True)

            def run_attention(mixer, head_q, head_k, krows, scale, tab_of=None, na=False, diff=False):
                for h in range(4):
                    odd = h % 2
                    chunk = 2 * mixer + h // 2
                    for tbi, (t0, n) in enumerate(q_tbs):
                        if tbi < 4:
                            if na:
                                kts = list(NA_KT[tbi]) + [16, 17]
                            else:
                                kts = list(range(NKT))
                        else:
                            kts = [16, 17]
                        vrows = 128 if odd else 65
                        if not diff:
                            tiles = []
                            for kt in kts:
                                tab = None
                                if na and kt < 16:
                                    i0 = 8 * tbi - 2 * kt + 7
                                    tab = natab[:, h, (i0 + 3) * 64:(i0 + 3) * 64 + 512]
                                tiles.append((KT[0:krows, head_k(h), kt * 128:(kt + 1) * 128], v_slot(kt, h), tab))
                            Op = attention(QT[0:krows, h, t0:t0 + n], tiles, n, scale, vrows)
                            pending.append(lambda Op=Op, odd=odd, n=n, chunk=chunk, t0=t0: normalize(Op, odd, n, mixT[:, chunk, t0:t0 + n]))
                        else:
                            r0 = 64 if odd else 0
                            Ops = [do_ring.next(), do_ring.next()]
                            nk = len(kts)
                            Sps = {}

                            def qk2(i, h=h, t0=t0, n=n, kts=kts):
                                kt = kts[i]
                                Sps[i] = []
                                for pr in range(2):
                                    sp_ = ds_ring.next()[:, 0:n]
                                    mm(sp_, KT[32 * pr:32 * pr + 32, h, kt * 128:(kt + 1) * 128], QT[32 * pr:32 * pr + 32, h, t0:t0 + n])
                                    Sps[i].append(sp_)

                            qk2(0)
                            dscratch[0] = Ops[0]
                            flush_pending()
                            for i, kt in enumerate(kts):
                                if i + 1 < nk:
                                    qk2(i + 1)
                                sp2 = Sps.pop(i)
                                for pr in range(2):
                                    P = Pr.next()[:, 0:n]
                                    act(P, sp2[pr], AF.Exp, scale=float(scale))
                                    mm(Ops[pr][0:vrows, 0:n], v_slot(kt, h), P, start=(i == 0), stop=(i == nk - 1))

                            def fin(Ops=Ops, odd=odd, n=n, r0=r0, chunk=chunk, t0=t0):
                                scr_ring = Ring([dscratch[0]])
                                normalize(Ops[0], odd, n, ABo[0], bcr=scr_ring)
                                normalize(Ops[1], odd, n, ABo[1], bcr=scr_ring)
                                o = ABo[2]
                                stt(o[r0:r0 + 64, 0:n], ABo[1][r0:r0 + 64, 0:n], nlam[r0:r0 + 64, :], ABo[0][r0:r0 + 64, 0:n], ALU.mult, ALU.add)
                                sq = sqr.next()
                                act(sq[r0:r0 + 64, 0:n], o[r0:r0 + 64, 0:n], AF.Square)
                                ssp = dscratch[0]
                                mm(ssp[r0:r0 + 64, 0:n], blk64[r0:r0 + 64, r0:r0 + 64], sq[r0:r0 + 64, 0:n])
                                ln = lnr.next()
                                act(ln[r0:r0 + 64, 0:n], ssp[r0:r0 + 64, 0:n], AF.Ln, scale=1.0 / 64, bias=float(EPS))
                                rs = rsr.next()
                                act(rs[r0:r0 + 64, 0:n], ln[r0:r0 + 64, 0:n], AF.Exp, scale=-0.5)
                                stt(mixT[r0:r0 + 64, chunk, t0:t0 + n], o[r0:r0 + 64, 0:n], gsub[r0:r0 + 64, :], rs[r0:r0 + 64, 0:n],
                                    ALU.mult, ALU.mult)
                            pending.append(fin)
                    if diff:
                        dscratch[0] = do_ring.next()
                    flush_pending()

            dma('pool', wA[:, :, 0:416], win_d[l, :, 0:416].rearrange("(a p) n -> p a n", p=128))
            dma('pool', wuq_sb, wuq_d[l].rearrange("(a p) n -> p a n", p=128))
            dma('pool', wukv_sb, wukv_d[l])
            dma('sp', ropeT, tabs_d[0])
            tiles = []
            for tbi, (t0, n) in enumerate(TBS):
                lat = tbi < 4
                rope = (Rmla[0:96, 0:96], ropeT[0:96, 0, t0:t0 + n], ropeT[0:96, 1, t0:t0 + n]) if lat else None
                st = {}

                def A_cq(t0=t0, n=n, st=st):
                    st['raws'] = []
                    for c in range(2):
                        rp = raw_ring.next()[:, 0:n]
                        proj_fm(rp, (c * 128, (c + 1) * 128), t0, n)
                        st['raws'].append(rp)

                def B_cq(t0=t0, n=n, st=st):
                    ssp = ss_ring.next()[:, 0:n]
                    for c in range(2):
                        sq = sqr.next()[:, 0:n]
                        act(sq, st['raws'][c], AF.Square)
                        mm(ssp, allones, sq, start=(c == 0), stop=(c == 1))
                    ln = lnr.next()[:, 0:n]
                    act(ln, ssp, AF.Ln, scale=1.0 / 256, bias=float(EPS))
                    rs = rsr.next()[:, 0:n]
                    act(rs, ln, AF.Exp, scale=-0.5)
                    for c in range(2):
                        stt(cqn[:, c, 0:n], st['raws'][c], pvc('qa_g', c), rs, ALU.mult, ALU.mult)

                tiles.append([A_cq, B_cq, None])

                def src_ckv(t0=t0, n=n):
                    rp = raw_ring.next()[:, 0:n]
                    proj_fm(rp, (256, 384), t0, n)
                    return rp
                tiles.append(chain(src_ckv, 128, n, allones, 1.0 / 128, pvc('kva_g'), ckvnT[:, t0:t0 + n], None))

                def A_kr(t0=t0, n=n):
                    rpk = raw_ring.next()
                    for kc in range(8):
                        mm(rpk[64:96, 0:n], wA[:, kc, 384:416], hT[:, kc, t0:t0 + n], start=(kc == 0), stop=(kc == 7))
                    act(krs_buf[64:96, 0:n], rpk[64:96, 0:n], AF.Copy)
                tiles.append([A_kr, None, None])

                for h in range(4):
                    def src_q(h=h, n=n):
                        rp = raw_ring.next()[0:96, 0:n]
                        for c in range(2):
                            mm(rp, wuq_sb[:, c, h * 96:(h + 1) * 96], cqn[:, c, 0:n], start=(c == 0), stop=(c == 1))
                        return rp
                    tiles.append(chain(src_q, 96, n, allones[0:96, 0:96], 1.0 / 96, pvc('mq_g', 0, 96), QT[0:96, h, t0:t0 + n], rope))
                for h in range(4):
                    def src_k(h=h, t0=t0, n=n):
                        rp = raw_ring.next()[0:64, 0:n]
                        mm(rp, wukv_sb[:, h * 128:h * 128 + 64], ckvnT[:, t0:t0 + n])
                        kr_ = kraw.next()
                        act(kr_[0:64, 0:n], rp, AF.Copy)
                        cp(kr_[64:96, 0:n], krs_buf[64:96, 0:n])
                        return kr_[0:96, 0:n]
                    tiles.append(chain(src_k, 96, n, allones[0:96, 0:96], 1.0 / 96, pvc('mk_g', 0, 96), KT[0:96, h, t0:t0 + n], rope))
            run_pipe(tiles)
            ck('mlap')
            for kt in range(NKT):
                vp = vp_ring.next()
                for h in range(4):
                    mm(vp[:, h * 64:(h + 1) * 64], ckvnT[:, kt * 128:(kt + 1) * 128], wukv_sb[:, h * 128 + 64:h * 128 + 128])
                for h in range(4):
                    if _os.environ.get('NOVCP'):
                        continue
                    vcopy(v_dst(kt, h), vp[:, h * 64:(h + 1) * 64], vp, True)
            ck('mlav')
            run_attention(0, None, lambda h: h, 96, 96 ** -0.5)
            ck('mla')

            dma('pool', wA[:, :, 0:768], win_d[l, :, 416:1184].rearrange("(a p) n -> p a n", p=128))
            dma('sp', ropeT, tabs_d[1])
            tiles = []
            for tbi, (t0, n) in enumerate(TBS):
                lat = tbi < 4
                rope = (Rdiff[0:64, 0:64], ropeT[0:64, 0, t0:t0 + n], ropeT[0:64, 1, t0:t0 + n]) if lat else None
                for h in range(4):
                    for (cb, gname, dst) in ((0, 'dq_g', QT), (256, 'dk_g', KT)):
                        def src(h=h, cb=cb, t0=t0, n=n):
                            rp = raw_ring.next()[0:64, 0:n]
                            proj_fm(rp, (cb + h * 64, cb + h * 64 + 64), t0, n)
                            return rp
                        tiles.append(chain(src, 64, n, blk32[0:64, 0:64], 1.0 / 32, pvc(gname, 0, 64), dst[0:64, h, t0:t0 + n], rope))
            run_pipe(tiles)
            proj_v(512, [[0], [1], [2], [3]])
            run_attention(1, None, lambda h: h, 32, 32 ** -0.5, diff=True)
            ck('diff')

            dma('pool', wA[:, :, 0:768], win_d[l, :, 1184:1952].rearrange("(a p) n -> p a n", p=128))
            for h in range(4):
                dma('sp', nastg, nab_d[l, h])
                act(natab[:, h, :], nastg, AF.Exp)
                for (c0_, c1_) in ((0, TL), (TL, T)):
                    dma('pool', QT[64:96, h, c0_:c1_], aug_d[0][:, c0_:c1_])
                    dma('pool', KT[64:96, h, c0_:c1_], aug_d[1][:, c0_:c1_])
            tiles = []
            for tbi, (t0, n) in enumerate(TBS):
                for h in range(4):
                    for (cb, gname, dst) in ((0, 'nq_g', QT), (256, 'nk_g', KT)):
                        def src(h=h, cb=cb, t0=t0, n=n):
                            rp = raw_ring.next()[0:64, 0:n]
                            proj_fm(rp, (cb + h * 64, cb + h * 64 + 64), t0, n)
                            return rp
                        tiles.append(chain(src, 64, n, allones[0:64, 0:64], 1.0 / 64, pvc(gname, 0, 64), dst[0:64, h, t0:t0 + n], None))
            run_pipe(tiles)
            proj_v(512, [[0], [1], [2], [3]])
            run_attention(2, None, lambda h: h, 96, 64 ** -0.5, na=True)
            ck('na')

            dma('pool', wA[:, :, 0:512], win_d[l, :, 1952:2464].rearrange("(a p) n -> p a n", p=128))
            dma('sp', ropeT, tabs_d[2])
            tiles = []
            for tbi, (t0, n) in enumerate(TBS):
                lat = tbi < 4
                rope = (Rgqa[0:64, 0:64], ropeT[0:64, 0, t0:t0 + n], ropeT[0:64, 1, t0:t0 + n]) if lat else None
                for (cb, gname, dst, nh) in ((0, 'gq_g', QT, 4), (256, 'gk_g', KT, 2)):
                    for h in range(nh):
                        def src(h=h, cb=cb, t0=t0, n=n):
                            rp = raw_ring.next()[0:64, 0:n]
                            proj_fm(rp, (cb + h * 64, cb + h * 64 + 64), t0, n)
                            return rp
                        tiles.append(chain(src, 64, n, allones[0:64, 0:64], 1.0 / 64, pvc(gname, 0, 64), dst[0:64, h, t0:t0 + n], rope))
            run_pipe(tiles)
            ck('gqap')
            proj_v(384, [[0, 1], [2, 3]])
            ck('gqav')
            run_attention(3, None, lambda h: h // 2, 64, 64 ** -0.5)
            ck('gqa')

            def load_group(gi):
                g0_, gn_ = FFG[gi]
                wn_ = gn_ * 128
                wb = wup_bufs[gi % 2]
                dma('pool', wb[:, :, 0, 0:wn_], wup_d[l, :, g0_ * 128:g0_ * 128 + wn_].rearrange("(a p) n -> p a n", p=128))
                dma('pool', wb[:, :, 1, 0:wn_], wup_d[l, :, DFF + g0_ * 128:DFF + g0_ * 128 + wn_].rearrange("(a p) n -> p a n", p=128))
                dma('pool', wdn_bufs[gi % 2][:, 0:gn_, :], wdn_d[l, g0_ * 128:(g0_ + gn_) * 128, :].rearrange("(a p) n -> p a n", p=128))

            load_group(0)

            dma('pool', wout_sb, wout_d[l].rearrange("(a p) n -> p a n", p=128))
            all_ps = Ring([PB(i) for i in range(7)])
            ss_ring = Ring([PB(5), PB(6)])
            op_ring = Ring([PB(i) for i in range(5)])

            def outproj_loader(tbi, t0, n):
                dma('sp', xblk[:, :, 0:n], xT_d[:, :, t0:t0 + n], reads=xkeys(tbi))
                w = 0 if tbi < 4 else 1
                for j in range(8):
                    pb = op_ring.next()[:, 0:n]
                    for c in range(8):
                        mm(pb, wout_sb[:, c, j * 128:(j + 1) * 128], mixT[:, c, t0:t0 + n], start=(c == 0), stop=(c == 7))
                    stt(xblk[:, j, 0:n], pb, modT[:, l, 2 * 8 + j, w:w + 1], xblk[:, j, 0:n], ALU.mult, ALU.add)
                dma('sp', xT_d[:, :, t0:t0 + n], xblk[:, :, 0:n], writes=xkeys(tbi))

            def outproj_loader_lastctx(tbi, t0, n):
                if tbi == 4:
                    load_x_block(tbi, t0, n)
                else:
                    outproj_loader(tbi, t0, n)

            modulate(l, 1, outproj_loader if with_ctx else outproj_loader_lastctx)

            ck('outproj')
            f_tbs = TBS if with_ctx else TBS[:4]
            for ub in ubuf + ubuf2:
                mset(ub[:, 0:1], 0.0, eng='pool')
                mset(ub[:, 2049:2051], 0.0, eng='pool')
                mset(ub[:, 2307:2308], 0.0, eng='pool')
            up_ring = Ring([PB(0), PB(1), PB(2), PB(3)])
            dn_ring = Ring([PB(4), PB(5), PB(6)])
            cring = Ring(ctile)

            def ucol(t0):
                return (1 + t0) if t0 < TL else (2051 + t0 - TL)

            for gi, (g0, gn) in enumerate(FFG):
                wup_sb = wup_bufs[gi % 2]
                wdn_sb = wdn_bufs[gi % 2]
                if gi + 1 < len(FFG):
                    load_group(gi + 1)
                for cl in range(gn):
                    c = g0 + cl
                    ub = ubuf if c % 2 == 0 else ubuf2
                    for (t0, n) in f_tbs:
                        u0 = ucol(t0)
                        for ag in range(2):
                            pb = up_ring.next()[:, 0:n]
                            for kc in range(8):
                                mm(pb, wup_sb[:, kc, ag, cl * 128:(cl + 1) * 128], hT[:, kc, t0:t0 + n], start=(kc == 0), stop=(kc == 7))
                            act(ub[ag][:, u0:u0 + n], pb, AF.Copy)
                    for (t0, n) in f_tbs:
                        u0 = ucol(t0)
                        outs = []
                        for ag in range(2):
                            col = c if ag == 0 else NCH + c
                            ct = cring.next()[:, 0:n]
                            act(ct, ub[ag][:, u0:u0 + n], AF.Identity, scale=pvc('cw1', col), bias=pvc('cb', col))
                            stt(ct, ub[ag][:, u0 - 1:u0 - 1 + n], pvc('cw0', col), ct, ALU.mult, ALU.add)
                            stt(ct, ub[ag][:, u0 + 1:u0 + 1 + n], pvc('cw2', col), ct, ALU.mult, ALU.add)
                            outs.append(ct)
                        act(outs[1], outs[1], AF.Silu)
                        tt(actT[:, cl, t0:t0 + n], outs[1], outs[0], ALU.mult, eng='pool')
                dtiles = [(tbi, t0, n, j) for tbi, (t0, n) in enumerate(f_tbs) for j in range(8)]
                LA = 4
                xts = {}

                def xload(i):
                    tbi_, t0_, n_, j_ = dtiles[i]
                    xt_ = xring.next()[:, 0:n_]
                    dma('sp', xt_, xT_d[:, j_, t0_:t0_ + n_], reads=[xkey(tbi_), "xf:%d:%d" % (tbi_, j_)])
                    xts[i] = xt_

                for i in range(min(LA, len(dtiles))):
                    xload(i)
                for i, (tbi, t0, n, j) in enumerate(dtiles):
                    if i + LA < len(dtiles):
                        xload(i + LA)
                    w = 0 if tbi < 4 else 1
                    pb = dn_ring.next()[:, 0:n]
                    for cl in range(gn):
                        mm(pb, wdn_sb[:, cl, j * 128:(j + 1) * 128], actT[:, cl, t0:t0 + n], start=(cl == 0), stop=(cl == gn - 1))
                    xt = xts.pop(i)
                    stt(xt, pb, modT[:, l, 5 * 8 + j, w:w + 1], xt, ALU.mult, ALU.add)
                    dma('act', xT_d[:, j, t0:t0 + n], xt, writes=["xf:%d:%d" % (tbi, j)])

    except StopBuild:
        pass
    import os as _os
    for tbi, (t0, n) in enumerate(TBS[:4]):
        dma('sp', xblk[:, :, 0:n], xT_d[:, :, t0:t0 + n], reads=xkeys(tbi))
        if _os.environ.get('SKIP_FINAL'):
            dma('sp', y_d[t0:t0 + n, :].rearrange("(a p) d -> p a d", p=128), xblk[:, 0:4, :].rearrange("p a (b c) -> p (a b) c", b=1)[:, :, :].rearrange("p a c -> p a c") if False else xld[:, 0:4, :])
            continue
        for a in range(4):
            for half in range(2):
                pb = PB((a * 2 + half) % int(_os.environ.get("NB", "6")))
                for jj in range(4):
                    j = half * 4 + jj
                    tr(pb[:, jj * 128:(jj + 1) * 128], xblk[:, j, a * 128:(a + 1) * 128], ident)
                if half == 0:
                    act(xld[:, a, 0:512], pb, AF.Copy)
                else:
                    cp(xld[:, a, 512:1024], pb)
        dma('sp', y_d[t0:t0 + n, :].rearrange("(a p) d -> p a d", p=128), xld[:, 0:4, :])
    if dbg:
        for tbi, (t0, n) in enumerate(TBS):
            dma('sp', xblk[:, :, 0:n], xT_d[:, :, t0:t0 + n], reads=xkeys(tbi))
            dma('sp', dbg_d[:, :, t0:t0 + n], xblk[:, :, 0:n])
    S.emit()
    return nc, S


def _rope_tab(rot):
    nf = rot // 4
    inv = np.power(np.float32(10000.0), -np.arange(nf, dtype=np.float32) / np.float32(nf)).astype(np.float32)
    t = np.arange(TL)
    row = (t // 64).astype(np.float32)
    col = (t % 64).astype(np.float32)
    ar = row[:, None] * inv
    ac = col[:, None] * inv
    ang = np.concatenate([ar, ar, ac, ac], axis=-1).astype(np.float32)
    return np.cos(ang).T.astype(np.float32), np.sin(ang).T.astype(np.float32)


def _constants():
    cf = np.zeros((128, 256), np.float32)
    cf[:, 0:128] = np.eye(128, dtype=np.float32)
    cf[:, 128:256] = 1.0
    cb = np.zeros((128, 6, 128), np.float32)
    cb[:, 0, :] = 1.0
    for b in range(4):
        cb[b * 32:(b + 1) * 32, 1, b * 32:(b + 1) * 32] = 1.0
    for b in range(2):
        cb[b * 64:(b + 1) * 64, 2, b * 64:(b + 1) * 64] = 1.0

    def add_rot(Rm, base, n):
        for i in range(n):
            Rm[base + i + n, base + i] = -1.0
            Rm[base + i, base + i + n] = 1.0
            Rm[base + 3 * n + i, base + 2 * n + i] = -1.0
            Rm[base + 2 * n + i, base + 3 * n + i] = 1.0
    add_rot(cb[:, 3, :], 64, 8)
    add_rot(cb[:, 4, :], 0, 8)
    add_rot(cb[:, 4, :], 32, 8)
    add_rot(cb[:, 5, :], 0, 16)
    aug = np.zeros((2, 32, T), np.float32)
    for q in range(TL):
        qr = q // 64
        rs = min(max(qr - 4, 0), 24)
        aug[0, :, q] = -BIG
        aug[0, rs:rs + 8, q] = 0.0
        aug[1, qr, q] = 1.0
    tabs = np.zeros((3, 128, 2, TL), np.float32)
    c32, s32 = _rope_tab(32)
    c64, s64 = _rope_tab(64)
    tabs[0, 0:64, 0, :] = 1.0
    tabs[0, 64:96, 0, :] = c32
    tabs[0, 64:96, 1, :] = s32
    tabs[1, 0:32, 0, :] = c32
    tabs[1, 32:64, 0, :] = c32
    tabs[1, 0:32, 1, :] = s32
    tabs[1, 32:64, 1, :] = s32
    tabs[2, 0:64, 0, :] = c64
    tabs[2, 0:64, 1, :] = s64
    return cf, cb.reshape(128, 768), aug, tabs


def _na_index():
    idx = np.full((128, 22, 64), 15 * 31, np.int64)
    for p in range(128):
        half, kc = p // 64, p % 64
        for pos in range(22):
            i = pos - 3 - half
            if i < 0 or i > 14:
                continue
            for qc in range(64):
                ws = min(max(qc - 8, 0), 48)
                if ws <= kc < ws + 16:
                    co = min(max(kc - qc + 15, 0), 30)
                    idx[p, pos, qc] = (14 - i) * 31 + co
    return idx.reshape(128, NAW)


def _tile_rows(v, reps, rows=128):
    out = np.zeros((rows,), np.float32)
    t = np.tile(np.asarray(v, np.float32), reps)
    out[:t.shape[0]] = t
    return out


def _prep_shared(inp):
    cf, cb, aug, tabs = _constants()
    pvA = np.zeros((128, L, NV), np.float32)

    def put(name, l, arr2d):
        c0, w = PV[name]
        pvA[:, l, c0:c0 + w] = arr2d

    for l in range(L):
        put('g_mix', l, inp['g_mix'][l].reshape(8, 128).T)
        put('g_ffn', l, inp['g_ffn'][l].reshape(8, 128).T)
        put('qa_g', l, inp['mla_q_a_g'][l].reshape(2, 128).T)
        put('kva_g', l, inp['mla_kv_a_g'][l].reshape(1, 128).T)
        put('mq_g', l, _tile_rows(inp['mla_q_g'][l], 1)[:, None])
        put('mk_g', l, _tile_rows(inp['mla_k_g'][l], 1)[:, None])
        put('dq_g', l, _tile_rows(inp['diff_q_g'][l], 4)[:, None])
        put('dk_g', l, _tile_rows(inp['diff_k_g'][l], 4)[:, None])
        put('dsub_g', l, _tile_rows(inp['diff_subln_g'][l], 2)[:, None])
        put('nq_g', l, _tile_rows(inp['na_q_g'][l], 2)[:, None])
        put('nk_g', l, _tile_rows(inp['na_k_g'][l], 2)[:, None])
        put('gq_g', l, _tile_rows(inp['gqa_q_g'][l], 2)[:, None])
        put('gk_g', l, _tile_rows(inp['gqa_k_g'][l], 2)[:, None])
        for i in range(3):
            put('cw%d' % i, l, inp['conv_w'][l, i].reshape(44, 128).T)
        put('cb', l, inp['conv_b'][l].reshape(44, 128).T)
        lv = np.concatenate([inp['diff_lq1'][l], inp['diff_lk1'][l], inp['diff_lq2'][l], inp['diff_lk2'][l]]).astype(np.float32)
        put('lvec', l, np.broadcast_to(lv[None, :], (128, 128)))
    idx = _na_index()
    nab = np.zeros((L, 4, 128, NAW), np.float32)
    for l in range(L):
        for h in range(4):
            src = np.concatenate([np.asarray(inp['na_rpb'][l, h], np.float32).ravel(), np.array([-10000.0], np.float32)])
            nab[l, h] = src[idx]
    f = lambda a: np.ascontiguousarray(np.asarray(a, np.float32))
    return {
        "w_mod": f(inp['w_mod']), "b_mod": f(inp['b_mod']), "w_in": f(inp['w_in']), "w_out": f(inp['w_out']),
        "w_uq": f(inp['mla_w_uq']), "w_ukv": f(inp['mla_w_ukv']), "w_up": f(inp['w_up']), "w_down": f(inp['w_down']),
        "pv": np.ascontiguousarray(pvA.reshape(128, L * NV)), "cf32": cf, "cbf": np.ascontiguousarray(cb),
        "aug": aug, "tabs": tabs, "nab": nab,
    }


_CACHE = {}


def kernel(**inp):
    n_layers = inp.pop('_n_layers', L)
    dbg = inp.pop('_dbg', False)
    stop = inp.pop('_stop', None)
    ncores = inp.pop('_ncores', 8)
    key = (n_layers, dbg, stop)
    if key not in _CACHE:
        _CACHE[key] = build_program(n_layers, dbg, stop)[0]
    nc = _CACHE[key]
    shared = _prep_shared(inp)
    x = np.asarray(inp['x'], np.float32)
    ctx = np.asarray(inp['ctx'], np.float32)
    c = np.asarray(inp['c'], np.float32)
    cc = np.asarray(inp['c_ctx'], np.float32)
    in_maps = []
    for b in range(ncores):
        m = dict(shared)
        m["xc"] = np.ascontiguousarray(np.concatenate([x[b], ctx[b]], axis=0))
        cT = np.zeros((128, 8, 2), np.float32)
        cT[:, :, 0] = c[b].reshape(8, 128).T
        cT[:, :, 1] = cc.reshape(8, 128).T
        m["cT"] = np.ascontiguousarray(cT.reshape(128, 16))
        in_maps.append(m)
    res = run_bass_kernel_spmd(nc, in_maps, core_ids=list(range(ncores)))
    out = np.stack([np.asarray(r["y"], np.float32) for r in res.results], axis=0)
    if dbg:
        kernel.dbg = [np.asarray(r["dbg"], np.float32) for r in res.results]
    return out
```

```python
import math
import numpy as np
import concourse.bass as bass
import concourse.mybir as mybir
from concourse.bass_utils import run_bass_kernel_spmd

F32 = mybir.dt.float32
BF16 = mybir.dt.bfloat16
AF = mybir.ActivationFunctionType
ALU = mybir.AluOpType
AX = mybir.AxisListType

ENGS = ['pe', 'act', 'dve', 'pool', 'sp']
CELL = 256
_ESZ = {}


def esz(dt):
    if dt not in _ESZ:
        _ESZ[dt] = mybir.dt.size(dt)
    return _ESZ[dt]


def ap_cells(ap):
    space = str(ap.space)
    sp = 0 if space == 'SB' else 1
    dims = ap.ap
    pstep, pcount = dims[0]
    e = esz(ap.dtype)
    off = ap.offset
    p0 = off // pstep
    foff = off % pstep
    ranges = [(foff, foff + 1)]
    for (st, cnt) in dims[1:]:
        if cnt <= 1:
            continue
        if len(ranges) * cnt <= 512 and abs(st) * e >= CELL:
            ranges = [(lo + i * st, hi + i * st) for (lo, hi) in ranges for i in range(cnt)]
        else:
            ext = (cnt - 1) * st
            if ext >= 0:
                ranges = [(lo, hi + ext) for (lo, hi) in ranges]
            else:
                ranges = [(lo + ext, hi) for (lo, hi) in ranges]
    cs = set()
    for (lo, hi) in ranges:
        c0 = (lo * e) // CELL
        c1 = (hi * e - 1) // CELL
        for c in range(c0, c1 + 1):
            cs.add(c)
    if sp == 1:
        return sorted(set(4 * 4096 + (c * CELL) // 2048 for c in cs))
    q0 = p0 // 32
    q1 = (p0 + pcount - 1) // 32
    out = []
    for q in range(q0, q1 + 1):
        base = (sp * 4 + q) * 4096
        for c in cs:
            out.append(base + c)
    return out


class Sched:
    def __init__(self, nc, n_lanes=48, same_eng_sync=True):
        self.nc = nc
        self.ops = {e: [] for e in ENGS}
        self.cw = {}
        self.cr = {}
        self.known = {e: {} for e in ENGS}
        self.snap = {}
        self.n_lanes = n_lanes
        self.lane_count = [0] * n_lanes
        self.next_lane = 0
        self.same_eng_sync = same_eng_sync
        self.eng_sem = {e: nc.alloc_semaphore("sem_" + e) for e in ENGS}
        self.lane_sem = [nc.alloc_semaphore("lane%d" % i) for i in range(n_lanes)]

    def _cells(self, items):
        cs = []
        for it in items:
            if it is None:
                continue
            if isinstance(it, (str, tuple)):
                cs.append(it)
            else:
                cs.extend(ap_cells(it))
        return cs

    def add(self, eng, fn, reads=(), writes=(), dma=False):
        ops = self.ops[eng]
        idx = len(ops)
        rc = self._cells(reads)
        wc = self._cells(writes)
        need = {}

        def want(tok, war=False):
            key, seq = tok
            if key == eng:
                if eng == 'pe' or war or not self.same_eng_sync:
                    return
            if need.get(key, -1) < seq:
                need[key] = seq

        for c in rc:
            t = self.cw.get(c)
            if t is not None:
                want(t)
        for c in wc:
            t = self.cw.get(c)
            if t is not None:
                want(t)
            rs = self.cr.get(c)
            if rs:
                for k, s in rs.items():
                    want((k, s), war=True)
        if dma:
            lane = self.next_lane
            self.next_lane = (lane + 1) % self.n_lanes
            cnt = self.lane_count[lane] + 1
            self.lane_count[lane] = cnt
            tok = (('L', lane), cnt)
            if cnt > 1:
                want((('L', lane), cnt - 1))
        else:
            tok = (eng, idx)
        kn = self.known[eng]
        waits = []
        for key, seq in need.items():
            if kn.get(key, -1) >= seq:
                continue
            waits.append((key, seq))
            kn[key] = seq
            if isinstance(key, str):
                self.ops[key][seq]['signal'] = True
                sn = self.snap.get((key, seq))
                if sn:
                    for k2, s2 in sn.items():
                        if kn.get(k2, -1) < s2:
                            kn[k2] = s2
        if not dma:
            self.snap[(eng, idx)] = dict(kn)
        ops.append(dict(fn=fn, waits=waits, tok=tok, dma=dma, signal=False))
        for c in wc:
            self.cw[c] = tok
            self.cr[c] = {}
        for c in rc:
            d = self.cr.get(c)
            if d is None:
                d = {}
                self.cr[c] = d
            if d.get(tok[0], -1) < tok[1]:
                d[tok[0]] = tok[1]
        return tok

    def emit(self):
        nc = self.nc
        for e in ENGS:
            c = 0
            for op in self.ops[e]:
                if op['signal'] and not op['dma']:
                    c += 1
                    op['sigval'] = c
        emap = {'pe': 'tensor', 'act': 'scalar', 'dve': 'vector', 'pool': 'gpsimd', 'sp': 'sync'}

        def run(e, E):
            for op in self.ops[e]:
                for key, seq in op['waits']:
                    if isinstance(key, str):
                        E.wait_ge(self.eng_sem[key], self.ops[key][seq]['sigval'])
                    else:
                        E.wait_ge(self.lane_sem[key[1]], 16 * seq)
                inst = op['fn'](E)
                if op['dma']:
                    inst.then_inc(self.lane_sem[op['tok'][0][1]], 16)
                elif op['signal']:
                    inst.then_inc(self.eng_sem[e], 1)
            if e == 'sp':
                for i, cnt in enumerate(self.lane_count):
                    if cnt:
                        E.wait_ge(self.lane_sem[i], 16 * cnt)

        with nc.Block() as block:
            for e in ENGS:
                getattr(block, emap[e])(lambda E, e=e: run(e, E))

    def stats(self):
        return {e: (len(self.ops[e]), sum(len(o['waits']) for o in self.ops[e])) for e in ENGS}


class Ring:
    def __init__(self, items):
        self.items = list(items)
        self.i = 0

    def next(self):
        r = self.items[self.i]
        self.i = (self.i + 1) % len(self.items)
        return r


D = 1024
L = 4
TL = 2048
TC = 256
T = TL + TC
NKT = T // 128
TBS = [(0, 512), (512, 512), (1024, 512), (1536, 512), (2048, 256)]
IN_COLS = 2464
DFF = 2816
NCH = DFF // 128
EPS = 1e-6
BIG = 30000.0
NA_KT = {0: range(0, 6), 1: range(2, 10), 2: range(6, 14), 3: range(10, 16)}
NAW = 22 * 64
FFG = [(0, 4), (4, 4), (8, 4), (12, 4), (16, 4), (20, 2)]

PV = {}
_c = 0
for _n, _w in [('g_mix', 8), ('g_ffn', 8), ('qa_g', 2), ('kva_g', 1), ('mq_g', 1), ('mk_g', 1),
               ('dq_g', 1), ('dk_g', 1), ('dsub_g', 1), ('nq_g', 1), ('nk_g', 1), ('gq_g', 1), ('gk_g', 1),
               ('cw0', 44), ('cw1', 44), ('cw2', 44), ('cb', 44), ('lvec', 128)]:
    PV[_n] = (_c, _w)
    _c += _w
NV = _c


def lambda_init(l):
    return 0.8 - 0.6 * math.exp(-0.3 * l)


class StopBuild(Exception):
    pass


def build_program(n_layers=L, dbg=False, stop_after=None):
    nc = bass.Bass("TRN2", target_bir_lowering=False)

    def ck(name):
        if stop_after == name:
            raise StopBuild()

    def din(name, shape):
        return nc.dram_tensor(name, list(shape), F32, kind="ExternalInput").ap()

    xc_d = din("xc", [T, D])
    cT_d = din("cT", [128, 16])
    wmod_d = din("w_mod", [L, D, 6 * D])
    bmod_d = din("b_mod", [L, 6 * D])
    win_d = din("w_in", [L, D, IN_COLS])
    wout_d = din("w_out", [L, D, D])
    wuq_d = din("w_uq", [L, 256, 384])
    wukv_d = din("w_ukv", [L, 128, 512])
    wup_d = din("w_up", [L, D, 2 * DFF])
    wdn_d = din("w_down", [L, DFF, D])
    pv_d = din("pv", [128, L * NV])
    cf_d = din("cf32", [128, 256])
    cb_d = din("cbf", [128, 6 * 128])
    aug_d = din("aug", [2, 32, T])
    tabs_d = din("tabs", [3, 128, 2, TL])
    nab_d = din("nab", [L, 4, 128, NAW])
    y_d = nc.dram_tensor("y", [TL, D], F32, kind="ExternalOutput").ap()
    xT_d = nc.dram_tensor("xT_scr", [128, 8, T], F32).ap()
    if dbg:
        dbg_d = nc.dram_tensor("dbg", [128, 8, T], F32, kind="ExternalOutput").ap()

    ARENA_F32 = 53000
    arena = nc.alloc_sbuf_tensor("arena", [128, ARENA_F32], F32)
    psum = nc.alloc_psum_tensor("psum", [128, 4096], F32)
    S = Sched(nc)

    pos = [0]

    def alloc_b(nbytes):
        a = pos[0]
        n = (nbytes + 255) // 256 * 256
        pos[0] += n
        assert pos[0] <= ARENA_F32 * 4, pos[0]
        return a

    def f32v(boff, n):
        return arena[:, boff // 4: boff // 4 + n]

    def bf16v(boff, n):
        return arena[:, boff // 4: boff // 4 + (n + 1) // 2].bitcast(BF16)

    def PB(i):
        return psum[:, i * 512:(i + 1) * 512]

    ident = f32v(alloc_b(512), 128)
    ones_f = f32v(alloc_b(512), 128)
    cbf = bf16v(alloc_b(6 * 256), 6 * 128).rearrange("p (a b) -> p a b", a=6)
    allones, blk32, blk64, Rmla, Rdiff, Rgqa = [cbf[:, i, :] for i in range(6)]
    pv = f32v(alloc_b(L * NV * 4), L * NV).rearrange("p (l n) -> p l n", l=L)
    modT = f32v(alloc_b(L * 96 * 4), L * 96).rearrange("p (l s w) -> p l s w", l=L, s=48)
    cT = f32v(alloc_b(64), 16)
    scT_f = f32v(alloc_b(64), 16)
    scT = bf16v(alloc_b(32), 16).rearrange("p (a b) -> p a b", a=8)
    Gv = f32v(alloc_b(4 * 16 * 4), 64).rearrange("p (a j w) -> p a j w", a=2, j=8)
    misc = f32v(alloc_b(64 * 4), 64)
    hT = bf16v(alloc_b(8 * T * 2), 8 * T).rearrange("p (a t) -> p a t", a=8)
    mix_off = alloc_b(8 * T * 2)
    mixT = bf16v(mix_off, 8 * T).rearrange("p (a t) -> p a t", a=8)
    qkv_off = alloc_b(3 * 4 * T * 2)
    QT = bf16v(qkv_off, 4 * T).rearrange("p (a t) -> p a t", a=4)
    KT = bf16v(qkv_off + 4 * T * 2, 4 * T).rearrange("p (a t) -> p a t", a=4)
    VA = bf16v(qkv_off + 8 * T * 2, NKT * 4 * 128).rearrange("p (k h c) -> p k h c", k=NKT, h=4)
    xblk = f32v(qkv_off + 16384, 8 * 512).rearrange("p (a n) -> p a n", a=8)
    xblk2 = f32v(qkv_off, 8 * 512).rearrange("p (a n) -> p a n", a=8)
    xld = f32v(qkv_off + 32768, 4 * 1024).rearrange("p (a n) -> p a n", a=4)
    GN = 4
    fo = mix_off
    actT = bf16v(fo, GN * T).rearrange("p (a t) -> p a t", a=GN); fo += GN * T * 2
    UW = T + 4
    UWP = (UW * 4 + 255) // 256 * 256
    ubuf = [f32v(fo + i * UWP, UW) for i in range(2)]; fo += 2 * UWP
    assert fo <= qkv_off + 512, (fo, qkv_off)
    WUPB = 8 * 2 * GN * 128 * 2
    wup_bufs = [bf16v(qkv_off + 32768, 8 * 2 * GN * 128).rearrange("p (k g n) -> p k g n", k=8, g=2),
                bf16v(qkv_off + 512, 8 * 2 * GN * 128).rearrange("p (k g n) -> p k g n", k=8, g=2)]
    assert qkv_off + 32768 + WUPB <= qkv_off + 3 * 4 * T * 2 and 512 + WUPB <= 32768
    fo = qkv_off
    assert fo <= qkv_off + 3 * 4 * T * 2, (fo, qkv_off + 3 * 4 * T * 2)
    tab_off = alloc_b(2 * TL * 4)
    ropeT = f32v(tab_off, 2 * TL).rearrange("p (a t) -> p a t", a=2)
    natab = bf16v(tab_off, 4 * NAW).rearrange("p (h n) -> p h n", h=4)
    wA = bf16v(alloc_b(8 * 768 * 2), 8 * 768).rearrange("p (a n) -> p a n", a=8)
    wout_sb = bf16v(tab_off, 8 * 1024).rearrange("p (a n) -> p a n", a=8)
    ubuf2 = [f32v(tab_off + i * UWP, UW) for i in range(2)]
    assert 2 * UWP <= 2 * TL * 4 + 8 * 768 * 2
    wuq_sb = bf16v(alloc_b(2 * 384 * 2), 2 * 384).rearrange("p (a n) -> p a n", a=2)
    wukv_sb = bf16v(alloc_b(512 * 2), 512)
    ckvnT = bf16v(alloc_b(T * 2), T)
    cqn = bf16v(alloc_b(2 * 512 * 2), 2 * 512).rearrange("p (a n) -> p a n", a=2)
    r1_off = tab_off + 2 * UWP
    wdn_bufs = [bf16v(r1_off + i * GN * 1024 * 2, GN * 1024).rearrange("p (a n) -> p a n", a=GN) for i in range(2)]
    tmp_off = alloc_b(24 * 1024)
    assert r1_off + 2 * GN * 1024 * 2 <= tmp_off, (r1_off, tmp_off)
    krs_buf = f32v(alloc_b(2048), 512)
    sq_extra = alloc_b(2048)

    def tf(i):
        return f32v(tmp_off + i * 2048, 512)

    def tb16(i, half=0):
        return bf16v(tmp_off + i * 2048 + half * 1024, 512)

    kraw = Ring([tf(0), tf(1)])
    sqr = Ring([tb16(2, 0), tb16(2, 1), bf16v(sq_extra, 512), bf16v(sq_extra + 1024, 512)])
    lnr = Ring([tf(3), tf(4)])
    rsr = Ring([tf(5), tf(6)])
    qnr = Ring([tb16(7, 0), tb16(7, 1)])
    t1r = Ring([tf(8), tf(9)])
    t2r = Ring([tf(10), tf(11)])
    Pr = Ring([tb16(0, 0), tb16(0, 1), tb16(1, 0), tb16(1, 1)])
    P2r = Ring([tb16(2, 0), tb16(2, 1)])
    Osb = Ring([tf(3), tf(4)])
    zrow = Ring([tf(5), tf(6)])
    ABo = [tf(7), tf(8), tf(9)]
    xring = Ring([tf(6), tf(7), tf(8), tf(9), tf(10), tf(11)])
    ctile = [tf(0), tf(1), tf(2), tf(3), tf(4), tf(5)]
    nastg = f32v(tmp_off, NAW)
    wmr = Ring([bf16v(tmp_off + i * 4096, 2048) for i in range(3)])
    m_sb = f32v(qkv_off, 6 * D)
    bm_sb = f32v(qkv_off + 24576, 6 * D)

    def mm(out, lhsT, rhs, start=True, stop=True):
        S.add('pe', lambda E: E.matmul(out, lhsT, rhs, start=start, stop=stop), reads=[lhsT, rhs], writes=[out])

    def tr(out, in_, idn):
        S.add('pe', lambda E: E.transpose(out, in_, idn), reads=[in_, idn], writes=[out])

    def act(out, in_, func, scale=None, bias=None, eng='act'):
        kw = {}
        rd = [in_]
        if scale is not None:
            kw['scale'] = scale
            if not isinstance(scale, float):
                rd.append(scale)
        if bias is not None:
            kw['bias'] = bias
            if not isinstance(bias, float):
                rd.append(bias)
        S.add('act', lambda E: E.activation(out, in_, func, **kw), reads=rd, writes=[out])

    def stt(out, in0, scalar, in1, op0, op1):
        rd = [in0, in1] + ([] if isinstance(scalar, float) else [scalar])
        S.add('dve', lambda E: E.scalar_tensor_tensor(out=out, in0=in0, scalar=scalar, in1=in1, op0=op0, op1=op1),
              reads=rd, writes=[out])

    def tt(out, in0, in1, op, eng='dve'):
        S.add(eng, lambda E: E.tensor_tensor(out=out, in0=in0, in1=in1, op=op), reads=[in0, in1], writes=[out])

    def ts(out, in0, s1, s2, op0, op1=None, eng='dve'):
        rd = [in0] + [s for s in (s1, s2) if s is not None and not isinstance(s, float)]
        if op1 is None:
            S.add(eng, lambda E: E.tensor_scalar(out=out, in0=in0, scalar1=s1, scalar2=None, op0=op0), reads=rd, writes=[out])
        else:
            S.add(eng, lambda E: E.tensor_scalar(out=out, in0=in0, scalar1=s1, scalar2=s2, op0=op0, op1=op1), reads=rd, writes=[out])

    def cp(out, in_, eng='dve'):
        S.add(eng, lambda E: E.tensor_copy(out=out, in_=in_), reads=[in_], writes=[out])

    def vcopy(out, in_, bank, use_act):
        if use_act:
            S.add('act', lambda E: E.activation(out, in_, AF.Copy), reads=[bank], writes=[out])
        else:
            S.add('dve', lambda E: E.tensor_copy(out=out, in_=in_), reads=[bank], writes=[out])

    def mset(ap, val, eng='dve'):
        S.add(eng, lambda E: E.memset(ap, val), writes=[ap])

    def dma(q, out, in_, reads=None, writes=None):
        r = [in_] if reads is None else reads
        w = [out] if writes is None else writes
        r = [a for a in r if isinstance(a, (str, tuple)) or str(a.space) != 'DRAM']
        w = [a for a in w if isinstance(a, (str, tuple)) or str(a.space) != 'DRAM']
        S.add(q, lambda E: E.dma_start(out=out, in_=in_), reads=r, writes=w, dma=True)

    def xkey(tbi):
        return "x:%d" % tbi

    def xkeys(tbi):
        return ["x:%d" % tbi] + ["xf:%d:%d" % (tbi, j) for j in range(8)]

    try:
        import os as _os
        if _os.environ.get('SKIP_CONST'):
            raise StopBuild()
        dma('sp', ident, cf_d[:, 0:128])
        dma('sp', ones_f, cf_d[:, 128:256])
        dma('pool', cbf, cb_d.rearrange("p (a b) -> p a b", a=6))
        dma('sp', pv, pv_d.rearrange("p (l n) -> p l n", l=L))
        dma('sp', cT, cT_d)
        act(scT_f, cT, AF.Silu)
        cp(scT, scT_f.rearrange("p (a b) -> p a b", a=8))

        ck('c0')
        for tbi, (t0, n) in enumerate(TBS):
            ntt = n // 128
            dma('sp', xld[:, 0:ntt, :], xc_d[t0:t0 + n, :].rearrange("(a p) d -> p a d", p=128))
            for j in range(8):
                pb = PB(j % 2)
                for a in range(ntt):
                    tr(pb[:, a * 128:(a + 1) * 128], xld[:, a, j * 128:(j + 1) * 128], ident)
                if j % 2 == 0:
                    act(xblk[:, j, 0:n], pb[:, 0:n], AF.Copy)
                else:
                    cp(xblk[:, j, 0:n], pb[:, 0:n])
            dma('sp', xT_d[:, :, t0:t0 + n], xblk[:, :, 0:n], writes=xkeys(tbi))

        ck('xt')
        for l in range(n_layers):
            dma('sp', bm_sb[0:1, :], bmod_d[l:l + 1, :])
            dma('sp', bm_sb[1:2, :], bmod_d[l:l + 1, :])
            for cg in range(3):
                for kc in range(8):
                    piece = wmr.next()
                    dma('pool', piece, wmod_d[l, kc * 128:(kc + 1) * 128, cg * 2048:(cg + 1) * 2048])
                    for i in range(4):
                        mm(PB(i)[0:2, :], scT[:, kc, :], piece[:, i * 512:(i + 1) * 512], start=(kc == 0), stop=(kc == 7))
                for i in range(4):
                    c0 = cg * 2048 + i * 512
                    tt(m_sb[0:2, c0:c0 + 512], PB(i)[0:2, :], bm_sb[0:2, c0:c0 + 512], ALU.add)
            pt = PB(4)
            for s in range(48):
                tr(pt[:, 2 * s:2 * s + 2], m_sb[0:2, s * 128:(s + 1) * 128], ident[0:2, 0:2])
            cp(modT[:, l].rearrange("p s w -> p (s w)"), pt[:, 0:96])

        ck('mod')
        def rstd_from(src, R, N, ones_m, inv_d):
            sq = sqr.next()[0:R, 0:N]
            act(sq, src, AF.Square)
            ssp = ss_ring.next()[0:R, 0:N]
            mm(ssp, ones_m, sq)
            ln = lnr.next()[0:R, 0:N]
            act(ln, ssp, AF.Ln, scale=float(inv_d), bias=float(EPS))
            rs = rsr.next()[0:R, 0:N]
            act(rs, ln, AF.Exp, scale=-0.5)
            return rs

        def norm_rope(src, R, N, ones_m, inv_d, gain, out, rope=None):
            rs = rstd_from(src, R, N, ones_m, inv_d)
            if rope is None:
                stt(out, src, gain, rs, ALU.mult, ALU.mult)
                return
            Rm, cosT, sinT = rope
            qn = qnr.next()[0:R, 0:N]
            stt(qn, src, gain, rs, ALU.mult, ALU.mult)
            rp = rot_ring.next()[0:R, 0:N]
            mm(rp, Rm, qn)
            t1 = t1r.next()[0:R, 0:N]
            tt(t1, qn, cosT, ALU.mult, eng='pool')
            t2 = t2r.next()[0:R, 0:N]
            tt(t2, rp, sinT, ALU.mult)
            tt(out, t1, t2, ALU.add, eng='pool')

        def chain(src_fn, R, N, ones_m, inv_d, gain, out, rope=None):
            st = {}

            def A():
                st['src'] = src_fn()
                sq = sqr.next()[0:R, 0:N]
                act(sq, st['src'], AF.Square)
                st['sq'] = sq

            def B():
                ssp = ss_ring.next()[0:R, 0:N]
                mm(ssp, ones_m, st['sq'])
                ln = lnr.next()[0:R, 0:N]
                act(ln, ssp, AF.Ln, scale=float(inv_d), bias=float(EPS))
                rs = rsr.next()[0:R, 0:N]
                act(rs, ln, AF.Exp, scale=-0.5)
                if rope is None:
                    stt(out, st['src'], gain, rs, ALU.mult, ALU.mult)
                else:
                    qn = qnr.next()[0:R, 0:N]
                    stt(qn, st['src'], gain, rs, ALU.mult, ALU.mult)
                    st['qn'] = qn

            def C():
                Rm, cosT, sinT = rope
                qn = st['qn']
                rp = rot_ring.next()[0:R, 0:N]
                mm(rp, Rm, qn)
                t1 = t1r.next()[0:R, 0:N]
                tt(t1, qn, cosT, ALU.mult, eng='pool')
                t2 = t2r.next()[0:R, 0:N]
                tt(t2, rp, sinT, ALU.mult)
                tt(out, t1, t2, ALU.add, eng='pool')

            return [A, B, C if rope is not None else None]

        def run_pipe(tiles):
            n = len(tiles)
            for step in range(n + 2):
                if step < n and tiles[step][0]:
                    tiles[step][0]()
                if 0 <= step - 1 < n and tiles[step - 1][1]:
                    tiles[step - 1][1]()
                if 0 <= step - 2 < n and tiles[step - 2][2]:
                    tiles[step - 2][2]()

        def modulate(l, which, compute=None):
            xbs = [xblk, xblk2]

            def load(tbi):
                t0_, n_ = TBS[tbi]
                dma('sp', xbs[tbi % 2][:, :, 0:n_], xT_d[:, :, t0_:t0_ + n_], reads=xkeys(tbi))

            load(0)
            for tbi, (t0, n) in enumerate(TBS):
                w = 0 if tbi < 4 else 1
                xb = xbs[tbi % 2]
                if tbi + 1 < len(TBS):
                    load(tbi + 1)
                if compute is not None:
                    compute(tbi, t0, n, xb)
                ssp = ss_ring.next()[:, 0:n]
                for j in range(8):
                    sq = sqr.next()[:, 0:n]
                    act(sq, xb[:, j, 0:n], AF.Square)
                    mm(ssp, allones, sq, start=(j == 0), stop=(j == 7))
                ln = lnr.next()[:, 0:n]
                act(ln, ssp, AF.Ln, scale=1.0 / D, bias=float(EPS))
                rs = rsr.next()[:, 0:n]
                act(rs, ln, AF.Exp, scale=-0.5)
                for j in range(8):
                    t1 = t1r.next()[:, 0:n]
                    stt(t1, xb[:, j, 0:n], Gv[:, which, j, w:w + 1], rs, ALU.mult, ALU.mult)
                    shift = modT[:, l, (3 * which) * 8 + j, w:w + 1]
                    act(hT[:, j, t0:t0 + n], t1, AF.Identity, bias=shift)


        pending = []

        def flush_pending():
            while pending:
                pending.pop(0)()

        def attention(q_of, keytiles, N, scale, vrows, LOOK=2):
            Op = o_ring.next()
            nk = len(keytiles)
            Sps = {}

            def qk(i):
                Sps[i] = s_ring.next()[:, 0:N]
                mm(Sps[i], keytiles[i][0], q_of)

            for i in range(min(LOOK, nk)):
                qk(i)
            flush_pending()
            for i, (k_ap, v_ap, tab) in enumerate(keytiles):
                if i + LOOK < nk:
                    qk(i + LOOK)
                Sp = Sps.pop(i)
                P = Pr.next()[:, 0:N]
                act(P, Sp, AF.Exp, scale=float(scale))
                if tab is not None:
                    P2 = P2r.next()[:, 0:N]
                    tt(P2, P, tab[:, 0:N], ALU.mult)
                    P = P2
                mm(Op[0:vrows, 0:N], v_ap, P, start=(i == 0), stop=(i == nk - 1))
            return Op

        def normalize(Op, odd, N, out_sb, bcr=None):
            zp = 0 if odd else 64
            r0 = 64 if odd else 0
            zr = zrow.next()
            act(zr[zp:zp + 1, 0:N], Op[zp:zp + 1, 0:N], AF.Ln)
            zr2 = zrow.next()
            act(zr2[zp:zp + 1, 0:N], zr[zp:zp + 1, 0:N], AF.Exp, scale=-1.0)
            bc = (bcr or bc_ring).next()
            mm(bc[:, 0:N], ones_f[zp:zp + 1, :], zr2[zp:zp + 1, 0:N])
            osb = Osb.next()
            act(osb[r0:r0 + 64, 0:N], Op[r0:r0 + 64, 0:N], AF.Copy)
            tt(out_sb[r0:r0 + 64, 0:N], osb[r0:r0 + 64, 0:N], bc[r0:r0 + 64, 0:N], ALU.mult)

        def v_slot(kt, h):
            odd = h % 2
            return VA[:, kt, h, 0:128] if odd else VA[:, kt, h, 0:65]

        def v_dst(kt, h):
            odd = h % 2
            return VA[:, kt, h, 64:128] if odd else VA[:, kt, h, 0:64]

        def init_va():
            mset(VA.rearrange("p k h c -> p (k h c)"), 0.0, eng='dve')
            for h in range(4):
                col = 0 if h % 2 else 64
                mset(VA[:, :, h, col:col + 1], 1.0, eng='dve')

        for l in range(n_layers):
            with_ctx = l < L - 1
            li = lambda_init(l)
            q_tbs = TBS if with_ctx else TBS[:4]
            ss_ring = Ring([PB(3), PB(4)])
            rot_ring = Ring([PB(5), PB(6)])
            raw_ring = Ring([PB(0), PB(1), PB(2)])
            vp_ring = Ring([PB(int(_os.environ.get("VPB", "6")))])
            s_ring = Ring([PB(0), PB(1), PB(2)])
            o_ring = Ring([PB(3), PB(4)])
            bc_ring = Ring([PB(5)])
            ds_ring = Ring([PB(0), PB(1), PB(2), PB(3)])
            do_ring = Ring([PB(4), PB(5), PB(6)])
            dscratch = [PB(6)]

            def pvc(name, j=0, rows=128):
                c0, w = PV[name]
                return pv[0:rows, l, c0 + j:c0 + j + 1]

            for which, gname in ((0, 'g_mix'), (1, 'g_ffn')):
                c0, _ = PV[gname]
                for w in range(2):
                    sc = modT[:, l, (3 * which + 1) * 8:(3 * which + 2) * 8, w]
                    stt(Gv[:, which, :, w], sc, 1.0, pv[:, l, c0:c0 + 8], ALU.add, ALU.mult)
            c0, _ = PV['lvec']
            lv = pv[:, l, c0:c0 + 128].rearrange("p (a d) -> p a d", a=4)
            prod = tf(0)[:, 0:64].rearrange("p (a d) -> p a d", a=2)
            tt(prod[:, 0, :], lv[:, 0, :], lv[:, 1, :], ALU.mult)
            tt(prod[:, 1, :], lv[:, 2, :], lv[:, 3, :], ALU.mult)
            S.add('dve', lambda E, prod=prod: E.tensor_reduce(out=misc[:, 0:2], in_=prod, axis=AX.X, op=ALU.add),
                  reads=[prod], writes=[misc[:, 0:2]])
            act(misc[:, 2:4], misc[:, 0:2], AF.Exp)
            tt(misc[:, 4:5], misc[:, 3:4], misc[:, 2:3], ALU.subtract)
            ts(misc[:, 5:6], misc[:, 4:5], float(-li), None, ALU.add)
            ts(misc[:, 6:7], pvc('dsub_g'), float(1.0 - li), None, ALU.mult)
            nlam = misc[:, 5:6]
            gsub = misc[:, 6:7]

            modulate(l, 0)
            ck('m1')
            init_va()
            ck('va')

            def proj_fm(out_ps, wcols, t0, n):
                for kc in range(8):
                    mm(out_ps, wA[:, kc, wcols[0]:wcols[1]], hT[:, kc, t0:t0 + n], start=(kc == 0), stop=(kc == 7))

            def proj_v(col0, heads, ncols_per_head=64, slot_of=None):
                nh = len(heads)
                for kt in range(NKT):
                    vp = vp_ring.next()
                    for kc in range(8):
                        mm(vp[:, 0:nh * 64], hT[:, kc, kt * 128:(kt + 1) * 128], wA[:, kc, col0:col0 + nh * 64],
                           start=(kc == 0), stop=(kc == 7))
                    for i, hs in enumerate(heads):
                        for h in hs:
                            vcopy(v_dst(kt, h), vp[:, i * 64:(i + 1) * 64], vp, True)

            def run_attention(mixer, head_q, head_k, krows, scale, tab_of=None, na=False, diff=False):
                for h in range(4):
                    odd = h % 2
                    chunk = 2 * mixer + h // 2
                    for tbi, (t0, n) in enumerate(q_tbs):
                        if tbi < 4:
                            if na:
                                kts = list(NA_KT[tbi]) + [16, 17]
                            else:
                                kts = list(range(NKT))
                        else:
                            kts = [16, 17]
                        vrows = 128 if odd else 65
                        if not diff:
                            tiles = []
                            for kt in kts:
                                tab = None
                                if na and kt < 16:
                                    i0 = 8 * tbi - 2 * kt + 7
                                    tab = natab[:, h, (i0 + 3) * 64:(i0 + 3) * 64 + 512]
                                tiles.append((KT[0:krows, head_k(h), kt * 128:(kt + 1) * 128], v_slot(kt, h), tab))
                            Op = attention(QT[0:krows, h, t0:t0 + n], tiles, n, scale, vrows)
                            pending.append(lambda Op=Op, odd=odd, n=n, chunk=chunk, t0=t0: normalize(Op, odd, n, mixT[:, chunk, t0:t0 + n]))
                        else:
                            r0 = 64 if odd else 0
                            Ops = [do_ring.next(), do_ring.next()]
                            nk = len(kts)
                            Sps = {}

                            def qk2(i, h=h, t0=t0, n=n, kts=kts):
                                kt = kts[i]
                                Sps[i] = []
                                for pr in range(2):
                                    sp_ = ds_ring.next()[:, 0:n]
                                    mm(sp_, KT[32 * pr:32 * pr + 32, h, kt * 128:(kt + 1) * 128], QT[32 * pr:32 * pr + 32, h, t0:t0 + n])
                                    Sps[i].append(sp_)

                            qk2(0)
                            dscratch[0] = Ops[0]
                            flush_pending()
                            for i, kt in enumerate(kts):
                                if i + 1 < nk:
                                    qk2(i + 1)
                                sp2 = Sps.pop(i)
                                for pr in range(2):
                                    P = Pr.next()[:, 0:n]
                                    act(P, sp2[pr], AF.Exp, scale=float(scale))
                                    mm(Ops[pr][0:vrows, 0:n], v_slot(kt, h), P, start=(i == 0), stop=(i == nk - 1))

                            def fin(Ops=Ops, odd=odd, n=n, r0=r0, chunk=chunk, t0=t0):
                                scr_ring = Ring([dscratch[0]])
                                normalize(Ops[0], odd, n, ABo[0], bcr=scr_ring)
                                normalize(Ops[1], odd, n, ABo[1], bcr=scr_ring)
                                o = ABo[2]
                                stt(o[r0:r0 + 64, 0:n], ABo[1][r0:r0 + 64, 0:n], nlam[r0:r0 + 64, :], ABo[0][r0:r0 + 64, 0:n], ALU.mult, ALU.add)
                                sq = sqr.next()
                                act(sq[r0:r0 + 64, 0:n], o[r0:r0 + 64, 0:n], AF.Square)
                                ssp = dscratch[0]
                                mm(ssp[r0:r0 + 64, 0:n], blk64[r0:r0 + 64, r0:r0 + 64], sq[r0:r0 + 64, 0:n])
                                ln = lnr.next()
                                act(ln[r0:r0 + 64, 0:n], ssp[r0:r0 + 64, 0:n], AF.Ln, scale=1.0 / 64, bias=float(EPS))
                                rs = rsr.next()
                                act(rs[r0:r0 + 64, 0:n], ln[r0:r0 + 64, 0:n], AF.Exp, scale=-0.5)
                                stt(mixT[r0:r0 + 64, chunk, t0:t0 + n], o[r0:r0 + 64, 0:n], gsub[r0:r0 + 64, :], rs[r0:r0 + 64, 0:n],
                                    ALU.mult, ALU.mult)
                            pending.append(fin)
                    if diff:
                        dscratch[0] = do_ring.next()
                    flush_pending()

            dma('pool', wA[:, :, 0:416], win_d[l, :, 0:416].rearrange("(a p) n -> p a n", p=128))
            dma('pool', wuq_sb, wuq_d[l].rearrange("(a p) n -> p a n", p=128))
            dma('pool', wukv_sb, wukv_d[l])
            dma('sp', ropeT, tabs_d[0])
            tiles = []
            for tbi, (t0, n) in enumerate(TBS):
                lat = tbi < 4
                rope = (Rmla[0:96, 0:96], ropeT[0:96, 0, t0:t0 + n], ropeT[0:96, 1, t0:t0 + n]) if lat else None
                st = {}

                def A_cq(t0=t0, n=n, st=st):
                    st['raws'] = []
                    for c in range(2):
                        rp = raw_ring.next()[:, 0:n]
                        proj_fm(rp, (c * 128, (c + 1) * 128), t0, n)
                        st['raws'].append(rp)

                def B_cq(t0=t0, n=n, st=st):
                    ssp = ss_ring.next()[:, 0:n]
                    for c in range(2):
                        sq = sqr.next()[:, 0:n]
                        act(sq, st['raws'][c], AF.Square)
                        mm(ssp, allones, sq, start=(c == 0), stop=(c == 1))
                    ln = lnr.next()[:, 0:n]
                    act(ln, ssp, AF.Ln, scale=1.0 / 256, bias=float(EPS))
                    rs = rsr.next()[:, 0:n]
                    act(rs, ln, AF.Exp, scale=-0.5)
                    for c in range(2):
                        stt(cqn[:, c, 0:n], st['raws'][c], pvc('qa_g', c), rs, ALU.mult, ALU.mult)

                tiles.append([A_cq, B_cq, None])

                def src_ckv(t0=t0, n=n):
                    rp = raw_ring.next()[:, 0:n]
                    proj_fm(rp, (256, 384), t0, n)
                    return rp
                tiles.append(chain(src_ckv, 128, n, allones, 1.0 / 128, pvc('kva_g'), ckvnT[:, t0:t0 + n], None))

                def A_kr(t0=t0, n=n):
                    rpk = raw_ring.next()
                    for kc in range(8):
                        mm(rpk[64:96, 0:n], wA[:, kc, 384:416], hT[:, kc, t0:t0 + n], start=(kc == 0), stop=(kc == 7))
                    act(krs_buf[64:96, 0:n], rpk[64:96, 0:n], AF.Copy)
                tiles.append([A_kr, None, None])

                for h in range(4):
                    def src_q(h=h, n=n):
                        rp = raw_ring.next()[0:96, 0:n]
                        for c in range(2):
                            mm(rp, wuq_sb[:, c, h * 96:(h + 1) * 96], cqn[:, c, 0:n], start=(c == 0), stop=(c == 1))
                        return rp
                    tiles.append(chain(src_q, 96, n, allones[0:96, 0:96], 1.0 / 96, pvc('mq_g', 0, 96), QT[0:96, h, t0:t0 + n], rope))
                for h in range(4):
                    def src_k(h=h, t0=t0, n=n):
                        rp = raw_ring.next()[0:64, 0:n]
                        mm(rp, wukv_sb[:, h * 128:h * 128 + 64], ckvnT[:, t0:t0 + n])
                        kr_ = kraw.next()
                        act(kr_[0:64, 0:n], rp, AF.Copy)
                        cp(kr_[64:96, 0:n], krs_buf[64:96, 0:n])
                        return kr_[0:96, 0:n]
                    tiles.append(chain(src_k, 96, n, allones[0:96, 0:96], 1.0 / 96, pvc('mk_g', 0, 96), KT[0:96, h, t0:t0 + n], rope))
            run_pipe(tiles)
            ck('mlap')
            for kt in range(NKT):
                vp = vp_ring.next()
                for h in range(4):
                    mm(vp[:, h * 64:(h + 1) * 64], ckvnT[:, kt * 128:(kt + 1) * 128], wukv_sb[:, h * 128 + 64:h * 128 + 128])
                for h in range(4):
                    if _os.environ.get('NOVCP'):
                        continue
                    vcopy(v_dst(kt, h), vp[:, h * 64:(h + 1) * 64], vp, True)
            ck('mlav')
            run_attention(0, None, lambda h: h, 96, 96 ** -0.5)
            ck('mla')

            dma('pool', wA[:, :, 0:768], win_d[l, :, 416:1184].rearrange("(a p) n -> p a n", p=128))
            dma('sp', ropeT, tabs_d[1])
            tiles = []
            for tbi, (t0, n) in enumerate(TBS):
                lat = tbi < 4
                rope = (Rdiff[0:64, 0:64], ropeT[0:64, 0, t0:t0 + n], ropeT[0:64, 1, t0:t0 + n]) if lat else None
                for h in range(4):
                    for (cb, gname, dst) in ((0, 'dq_g', QT), (256, 'dk_g', KT)):
                        def src(h=h, cb=cb, t0=t0, n=n):
                            rp = raw_ring.next()[0:64, 0:n]
                            proj_fm(rp, (cb + h * 64, cb + h * 64 + 64), t0, n)
                            return rp
                        tiles.append(chain(src, 64, n, blk32[0:64, 0:64], 1.0 / 32, pvc(gname, 0, 64), dst[0:64, h, t0:t0 + n], rope))
            run_pipe(tiles)
            proj_v(512, [[0], [1], [2], [3]])
            run_attention(1, None, lambda h: h, 32, 32 ** -0.5, diff=True)
            ck('diff')

            dma('pool', wA[:, :, 0:768], win_d[l, :, 1184:1952].rearrange("(a p) n -> p a n", p=128))
            for h in range(4):
                dma('sp', nastg, nab_d[l, h])
                act(natab[:, h, :], nastg, AF.Exp)
                for (c0_, c1_) in ((0, TL), (TL, T)):
                    dma('pool', QT[64:96, h, c0_:c1_], aug_d[0][:, c0_:c1_])
                    dma('pool', KT[64:96, h, c0_:c1_], aug_d[1][:, c0_:c1_])
            tiles = []
            for tbi, (t0, n) in enumerate(TBS):
                for h in range(4):
                    for (cb, gname, dst) in ((0, 'nq_g', QT), (256, 'nk_g', KT)):
                        def src(h=h, cb=cb, t0=t0, n=n):
                            rp = raw_ring.next()[0:64, 0:n]
                            proj_fm(rp, (cb + h * 64, cb + h * 64 + 64), t0, n)
                            return rp
                        tiles.append(chain(src, 64, n, allones[0:64, 0:64], 1.0 / 64, pvc(gname, 0, 64), dst[0:64, h, t0:t0 + n], None))
            run_pipe(tiles)
            proj_v(512, [[0], [1], [2], [3]])
            run_attention(2, None, lambda h: h, 96, 64 ** -0.5, na=True)
            ck('na')

            dma('pool', wA[:, :, 0:512], win_d[l, :, 1952:2464].rearrange("(a p) n -> p a n", p=128))
            dma('sp', ropeT, tabs_d[2])
            tiles = []
            for tbi, (t0, n) in enumerate(TBS):
                lat = tbi < 4
                rope = (Rgqa[0:64, 0:64], ropeT[0:64, 0, t0:t0 + n], ropeT[0:64, 1, t0:t0 + n]) if lat else None
                for (cb, gname, dst, nh) in ((0, 'gq_g', QT, 4), (256, 'gk_g', KT, 2)):
                    for h in range(nh):
                        def src(h=h, cb=cb, t0=t0, n=n):
                            rp = raw_ring.next()[0:64, 0:n]
                            proj_fm(rp, (cb + h * 64, cb + h * 64 + 64), t0, n)
                            return rp
                        tiles.append(chain(src, 64, n, allones[0:64, 0:64], 1.0 / 64, pvc(gname, 0, 64), dst[0:64, h, t0:t0 + n], rope))
            run_pipe(tiles)
            ck('gqap')
            proj_v(384, [[0, 1], [2, 3]])
            ck('gqav')
            run_attention(3, None, lambda h: h // 2, 64, 64 ** -0.5)
            ck('gqa')

            def load_group(gi):
                g0_, gn_ = FFG[gi]
                wn_ = gn_ * 128
                wb = wup_bufs[gi % 2]
                dma('pool', wb[:, :, 0, 0:wn_], wup_d[l, :, g0_ * 128:g0_ * 128 + wn_].rearrange("(a p) n -> p a n", p=128))
                dma('pool', wb[:, :, 1, 0:wn_], wup_d[l, :, DFF + g0_ * 128:DFF + g0_ * 128 + wn_].rearrange("(a p) n -> p a n", p=128))
                dma('pool', wdn_bufs[gi % 2][:, 0:gn_, :], wdn_d[l, g0_ * 128:(g0_ + gn_) * 128, :].rearrange("(a p) n -> p a n", p=128))

            load_group(0)

            dma('pool', wout_sb, wout_d[l].rearrange("(a p) n -> p a n", p=128))
            all_ps = Ring([PB(i) for i in range(7)])
            ss_ring = Ring([PB(5), PB(6)])
            op_ring = Ring([PB(i) for i in range(5)])

            def outproj_compute(tbi, t0, n, xb):
                if tbi == 4 and not with_ctx:
                    return
                w = 0 if tbi < 4 else 1
                for j in range(8):
                    pb = op_ring.next()[:, 0:n]
                    for c in range(8):
                        mm(pb, wout_sb[:, c, j * 128:(j + 1) * 128], mixT[:, c, t0:t0 + n], start=(c == 0), stop=(c == 7))
                    stt(xb[:, j, 0:n], pb, modT[:, l, 2 * 8 + j, w:w + 1], xb[:, j, 0:n], ALU.mult, ALU.add)
                dma('act', xT_d[:, :, t0:t0 + n], xb[:, :, 0:n], writes=xkeys(tbi))

            modulate(l, 1, outproj_compute)

            ck('outproj')
            f_tbs = TBS if with_ctx else TBS[:4]
            for ub in ubuf + ubuf2:
                mset(ub[:, 0:1], 0.0, eng='pool')
                mset(ub[:, 2049:2051], 0.0, eng='pool')
                mset(ub[:, 2307:2308], 0.0, eng='pool')
            up_ring = Ring([PB(0), PB(1), PB(2), PB(3)])
            dn_ring = Ring([PB(4), PB(5), PB(6)])
            cring = Ring(ctile)

            def ucol(t0):
                return (1 + t0) if t0 < TL else (2051 + t0 - TL)

            for gi, (g0, gn) in enumerate(FFG):
                wup_sb = wup_bufs[gi % 2]
                wdn_sb = wdn_bufs[gi % 2]
                if gi + 1 < len(FFG):
                    load_group(gi + 1)
                for cl in range(gn):
                    c = g0 + cl
                    ub = ubuf if c % 2 == 0 else ubuf2
                    for (t0, n) in f_tbs:
                        u0 = ucol(t0)
                        for ag in range(2):
                            pb = up_ring.next()[:, 0:n]
                            for kc in range(8):
                                mm(pb, wup_sb[:, kc, ag, cl * 128:(cl + 1) * 128], hT[:, kc, t0:t0 + n], start=(kc == 0), stop=(kc == 7))
                            act(ub[ag][:, u0:u0 + n], pb, AF.Copy)
                    for (t0, n) in f_tbs:
                        u0 = ucol(t0)
                        outs = []
                        for ag in range(2):
                            col = c if ag == 0 else NCH + c
                            ct = cring.next()[:, 0:n]
                            act(ct, ub[ag][:, u0:u0 + n], AF.Identity, scale=pvc('cw1', col), bias=pvc('cb', col))
                            stt(ct, ub[ag][:, u0 - 1:u0 - 1 + n], pvc('cw0', col), ct, ALU.mult, ALU.add)
                            stt(ct, ub[ag][:, u0 + 1:u0 + 1 + n], pvc('cw2', col), ct, ALU.mult, ALU.add)
                            outs.append(ct)
                        act(outs[1], outs[1], AF.Silu)
                        tt(actT[:, cl, t0:t0 + n], outs[1], outs[0], ALU.mult, eng='pool')
                dtiles = [(tbi, t0, n, j) for tbi, (t0, n) in enumerate(f_tbs) for j in range(8)]
                LA = 4
                xts = {}

                def xload(i):
                    tbi_, t0_, n_, j_ = dtiles[i]
                    xt_ = xring.next()[:, 0:n_]
                    dma('sp', xt_, xT_d[:, j_, t0_:t0_ + n_], reads=[xkey(tbi_), "xf:%d:%d" % (tbi_, j_)])
                    xts[i] = xt_

                for i in range(min(LA, len(dtiles))):
                    xload(i)
                for i, (tbi, t0, n, j) in enumerate(dtiles):
                    if i + LA < len(dtiles):
                        xload(i + LA)
                    w = 0 if tbi < 4 else 1
                    pb = dn_ring.next()[:, 0:n]
                    for cl in range(gn):
                        mm(pb, wdn_sb[:, cl, j * 128:(j + 1) * 128], actT[:, cl, t0:t0 + n], start=(cl == 0), stop=(cl == gn - 1))
                    xt = xts.pop(i)
                    stt(xt, pb, modT[:, l, 5 * 8 + j, w:w + 1], xt, ALU.mult, ALU.add)
                    dma('act', xT_d[:, j, t0:t0 + n], xt, writes=["xf:%d:%d" % (tbi, j)])

    except StopBuild:
        pass
    import os as _os
    for tbi, (t0, n) in enumerate(TBS[:4]):
        dma('sp', xblk[:, :, 0:n], xT_d[:, :, t0:t0 + n], reads=xkeys(tbi))
        if _os.environ.get('SKIP_FINAL'):
            dma('sp', y_d[t0:t0 + n, :].rearrange("(a p) d -> p a d", p=128), xblk[:, 0:4, :].rearrange("p a (b c) -> p (a b) c", b=1)[:, :, :].rearrange("p a c -> p a c") if False else xld[:, 0:4, :])
            continue
        for a in range(4):
            for half in range(2):
                pb = PB((a * 2 + half) % int(_os.environ.get("NB", "6")))
                for jj in range(4):
                    j = half * 4 + jj
                    tr(pb[:, jj * 128:(jj + 1) * 128], xblk[:, j, a * 128:(a + 1) * 128], ident)
                if half == 0:
                    act(xld[:, a, 0:512], pb, AF.Copy)
                else:
                    cp(xld[:, a, 512:1024], pb)
        dma('sp', y_d[t0:t0 + n, :].rearrange("(a p) d -> p a d", p=128), xld[:, 0:4, :])
    if dbg:
        for tbi, (t0, n) in enumerate(TBS):
            dma('sp', xblk[:, :, 0:n], xT_d[:, :, t0:t0 + n], reads=xkeys(tbi))
            dma('sp', dbg_d[:, :, t0:t0 + n], xblk[:, :, 0:n])
    S.emit()
    return nc, S


def _rope_tab(rot):
    nf = rot // 4
    inv = np.power(np.float32(10000.0), -np.arange(nf, dtype=np.float32) / np.float32(nf)).astype(np.float32)
    t = np.arange(TL)
    row = (t // 64).astype(np.float32)
    col = (t % 64).astype(np.float32)
    ar = row[:, None] * inv
    ac = col[:, None] * inv
    ang = np.concatenate([ar, ar, ac, ac], axis=-1).astype(np.float32)
    return np.cos(ang).T.astype(np.float32), np.sin(ang).T.astype(np.float32)


def _constants():
    cf = np.zeros((128, 256), np.float32)
    cf[:, 0:128] = np.eye(128, dtype=np.float32)
    cf[:, 128:256] = 1.0
    cb = np.zeros((128, 6, 128), np.float32)
    cb[:, 0, :] = 1.0
    for b in range(4):
        cb[b * 32:(b + 1) * 32, 1, b * 32:(b + 1) * 32] = 1.0
    for b in range(2):
        cb[b * 64:(b + 1) * 64, 2, b * 64:(b + 1) * 64] = 1.0

    def add_rot(Rm, base, n):
        for i in range(n):
            Rm[base + i + n, base + i] = -1.0
            Rm[base + i, base + i + n] = 1.0
            Rm[base + 3 * n + i, base + 2 * n + i] = -1.0
            Rm[base + 2 * n + i, base + 3 * n + i] = 1.0
    add_rot(cb[:, 3, :], 64, 8)
    add_rot(cb[:, 4, :], 0, 8)
    add_rot(cb[:, 4, :], 32, 8)
    add_rot(cb[:, 5, :], 0, 16)
    aug = np.zeros((2, 32, T), np.float32)
    for q in range(TL):
        qr = q // 64
        rs = min(max(qr - 4, 0), 24)
        aug[0, :, q] = -BIG
        aug[0, rs:rs + 8, q] = 0.0
        aug[1, qr, q] = 1.0
    tabs = np.zeros((3, 128, 2, TL), np.float32)
    c32, s32 = _rope_tab(32)
    c64, s64 = _rope_tab(64)
    tabs[0, 0:64, 0, :] = 1.0
    tabs[0, 64:96, 0, :] = c32
    tabs[0, 64:96, 1, :] = s32
    tabs[1, 0:32, 0, :] = c32
    tabs[1, 32:64, 0, :] = c32
    tabs[1, 0:32, 1, :] = s32
    tabs[1, 32:64, 1, :] = s32
    tabs[2, 0:64, 0, :] = c64
    tabs[2, 0:64, 1, :] = s64
    return cf, cb.reshape(128, 768), aug, tabs


def _na_index():
    idx = np.full((128, 22, 64), 15 * 31, np.int64)
    for p in range(128):
        half, kc = p // 64, p % 64
        for pos in range(22):
            i = pos - 3 - half
            if i < 0 or i > 14:
                continue
            for qc in range(64):
                ws = min(max(qc - 8, 0), 48)
                if ws <= kc < ws + 16:
                    co = min(max(kc - qc + 15, 0), 30)
                    idx[p, pos, qc] = (14 - i) * 31 + co
    return idx.reshape(128, NAW)


def _tile_rows(v, reps, rows=128):
    out = np.zeros((rows,), np.float32)
    t = np.tile(np.asarray(v, np.float32), reps)
    out[:t.shape[0]] = t
    return out


def _prep_shared(inp):
    cf, cb, aug, tabs = _constants()
    pvA = np.zeros((128, L, NV), np.float32)

    def put(name, l, arr2d):
        c0, w = PV[name]
        pvA[:, l, c0:c0 + w] = arr2d

    for l in range(L):
        put('g_mix', l, inp['g_mix'][l].reshape(8, 128).T)
        put('g_ffn', l, inp['g_ffn'][l].reshape(8, 128).T)
        put('qa_g', l, inp['mla_q_a_g'][l].reshape(2, 128).T)
        put('kva_g', l, inp['mla_kv_a_g'][l].reshape(1, 128).T)
        put('mq_g', l, _tile_rows(inp['mla_q_g'][l], 1)[:, None])
        put('mk_g', l, _tile_rows(inp['mla_k_g'][l], 1)[:, None])
        put('dq_g', l, _tile_rows(inp['diff_q_g'][l], 4)[:, None])
        put('dk_g', l, _tile_rows(inp['diff_k_g'][l], 4)[:, None])
        put('dsub_g', l, _tile_rows(inp['diff_subln_g'][l], 2)[:, None])
        put('nq_g', l, _tile_rows(inp['na_q_g'][l], 2)[:, None])
        put('nk_g', l, _tile_rows(inp['na_k_g'][l], 2)[:, None])
        put('gq_g', l, _tile_rows(inp['gqa_q_g'][l], 2)[:, None])
        put('gk_g', l, _tile_rows(inp['gqa_k_g'][l], 2)[:, None])
        for i in range(3):
            put('cw%d' % i, l, inp['conv_w'][l, i].reshape(44, 128).T)
        put('cb', l, inp['conv_b'][l].reshape(44, 128).T)
        lv = np.concatenate([inp['diff_lq1'][l], inp['diff_lk1'][l], inp['diff_lq2'][l], inp['diff_lk2'][l]]).astype(np.float32)
        put('lvec', l, np.broadcast_to(lv[None, :], (128, 128)))
    idx = _na_index()
    nab = np.zeros((L, 4, 128, NAW), np.float32)
    for l in range(L):
        for h in range(4):
            src = np.concatenate([np.asarray(inp['na_rpb'][l, h], np.float32).ravel(), np.array([-10000.0], np.float32)])
            nab[l, h] = src[idx]
    f = lambda a: np.ascontiguousarray(np.asarray(a, np.float32))
    return {
        "w_mod": f(inp['w_mod']), "b_mod": f(inp['b_mod']), "w_in": f(inp['w_in']), "w_out": f(inp['w_out']),
        "w_uq": f(inp['mla_w_uq']), "w_ukv": f(inp['mla_w_ukv']), "w_up": f(inp['w_up']), "w_down": f(inp['w_down']),
        "pv": np.ascontiguousarray(pvA.reshape(128, L * NV)), "cf32": cf, "cbf": np.ascontiguousarray(cb),
        "aug": aug, "tabs": tabs, "nab": nab,
    }


_CACHE = {}


def kernel(**inp):
    n_layers = inp.pop('_n_layers', L)
    dbg = inp.pop('_dbg', False)
    stop = inp.pop('_stop', None)
    ncores = inp.pop('_ncores', 8)
    key = (n_layers, dbg, stop)
    if key not in _CACHE:
        _CACHE[key] = build_program(n_layers, dbg, stop)[0]
    nc = _CACHE[key]
    shared = _prep_shared(inp)
    x = np.asarray(inp['x'], np.float32)
    ctx = np.asarray(inp['ctx'], np.float32)
    c = np.asarray(inp['c'], np.float32)
    cc = np.asarray(inp['c_ctx'], np.float32)
    in_maps = []
    for b in range(ncores):
        m = dict(shared)
        m["xc"] = np.ascontiguousarray(np.concatenate([x[b], ctx[b]], axis=0))
        cT = np.zeros((128, 8, 2), np.float32)
        cT[:, :, 0] = c[b].reshape(8, 128).T
        cT[:, :, 1] = cc.reshape(8, 128).T
        m["cT"] = np.ascontiguousarray(cT.reshape(128, 16))
        in_maps.append(m)
    res = run_bass_kernel_spmd(nc, in_maps, core_ids=list(range(ncores)))
    out = np.stack([np.asarray(r["y"], np.float32) for r in res.results], axis=0)
    if dbg:
        kernel.dbg = [np.asarray(r["dbg"], np.float32) for r in res.results]
    return out
```

```python
import math
import numpy as np
import concourse.bass as bass
import concourse.mybir as mybir
from concourse.bass_utils import run_bass_kernel_spmd

F32 = mybir.dt.float32
BF16 = mybir.dt.bfloat16
AF = mybir.ActivationFunctionType
ALU = mybir.AluOpType
AX = mybir.AxisListType

ENGS = ['pe', 'act', 'dve', 'pool', 'sp']
CELL = 256
_ESZ = {}


def esz(dt):
    if dt not in _ESZ:
        _ESZ[dt] = mybir.dt.size(dt)
    return _ESZ[dt]


def ap_cells(ap):
    space = str(ap.space)
    sp = 0 if space == 'SB' else 1
    dims = ap.ap
    pstep, pcount = dims[0]
    e = esz(ap.dtype)
    off = ap.offset
    p0 = off // pstep
    foff = off % pstep
    ranges = [(foff, foff + 1)]
    for (st, cnt) in dims[1:]:
        if cnt <= 1:
            continue
        if len(ranges) * cnt <= 512 and abs(st) * e >= CELL:
            ranges = [(lo + i * st, hi + i * st) for (lo, hi) in ranges for i in range(cnt)]
        else:
            ext = (cnt - 1) * st
            if ext >= 0:
                ranges = [(lo, hi + ext) for (lo, hi) in ranges]
            else:
                ranges = [(lo + ext, hi) for (lo, hi) in ranges]
    cs = set()
    for (lo, hi) in ranges:
        c0 = (lo * e) // CELL
        c1 = (hi * e - 1) // CELL
        for c in range(c0, c1 + 1):
            cs.add(c)
    if sp == 1:
        return sorted(set(4 * 4096 + (c * CELL) // 2048 for c in cs))
    q0 = p0 // 32
    q1 = (p0 + pcount - 1) // 32
    out = []
    for q in range(q0, q1 + 1):
        base = (sp * 4 + q) * 4096
        for c in cs:
            out.append(base + c)
    return out


class Sched:
    def __init__(self, nc, n_lanes=48, same_eng_sync=True):
        self.nc = nc
        self.ops = {e: [] for e in ENGS}
        self.cw = {}
        self.cr = {}
        self.known = {e: {} for e in ENGS}
        self.snap = {}
        self.n_lanes = n_lanes
        self.lane_count = [0] * n_lanes
        self.next_lane = 0
        self.same_eng_sync = same_eng_sync
        self.eng_sem = {e: nc.alloc_semaphore("sem_" + e) for e in ENGS}
        self.lane_sem = [nc.alloc_semaphore("lane%d" % i) for i in range(n_lanes)]

    def _cells(self, items):
        cs = []
        for it in items:
            if it is None:
                continue
            if isinstance(it, (str, tuple)):
                cs.append(it)
            else:
                cs.extend(ap_cells(it))
        return cs

    def add(self, eng, fn, reads=(), writes=(), dma=False):
        ops = self.ops[eng]
        idx = len(ops)
        rc = self._cells(reads)
        wc = self._cells(writes)
        need = {}

        def want(tok, war=False):
            key, seq = tok
            if key == eng:
                if eng == 'pe' or war or not self.same_eng_sync:
                    return
            if need.get(key, -1) < seq:
                need[key] = seq

        for c in rc:
            t = self.cw.get(c)
            if t is not None:
                want(t)
        for c in wc:
            t = self.cw.get(c)
            if t is not None:
                want(t)
            rs = self.cr.get(c)
            if rs:
                for k, s in rs.items():
                    want((k, s), war=True)
        if dma:
            lane = self.next_lane
            self.next_lane = (lane + 1) % self.n_lanes
            cnt = self.lane_count[lane] + 1
            self.lane_count[lane] = cnt
            tok = (('L', lane), cnt)
            if cnt > 1:
                want((('L', lane), cnt - 1))
        else:
            tok = (eng, idx)
        kn = self.known[eng]
        waits = []
        for key, seq in need.items():
            if kn.get(key, -1) >= seq:
                continue
            waits.append((key, seq))
            kn[key] = seq
            if isinstance(key, str):
                self.ops[key][seq]['signal'] = True
                sn = self.snap.get((key, seq))
                if sn:
                    for k2, s2 in sn.items():
                        if kn.get(k2, -1) < s2:
                            kn[k2] = s2
        if not dma:
            self.snap[(eng, idx)] = dict(kn)
        ops.append(dict(fn=fn, waits=waits, tok=tok, dma=dma, signal=False))
        for c in wc:
            self.cw[c] = tok
            self.cr[c] = {}
        for c in rc:
            d = self.cr.get(c)
            if d is None:
                d = {}
                self.cr[c] = d
            if d.get(tok[0], -1) < tok[1]:
                d[tok[0]] = tok[1]
        return tok

    def emit(self):
        nc = self.nc
        for e in ENGS:
            c = 0
            for op in self.ops[e]:
                if op['signal'] and not op['dma']:
                    c += 1
                    op['sigval'] = c
        emap = {'pe': 'tensor', 'act': 'scalar', 'dve': 'vector', 'pool': 'gpsimd', 'sp': 'sync'}

        def run(e, E):
            for op in self.ops[e]:
                for key, seq in op['waits']:
                    if isinstance(key, str):
                        E.wait_ge(self.eng_sem[key], self.ops[key][seq]['sigval'])
                    else:
                        E.wait_ge(self.lane_sem[key[1]], 16 * seq)
                inst = op['fn'](E)
                if op['dma']:
                    inst.then_inc(self.lane_sem[op['tok'][0][1]], 16)
                elif op['signal']:
                    inst.then_inc(self.eng_sem[e], 1)
            if e == 'sp':
                for i, cnt in enumerate(self.lane_count):
                    if cnt:
                        E.wait_ge(self.lane_sem[i], 16 * cnt)

        with nc.Block() as block:
            for e in ENGS:
                getattr(block, emap[e])(lambda E, e=e: run(e, E))

    def stats(self):
        return {e: (len(self.ops[e]), sum(len(o['waits']) for o in self.ops[e])) for e in ENGS}


class Ring:
    def __init__(self, items):
        self.items = list(items)
        self.i = 0

    def next(self):
        r = self.items[self.i]
        self.i = (self.i + 1) % len(self.items)
        return r


D = 1024
L = 4
TL = 2048
TC = 256
T = TL + TC
NKT = T // 128
TBS = [(0, 512), (512, 512), (1024, 512), (1536, 512), (2048, 256)]
IN_COLS = 2464
DFF = 2816
NCH = DFF // 128
EPS = 1e-6
BIG = 30000.0
NA_KT = {0: range(0, 6), 1: range(2, 10), 2: range(6, 14), 3: range(10, 16)}
NAW = 22 * 64
FFG = [(0, 4), (4, 4), (8, 4), (12, 4), (16, 4), (20, 2)]

PV = {}
_c = 0
for _n, _w in [('g_mix', 8), ('g_ffn', 8), ('qa_g', 2), ('kva_g', 1), ('mq_g', 1), ('mk_g', 1),
               ('dq_g', 1), ('dk_g', 1), ('dsub_g', 1), ('nq_g', 1), ('nk_g', 1), ('gq_g', 1), ('gk_g', 1),
               ('cw0', 44), ('cw1', 44), ('cw2', 44), ('cb', 44), ('lvec', 128)]:
    PV[_n] = (_c, _w)
    _c += _w
NV = _c


def lambda_init(l):
    return 0.8 - 0.6 * math.exp(-0.3 * l)


class StopBuild(Exception):
    pass


def build_program(n_layers=L, dbg=False, stop_after=None):
    nc = bass.Bass("TRN2", target_bir_lowering=False)

    def ck(name):
        if stop_after == name:
            raise StopBuild()

    def din(name, shape):
        return nc.dram_tensor(name, list(shape), F32, kind="ExternalInput").ap()

    xc_d = din("xc", [T, D])
    cT_d = din("cT", [128, 16])
    wmod_d = din("w_mod", [L, D, 6 * D])
    bmod_d = din("b_mod", [L, 6 * D])
    win_d = din("w_in", [L, D, IN_COLS])
    wout_d = din("w_out", [L, D, D])
    wuq_d = din("w_uq", [L, 256, 384])
    wukv_d = din("w_ukv", [L, 128, 512])
    wup_d = din("w_up", [L, D, 2 * DFF])
    wdn_d = din("w_down", [L, DFF, D])
    pv_d = din("pv", [128, L * NV])
    cf_d = din("cf32", [128, 256])
    cb_d = din("cbf", [128, 6 * 128])
    aug_d = din("aug", [2, 32, T])
    tabs_d = din("tabs", [3, 128, 2, TL])
    nab_d = din("nab", [L, 4, 128, NAW])
    y_d = nc.dram_tensor("y", [TL, D], F32, kind="ExternalOutput").ap()
    xT_d = nc.dram_tensor("xT_scr", [128, 8, T], F32).ap()
    if dbg:
        dbg_d = nc.dram_tensor("dbg", [128, 8, T], F32, kind="ExternalOutput").ap()

    ARENA_F32 = 53000
    arena = nc.alloc_sbuf_tensor("arena", [128, ARENA_F32], F32)
    psum = nc.alloc_psum_tensor("psum", [128, 4096], F32)
    S = Sched(nc)

    pos = [0]

    def alloc_b(nbytes):
        a = pos[0]
        n = (nbytes + 255) // 256 * 256
        pos[0] += n
        assert pos[0] <= ARENA_F32 * 4, pos[0]
        return a

    def f32v(boff, n):
        return arena[:, boff // 4: boff // 4 + n]

    def bf16v(boff, n):
        return arena[:, boff // 4: boff // 4 + (n + 1) // 2].bitcast(BF16)

    def PB(i):
        return psum[:, i * 512:(i + 1) * 512]

    ident = f32v(alloc_b(512), 128)
    ones_f = f32v(alloc_b(512), 128)
    cbf = bf16v(alloc_b(6 * 256), 6 * 128).rearrange("p (a b) -> p a b", a=6)
    allones, blk32, blk64, Rmla, Rdiff, Rgqa = [cbf[:, i, :] for i in range(6)]
    pv = f32v(alloc_b(L * NV * 4), L * NV).rearrange("p (l n) -> p l n", l=L)
    modT = f32v(alloc_b(L * 96 * 4), L * 96).rearrange("p (l s w) -> p l s w", l=L, s=48)
    cT = f32v(alloc_b(64), 16)
    scT_f = f32v(alloc_b(64), 16)
    scT = bf16v(alloc_b(32), 16).rearrange("p (a b) -> p a b", a=8)
    Gv = f32v(alloc_b(4 * 16 * 4), 64).rearrange("p (a j w) -> p a j w", a=2, j=8)
    misc = f32v(alloc_b(64 * 4), 64)
    hT = bf16v(alloc_b(8 * T * 2), 8 * T).rearrange("p (a t) -> p a t", a=8)
    mix_off = alloc_b(8 * T * 2)
    mixT = bf16v(mix_off, 8 * T).rearrange("p (a t) -> p a t", a=8)
    qkv_off = alloc_b(3 * 4 * T * 2)
    QT = bf16v(qkv_off, 4 * T).rearrange("p (a t) -> p a t", a=4)
    KT = bf16v(qkv_off + 4 * T * 2, 4 * T).rearrange("p (a t) -> p a t", a=4)
    VA = bf16v(qkv_off + 8 * T * 2, NKT * 4 * 128).rearrange("p (k h c) -> p k h c", k=NKT, h=4)
    xblk = f32v(qkv_off + 16384, 8 * 512).rearrange("p (a n) -> p a n", a=8)
    xblk2 = f32v(qkv_off, 8 * 512).rearrange("p (a n) -> p a n", a=8)
    xld = f32v(qkv_off + 32768, 4 * 1024).rearrange("p (a n) -> p a n", a=4)
    GN = 4
    fo = mix_off
    actT = bf16v(fo, GN * T).rearrange("p (a t) -> p a t", a=GN); fo += GN * T * 2
    UW = T + 4
    UWP = (UW * 4 + 255) // 256 * 256
    ubuf = [f32v(fo + i * UWP, UW) for i in range(2)]; fo += 2 * UWP
    assert fo <= qkv_off + 512, (fo, qkv_off)
    WUPB = 8 * 2 * GN * 128 * 2
    wup_bufs = [bf16v(qkv_off + 32768, 8 * 2 * GN * 128).rearrange("p (k g n) -> p k g n", k=8, g=2),
                bf16v(qkv_off + 512, 8 * 2 * GN * 128).rearrange("p (k g n) -> p k g n", k=8, g=2)]
    assert qkv_off + 32768 + WUPB <= qkv_off + 3 * 4 * T * 2 and 512 + WUPB <= 32768
    fo = qkv_off
    assert fo <= qkv_off + 3 * 4 * T * 2, (fo, qkv_off + 3 * 4 * T * 2)
    tab_off = alloc_b(2 * TL * 4)
    ropeT = f32v(tab_off, 2 * TL).rearrange("p (a t) -> p a t", a=2)
    natab = bf16v(tab_off, 4 * NAW).rearrange("p (h n) -> p h n", h=4)
    wA = bf16v(alloc_b(8 * 768 * 2), 8 * 768).rearrange("p (a n) -> p a n", a=8)
    wout_sb = bf16v(tab_off, 8 * 1024).rearrange("p (a n) -> p a n", a=8)
    ubuf2 = [f32v(tab_off + i * UWP, UW) for i in range(2)]
    assert 2 * UWP <= 2 * TL * 4 + 8 * 768 * 2
    wuq_sb = bf16v(alloc_b(2 * 384 * 2), 2 * 384).rearrange("p (a n) -> p a n", a=2)
    wukv_sb = bf16v(alloc_b(512 * 2), 512)
    ckvnT = bf16v(alloc_b(T * 2), T)
    cqn = bf16v(alloc_b(2 * 512 * 2), 2 * 512).rearrange("p (a n) -> p a n", a=2)
    r1_off = tab_off + 2 * UWP
    wdn_bufs = [bf16v(r1_off + i * GN * 1024 * 2, GN * 1024).rearrange("p (a n) -> p a n", a=GN) for i in range(2)]
    tmp_off = alloc_b(24 * 1024)
    assert r1_off + 2 * GN * 1024 * 2 <= tmp_off, (r1_off, tmp_off)
    krs_buf = f32v(alloc_b(2048), 512)
    sq_extra = alloc_b(2048)

    def tf(i):
        return f32v(tmp_off + i * 2048, 512)

    def tb16(i, half=0):
        return bf16v(tmp_off + i * 2048 + half * 1024, 512)

    kraw = Ring([tf(0), tf(1)])
    sqr = Ring([tb16(2, 0), tb16(2, 1), bf16v(sq_extra, 512), bf16v(sq_extra + 1024, 512)])
    lnr = Ring([tf(3), tf(4)])
    rsr = Ring([tf(5), tf(6)])
    qnr = Ring([tb16(7, 0), tb16(7, 1)])
    t1r = Ring([tf(8), tf(9)])
    t2r = Ring([tf(10), tf(11)])
    Pr = Ring([tb16(0, 0), tb16(0, 1), tb16(1, 0), tb16(1, 1)])
    P2r = Ring([tb16(2, 0), tb16(2, 1)])
    Osb = Ring([tf(3), tf(4)])
    zrow = Ring([tf(5), tf(6)])
    ABo = [tf(7), tf(8), tf(9)]
    xring = Ring([tf(6), tf(7), tf(8), tf(9), tf(10), tf(11)])
    ctile = [tf(0), tf(1), tf(2), tf(3), tf(4), tf(5)]
    nastg = f32v(tmp_off, NAW)
    wmr = Ring([bf16v(tmp_off + i * 4096, 2048) for i in range(3)])
    m_sb = f32v(qkv_off, 6 * D)
    bm_sb = f32v(qkv_off + 24576, 6 * D)

    def mm(out, lhsT, rhs, start=True, stop=True):
        S.add('pe', lambda E: E.matmul(out, lhsT, rhs, start=start, stop=stop), reads=[lhsT, rhs], writes=[out])

    def tr(out, in_, idn):
        S.add('pe', lambda E: E.transpose(out, in_, idn), reads=[in_, idn], writes=[out])

    def act(out, in_, func, scale=None, bias=None, eng='act'):
        kw = {}
        rd = [in_]
        if scale is not None:
            kw['scale'] = scale
            if not isinstance(scale, float):
                rd.append(scale)
        if bias is not None:
            kw['bias'] = bias
            if not isinstance(bias, float):
                rd.append(bias)
        S.add('act', lambda E: E.activation(out, in_, func, **kw), reads=rd, writes=[out])

    def stt(out, in0, scalar, in1, op0, op1):
        rd = [in0, in1] + ([] if isinstance(scalar, float) else [scalar])
        S.add('dve', lambda E: E.scalar_tensor_tensor(out=out, in0=in0, scalar=scalar, in1=in1, op0=op0, op1=op1),
              reads=rd, writes=[out])

    def tt(out, in0, in1, op, eng='dve'):
        S.add(eng, lambda E: E.tensor_tensor(out=out, in0=in0, in1=in1, op=op), reads=[in0, in1], writes=[out])

    def ts(out, in0, s1, s2, op0, op1=None, eng='dve'):
        rd = [in0] + [s for s in (s1, s2) if s is not None and not isinstance(s, float)]
        if op1 is None:
            S.add(eng, lambda E: E.tensor_scalar(out=out, in0=in0, scalar1=s1, scalar2=None, op0=op0), reads=rd, writes=[out])
        else:
            S.add(eng, lambda E: E.tensor_scalar(out=out, in0=in0, scalar1=s1, scalar2=s2, op0=op0, op1=op1), reads=rd, writes=[out])

    def cp(out, in_, eng='dve'):
        S.add(eng, lambda E: E.tensor_copy(out=out, in_=in_), reads=[in_], writes=[out])

    def vcopy(out, in_, bank, use_act):
        if use_act:
            S.add('act', lambda E: E.activation(out, in_, AF.Copy), reads=[bank], writes=[out])
        else:
            S.add('dve', lambda E: E.tensor_copy(out=out, in_=in_), reads=[bank], writes=[out])

    def mset(ap, val, eng='dve'):
        S.add(eng, lambda E: E.memset(ap, val), writes=[ap])

    def dma(q, out, in_, reads=None, writes=None):
        r = [in_] if reads is None else reads
        w = [out] if writes is None else writes
        r = [a for a in r if isinstance(a, (str, tuple)) or str(a.space) != 'DRAM']
        w = [a for a in w if isinstance(a, (str, tuple)) or str(a.space) != 'DRAM']
        S.add(q, lambda E: E.dma_start(out=out, in_=in_), reads=r, writes=w, dma=True)

    def xkey(tbi):
        return "x:%d" % tbi

    def xkeys(tbi):
        return ["x:%d" % tbi] + ["xf:%d:%d" % (tbi, j) for j in range(8)]

    try:
        import os as _os
        if _os.environ.get('SKIP_CONST'):
            raise StopBuild()
        dma('sp', ident, cf_d[:, 0:128])
        dma('sp', ones_f, cf_d[:, 128:256])
        dma('pool', cbf, cb_d.rearrange("p (a b) -> p a b", a=6))
        dma('sp', pv, pv_d.rearrange("p (l n) -> p l n", l=L))
        dma('sp', cT, cT_d)
        act(scT_f, cT, AF.Silu)
        cp(scT, scT_f.rearrange("p (a b) -> p a b", a=8))

        ck('c0')
        for tbi, (t0, n) in enumerate(TBS):
            ntt = n // 128
            dma('sp', xld[:, 0:ntt, :], xc_d[t0:t0 + n, :].rearrange("(a p) d -> p a d", p=128))
            for j in range(8):
                pb = PB(j % 2)
                for a in range(ntt):
                    tr(pb[:, a * 128:(a + 1) * 128], xld[:, a, j * 128:(j + 1) * 128], ident)
                if j % 2 == 0:
                    act(xblk[:, j, 0:n], pb[:, 0:n], AF.Copy)
                else:
                    cp(xblk[:, j, 0:n], pb[:, 0:n])
            dma('sp', xT_d[:, :, t0:t0 + n], xblk[:, :, 0:n], writes=xkeys(tbi))

        ck('xt')
        for l in range(n_layers):
            dma('sp', bm_sb[0:1, :], bmod_d[l:l + 1, :])
            dma('sp', bm_sb[1:2, :], bmod_d[l:l + 1, :])
            for cg in range(3):
                for kc in range(8):
                    piece = wmr.next()
                    dma('pool', piece, wmod_d[l, kc * 128:(kc + 1) * 128, cg * 2048:(cg + 1) * 2048])
                    for i in range(4):
                        mm(PB(i)[0:2, :], scT[:, kc, :], piece[:, i * 512:(i + 1) * 512], start=(kc == 0), stop=(kc == 7))
                for i in range(4):
                    c0 = cg * 2048 + i * 512
                    tt(m_sb[0:2, c0:c0 + 512], PB(i)[0:2, :], bm_sb[0:2, c0:c0 + 512], ALU.add)
            pt = PB(4)
            for s in range(48):
                tr(pt[:, 2 * s:2 * s + 2], m_sb[0:2, s * 128:(s + 1) * 128], ident[0:2, 0:2])
            cp(modT[:, l].rearrange("p s w -> p (s w)"), pt[:, 0:96])

        ck('mod')
        def rstd_from(src, R, N, ones_m, inv_d):
            sq = sqr.next()[0:R, 0:N]
            act(sq, src, AF.Square)
            ssp = ss_ring.next()[0:R, 0:N]
            mm(ssp, ones_m, sq)
            ln = lnr.next()[0:R, 0:N]
            act(ln, ssp, AF.Ln, scale=float(inv_d), bias=float(EPS))
            rs = rsr.next()[0:R, 0:N]
            act(rs, ln, AF.Exp, scale=-0.5)
            return rs

        def norm_rope(src, R, N, ones_m, inv_d, gain, out, rope=None):
            rs = rstd_from(src, R, N, ones_m, inv_d)
            if rope is None:
                stt(out, src, gain, rs, ALU.mult, ALU.mult)
                return
            Rm, cosT, sinT = rope
            qn = qnr.next()[0:R, 0:N]
            stt(qn, src, gain, rs, ALU.mult, ALU.mult)
            rp = rot_ring.next()[0:R, 0:N]
            mm(rp, Rm, qn)
            t1 = t1r.next()[0:R, 0:N]
            tt(t1, qn, cosT, ALU.mult, eng='pool')
            t2 = t2r.next()[0:R, 0:N]
            tt(t2, rp, sinT, ALU.mult)
            tt(out, t1, t2, ALU.add, eng='pool')

        def chain(src_fn, R, N, ones_m, inv_d, gain, out, rope=None):
            st = {}

            def A():
                st['src'] = src_fn()
                sq = sqr.next()[0:R, 0:N]
                act(sq, st['src'], AF.Square)
                st['sq'] = sq

            def B():
                ssp = ss_ring.next()[0:R, 0:N]
                mm(ssp, ones_m, st['sq'])
                ln = lnr.next()[0:R, 0:N]
                act(ln, ssp, AF.Ln, scale=float(inv_d), bias=float(EPS))
                rs = rsr.next()[0:R, 0:N]
                act(rs, ln, AF.Exp, scale=-0.5)
                if rope is None:
                    stt(out, st['src'], gain, rs, ALU.mult, ALU.mult)
                else:
                    qn = qnr.next()[0:R, 0:N]
                    stt(qn, st['src'], gain, rs, ALU.mult, ALU.mult)
                    st['qn'] = qn

            def C():
                Rm, cosT, sinT = rope
                qn = st['qn']
                rp = rot_ring.next()[0:R, 0:N]
                mm(rp, Rm, qn)
                t1 = t1r.next()[0:R, 0:N]
                tt(t1, qn, cosT, ALU.mult, eng='pool')
                t2 = t2r.next()[0:R, 0:N]
                tt(t2, rp, sinT, ALU.mult)
                tt(out, t1, t2, ALU.add, eng='pool')

            return [A, B, C if rope is not None else None]

        def run_pipe(tiles):
            n = len(tiles)
            for step in range(n + 2):
                if step < n and tiles[step][0]:
                    tiles[step][0]()
                if 0 <= step - 1 < n and tiles[step - 1][1]:
                    tiles[step - 1][1]()
                if 0 <= step - 2 < n and tiles[step - 2][2]:
                    tiles[step - 2][2]()

        def modulate(l, which, compute=None):
            xbs = [xblk, xblk2]

            def load(tbi):
                t0_, n_ = TBS[tbi]
                dma('sp', xbs[tbi % 2][:, :, 0:n_], xT_d[:, :, t0_:t0_ + n_], reads=xkeys(tbi))

            load(0)
            for tbi, (t0, n) in enumerate(TBS):
                w = 0 if tbi < 4 else 1
                xb = xbs[tbi % 2]
                if tbi + 1 < len(TBS):
                    load(tbi + 1)
                if compute is not None:
                    compute(tbi, t0, n, xb)
                ssp = ss_ring.next()[:, 0:n]
                for j in range(8):
                    sq = sqr.next()[:, 0:n]
                    act(sq, xb[:, j, 0:n], AF.Square)
                    mm(ssp, allones, sq, start=(j == 0), stop=(j == 7))
                ln = lnr.next()[:, 0:n]
                act(ln, ssp, AF.Ln, scale=1.0 / D, bias=float(EPS))
                rs = rsr.next()[:, 0:n]
                act(rs, ln, AF.Exp, scale=-0.5)
                for j in range(8):
                    t1 = t1r.next()[:, 0:n]
                    stt(t1, xb[:, j, 0:n], Gv[:, which, j, w:w + 1], rs, ALU.mult, ALU.mult)
                    shift = modT[:, l, (3 * which) * 8 + j, w:w + 1]
                    act(hT[:, j, t0:t0 + n], t1, AF.Identity, bias=shift)


        pending = []

        def flush_pending():
            while pending:
                pending.pop(0)()

        def attention(q_of, keytiles, N, scale, vrows, LOOK=2):
            Op = o_ring.next()
            nk = len(keytiles)
            Sps = {}

            def qk(i):
                Sps[i] = s_ring.next()[:, 0:N]
                mm(Sps[i], keytiles[i][0], q_of)

            for i in range(min(LOOK, nk)):
                qk(i)
            flush_pending()
            for i, (k_ap, v_ap, tab) in enumerate(keytiles):
                if i + LOOK < nk:
                    qk(i + LOOK)
                Sp = Sps.pop(i)
                P = Pr.next()[:, 0:N]
                act(P, Sp, AF.Exp, scale=float(scale))
                if tab is not None:
                    P2 = P2r.next()[:, 0:N]
                    tt(P2, P, tab[:, 0:N], ALU.mult)
                    P = P2
                mm(Op[0:vrows, 0:N], v_ap, P, start=(i == 0), stop=(i == nk - 1))
            return Op

        def normalize(Op, odd, N, out_sb, bcr=None):
            zp = 0 if odd else 64
            r0 = 64 if odd else 0
            zr = zrow.next()
            act(zr[zp:zp + 1, 0:N], Op[zp:zp + 1, 0:N], AF.Ln)
            zr2 = zrow.next()
            act(zr2[zp:zp + 1, 0:N], zr[zp:zp + 1, 0:N], AF.Exp, scale=-1.0)
            bc = (bcr or bc_ring).next()
            mm(bc[:, 0:N], ones_f[zp:zp + 1, :], zr2[zp:zp + 1, 0:N])
            osb = Osb.next()
            act(osb[r0:r0 + 64, 0:N], Op[r0:r0 + 64, 0:N], AF.Copy)
            tt(out_sb[r0:r0 + 64, 0:N], osb[r0:r0 + 64, 0:N], bc[r0:r0 + 64, 0:N], ALU.mult)

        def v_slot(kt, h):
            odd = h % 2
            return VA[:, kt, h, 0:128] if odd else VA[:, kt, h, 0:65]

        def v_dst(kt, h):
            odd = h % 2
            return VA[:, kt, h, 64:128] if odd else VA[:, kt, h, 0:64]

        def init_va():
            mset(VA.rearrange("p k h c -> p (k h c)"), 0.0, eng='dve')
            for h in range(4):
                col = 0 if h % 2 else 64
                mset(VA[:, :, h, col:col + 1], 1.0, eng='dve')

        for l in range(n_layers):
            with_ctx = l < L - 1
            li = lambda_init(l)
            q_tbs = TBS if with_ctx else TBS[:4]
            ss_ring = Ring([PB(3), PB(4)])
            rot_ring = Ring([PB(5), PB(6)])
            raw_ring = Ring([PB(0), PB(1), PB(2)])
            vp_ring = Ring([PB(int(_os.environ.get("VPB", "6")))])
            s_ring = Ring([PB(0), PB(1), PB(2)])
            o_ring = Ring([PB(3), PB(4)])
            bc_ring = Ring([PB(5)])
            ds_ring = Ring([PB(0), PB(1), PB(2), PB(3)])
            do_ring = Ring([PB(4), PB(5), PB(6)])
            dscratch = [PB(6)]

            def pvc(name, j=0, rows=128):
                c0, w = PV[name]
                return pv[0:rows, l, c0 + j:c0 + j + 1]

            for which, gname in ((0, 'g_mix'), (1, 'g_ffn')):
                c0, _ = PV[gname]
                for w in range(2):
                    sc = modT[:, l, (3 * which + 1) * 8:(3 * which + 2) * 8, w]
                    stt(Gv[:, which, :, w], sc, 1.0, pv[:, l, c0:c0 + 8], ALU.add, ALU.mult)
            c0, _ = PV['lvec']
            lv = pv[:, l, c0:c0 + 128].rearrange("p (a d) -> p a d", a=4)
            prod = tf(0)[:, 0:64].rearrange("p (a d) -> p a d", a=2)
            tt(prod[:, 0, :], lv[:, 0, :], lv[:, 1, :], ALU.mult)
            tt(prod[:, 1, :], lv[:, 2, :], lv[:, 3, :], ALU.mult)
            S.add('dve', lambda E, prod=prod: E.tensor_reduce(out=misc[:, 0:2], in_=prod, axis=AX.X, op=ALU.add),
                  reads=[prod], writes=[misc[:, 0:2]])
            act(misc[:, 2:4], misc[:, 0:2], AF.Exp)
            tt(misc[:, 4:5], misc[:, 3:4], misc[:, 2:3], ALU.subtract)
            ts(misc[:, 5:6], misc[:, 4:5], float(-li), None, ALU.add)
            ts(misc[:, 6:7], pvc('dsub_g'), float(1.0 - li), None, ALU.mult)
            nlam = misc[:, 5:6]
            gsub = misc[:, 6:7]

            modulate(l, 0)
            ck('m1')
            init_va()
            ck('va')

            def proj_fm(out_ps, wcols, t0, n):
                for kc in range(8):
                    mm(out_ps, wA[:, kc, wcols[0]:wcols[1]], hT[:, kc, t0:t0 + n], start=(kc == 0), stop=(kc == 7))

            def proj_v(col0, heads, ncols_per_head=64, slot_of=None):
                nh = len(heads)
                for kt in range(NKT):
                    vp = vp_ring.next()
                    for kc in range(8):
                        mm(vp[:, 0:nh * 64], hT[:, kc, kt * 128:(kt + 1) * 128], wA[:, kc, col0:col0 + nh * 64],
                           start=(kc == 0), stop=(kc == 7))
                    for i, hs in enumerate(heads):
                        for h in hs:
                            vcopy(v_dst(kt, h), vp[:, i * 64:(i + 1) * 64], vp, True)

            def run_attention(mixer, head_q, head_k, krows, scale, tab_of=None, na=False, diff=False):
                for h in range(4):
                    odd = h % 2
                    chunk = 2 * mixer + h // 2
                    for tbi, (t0, n) in enumerate(q_tbs):
                        if tbi < 4:
                            if na:
                                kts = list(NA_KT[tbi]) + [16, 17]
                            else:
                                kts = list(range(NKT))
                        else:
                            kts = [16, 17]
                        vrows = 128 if odd else 65
                        if not diff:
                            tiles = []
                            for kt in kts:
                                tab = None
                                if na and kt < 16:
                                    i0 = 8 * tbi - 2 * kt + 7
                                    tab = natab[:, h, (i0 + 3) * 64:(i0 + 3) * 64 + 512]
                                tiles.append((KT[0:krows, head_k(h), kt * 128:(kt + 1) * 128], v_slot(kt, h), tab))
                            Op = attention(QT[0:krows, h, t0:t0 + n], tiles, n, scale, vrows)
                            pending.append(lambda Op=Op, odd=odd, n=n, chunk=chunk, t0=t0: normalize(Op, odd, n, mixT[:, chunk, t0:t0 + n]))
                        else:
                            r0 = 64 if odd else 0
                            Ops = [do_ring.next(), do_ring.next()]
                            nk = len(kts)
                            Sps = {}

                            def qk2(i, h=h, t0=t0, n=n, kts=kts):
                                kt = kts[i]
                                Sps[i] = []
                                for pr in range(2):
                                    sp_ = ds_ring.next()[:, 0:n]
                                    mm(sp_, KT[32 * pr:32 * pr + 32, h, kt * 128:(kt + 1) * 128], QT[32 * pr:32 * pr + 32, h, t0:t0 + n])
                                    Sps[i].append(sp_)

                            qk2(0)
                            dscratch[0] = Ops[0]
                            flush_pending()
                            for i, kt in enumerate(kts):
                                if i + 1 < nk:
                                    qk2(i + 1)
                                sp2 = Sps.pop(i)
                                for pr in range(2):
                                    P = Pr.next()[:, 0:n]
                                    act(P, sp2[pr], AF.Exp, scale=float(scale))
                                    mm(Ops[pr][0:vrows, 0:n], v_slot(kt, h), P, start=(i == 0), stop=(i == nk - 1))

                            def fin(Ops=Ops, odd=odd, n=n, r0=r0, chunk=chunk, t0=t0):
                                scr_ring = Ring([dscratch[0]])
                                normalize(Ops[0], odd, n, ABo[0], bcr=scr_ring)
                                normalize(Ops[1], odd, n, ABo[1], bcr=scr_ring)
                                o = ABo[2]
                                stt(o[r0:r0 + 64, 0:n], ABo[1][r0:r0 + 64, 0:n], nlam[r0:r0 + 64, :], ABo[0][r0:r0 + 64, 0:n], ALU.mult, ALU.add)
                                sq = sqr.next()
                                act(sq[r0:r0 + 64, 0:n], o[r0:r0 + 64, 0:n], AF.Square)
                                ssp = dscratch[0]
                                mm(ssp[r0:r0 + 64, 0:n], blk64[r0:r0 + 64, r0:r0 + 64], sq[r0:r0 + 64, 0:n])
                                ln = lnr.next()
                                act(ln[r0:r0 + 64, 0:n], ssp[r0:r0 + 64, 0:n], AF.Ln, scale=1.0 / 64, bias=float(EPS))
                                rs = rsr.next()
                                act(rs[r0:r0 + 64, 0:n], ln[r0:r0 + 64, 0:n], AF.Exp, scale=-0.5)
                                stt(mixT[r0:r0 + 64, chunk, t0:t0 + n], o[r0:r0 + 64, 0:n], gsub[r0:r0 + 64, :], rs[r0:r0 + 64, 0:n],
                                    ALU.mult, ALU.mult)
                            pending.append(fin)
                    if diff:
                        dscratch[0] = do_ring.next()
                    flush_pending()

            dma('pool', wA[:, :, 0:416], win_d[l, :, 0:416].rearrange("(a p) n -> p a n", p=128))
            dma('pool', wuq_sb, wuq_d[l].rearrange("(a p) n -> p a n", p=128))
            dma('pool', wukv_sb, wukv_d[l])
            dma('sp', ropeT, tabs_d[0])
            tiles = []
            for tbi, (t0, n) in enumerate(TBS):
                lat = tbi < 4
                rope = (Rmla[0:96, 0:96], ropeT[0:96, 0, t0:t0 + n], ropeT[0:96, 1, t0:t0 + n]) if lat else None
                st = {}

                def A_cq(t0=t0, n=n, st=st):
                    st['raws'] = []
                    for c in range(2):
                        rp = raw_ring.next()[:, 0:n]
                        proj_fm(rp, (c * 128, (c + 1) * 128), t0, n)
                        st['raws'].append(rp)

                def B_cq(t0=t0, n=n, st=st):
                    ssp = ss_ring.next()[:, 0:n]
                    for c in range(2):
                        sq = sqr.next()[:, 0:n]
                        act(sq, st['raws'][c], AF.Square)
                        mm(ssp, allones, sq, start=(c == 0), stop=(c == 1))
                    ln = lnr.next()[:, 0:n]
                    act(ln, ssp, AF.Ln, scale=1.0 / 256, bias=float(EPS))
                    rs = rsr.next()[:, 0:n]
                    act(rs, ln, AF.Exp, scale=-0.5)
                    for c in range(2):
                        stt(cqn[:, c, 0:n], st['raws'][c], pvc('qa_g', c), rs, ALU.mult, ALU.mult)

                tiles.append([A_cq, B_cq, None])

                def src_ckv(t0=t0, n=n):
                    rp = raw_ring.next()[:, 0:n]
                    proj_fm(rp, (256, 384), t0, n)
                    return rp
                tiles.append(chain(src_ckv, 128, n, allones, 1.0 / 128, pvc('kva_g'), ckvnT[:, t0:t0 + n], None))

                def A_kr(t0=t0, n=n):
                    rpk = raw_ring.next()
                    for kc in range(8):
                        mm(rpk[64:96, 0:n], wA[:, kc, 384:416], hT[:, kc, t0:t0 + n], start=(kc == 0), stop=(kc == 7))
                    act(krs_buf[64:96, 0:n], rpk[64:96, 0:n], AF.Copy)
                tiles.append([A_kr, None, None])

                for h in range(4):
                    def src_q(h=h, n=n):
                        rp = raw_ring.next()[0:96, 0:n]
                        for c in range(2):
                            mm(rp, wuq_sb[:, c, h * 96:(h + 1) * 96], cqn[:, c, 0:n], start=(c == 0), stop=(c == 1))
                        return rp
                    tiles.append(chain(src_q, 96, n, allones[0:96, 0:96], 1.0 / 96, pvc('mq_g', 0, 96), QT[0:96, h, t0:t0 + n], rope))
                for h in range(4):
                    def src_k(h=h, t0=t0, n=n):
                        rp = raw_ring.next()[0:64, 0:n]
                        mm(rp, wukv_sb[:, h * 128:h * 128 + 64], ckvnT[:, t0:t0 + n])
                        kr_ = kraw.next()
                        act(kr_[0:64, 0:n], rp, AF.Copy)
                        cp(kr_[64:96, 0:n], krs_buf[64:96, 0:n])
                        return kr_[0:96, 0:n]
                    tiles.append(chain(src_k, 96, n, allones[0:96, 0:96], 1.0 / 96, pvc('mk_g', 0, 96), KT[0:96, h, t0:t0 + n], rope))
            run_pipe(tiles)
            ck('mlap')
            for kt in range(NKT):
                vp = vp_ring.next()
                for h in range(4):
                    mm(vp[:, h * 64:(h + 1) * 64], ckvnT[:, kt * 128:(kt + 1) * 128], wukv_sb[:, h * 128 + 64:h * 128 + 128])
                for h in range(4):
                    if _os.environ.get('NOVCP'):
                        continue
                    vcopy(v_dst(kt, h), vp[:, h * 64:(h + 1) * 64], vp, True)
            ck('mlav')
            run_attention(0, None, lambda h: h, 96, 96 ** -0.5)
            ck('mla')

            dma('pool', wA[:, :, 0:768], win_d[l, :, 416:1184].rearrange("(a p) n -> p a n", p=128))
            dma('sp', ropeT, tabs_d[1])
            tiles = []
            for tbi, (t0, n) in enumerate(TBS):
                lat = tbi < 4
                rope = (Rdiff[0:64, 0:64], ropeT[0:64, 0, t0:t0 + n], ropeT[0:64, 1, t0:t0 + n]) if lat else None
                for h in range(4):
                    for (cb, gname, dst) in ((0, 'dq_g', QT), (256, 'dk_g', KT)):
                        def src(h=h, cb=cb, t0=t0, n=n):
                            rp = raw_ring.next()[0:64, 0:n]
                            proj_fm(rp, (cb + h * 64, cb + h * 64 + 64), t0, n)
                            return rp
                        tiles.append(chain(src, 64, n, blk32[0:64, 0:64], 1.0 / 32, pvc(gname, 0, 64), dst[0:64, h, t0:t0 + n], rope))
            run_pipe(tiles)
            proj_v(512, [[0], [1], [2], [3]])
            run_attention(1, None, lambda h: h, 32, 32 ** -0.5, diff=True)
            ck('diff')

            dma('pool', wA[:, :, 0:768], win_d[l, :, 1184:1952].rearrange("(a p) n -> p a n", p=128))
            for h in range(4):
                dma('sp', nastg, nab_d[l, h])
                act(natab[:, h, :], nastg, AF.Exp)
                for (c0_, c1_) in ((0, TL), (TL, T)):
                    dma('pool', QT[64:96, h, c0_:c1_], aug_d[0][:, c0_:c1_])
                    dma('pool', KT[64:96, h, c0_:c1_], aug_d[1][:, c0_:c1_])
            tiles = []
            for tbi, (t0, n) in enumerate(TBS):
                for h in range(4):
                    for (cb, gname, dst) in ((0, 'nq_g', QT), (256, 'nk_g', KT)):
                        def src(h=h, cb=cb, t0=t0, n=n):
                            rp = raw_ring.next()[0:64, 0:n]
                            proj_fm(rp, (cb + h * 64, cb + h * 64 + 64), t0, n)
                            return rp
                        tiles.append(chain(src, 64, n, allones[0:64, 0:64], 1.0 / 64, pvc(gname, 0, 64), dst[0:64, h, t0:t0 + n], None))
            run_pipe(tiles)
            proj_v(512, [[0], [1], [2], [3]])
            run_attention(2, None, lambda h: h, 96, 64 ** -0.5, na=True)
            ck('na')

            dma('pool', wA[:, :, 0:512], win_d[l, :, 1952:2464].rearrange("(a p) n -> p a n", p=128))
            for g in range(2):
                for r in range(2):
                    c0_ = 512 + (2 * g + r) * 64
                    dma('pool', wA[:, :, c0_:c0_ + 64],
                        win_d[l, :, 1952 + 256 + g * 64:1952 + 256 + (g + 1) * 64].rearrange("(a p) n -> p a n", p=128))
            dma('sp', ropeT, tabs_d[2])
            tiles = []
            for tbi, (t0, n) in enumerate(TBS):
                lat = tbi < 4
                rope = (Rgqa, ropeT[:, 0, t0:t0 + n], ropeT[:, 1, t0:t0 + n]) if lat else None
                for (cb, gname, dst) in ((0, 'gq_g', QT), (512, 'gk_g', KT)):
                    for j in range(2):
                        def src(j=j, cb=cb, t0=t0, n=n):
                            rp = raw_ring.next()[:, 0:n]
                            proj_fm(rp, (cb + j * 128, cb + (j + 1) * 128), t0, n)
                            return rp
                        tiles.append(chain(src, 128, n, blk64, 1.0 / 64, pvc(gname), dst[:, j, t0:t0 + n], rope))
            run_pipe(tiles)
            ck('gqap')
            proj_v(384, [[0, 1], [2, 3]])
            ck('gqav')
            for j in range(2):
                for tbi, (t0, n) in enumerate(q_tbs):
                    kts = list(range(NKT)) if tbi < 4 else [16, 17]
                    nk = len(kts)
                    Ops = [do_ring.next(), do_ring.next()]
                    Sps = {}

                    def qk2(i, j=j, t0=t0, n=n, kts=kts):
                        kt = kts[i]
                        Sps[i] = []
                        for r in range(2):
                            sp_ = ds_ring.next()[:, 0:n]
                            mm(sp_, KT[64 * r:64 * r + 64, j, kt * 128:(kt + 1) * 128], QT[64 * r:64 * r + 64, j, t0:t0 + n])
                            Sps[i].append(sp_)

                    qk2(0)
                    dscratch[0] = Ops[0]
                    flush_pending()
                    for i, kt in enumerate(kts):
                        if i + 1 < nk:
                            qk2(i + 1)
                        sp2 = Sps.pop(i)
                        for r in range(2):
                            h = 2 * j + r
                            P = Pr.next()[:, 0:n]
                            act(P, sp2[r], AF.Exp, scale=float(64 ** -0.5))
                            mm(Ops[r][0:(128 if r else 65), 0:n], v_slot(kt, h), P, start=(i == 0), stop=(i == nk - 1))

                    def fin(Ops=Ops, j=j, n=n, t0=t0):
                        scr_ring = Ring([dscratch[0]])
                        for r in range(2):
                            normalize(Ops[r], r, n, mixT[:, 6 + j, t0:t0 + n], bcr=scr_ring)
                    pending.append(fin)
                dscratch[0] = do_ring.next()
                flush_pending()
            ck('gqa')

            def load_group(gi):
                g0_, gn_ = FFG[gi]
                wn_ = gn_ * 128
                wb = wup_bufs[gi % 2]
                dma('pool', wb[:, :, 0, 0:wn_], wup_d[l, :, g0_ * 128:g0_ * 128 + wn_].rearrange("(a p) n -> p a n", p=128))
                dma('pool', wb[:, :, 1, 0:wn_], wup_d[l, :, DFF + g0_ * 128:DFF + g0_ * 128 + wn_].rearrange("(a p) n -> p a n", p=128))
                dma('pool', wdn_bufs[gi % 2][:, 0:gn_, :], wdn_d[l, g0_ * 128:(g0_ + gn_) * 128, :].rearrange("(a p) n -> p a n", p=128))

            load_group(0)

            dma('pool', wout_sb, wout_d[l].rearrange("(a p) n -> p a n", p=128))
            all_ps = Ring([PB(i) for i in range(7)])
            ss_ring = Ring([PB(5), PB(6)])
            op_ring = Ring([PB(i) for i in range(5)])

            def outproj_compute(tbi, t0, n, xb):
                if tbi == 4 and not with_ctx:
                    return
                w = 0 if tbi < 4 else 1
                for j in range(8):
                    pb = op_ring.next()[:, 0:n]
                    for c in range(8):
                        mm(pb, wout_sb[:, c, j * 128:(j + 1) * 128], mixT[:, c, t0:t0 + n], start=(c == 0), stop=(c == 7))
                    stt(xb[:, j, 0:n], pb, modT[:, l, 2 * 8 + j, w:w + 1], xb[:, j, 0:n], ALU.mult, ALU.add)
                dma('act', xT_d[:, :, t0:t0 + n], xb[:, :, 0:n], writes=xkeys(tbi))

            modulate(l, 1, outproj_compute)

            ck('outproj')
            f_tbs = TBS if with_ctx else TBS[:4]
            for ub in ubuf + ubuf2:
                mset(ub[:, 0:1], 0.0, eng='pool')
                mset(ub[:, 2049:2051], 0.0, eng='pool')
                mset(ub[:, 2307:2308], 0.0, eng='pool')
            up_ring = Ring([PB(0), PB(1), PB(2), PB(3)])
            dn_ring = Ring([PB(4), PB(5), PB(6)])
            cring = Ring(ctile)

            def ucol(t0):
                return (1 + t0) if t0 < TL else (2051 + t0 - TL)

            for gi, (g0, gn) in enumerate(FFG):
                wup_sb = wup_bufs[gi % 2]
                wdn_sb = wdn_bufs[gi % 2]
                if gi + 1 < len(FFG):
                    load_group(gi + 1)
                for cl in range(gn):
                    c = g0 + cl
                    ub = ubuf if c % 2 == 0 else ubuf2
                    for (t0, n) in f_tbs:
                        u0 = ucol(t0)
                        for ag in range(2):
                            pb = up_ring.next()[:, 0:n]
                            for kc in range(8):
                                mm(pb, wup_sb[:, kc, ag, cl * 128:(cl + 1) * 128], hT[:, kc, t0:t0 + n], start=(kc == 0), stop=(kc == 7))
                            act(ub[ag][:, u0:u0 + n], pb, AF.Copy)
                    for (t0, n) in f_tbs:
                        u0 = ucol(t0)
                        outs = []
                        for ag in range(2):
                            col = c if ag == 0 else NCH + c
                            ct = cring.next()[:, 0:n]
                            act(ct, ub[ag][:, u0:u0 + n], AF.Identity, scale=pvc('cw1', col), bias=pvc('cb', col))
                            stt(ct, ub[ag][:, u0 - 1:u0 - 1 + n], pvc('cw0', col), ct, ALU.mult, ALU.add)
                            stt(ct, ub[ag][:, u0 + 1:u0 + 1 + n], pvc('cw2', col), ct, ALU.mult, ALU.add)
                            outs.append(ct)
                        act(outs[1], outs[1], AF.Silu)
                        tt(actT[:, cl, t0:t0 + n], outs[1], outs[0], ALU.mult, eng='pool')
                dtiles = [(tbi, t0, n, j) for tbi, (t0, n) in enumerate(f_tbs) for j in range(8)]
                LA = 4
                xts = {}

                def xload(i):
                    tbi_, t0_, n_, j_ = dtiles[i]
                    xt_ = xring.next()[:, 0:n_]
                    dma('sp', xt_, xT_d[:, j_, t0_:t0_ + n_], reads=[xkey(tbi_), "xf:%d:%d" % (tbi_, j_)])
                    xts[i] = xt_

                for i in range(min(LA, len(dtiles))):
                    xload(i)
                for i, (tbi, t0, n, j) in enumerate(dtiles):
                    if i + LA < len(dtiles):
                        xload(i + LA)
                    w = 0 if tbi < 4 else 1
                    pb = dn_ring.next()[:, 0:n]
                    for cl in range(gn):
                        mm(pb, wdn_sb[:, cl, j * 128:(j + 1) * 128], actT[:, cl, t0:t0 + n], start=(cl == 0), stop=(cl == gn - 1))
                    xt = xts.pop(i)
                    stt(xt, pb, modT[:, l, 5 * 8 + j, w:w + 1], xt, ALU.mult, ALU.add)
                    dma('act', xT_d[:, j, t0:t0 + n], xt, writes=["xf:%d:%d" % (tbi, j)])

    except StopBuild:
        pass
    import os as _os
    for tbi, (t0, n) in enumerate(TBS[:4]):
        dma('sp', xblk[:, :, 0:n], xT_d[:, :, t0:t0 + n], reads=xkeys(tbi))
        if _os.environ.get('SKIP_FINAL'):
            dma('sp', y_d[t0:t0 + n, :].rearrange("(a p) d -> p a d", p=128), xblk[:, 0:4, :].rearrange("p a (b c) -> p (a b) c", b=1)[:, :, :].rearrange("p a c -> p a c") if False else xld[:, 0:4, :])
            continue
        for a in range(4):
            for half in range(2):
                pb = PB((a * 2 + half) % int(_os.environ.get("NB", "6")))
                for jj in range(4):
                    j = half * 4 + jj
                    tr(pb[:, jj * 128:(jj + 1) * 128], xblk[:, j, a * 128:(a + 1) * 128], ident)
                if half == 0:
                    act(xld[:, a, 0:512], pb, AF.Copy)
                else:
                    cp(xld[:, a, 512:1024], pb)
        dma('sp', y_d[t0:t0 + n, :].rearrange("(a p) d -> p a d", p=128), xld[:, 0:4, :])
    if dbg:
        for tbi, (t0, n) in enumerate(TBS):
            dma('sp', xblk[:, :, 0:n], xT_d[:, :, t0:t0 + n], reads=xkeys(tbi))
            dma('sp', dbg_d[:, :, t0:t0 + n], xblk[:, :, 0:n])
    S.emit()
    return nc, S


def _rope_tab(rot):
    nf = rot // 4
    inv = np.power(np.float32(10000.0), -np.arange(nf, dtype=np.float32) / np.float32(nf)).astype(np.float32)
    t = np.arange(TL)
    row = (t // 64).astype(np.float32)
    col = (t % 64).astype(np.float32)
    ar = row[:, None] * inv
    ac = col[:, None] * inv
    ang = np.concatenate([ar, ar, ac, ac], axis=-1).astype(np.float32)
    return np.cos(ang).T.astype(np.float32), np.sin(ang).T.astype(np.float32)


def _constants():
    cf = np.zeros((128, 256), np.float32)
    cf[:, 0:128] = np.eye(128, dtype=np.float32)
    cf[:, 128:256] = 1.0
    cb = np.zeros((128, 6, 128), np.float32)
    cb[:, 0, :] = 1.0
    for b in range(4):
        cb[b * 32:(b + 1) * 32, 1, b * 32:(b + 1) * 32] = 1.0
    for b in range(2):
        cb[b * 64:(b + 1) * 64, 2, b * 64:(b + 1) * 64] = 1.0

    def add_rot(Rm, base, n):
        for i in range(n):
            Rm[base + i + n, base + i] = -1.0
            Rm[base + i, base + i + n] = 1.0
            Rm[base + 3 * n + i, base + 2 * n + i] = -1.0
            Rm[base + 2 * n + i, base + 3 * n + i] = 1.0
    add_rot(cb[:, 3, :], 64, 8)
    add_rot(cb[:, 4, :], 0, 8)
    add_rot(cb[:, 4, :], 32, 8)
    add_rot(cb[:, 5, :], 0, 16)
    add_rot(cb[:, 5, :], 64, 16)
    aug = np.zeros((2, 32, T), np.float32)
    for q in range(TL):
        qr = q // 64
        rs = min(max(qr - 4, 0), 24)
        aug[0, :, q] = -BIG
        aug[0, rs:rs + 8, q] = 0.0
        aug[1, qr, q] = 1.0
    tabs = np.zeros((3, 128, 2, TL), np.float32)
    c32, s32 = _rope_tab(32)
    c64, s64 = _rope_tab(64)
    tabs[0, 0:64, 0, :] = 1.0
    tabs[0, 64:96, 0, :] = c32
    tabs[0, 64:96, 1, :] = s32
    tabs[1, 0:32, 0, :] = c32
    tabs[1, 32:64, 0, :] = c32
    tabs[1, 0:32, 1, :] = s32
    tabs[1, 32:64, 1, :] = s32
    tabs[2, 0:64, 0, :] = c64
    tabs[2, 0:64, 1, :] = s64
    tabs[2, 64:128, 0, :] = c64
    tabs[2, 64:128, 1, :] = s64
    return cf, cb.reshape(128, 768), aug, tabs


def _na_index():
    idx = np.full((128, 22, 64), 15 * 31, np.int64)
    for p in range(128):
        half, kc = p // 64, p % 64
        for pos in range(22):
            i = pos - 3 - half
            if i < 0 or i > 14:
                continue
            for qc in range(64):
                ws = min(max(qc - 8, 0), 48)
                if ws <= kc < ws + 16:
                    co = min(max(kc - qc + 15, 0), 30)
                    idx[p, pos, qc] = (14 - i) * 31 + co
    return idx.reshape(128, NAW)


def _tile_rows(v, reps, rows=128):
    out = np.zeros((rows,), np.float32)
    t = np.tile(np.asarray(v, np.float32), reps)
    out[:t.shape[0]] = t
    return out


def _prep_shared(inp):
    cf, cb, aug, tabs = _constants()
    pvA = np.zeros((128, L, NV), np.float32)

    def put(name, l, arr2d):
        c0, w = PV[name]
        pvA[:, l, c0:c0 + w] = arr2d

    for l in range(L):
        put('g_mix', l, inp['g_mix'][l].reshape(8, 128).T)
        put('g_ffn', l, inp['g_ffn'][l].reshape(8, 128).T)
        put('qa_g', l, inp['mla_q_a_g'][l].reshape(2, 128).T)
        put('kva_g', l, inp['mla_kv_a_g'][l].reshape(1, 128).T)
        put('mq_g', l, _tile_rows(inp['mla_q_g'][l], 1)[:, None])
        put('mk_g', l, _tile_rows(inp['mla_k_g'][l], 1)[:, None])
        put('dq_g', l, _tile_rows(inp['diff_q_g'][l], 4)[:, None])
        put('dk_g', l, _tile_rows(inp['diff_k_g'][l], 4)[:, None])
        put('dsub_g', l, _tile_rows(inp['diff_subln_g'][l], 2)[:, None])
        put('nq_g', l, _tile_rows(inp['na_q_g'][l], 2)[:, None])
        put('nk_g', l, _tile_rows(inp['na_k_g'][l], 2)[:, None])
        put('gq_g', l, _tile_rows(inp['gqa_q_g'][l], 2)[:, None])
        put('gk_g', l, _tile_rows(inp['gqa_k_g'][l], 2)[:, None])
        for i in range(3):
            put('cw%d' % i, l, inp['conv_w'][l, i].reshape(44, 128).T)
        put('cb', l, inp['conv_b'][l].reshape(44, 128).T)
        lv = np.concatenate([inp['diff_lq1'][l], inp['diff_lk1'][l], inp['diff_lq2'][l], inp['diff_lk2'][l]]).astype(np.float32)
        put('lvec', l, np.broadcast_to(lv[None, :], (128, 128)))
    idx = _na_index()
    nab = np.zeros((L, 4, 128, NAW), np.float32)
    for l in range(L):
        for h in range(4):
            src = np.concatenate([np.asarray(inp['na_rpb'][l, h], np.float32).ravel(), np.array([-10000.0], np.float32)])
            nab[l, h] = src[idx]
    f = lambda a: np.ascontiguousarray(np.asarray(a, np.float32))
    return {
        "w_mod": f(inp['w_mod']), "b_mod": f(inp['b_mod']), "w_in": f(inp['w_in']), "w_out": f(inp['w_out']),
        "w_uq": f(inp['mla_w_uq']), "w_ukv": f(inp['mla_w_ukv']), "w_up": f(inp['w_up']), "w_down": f(inp['w_down']),
        "pv": np.ascontiguousarray(pvA.reshape(128, L * NV)), "cf32": cf, "cbf": np.ascontiguousarray(cb),
        "aug": aug, "tabs": tabs, "nab": nab,
    }


_CACHE = {}


def kernel(**inp):
    n_layers = inp.pop('_n_layers', L)
    dbg = inp.pop('_dbg', False)
    stop = inp.pop('_stop', None)
    ncores = inp.pop('_ncores', 8)
    key = (n_layers, dbg, stop)
    if key not in _CACHE:
        _CACHE[key] = build_program(n_layers, dbg, stop)[0]
    nc = _CACHE[key]
    shared = _prep_shared(inp)
    x = np.asarray(inp['x'], np.float32)
    ctx = np.asarray(inp['ctx'], np.float32)
    c = np.asarray(inp['c'], np.float32)
    cc = np.asarray(inp['c_ctx'], np.float32)
    in_maps = []
    for b in range(ncores):
        m = dict(shared)
        m["xc"] = np.ascontiguousarray(np.concatenate([x[b], ctx[b]], axis=0))
        cT = np.zeros((128, 8, 2), np.float32)
        cT[:, :, 0] = c[b].reshape(8, 128).T
        cT[:, :, 1] = cc.reshape(8, 128).T
        m["cT"] = np.ascontiguousarray(cT.reshape(128, 16))
        in_maps.append(m)
    res = run_bass_kernel_spmd(nc, in_maps, core_ids=list(range(ncores)))
    out = np.stack([np.asarray(r["y"], np.float32) for r in res.results], axis=0)
    if dbg:
        kernel.dbg = [np.asarray(r["dbg"], np.float32) for r in res.results]
    return out
```

```python
import math
import numpy as np
import concourse.bass as bass
import concourse.mybir as mybir
from concourse.bass_utils import run_bass_kernel_spmd

F32 = mybir.dt.float32
BF16 = mybir.dt.bfloat16
AF = mybir.ActivationFunctionType
ALU = mybir.AluOpType
AX = mybir.AxisListType

ENGS = ['pe', 'act', 'dve', 'pool', 'sp']
CELL = 256
_ESZ = {}


def esz(dt):
    if dt not in _ESZ:
        _ESZ[dt] = mybir.dt.size(dt)
    return _ESZ[dt]


def ap_cells(ap):
    space = str(ap.space)
    sp = 0 if space == 'SB' else 1
    dims = ap.ap
    pstep, pcount = dims[0]
    e = esz(ap.dtype)
    off = ap.offset
    p0 = off // pstep
    foff = off % pstep
    ranges = [(foff, foff + 1)]
    for (st, cnt) in dims[1:]:
        if cnt <= 1:
            continue
        if len(ranges) * cnt <= 512 and abs(st) * e >= CELL:
            ranges = [(lo + i * st, hi + i * st) for (lo, hi) in ranges for i in range(cnt)]
        else:
            ext = (cnt - 1) * st
            if ext >= 0:
                ranges = [(lo, hi + ext) for (lo, hi) in ranges]
            else:
                ranges = [(lo + ext, hi) for (lo, hi) in ranges]
    cs = set()
    for (lo, hi) in ranges:
        c0 = (lo * e) // CELL
        c1 = (hi * e - 1) // CELL
        for c in range(c0, c1 + 1):
            cs.add(c)
    if sp == 1:
        return sorted(set(4 * 4096 + (c * CELL) // 2048 for c in cs))
    q0 = p0 // 32
    q1 = (p0 + pcount - 1) // 32
    out = []
    for q in range(q0, q1 + 1):
        base = (sp * 4 + q) * 4096
        for c in cs:
            out.append(base + c)
    return out


class Sched:
    def __init__(self, nc, n_lanes=48, same_eng_sync=True):
        self.nc = nc
        self.ops = {e: [] for e in ENGS}
        self.cw = {}
        self.cr = {}
        self.known = {e: {} for e in ENGS}
        self.snap = {}
        self.n_lanes = n_lanes
        self.lane_count = [0] * n_lanes
        self.next_lane = 0
        self.same_eng_sync = same_eng_sync
        self.eng_sem = {e: nc.alloc_semaphore("sem_" + e) for e in ENGS}
        self.lane_sem = [nc.alloc_semaphore("lane%d" % i) for i in range(n_lanes)]

    def _cells(self, items):
        cs = []
        for it in items:
            if it is None:
                continue
            if isinstance(it, (str, tuple)):
                cs.append(it)
            else:
                cs.extend(ap_cells(it))
        return cs

    def add(self, eng, fn, reads=(), writes=(), dma=False):
        ops = self.ops[eng]
        idx = len(ops)
        rc = self._cells(reads)
        wc = self._cells(writes)
        need = {}

        def want(tok, war=False):
            key, seq = tok
            if key == eng:
                if eng == 'pe' or war or not self.same_eng_sync:
                    return
            if need.get(key, -1) < seq:
                need[key] = seq

        for c in rc:
            t = self.cw.get(c)
            if t is not None:
                want(t)
        for c in wc:
            t = self.cw.get(c)
            if t is not None:
                want(t)
            rs = self.cr.get(c)
            if rs:
                for k, s in rs.items():
                    want((k, s), war=True)
        if dma:
            lane = self.next_lane
            self.next_lane = (lane + 1) % self.n_lanes
            cnt = self.lane_count[lane] + 1
            self.lane_count[lane] = cnt
            tok = (('L', lane), cnt)
            if cnt > 1:
                want((('L', lane), cnt - 1))
        else:
            tok = (eng, idx)
        kn = self.known[eng]
        waits = []
        for key, seq in need.items():
            if kn.get(key, -1) >= seq:
                continue
            waits.append((key, seq))
            kn[key] = seq
            if isinstance(key, str):
                self.ops[key][seq]['signal'] = True
                sn = self.snap.get((key, seq))
                if sn:
                    for k2, s2 in sn.items():
                        if kn.get(k2, -1) < s2:
                            kn[k2] = s2
        if not dma:
            self.snap[(eng, idx)] = dict(kn)
        ops.append(dict(fn=fn, waits=waits, tok=tok, dma=dma, signal=False))
        for c in wc:
            self.cw[c] = tok
            self.cr[c] = {}
        for c in rc:
            d = self.cr.get(c)
            if d is None:
                d = {}
                self.cr[c] = d
            if d.get(tok[0], -1) < tok[1]:
                d[tok[0]] = tok[1]
        return tok

    def emit(self):
        nc = self.nc
        for e in ENGS:
            c = 0
            for op in self.ops[e]:
                if op['signal'] and not op['dma']:
                    c += 1
                    op['sigval'] = c
        emap = {'pe': 'tensor', 'act': 'scalar', 'dve': 'vector', 'pool': 'gpsimd', 'sp': 'sync'}

        def run(e, E):
            for op in self.ops[e]:
                for key, seq in op['waits']:
                    if isinstance(key, str):
                        E.wait_ge(self.eng_sem[key], self.ops[key][seq]['sigval'])
                    else:
                        E.wait_ge(self.lane_sem[key[1]], 16 * seq)
                inst = op['fn'](E)
                if op['dma']:
                    inst.then_inc(self.lane_sem[op['tok'][0][1]], 16)
                elif op['signal']:
                    inst.then_inc(self.eng_sem[e], 1)
            if e == 'sp':
                for i, cnt in enumerate(self.lane_count):
                    if cnt:
                        E.wait_ge(self.lane_sem[i], 16 * cnt)

        with nc.Block() as block:
            for e in ENGS:
                getattr(block, emap[e])(lambda E, e=e: run(e, E))

    def stats(self):
        return {e: (len(self.ops[e]), sum(len(o['waits']) for o in self.ops[e])) for e in ENGS}


class Ring:
    def __init__(self, items):
        self.items = list(items)
        self.i = 0

    def next(self):
        r = self.items[self.i]
        self.i = (self.i + 1) % len(self.items)
        return r


D = 1024
L = 4
TL = 2048
TC = 256
T = TL + TC
NKT = T // 128
TBS = [(0, 512), (512, 512), (1024, 512), (1536, 512), (2048, 256)]
IN_COLS = 2464
DFF = 2816
NCH = DFF // 128
EPS = 1e-6
BIG = 30000.0
NA_KT = {0: range(0, 6), 1: range(2, 10), 2: range(6, 14), 3: range(10, 16)}
NAW = 22 * 64
FFG = [(0, 4), (4, 4), (8, 4), (12, 4), (16, 4), (20, 2)]

PV = {}
_c = 0
for _n, _w in [('g_mix', 8), ('g_ffn', 8), ('qa_g', 2), ('kva_g', 1), ('mq_g', 1), ('mk_g', 1),
               ('dq_g', 1), ('dk_g', 1), ('dsub_g', 1), ('nq_g', 1), ('nk_g', 1), ('gq_g', 1), ('gk_g', 1),
               ('cw0', 44), ('cw1', 44), ('cw2', 44), ('cb', 44), ('lvec', 128)]:
    PV[_n] = (_c, _w)
    _c += _w
NV = _c


def lambda_init(l):
    return 0.8 - 0.6 * math.exp(-0.3 * l)


class StopBuild(Exception):
    pass


def build_program(n_layers=L, dbg=False, stop_after=None):
    nc = bass.Bass("TRN2", target_bir_lowering=False)

    def ck(name):
        if stop_after == name:
            raise StopBuild()

    def din(name, shape):
        return nc.dram_tensor(name, list(shape), F32, kind="ExternalInput").ap()

    xc_d = din("xc", [T, D])
    cT_d = din("cT", [128, 16])
    wmod_d = din("w_mod", [L, D, 6 * D])
    bmod_d = din("b_mod", [L, 6 * D])
    win_d = din("w_in", [L, D, IN_COLS])
    wout_d = din("w_out", [L, D, D])
    wuq_d = din("w_uq", [L, 256, 384])
    wukv_d = din("w_ukv", [L, 128, 512])
    wup_d = din("w_up", [L, D, 2 * DFF])
    wdn_d = din("w_down", [L, DFF, D])
    pv_d = din("pv", [128, L * NV])
    cf_d = din("cf32", [128, 256])
    cb_d = din("cbf", [128, 6 * 128])
    aug_d = din("aug", [2, 32, T])
    tabs_d = din("tabs", [3, 128, 2, TL])
    nab_d = din("nab", [L, 4, 128, NAW])
    y_d = nc.dram_tensor("y", [TL, D], F32, kind="ExternalOutput").ap()
    xT_d = nc.dram_tensor("xT_scr", [128, 8, T], F32).ap()
    if dbg:
        dbg_d = nc.dram_tensor("dbg", [128, 8, T], F32, kind="ExternalOutput").ap()

    ARENA_F32 = 53000
    arena = nc.alloc_sbuf_tensor("arena", [128, ARENA_F32], F32)
    psum = nc.alloc_psum_tensor("psum", [128, 4096], F32)
    S = Sched(nc)

    pos = [0]

    def alloc_b(nbytes):
        a = pos[0]
        n = (nbytes + 255) // 256 * 256
        pos[0] += n
        assert pos[0] <= ARENA_F32 * 4, pos[0]
        return a

    def f32v(boff, n):
        return arena[:, boff // 4: boff // 4 + n]

    def bf16v(boff, n):
        return arena[:, boff // 4: boff // 4 + (n + 1) // 2].bitcast(BF16)

    def PB(i):
        return psum[:, i * 512:(i + 1) * 512]

    ident = f32v(alloc_b(512), 128)
    ones_f = f32v(alloc_b(512), 128)
    cbf = bf16v(alloc_b(6 * 256), 6 * 128).rearrange("p (a b) -> p a b", a=6)
    allones, blk32, blk64, Rmla, Rdiff, Rgqa = [cbf[:, i, :] for i in range(6)]
    pv = f32v(alloc_b(L * NV * 4), L * NV).rearrange("p (l n) -> p l n", l=L)
    modT = f32v(alloc_b(L * 96 * 4), L * 96).rearrange("p (l s w) -> p l s w", l=L, s=48)
    cT = f32v(alloc_b(64), 16)
    scT_f = f32v(alloc_b(64), 16)
    scT = bf16v(alloc_b(32), 16).rearrange("p (a b) -> p a b", a=8)
    Gv = f32v(alloc_b(4 * 16 * 4), 64).rearrange("p (a j w) -> p a j w", a=2, j=8)
    misc = f32v(alloc_b(64 * 4), 64)
    hT = bf16v(alloc_b(8 * T * 2), 8 * T).rearrange("p (a t) -> p a t", a=8)
    mix_off = alloc_b(8 * T * 2)
    mixT = bf16v(mix_off, 8 * T).rearrange("p (a t) -> p a t", a=8)
    qkv_off = alloc_b(3 * 4 * T * 2)
    QT = bf16v(qkv_off, 4 * T).rearrange("p (a t) -> p a t", a=4)
    KT = bf16v(qkv_off + 4 * T * 2, 4 * T).rearrange("p (a t) -> p a t", a=4)
    VA = bf16v(qkv_off + 8 * T * 2, NKT * 4 * 128).rearrange("p (k h c) -> p k h c", k=NKT, h=4)
    xblk = f32v(qkv_off + 16384, 8 * 512).rearrange("p (a n) -> p a n", a=8)
    xblk2 = f32v(qkv_off, 8 * 512).rearrange("p (a n) -> p a n", a=8)
    xld = f32v(qkv_off + 32768, 4 * 1024).rearrange("p (a n) -> p a n", a=4)
    GN = 4
    fo = mix_off
    actT = bf16v(fo, GN * T).rearrange("p (a t) -> p a t", a=GN); fo += GN * T * 2
    UW = T + 4
    UWP = (UW * 4 + 255) // 256 * 256
    ubuf = [f32v(fo + i * UWP, UW) for i in range(2)]; fo += 2 * UWP
    assert fo <= qkv_off + 512, (fo, qkv_off)
    WUPB = 8 * 2 * GN * 128 * 2
    wup_bufs = [bf16v(qkv_off + 32768, 8 * 2 * GN * 128).rearrange("p (k g n) -> p k g n", k=8, g=2),
                bf16v(qkv_off + 512, 8 * 2 * GN * 128).rearrange("p (k g n) -> p k g n", k=8, g=2)]
    assert qkv_off + 32768 + WUPB <= qkv_off + 3 * 4 * T * 2 and 512 + WUPB <= 32768
    fo = qkv_off
    assert fo <= qkv_off + 3 * 4 * T * 2, (fo, qkv_off + 3 * 4 * T * 2)
    tab_off = alloc_b(2 * TL * 4)
    ropeT = f32v(tab_off, 2 * TL).rearrange("p (a t) -> p a t", a=2)
    natab = bf16v(tab_off, 4 * NAW).rearrange("p (h n) -> p h n", h=4)
    wA = bf16v(alloc_b(8 * 768 * 2), 8 * 768).rearrange("p (a n) -> p a n", a=8)
    wout_sb = bf16v(tab_off, 8 * 1024).rearrange("p (a n) -> p a n", a=8)
    ubuf2 = [f32v(tab_off + i * UWP, UW) for i in range(2)]
    assert 2 * UWP <= 2 * TL * 4 + 8 * 768 * 2
    wuq_sb = bf16v(alloc_b(2 * 384 * 2), 2 * 384).rearrange("p (a n) -> p a n", a=2)
    wukv_sb = bf16v(alloc_b(512 * 2), 512)
    ckvnT = bf16v(alloc_b(T * 2), T)
    cqn = bf16v(alloc_b(2 * 512 * 2), 2 * 512).rearrange("p (a n) -> p a n", a=2)
    r1_off = tab_off + 2 * UWP
    wdn_bufs = [bf16v(r1_off + i * GN * 1024 * 2, GN * 1024).rearrange("p (a n) -> p a n", a=GN) for i in range(2)]
    tmp_off = alloc_b(24 * 1024)
    assert r1_off + 2 * GN * 1024 * 2 <= tmp_off, (r1_off, tmp_off)
    krs_buf = f32v(alloc_b(2048), 512)
    sq_extra = alloc_b(2048)

    def tf(i):
        return f32v(tmp_off + i * 2048, 512)

    def tb16(i, half=0):
        return bf16v(tmp_off + i * 2048 + half * 1024, 512)

    kraw = Ring([tf(0), tf(1)])
    sqr = Ring([tb16(2, 0), tb16(2, 1), bf16v(sq_extra, 512), bf16v(sq_extra + 1024, 512)])
    lnr = Ring([tf(3), tf(4)])
    rsr = Ring([tf(5), tf(6)])
    qnr = Ring([tb16(7, 0), tb16(7, 1)])
    t1r = Ring([tf(8), tf(9)])
    t2r = Ring([tf(10), tf(11)])
    Pr = Ring([tb16(0, 0), tb16(0, 1), tb16(1, 0), tb16(1, 1)])
    P2r = Ring([tb16(2, 0), tb16(2, 1)])
    Osb = Ring([tf(3), tf(4)])
    zrow = Ring([tf(5), tf(6)])
    ABo = [tf(7), tf(8), tf(9)]
    xring = Ring([tf(6), tf(7), tf(8), tf(9), tf(10), tf(11)])
    ctile = [tf(0), tf(1), tf(2), tf(3), tf(4), tf(5)]
    nastg = f32v(tmp_off, NAW)
    wmr = Ring([bf16v(tmp_off + i * 4096, 2048) for i in range(3)])
    m_sb = f32v(qkv_off, 6 * D)
    bm_sb = f32v(qkv_off + 24576, 6 * D)

    def mm(out, lhsT, rhs, start=True, stop=True):
        S.add('pe', lambda E: E.matmul(out, lhsT, rhs, start=start, stop=stop), reads=[lhsT, rhs], writes=[out])

    def tr(out, in_, idn):
        S.add('pe', lambda E: E.transpose(out, in_, idn), reads=[in_, idn], writes=[out])

    def act(out, in_, func, scale=None, bias=None, eng='act'):
        kw = {}
        rd = [in_]
        if scale is not None:
            kw['scale'] = scale
            if not isinstance(scale, float):
                rd.append(scale)
        if bias is not None:
            kw['bias'] = bias
            if not isinstance(bias, float):
                rd.append(bias)
        S.add('act', lambda E: E.activation(out, in_, func, **kw), reads=rd, writes=[out])

    def stt(out, in0, scalar, in1, op0, op1):
        rd = [in0, in1] + ([] if isinstance(scalar, float) else [scalar])
        S.add('dve', lambda E: E.scalar_tensor_tensor(out=out, in0=in0, scalar=scalar, in1=in1, op0=op0, op1=op1),
              reads=rd, writes=[out])

    def tt(out, in0, in1, op, eng='dve'):
        S.add(eng, lambda E: E.tensor_tensor(out=out, in0=in0, in1=in1, op=op), reads=[in0, in1], writes=[out])

    def ts(out, in0, s1, s2, op0, op1=None, eng='dve'):
        rd = [in0] + [s for s in (s1, s2) if s is not None and not isinstance(s, float)]
        if op1 is None:
            S.add(eng, lambda E: E.tensor_scalar(out=out, in0=in0, scalar1=s1, scalar2=None, op0=op0), reads=rd, writes=[out])
        else:
            S.add(eng, lambda E: E.tensor_scalar(out=out, in0=in0, scalar1=s1, scalar2=s2, op0=op0, op1=op1), reads=rd, writes=[out])

    def cp(out, in_, eng='dve'):
        S.add(eng, lambda E: E.tensor_copy(out=out, in_=in_), reads=[in_], writes=[out])

    def vcopy(out, in_, bank, use_act):
        if use_act:
            S.add('act', lambda E: E.activation(out, in_, AF.Copy), reads=[bank], writes=[out])
        else:
            S.add('dve', lambda E: E.tensor_copy(out=out, in_=in_), reads=[bank], writes=[out])

    def mset(ap, val, eng='dve'):
        S.add(eng, lambda E: E.memset(ap, val), writes=[ap])

    def dma(q, out, in_, reads=None, writes=None):
        r = [in_] if reads is None else reads
        w = [out] if writes is None else writes
        r = [a for a in r if isinstance(a, (str, tuple)) or str(a.space) != 'DRAM']
        w = [a for a in w if isinstance(a, (str, tuple)) or str(a.space) != 'DRAM']
        S.add(q, lambda E: E.dma_start(out=out, in_=in_), reads=r, writes=w, dma=True)

    def xkey(tbi):
        return "x:%d" % tbi

    def xkeys(tbi):
        return ["x:%d" % tbi] + ["xf:%d:%d" % (tbi, j) for j in range(8)]

    try:
        import os as _os
        if _os.environ.get('SKIP_CONST'):
            raise StopBuild()
        dma('sp', ident, cf_d[:, 0:128])
        dma('sp', ones_f, cf_d[:, 128:256])
        dma('pool', cbf, cb_d.rearrange("p (a b) -> p a b", a=6))
        dma('sp', pv, pv_d.rearrange("p (l n) -> p l n", l=L))
        dma('sp', cT, cT_d)
        act(scT_f, cT, AF.Silu)
        cp(scT, scT_f.rearrange("p (a b) -> p a b", a=8))

        ck('c0')
        for tbi, (t0, n) in enumerate(TBS):
            ntt = n // 128
            dma('sp', xld[:, 0:ntt, :], xc_d[t0:t0 + n, :].rearrange("(a p) d -> p a d", p=128))
            for j in range(8):
                pb = PB(j % 2)
                for a in range(ntt):
                    tr(pb[:, a * 128:(a + 1) * 128], xld[:, a, j * 128:(j + 1) * 128], ident)
                if j % 2 == 0:
                    act(xblk[:, j, 0:n], pb[:, 0:n], AF.Copy)
                else:
                    cp(xblk[:, j, 0:n], pb[:, 0:n])
            dma('sp', xT_d[:, :, t0:t0 + n], xblk[:, :, 0:n], writes=xkeys(tbi))

        ck('xt')
        for l in range(n_layers):
            dma('sp', bm_sb[0:1, :], bmod_d[l:l + 1, :])
            dma('sp', bm_sb[1:2, :], bmod_d[l:l + 1, :])
            for cg in range(3):
                for kc in range(8):
                    piece = wmr.next()
                    dma('pool', piece, wmod_d[l, kc * 128:(kc + 1) * 128, cg * 2048:(cg + 1) * 2048])
                    for i in range(4):
                        mm(PB(i)[0:2, :], scT[:, kc, :], piece[:, i * 512:(i + 1) * 512], start=(kc == 0), stop=(kc == 7))
                for i in range(4):
                    c0 = cg * 2048 + i * 512
                    tt(m_sb[0:2, c0:c0 + 512], PB(i)[0:2, :], bm_sb[0:2, c0:c0 + 512], ALU.add)
            pt = PB(4)
            for s in range(48):
                tr(pt[:, 2 * s:2 * s + 2], m_sb[0:2, s * 128:(s + 1) * 128], ident[0:2, 0:2])
            cp(modT[:, l].rearrange("p s w -> p (s w)"), pt[:, 0:96])

        ck('mod')
        def rstd_from(src, R, N, ones_m, inv_d):
            sq = sqr.next()[0:R, 0:N]
            act(sq, src, AF.Square)
            ssp = ss_ring.next()[0:R, 0:N]
            mm(ssp, ones_m, sq)
            ln = lnr.next()[0:R, 0:N]
            act(ln, ssp, AF.Ln, scale=float(inv_d), bias=float(EPS))
            rs = rsr.next()[0:R, 0:N]
            act(rs, ln, AF.Exp, scale=-0.5)
            return rs

        def norm_rope(src, R, N, ones_m, inv_d, gain, out, rope=None):
            rs = rstd_from(src, R, N, ones_m, inv_d)
            if rope is None:
                stt(out, src, gain, rs, ALU.mult, ALU.mult)
                return
            Rm, cosT, sinT = rope
            qn = qnr.next()[0:R, 0:N]
            stt(qn, src, gain, rs, ALU.mult, ALU.mult)
            rp = rot_ring.next()[0:R, 0:N]
            mm(rp, Rm, qn)
            t1 = t1r.next()[0:R, 0:N]
            tt(t1, qn, cosT, ALU.mult, eng='pool')
            t2 = t2r.next()[0:R, 0:N]
            tt(t2, rp, sinT, ALU.mult)
            tt(out, t1, t2, ALU.add, eng='pool')

        def chain(src_fn, R, N, ones_m, inv_d, gain, out, rope=None):
            st = {}

            def A():
                st['src'] = src_fn()
                sq = sqr.next()[0:R, 0:N]
                act(sq, st['src'], AF.Square)
                st['sq'] = sq

            def B():
                ssp = ss_ring.next()[0:R, 0:N]
                mm(ssp, ones_m, st['sq'])
                ln = lnr.next()[0:R, 0:N]
                act(ln, ssp, AF.Ln, scale=float(inv_d), bias=float(EPS))
                rs = rsr.next()[0:R, 0:N]
                act(rs, ln, AF.Exp, scale=-0.5)
                if rope is None:
                    stt(out, st['src'], gain, rs, ALU.mult, ALU.mult)
                else:
                    qn = qnr.next()[0:R, 0:N]
                    stt(qn, st['src'], gain, rs, ALU.mult, ALU.mult)
                    st['qn'] = qn

            def C():
                Rm, cosT, sinT = rope
                qn = st['qn']
                rp = rot_ring.next()[0:R, 0:N]
                mm(rp, Rm, qn)
                t1 = t1r.next()[0:R, 0:N]
                tt(t1, qn, cosT, ALU.mult, eng='pool')
                t2 = t2r.next()[0:R, 0:N]
                tt(t2, rp, sinT, ALU.mult)
                tt(out, t1, t2, ALU.add, eng='pool')

            return [A, B, C if rope is not None else None]

        def run_pipe(tiles):
            n = len(tiles)
            for step in range(n + 2):
                if step < n and tiles[step][0]:
                    tiles[step][0]()
                if 0 <= step - 1 < n and tiles[step - 1][1]:
                    tiles[step - 1][1]()
                if 0 <= step - 2 < n and tiles[step - 2][2]:
                    tiles[step - 2][2]()

        def modulate(l, which, compute=None):
            xbs = [xblk, xblk2]

            def load(tbi):
                t0_, n_ = TBS[tbi]
                dma('sp', xbs[tbi % 2][:, :, 0:n_], xT_d[:, :, t0_:t0_ + n_], reads=xkeys(tbi))

            load(0)
            for tbi, (t0, n) in enumerate(TBS):
                w = 0 if tbi < 4 else 1
                xb = xbs[tbi % 2]
                if tbi + 1 < len(TBS):
                    load(tbi + 1)
                if compute is not None:
                    compute(tbi, t0, n, xb)
                ssp = ss_ring.next()[:, 0:n]
                for j in range(8):
                    sq = sqr.next()[:, 0:n]
                    act(sq, xb[:, j, 0:n], AF.Square)
                    mm(ssp, allones, sq, start=(j == 0), stop=(j == 7))
                ln = lnr.next()[:, 0:n]
                act(ln, ssp, AF.Ln, scale=1.0 / D, bias=float(EPS))
                rs = rsr.next()[:, 0:n]
                act(rs, ln, AF.Exp, scale=-0.5)
                for j in range(8):
                    t1 = t1r.next()[:, 0:n]
                    stt(t1, xb[:, j, 0:n], Gv[:, which, j, w:w + 1], rs, ALU.mult, ALU.mult)
                    shift = modT[:, l, (3 * which) * 8 + j, w:w + 1]
                    act(hT[:, j, t0:t0 + n], t1, AF.Identity, bias=shift)


        pending = []

        def flush_pending():
            while pending:
                pending.pop(0)()

        def attention(q_of, keytiles, N, scale, vrows, LOOK=2):
            Op = o_ring.next()
            nk = len(keytiles)
            Sps = {}

            def qk(i):
                Sps[i] = s_ring.next()[:, 0:N]
                mm(Sps[i], keytiles[i][0], q_of)

            for i in range(min(LOOK, nk)):
                qk(i)
            flush_pending()
            for i, (k_ap, v_ap, tab) in enumerate(keytiles):
                if i + LOOK < nk:
                    qk(i + LOOK)
                Sp = Sps.pop(i)
                P = Pr.next()[:, 0:N]
                act(P, Sp, AF.Exp, scale=float(scale))
                if tab is not None:
                    P2 = P2r.next()[:, 0:N]
                    tt(P2, P, tab[:, 0:N], ALU.mult)
                    P = P2
                mm(Op[0:vrows, 0:N], v_ap, P, start=(i == 0), stop=(i == nk - 1))
            return Op

        def normalize(Op, odd, N, out_sb, bcr=None):
            zp = 0 if odd else 64
            r0 = 64 if odd else 0
            zr = zrow.next()
            act(zr[zp:zp + 1, 0:N], Op[zp:zp + 1, 0:N], AF.Ln)
            zr2 = zrow.next()
            act(zr2[zp:zp + 1, 0:N], zr[zp:zp + 1, 0:N], AF.Exp, scale=-1.0)
            bc = (bcr or bc_ring).next()
            mm(bc[:, 0:N], ones_f[zp:zp + 1, :], zr2[zp:zp + 1, 0:N])
            osb = Osb.next()
            act(osb[r0:r0 + 64, 0:N], Op[r0:r0 + 64, 0:N], AF.Copy)
            tt(out_sb[r0:r0 + 64, 0:N], osb[r0:r0 + 64, 0:N], bc[r0:r0 + 64, 0:N], ALU.mult)

        def v_slot(kt, h):
            odd = h % 2
            return VA[:, kt, h, 0:128] if odd else VA[:, kt, h, 0:65]

        def v_dst(kt, h):
            odd = h % 2
            return VA[:, kt, h, 64:128] if odd else VA[:, kt, h, 0:64]

        def init_va():
            mset(VA.rearrange("p k h c -> p (k h c)"), 0.0, eng='dve')
            for h in range(4):
                col = 0 if h % 2 else 64
                mset(VA[:, :, h, col:col + 1], 1.0, eng='dve')

        for l in range(n_layers):
            with_ctx = l < L - 1
            li = lambda_init(l)
            q_tbs = TBS if with_ctx else TBS[:4]
            ss_ring = Ring([PB(3), PB(4)])
            rot_ring = Ring([PB(5), PB(6)])
            raw_ring = Ring([PB(0), PB(1), PB(2)])
            vp_ring = Ring([PB(int(_os.environ.get("VPB", "6")))])
            s_ring = Ring([PB(0), PB(1), PB(2)])
            o_ring = Ring([PB(3), PB(4)])
            bc_ring = Ring([PB(5)])
            ds_ring = Ring([PB(0), PB(1), PB(2), PB(3)])
            do_ring = Ring([PB(4), PB(5), PB(6)])
            dscratch = [PB(6)]

            def pvc(name, j=0, rows=128):
                c0, w = PV[name]
                return pv[0:rows, l, c0 + j:c0 + j + 1]

            for which, gname in ((0, 'g_mix'), (1, 'g_ffn')):
                c0, _ = PV[gname]
                for w in range(2):
                    sc = modT[:, l, (3 * which + 1) * 8:(3 * which + 2) * 8, w]
                    stt(Gv[:, which, :, w], sc, 1.0, pv[:, l, c0:c0 + 8], ALU.add, ALU.mult)
            c0, _ = PV['lvec']
            lv = pv[:, l, c0:c0 + 128].rearrange("p (a d) -> p a d", a=4)
            prod = tf(0)[:, 0:64].rearrange("p (a d) -> p a d", a=2)
            tt(prod[:, 0, :], lv[:, 0, :], lv[:, 1, :], ALU.mult)
            tt(prod[:, 1, :], lv[:, 2, :], lv[:, 3, :], ALU.mult)
            S.add('dve', lambda E, prod=prod: E.tensor_reduce(out=misc[:, 0:2], in_=prod, axis=AX.X, op=ALU.add),
                  reads=[prod], writes=[misc[:, 0:2]])
            act(misc[:, 2:4], misc[:, 0:2], AF.Exp)
            tt(misc[:, 4:5], misc[:, 3:4], misc[:, 2:3], ALU.subtract)
            ts(misc[:, 5:6], misc[:, 4:5], float(-li), None, ALU.add)
            ts(misc[:, 6:7], pvc('dsub_g'), float(1.0 - li), None, ALU.mult)
            nlam = misc[:, 5:6]
            gsub = misc[:, 6:7]

            modulate(l, 0)
            ck('m1')
            init_va()
            ck('va')

            def proj_fm(out_ps, wcols, t0, n):
                for kc in range(8):
                    mm(out_ps, wA[:, kc, wcols[0]:wcols[1]], hT[:, kc, t0:t0 + n], start=(kc == 0), stop=(kc == 7))

            def proj_v(col0, heads, ncols_per_head=64, slot_of=None):
                nh = len(heads)
                for kt in range(NKT):
                    vp = vp_ring.next()
                    for kc in range(8):
                        mm(vp[:, 0:nh * 64], hT[:, kc, kt * 128:(kt + 1) * 128], wA[:, kc, col0:col0 + nh * 64],
                           start=(kc == 0), stop=(kc == 7))
                    for i, hs in enumerate(heads):
                        for h in hs:
                            vcopy(v_dst(kt, h), vp[:, i * 64:(i + 1) * 64], vp, True)

            def run_attention(mixer, head_q, head_k, krows, scale, tab_of=None, na=False, diff=False):
                for h in range(4):
                    odd = h % 2
                    chunk = 2 * mixer + h // 2
                    for tbi, (t0, n) in enumerate(q_tbs):
                        if tbi < 4:
                            if na:
                                kts = list(NA_KT[tbi]) + [16, 17]
                            else:
                                kts = list(range(NKT))
                        else:
                            kts = [16, 17]
                        vrows = 128 if odd else 65
                        if not diff:
                            tiles = []
                            for kt in kts:
                                tab = None
                                if na and kt < 16:
                                    i0 = 8 * tbi - 2 * kt + 7
                                    tab = natab[:, h, (i0 + 3) * 64:(i0 + 3) * 64 + 512]
                                tiles.append((KT[0:krows, head_k(h), kt * 128:(kt + 1) * 128], v_slot(kt, h), tab))
                            Op = attention(QT[0:krows, h, t0:t0 + n], tiles, n, scale, vrows)
                            pending.append(lambda Op=Op, odd=odd, n=n, chunk=chunk, t0=t0: normalize(Op, odd, n, mixT[:, chunk, t0:t0 + n]))
                        else:
                            r0 = 64 if odd else 0
                            Ops = [do_ring.next(), do_ring.next()]
                            nk = len(kts)
                            Sps = {}

                            def qk2(i, h=h, t0=t0, n=n, kts=kts):
                                kt = kts[i]
                                Sps[i] = []
                                for pr in range(2):
                                    sp_ = ds_ring.next()[:, 0:n]
                                    mm(sp_, KT[32 * pr:32 * pr + 32, h, kt * 128:(kt + 1) * 128], QT[32 * pr:32 * pr + 32, h, t0:t0 + n])
                                    Sps[i].append(sp_)

                            qk2(0)
                            dscratch[0] = Ops[0]
                            flush_pending()
                            for i, kt in enumerate(kts):
                                if i + 1 < nk:
                                    qk2(i + 1)
                                sp2 = Sps.pop(i)
                                for pr in range(2):
                                    P = Pr.next()[:, 0:n]
                                    act(P, sp2[pr], AF.Exp, scale=float(scale))
                                    mm(Ops[pr][0:vrows, 0:n], v_slot(kt, h), P, start=(i == 0), stop=(i == nk - 1))

                            def fin(Ops=Ops, odd=odd, n=n, r0=r0, chunk=chunk, t0=t0):
                                scr_ring = Ring([dscratch[0]])
                                normalize(Ops[0], odd, n, ABo[0], bcr=scr_ring)
                                normalize(Ops[1], odd, n, ABo[1], bcr=scr_ring)
                                o = ABo[2]
                                stt(o[r0:r0 + 64, 0:n], ABo[1][r0:r0 + 64, 0:n], nlam[r0:r0 + 64, :], ABo[0][r0:r0 + 64, 0:n], ALU.mult, ALU.add)
                                sq = sqr.next()
                                act(sq[r0:r0 + 64, 0:n], o[r0:r0 + 64, 0:n], AF.Square)
                                ssp = dscratch[0]
                                mm(ssp[r0:r0 + 64, 0:n], blk64[r0:r0 + 64, r0:r0 + 64], sq[r0:r0 + 64, 0:n])
                                ln = lnr.next()
                                act(ln[r0:r0 + 64, 0:n], ssp[r0:r0 + 64, 0:n], AF.Ln, scale=1.0 / 64, bias=float(EPS))
                                rs = rsr.next()
                                act(rs[r0:r0 + 64, 0:n], ln[r0:r0 + 64, 0:n], AF.Exp, scale=-0.5)
                                stt(mixT[r0:r0 + 64, chunk, t0:t0 + n], o[r0:r0 + 64, 0:n], gsub[r0:r0 + 64, :], rs[r0:r0 + 64, 0:n],
                                    ALU.mult, ALU.mult)
                            pending.append(fin)
                    if diff:
                        dscratch[0] = do_ring.next()
                    flush_pending()

            dma('pool', wA[:, :, 0:416], win_d[l, :, 0:416].rearrange("(a p) n -> p a n", p=128))
            dma('pool', wuq_sb, wuq_d[l].rearrange("(a p) n -> p a n", p=128))
            dma('pool', wukv_sb, wukv_d[l])
            dma('sp', ropeT, tabs_d[0])
            tiles = []
            for tbi, (t0, n) in enumerate(TBS):
                lat = tbi < 4
                rope = (Rmla[0:96, 0:96], ropeT[0:96, 0, t0:t0 + n], ropeT[0:96, 1, t0:t0 + n]) if lat else None
                st = {}

                def A_cq(t0=t0, n=n, st=st):
                    st['raws'] = []
                    for c in range(2):
                        rp = raw_ring.next()[:, 0:n]
                        proj_fm(rp, (c * 128, (c + 1) * 128), t0, n)
                        st['raws'].append(rp)

                def B_cq(t0=t0, n=n, st=st):
                    ssp = ss_ring.next()[:, 0:n]
                    for c in range(2):
                        sq = sqr.next()[:, 0:n]
                        act(sq, st['raws'][c], AF.Square)
                        mm(ssp, allones, sq, start=(c == 0), stop=(c == 1))
                    ln = lnr.next()[:, 0:n]
                    act(ln, ssp, AF.Ln, scale=1.0 / 256, bias=float(EPS))
                    rs = rsr.next()[:, 0:n]
                    act(rs, ln, AF.Exp, scale=-0.5)
                    for c in range(2):
                        stt(cqn[:, c, 0:n], st['raws'][c], pvc('qa_g', c), rs, ALU.mult, ALU.mult)

                tiles.append([A_cq, B_cq, None])

                def src_ckv(t0=t0, n=n):
                    rp = raw_ring.next()[:, 0:n]
                    proj_fm(rp, (256, 384), t0, n)
                    return rp
                tiles.append(chain(src_ckv, 128, n, allones, 1.0 / 128, pvc('kva_g'), ckvnT[:, t0:t0 + n], None))

                def A_kr(t0=t0, n=n):
                    rpk = raw_ring.next()
                    for kc in range(8):
                        mm(rpk[64:96, 0:n], wA[:, kc, 384:416], hT[:, kc, t0:t0 + n], start=(kc == 0), stop=(kc == 7))
                    act(krs_buf[64:96, 0:n], rpk[64:96, 0:n], AF.Copy)
                tiles.append([A_kr, None, None])

                for h in range(4):
                    def src_q(h=h, n=n):
                        rp = raw_ring.next()[0:96, 0:n]
                        for c in range(2):
                            mm(rp, wuq_sb[:, c, h * 96:(h + 1) * 96], cqn[:, c, 0:n], start=(c == 0), stop=(c == 1))
                        return rp
                    tiles.append(chain(src_q, 96, n, allones[0:96, 0:96], 1.0 / 96, pvc('mq_g', 0, 96), QT[0:96, h, t0:t0 + n], rope))
                for h in range(4):
                    def src_k(h=h, t0=t0, n=n):
                        rp = raw_ring.next()[0:64, 0:n]
                        mm(rp, wukv_sb[:, h * 128:h * 128 + 64], ckvnT[:, t0:t0 + n])
                        kr_ = kraw.next()
                        act(kr_[0:64, 0:n], rp, AF.Copy)
                        cp(kr_[64:96, 0:n], krs_buf[64:96, 0:n])
                        return kr_[0:96, 0:n]
                    tiles.append(chain(src_k, 96, n, allones[0:96, 0:96], 1.0 / 96, pvc('mk_g', 0, 96), KT[0:96, h, t0:t0 + n], rope))
            run_pipe(tiles)
            ck('mlap')
            for kt in range(NKT):
                vp = vp_ring.next()
                for h in range(4):
                    mm(vp[:, h * 64:(h + 1) * 64], ckvnT[:, kt * 128:(kt + 1) * 128], wukv_sb[:, h * 128 + 64:h * 128 + 128])
                for h in range(4):
                    if _os.environ.get('NOVCP'):
                        continue
                    vcopy(v_dst(kt, h), vp[:, h * 64:(h + 1) * 64], vp, True)
            ck('mlav')
            run_attention(0, None, lambda h: h, 96, 96 ** -0.5)
            ck('mla')

            dma('pool', wA[:, :, 0:768], win_d[l, :, 416:1184].rearrange("(a p) n -> p a n", p=128))
            dma('sp', ropeT, tabs_d[1])
            tiles = []
            for tbi, (t0, n) in enumerate(TBS):
                lat = tbi < 4
                rope = (Rdiff[0:64, 0:64], ropeT[0:64, 0, t0:t0 + n], ropeT[0:64, 1, t0:t0 + n]) if lat else None
                for h in range(4):
                    for (cb, gname, dst) in ((0, 'dq_g', QT), (256, 'dk_g', KT)):
                        def src(h=h, cb=cb, t0=t0, n=n):
                            rp = raw_ring.next()[0:64, 0:n]
                            proj_fm(rp, (cb + h * 64, cb + h * 64 + 64), t0, n)
                            return rp
                        tiles.append(chain(src, 64, n, blk32[0:64, 0:64], 1.0 / 32, pvc(gname, 0, 64), dst[0:64, h, t0:t0 + n], rope))
            run_pipe(tiles)
            proj_v(512, [[0], [1], [2], [3]])
            run_attention(1, None, lambda h: h, 32, 32 ** -0.5, diff=True)
            ck('diff')

            dma('pool', wA[:, :, 0:768], win_d[l, :, 1184:1952].rearrange("(a p) n -> p a n", p=128))
            for h in range(4):
                dma('sp', nastg, nab_d[l, h])
                act(natab[:, h, :], nastg, AF.Exp)
                for (c0_, c1_) in ((0, TL), (TL, T)):
                    dma('pool', QT[64:96, h, c0_:c1_], aug_d[0][:, c0_:c1_])
                    dma('pool', KT[64:96, h, c0_:c1_], aug_d[1][:, c0_:c1_])
            tiles = []
            for tbi, (t0, n) in enumerate(TBS):
                for h in range(4):
                    for (cb, gname, dst) in ((0, 'nq_g', QT), (256, 'nk_g', KT)):
                        def src(h=h, cb=cb, t0=t0, n=n):
                            rp = raw_ring.next()[0:64, 0:n]
                            proj_fm(rp, (cb + h * 64, cb + h * 64 + 64), t0, n)
                            return rp
                        tiles.append(chain(src, 64, n, allones[0:64, 0:64], 1.0 / 64, pvc(gname, 0, 64), dst[0:64, h, t0:t0 + n], None))
            run_pipe(tiles)
            proj_v(512, [[0], [1], [2], [3]])
            run_attention(2, None, lambda h: h, 96, 64 ** -0.5, na=True)
            ck('na')

            dma('pool', wA[:, :, 0:512], win_d[l, :, 1952:2464].rearrange("(a p) n -> p a n", p=128))
            for g in range(2):
                for r in range(2):
                    c0_ = 512 + (2 * g + r) * 64
                    dma('pool', wA[:, :, c0_:c0_ + 64],
                        win_d[l, :, 1952 + 256 + g * 64:1952 + 256 + (g + 1) * 64].rearrange("(a p) n -> p a n", p=128))
            dma('sp', ropeT, tabs_d[2])
            tiles = []
            for tbi, (t0, n) in enumerate(TBS):
                lat = tbi < 4
                rope = (Rgqa, ropeT[:, 0, t0:t0 + n], ropeT[:, 1, t0:t0 + n]) if lat else None
                for (cb, gname, dst) in ((0, 'gq_g', QT), (512, 'gk_g', KT)):
                    for j in range(2):
                        def src(j=j, cb=cb, t0=t0, n=n):
                            rp = raw_ring.next()[:, 0:n]
                            proj_fm(rp, (cb + j * 128, cb + (j + 1) * 128), t0, n)
                            return rp
                        tiles.append(chain(src, 128, n, blk64, 1.0 / 64, pvc(gname), dst[:, j, t0:t0 + n], rope))
            run_pipe(tiles)
            ck('gqap')
            proj_v(384, [[0, 1], [2, 3]])
            ck('gqav')
            for j in range(2):
                for tbi, (t0, n) in enumerate(q_tbs):
                    kts = list(range(NKT)) if tbi < 4 else [16, 17]
                    nk = len(kts)
                    Ops = [do_ring.next(), do_ring.next()]
                    Sps = {}

                    def qk2(i, j=j, t0=t0, n=n, kts=kts):
                        kt = kts[i]
                        Sps[i] = []
                        for r in range(2):
                            sp_ = ds_ring.next()[:, 0:n]
                            mm(sp_, KT[64 * r:64 * r + 64, j, kt * 128:(kt + 1) * 128], QT[64 * r:64 * r + 64, j, t0:t0 + n])
                            Sps[i].append(sp_)

                    qk2(0)
                    dscratch[0] = Ops[0]
                    flush_pending()
                    for i, kt in enumerate(kts):
                        if i + 1 < nk:
                            qk2(i + 1)
                        sp2 = Sps.pop(i)
                        for r in range(2):
                            h = 2 * j + r
                            P = Pr.next()[:, 0:n]
                            act(P, sp2[r], AF.Exp, scale=float(64 ** -0.5))
                            mm(Ops[r][0:(128 if r else 65), 0:n], v_slot(kt, h), P, start=(i == 0), stop=(i == nk - 1))

                    def fin(Ops=Ops, j=j, n=n, t0=t0):
                        scr_ring = Ring([dscratch[0]])
                        for r in range(2):
                            normalize(Ops[r], r, n, mixT[:, 6 + j, t0:t0 + n], bcr=scr_ring)
                    pending.append(fin)
                dscratch[0] = do_ring.next()
                flush_pending()
            ck('gqa')

            def load_group(gi):
                g0_, gn_ = FFG[gi]
                wn_ = gn_ * 128
                wb = wup_bufs[gi % 2]
                dma('pool', wb[:, :, 0, 0:wn_], wup_d[l, :, g0_ * 128:g0_ * 128 + wn_].rearrange("(a p) n -> p a n", p=128))
                dma('pool', wb[:, :, 1, 0:wn_], wup_d[l, :, DFF + g0_ * 128:DFF + g0_ * 128 + wn_].rearrange("(a p) n -> p a n", p=128))
                dma('pool', wdn_bufs[gi % 2][:, 0:gn_, :], wdn_d[l, g0_ * 128:(g0_ + gn_) * 128, :].rearrange("(a p) n -> p a n", p=128))

            load_group(0)

            dma('pool', wout_sb, wout_d[l].rearrange("(a p) n -> p a n", p=128))
            all_ps = Ring([PB(i) for i in range(7)])
            ss_ring = Ring([PB(5), PB(6)])
            op_ring = Ring([PB(i) for i in range(5)])

            def outproj_compute(tbi, t0, n, xb):
                if tbi == 4 and not with_ctx:
                    return
                w = 0 if tbi < 4 else 1
                for j in range(8):
                    pb = op_ring.next()[:, 0:n]
                    for c in range(8):
                        mm(pb, wout_sb[:, c, j * 128:(j + 1) * 128], mixT[:, c, t0:t0 + n], start=(c == 0), stop=(c == 7))
                    stt(xb[:, j, 0:n], pb, modT[:, l, 2 * 8 + j, w:w + 1], xb[:, j, 0:n], ALU.mult, ALU.add)
                dma('act', xT_d[:, :, t0:t0 + n], xb[:, :, 0:n], writes=xkeys(tbi))

            modulate(l, 1, outproj_compute)

            ck('outproj')
            f_tbs = TBS if with_ctx else TBS[:4]
            for ub in ubuf + ubuf2:
                mset(ub[:, 0:1], 0.0, eng='pool')
                mset(ub[:, 2049:2051], 0.0, eng='pool')
                mset(ub[:, 2307:2308], 0.0, eng='pool')
            up_ring = Ring([PB(0), PB(1), PB(2), PB(3)])
            dn_ring = Ring([PB(4), PB(5), PB(6)])
            cring = Ring(ctile)

            def ucol(t0):
                return (1 + t0) if t0 < TL else (2051 + t0 - TL)

            def up_tile(c, gi, cl, t0, n):
                ub = ubuf if c % 2 == 0 else ubuf2
                wup_sb = wup_bufs[gi % 2]
                u0 = ucol(t0)
                for ag in range(2):
                    pb = up_ring.next()[:, 0:n]
                    for kc in range(8):
                        mm(pb, wup_sb[:, kc, ag, cl * 128:(cl + 1) * 128], hT[:, kc, t0:t0 + n], start=(kc == 0), stop=(kc == 7))
                    act(ub[ag][:, u0:u0 + n], pb, AF.Copy)

            def taps_tile(c, cl, t0, n):
                ub = ubuf if c % 2 == 0 else ubuf2
                u0 = ucol(t0)
                outs = []
                for ag in range(2):
                    col = c if ag == 0 else NCH + c
                    ct = cring.next()[:, 0:n]
                    act(ct, ub[ag][:, u0:u0 + n], AF.Identity, scale=pvc('cw1', col), bias=pvc('cb', col))
                    stt(ct, ub[ag][:, u0 - 1:u0 - 1 + n], pvc('cw0', col), ct, ALU.mult, ALU.add)
                    stt(ct, ub[ag][:, u0 + 1:u0 + 1 + n], pvc('cw2', col), ct, ALU.mult, ALU.add)
                    outs.append(ct)
                act(outs[1], outs[1], AF.Silu)
                tt(actT[:, cl, t0:t0 + n], outs[1], outs[0], ALU.mult, eng='pool')

            def down_group(gi):
                gn = FFG[gi][1]
                wdn_sb = wdn_bufs[gi % 2]
                dtiles = [(tbi, t0, n, j) for tbi, (t0, n) in enumerate(f_tbs) for j in range(8)]
                LA = 4
                xts = {}

                def xload(i):
                    tbi_, t0_, n_, j_ = dtiles[i]
                    xt_ = xring.next()[:, 0:n_]
                    dma('sp', xt_, xT_d[:, j_, t0_:t0_ + n_], reads=[xkey(tbi_), "xf:%d:%d" % (tbi_, j_)])
                    xts[i] = xt_

                for i in range(min(LA, len(dtiles))):
                    xload(i)
                for i, (tbi, t0, n, j) in enumerate(dtiles):
                    if i + LA < len(dtiles):
                        xload(i + LA)
                    w = 0 if tbi < 4 else 1
                    pb = dn_ring.next()[:, 0:n]
                    for cl in range(gn):
                        mm(pb, wdn_sb[:, cl, j * 128:(j + 1) * 128], actT[:, cl, t0:t0 + n], start=(cl == 0), stop=(cl == gn - 1))
                    xt = xts.pop(i)
                    stt(xt, pb, modT[:, l, 5 * 8 + j, w:w + 1], xt, ALU.mult, ALU.add)
                    dma('act', xT_d[:, j, t0:t0 + n], xt, writes=["xf:%d:%d" % (tbi, j)])

            chunk_info = [(g0 + cl, gi, cl, gn) for gi, (g0, gn) in enumerate(FFG) for cl in range(gn)]
            for idx, (c, gi, cl, gn) in enumerate(chunk_info):
                prev = chunk_info[idx - 1] if idx > 0 else None
                for (t0, n) in f_tbs:
                    up_tile(c, gi, cl, t0, n)
                    if prev is not None:
                        taps_tile(prev[0], prev[2], t0, n)
                if cl == 0:
                    if prev is not None:
                        down_group(prev[1])
                    if gi + 1 < len(FFG):
                        load_group(gi + 1)
            last = chunk_info[-1]
            for (t0, n) in f_tbs:
                taps_tile(last[0], last[2], t0, n)
            down_group(last[1])

    except StopBuild:
        pass
    import os as _os
    for tbi, (t0, n) in enumerate(TBS[:4]):
        dma('sp', xblk[:, :, 0:n], xT_d[:, :, t0:t0 + n], reads=xkeys(tbi))
        if _os.environ.get('SKIP_FINAL'):
            dma('sp', y_d[t0:t0 + n, :].rearrange("(a p) d -> p a d", p=128), xblk[:, 0:4, :].rearrange("p a (b c) -> p (a b) c", b=1)[:, :, :].rearrange("p a c -> p a c") if False else xld[:, 0:4, :])
            continue
        for a in range(4):
            for half in range(2):
                pb = PB((a * 2 + half) % int(_os.environ.get("NB", "6")))
                for jj in range(4):
                    j = half * 4 + jj
                    tr(pb[:, jj * 128:(jj + 1) * 128], xblk[:, j, a * 128:(a + 1) * 128], ident)
                if half == 0:
                    act(xld[:, a, 0:512], pb, AF.Copy)
                else:
                    cp(xld[:, a, 512:1024], pb)
        dma('sp', y_d[t0:t0 + n, :].rearrange("(a p) d -> p a d", p=128), xld[:, 0:4, :])
    if dbg:
        for tbi, (t0, n) in enumerate(TBS):
            dma('sp', xblk[:, :, 0:n], xT_d[:, :, t0:t0 + n], reads=xkeys(tbi))
            dma('sp', dbg_d[:, :, t0:t0 + n], xblk[:, :, 0:n])
    S.emit()
    return nc, S


def _rope_tab(rot):
    nf = rot // 4
    inv = np.power(np.float32(10000.0), -np.arange(nf, dtype=np.float32) / np.float32(nf)).astype(np.float32)
    t = np.arange(TL)
    row = (t // 64).astype(np.float32)
    col = (t % 64).astype(np.float32)
    ar = row[:, None] * inv
    ac = col[:, None] * inv
    ang = np.concatenate([ar, ar, ac, ac], axis=-1).astype(np.float32)
    return np.cos(ang).T.astype(np.float32), np.sin(ang).T.astype(np.float32)


def _constants():
    cf = np.zeros((128, 256), np.float32)
    cf[:, 0:128] = np.eye(128, dtype=np.float32)
    cf[:, 128:256] = 1.0
    cb = np.zeros((128, 6, 128), np.float32)
    cb[:, 0, :] = 1.0
    for b in range(4):
        cb[b * 32:(b + 1) * 32, 1, b * 32:(b + 1) * 32] = 1.0
    for b in range(2):
        cb[b * 64:(b + 1) * 64, 2, b * 64:(b + 1) * 64] = 1.0

    def add_rot(Rm, base, n):
        for i in range(n):
            Rm[base + i + n, base + i] = -1.0
            Rm[base + i, base + i + n] = 1.0
            Rm[base + 3 * n + i, base + 2 * n + i] = -1.0
            Rm[base + 2 * n + i, base + 3 * n + i] = 1.0
    add_rot(cb[:, 3, :], 64, 8)
    add_rot(cb[:, 4, :], 0, 8)
    add_rot(cb[:, 4, :], 32, 8)
    add_rot(cb[:, 5, :], 0, 16)
    add_rot(cb[:, 5, :], 64, 16)
    aug = np.zeros((2, 32, T), np.float32)
    for q in range(TL):
        qr = q // 64
        rs = min(max(qr - 4, 0), 24)
        aug[0, :, q] = -BIG
        aug[0, rs:rs + 8, q] = 0.0
        aug[1, qr, q] = 1.0
    tabs = np.zeros((3, 128, 2, TL), np.float32)
    c32, s32 = _rope_tab(32)
    c64, s64 = _rope_tab(64)
    tabs[0, 0:64, 0, :] = 1.0
    tabs[0, 64:96, 0, :] = c32
    tabs[0, 64:96, 1, :] = s32
    tabs[1, 0:32, 0, :] = c32
    tabs[1, 32:64, 0, :] = c32
    tabs[1, 0:32, 1, :] = s32
    tabs[1, 32:64, 1, :] = s32
    tabs[2, 0:64, 0, :] = c64
    tabs[2, 0:64, 1, :] = s64
    tabs[2, 64:128, 0, :] = c64
    tabs[2, 64:128, 1, :] = s64
    return cf, cb.reshape(128, 768), aug, tabs


def _na_index():
    idx = np.full((128, 22, 64), 15 * 31, np.int64)
    for p in range(128):
        half, kc = p // 64, p % 64
        for pos in range(22):
            i = pos - 3 - half
            if i < 0 or i > 14:
                continue
            for qc in range(64):
                ws = min(max(qc - 8, 0), 48)
                if ws <= kc < ws + 16:
                    co = min(max(kc - qc + 15, 0), 30)
                    idx[p, pos, qc] = (14 - i) * 31 + co
    return idx.reshape(128, NAW)


def _tile_rows(v, reps, rows=128):
    out = np.zeros((rows,), np.float32)
    t = np.tile(np.asarray(v, np.float32), reps)
    out[:t.shape[0]] = t
    return out


def _prep_shared(inp):
    cf, cb, aug, tabs = _constants()
    pvA = np.zeros((128, L, NV), np.float32)

    def put(name, l, arr2d):
        c0, w = PV[name]
        pvA[:, l, c0:c0 + w] = arr2d

    for l in range(L):
        put('g_mix', l, inp['g_mix'][l].reshape(8, 128).T)
        put('g_ffn', l, inp['g_ffn'][l].reshape(8, 128).T)
        put('qa_g', l, inp['mla_q_a_g'][l].reshape(2, 128).T)
        put('kva_g', l, inp['mla_kv_a_g'][l].reshape(1, 128).T)
        put('mq_g', l, _tile_rows(inp['mla_q_g'][l], 1)[:, None])
        put('mk_g', l, _tile_rows(inp['mla_k_g'][l], 1)[:, None])
        put('dq_g', l, _tile_rows(inp['diff_q_g'][l], 4)[:, None])
        put('dk_g', l, _tile_rows(inp['diff_k_g'][l], 4)[:, None])
        put('dsub_g', l, _tile_rows(inp['diff_subln_g'][l], 2)[:, None])
        put('nq_g', l, _tile_rows(inp['na_q_g'][l], 2)[:, None])
        put('nk_g', l, _tile_rows(inp['na_k_g'][l], 2)[:, None])
        put('gq_g', l, _tile_rows(inp['gqa_q_g'][l], 2)[:, None])
        put('gk_g', l, _tile_rows(inp['gqa_k_g'][l], 2)[:, None])
        for i in range(3):
            put('cw%d' % i, l, inp['conv_w'][l, i].reshape(44, 128).T)
        put('cb', l, inp['conv_b'][l].reshape(44, 128).T)
        lv = np.concatenate([inp['diff_lq1'][l], inp['diff_lk1'][l], inp['diff_lq2'][l], inp['diff_lk2'][l]]).astype(np.float32)
        put('lvec', l, np.broadcast_to(lv[None, :], (128, 128)))
    idx = _na_index()
    nab = np.zeros((L, 4, 128, NAW), np.float32)
    for l in range(L):
        for h in range(4):
            src = np.concatenate([np.asarray(inp['na_rpb'][l, h], np.float32).ravel(), np.array([-10000.0], np.float32)])
            nab[l, h] = src[idx]
    f = lambda a: np.ascontiguousarray(np.asarray(a, np.float32))
    return {
        "w_mod": f(inp['w_mod']), "b_mod": f(inp['b_mod']), "w_in": f(inp['w_in']), "w_out": f(inp['w_out']),
        "w_uq": f(inp['mla_w_uq']), "w_ukv": f(inp['mla_w_ukv']), "w_up": f(inp['w_up']), "w_down": f(inp['w_down']),
        "pv": np.ascontiguousarray(pvA.reshape(128, L * NV)), "cf32": cf, "cbf": np.ascontiguousarray(cb),
        "aug": aug, "tabs": tabs, "nab": nab,
    }


_CACHE = {}


def kernel(**inp):
    n_layers = inp.pop('_n_layers', L)
    dbg = inp.pop('_dbg', False)
    stop = inp.pop('_stop', None)
    ncores = inp.pop('_ncores', 8)
    key = (n_layers, dbg, stop)
    if key not in _CACHE:
        _CACHE[key] = build_program(n_layers, dbg, stop)[0]
    nc = _CACHE[key]
    shared = _prep_shared(inp)
    x = np.asarray(inp['x'], np.float32)
    ctx = np.asarray(inp['ctx'], np.float32)
    c = np.asarray(inp['c'], np.float32)
    cc = np.asarray(inp['c_ctx'], np.float32)
    in_maps = []
    for b in range(ncores):
        m = dict(shared)
        m["xc"] = np.ascontiguousarray(np.concatenate([x[b], ctx[b]], axis=0))
        cT = np.zeros((128, 8, 2), np.float32)
        cT[:, :, 0] = c[b].reshape(8, 128).T
        cT[:, :, 1] = cc.reshape(8, 128).T
        m["cT"] = np.ascontiguousarray(cT.reshape(128, 16))
        in_maps.append(m)
    res = run_bass_kernel_spmd(nc, in_maps, core_ids=list(range(ncores)))
    out = np.stack([np.asarray(r["y"], np.float32) for r in res.results], axis=0)
    if dbg:
        kernel.dbg = [np.asarray(r["dbg"], np.float32) for r in res.results]
    return out
```

```python
import math
import numpy as np
import concourse.bass as bass
import concourse.mybir as mybir
from concourse.bass_utils import run_bass_kernel_spmd

F32 = mybir.dt.float32
BF16 = mybir.dt.bfloat16
AF = mybir.ActivationFunctionType
ALU = mybir.AluOpType
AX = mybir.AxisListType

ENGS = ['pe', 'act', 'dve', 'pool', 'sp']
CELL = 256
_ESZ = {}


def esz(dt):
    if dt not in _ESZ:
        _ESZ[dt] = mybir.dt.size(dt)
    return _ESZ[dt]


def ap_cells(ap):
    space = str(ap.space)
    sp = 0 if space == 'SB' else 1
    dims = ap.ap
    pstep, pcount = dims[0]
    e = esz(ap.dtype)
    off = ap.offset
    p0 = off // pstep
    foff = off % pstep
    ranges = [(foff, foff + 1)]
    for (st, cnt) in dims[1:]:
        if cnt <= 1:
            continue
        if len(ranges) * cnt <= 512 and abs(st) * e >= CELL:
            ranges = [(lo + i * st, hi + i * st) for (lo, hi) in ranges for i in range(cnt)]
        else:
            ext = (cnt - 1) * st
            if ext >= 0:
                ranges = [(lo, hi + ext) for (lo, hi) in ranges]
            else:
                ranges = [(lo + ext, hi) for (lo, hi) in ranges]
    cs = set()
    for (lo, hi) in ranges:
        c0 = (lo * e) // CELL
        c1 = (hi * e - 1) // CELL
        for c in range(c0, c1 + 1):
            cs.add(c)
    if sp == 1:
        return sorted(set(4 * 4096 + (c * CELL) // 2048 for c in cs))
    q0 = p0 // 32
    q1 = (p0 + pcount - 1) // 32
    out = []
    for q in range(q0, q1 + 1):
        base = (sp * 4 + q) * 4096
        for c in cs:
            out.append(base + c)
    return out


class Sched:
    def __init__(self, nc, n_lanes=48, same_eng_sync=True):
        self.nc = nc
        self.ops = {e: [] for e in ENGS}
        self.cw = {}
        self.cr = {}
        self.known = {e: {} for e in ENGS}
        self.snap = {}
        self.n_lanes = n_lanes
        self.lane_count = [0] * n_lanes
        self.next_lane = 0
        self.same_eng_sync = same_eng_sync
        self.eng_sem = {e: nc.alloc_semaphore("sem_" + e) for e in ENGS}
        self.lane_sem = [nc.alloc_semaphore("lane%d" % i) for i in range(n_lanes)]

    def _cells(self, items):
        cs = []
        for it in items:
            if it is None:
                continue
            if isinstance(it, (str, tuple)):
                cs.append(it)
            else:
                cs.extend(ap_cells(it))
        return cs

    def add(self, eng, fn, reads=(), writes=(), dma=False):
        ops = self.ops[eng]
        idx = len(ops)
        rc = self._cells(reads)
        wc = self._cells(writes)
        need = {}

        def want(tok, war=False):
            key, seq = tok
            if key == eng:
                if eng == 'pe' or war or not self.same_eng_sync:
                    return
            if need.get(key, -1) < seq:
                need[key] = seq

        for c in rc:
            t = self.cw.get(c)
            if t is not None:
                want(t)
        for c in wc:
            t = self.cw.get(c)
            if t is not None:
                want(t)
            rs = self.cr.get(c)
            if rs:
                for k, s in rs.items():
                    want((k, s), war=True)
        if dma:
            lane = self.next_lane
            self.next_lane = (lane + 1) % self.n_lanes
            cnt = self.lane_count[lane] + 1
            self.lane_count[lane] = cnt
            tok = (('L', lane), cnt)
            if cnt > 1:
                want((('L', lane), cnt - 1))
        else:
            tok = (eng, idx)
        kn = self.known[eng]
        waits = []
        for key, seq in need.items():
            if kn.get(key, -1) >= seq:
                continue
            waits.append((key, seq))
            kn[key] = seq
            if isinstance(key, str):
                self.ops[key][seq]['signal'] = True
                sn = self.snap.get((key, seq))
                if sn:
                    for k2, s2 in sn.items():
                        if kn.get(k2, -1) < s2:
                            kn[k2] = s2
        if not dma:
            self.snap[(eng, idx)] = dict(kn)
        ops.append(dict(fn=fn, waits=waits, tok=tok, dma=dma, signal=False))
        for c in wc:
            self.cw[c] = tok
            self.cr[c] = {}
        for c in rc:
            d = self.cr.get(c)
            if d is None:
                d = {}
                self.cr[c] = d
            if d.get(tok[0], -1) < tok[1]:
                d[tok[0]] = tok[1]
        return tok

    def emit(self):
        nc = self.nc
        for e in ENGS:
            c = 0
            for op in self.ops[e]:
                if op['signal'] and not op['dma']:
                    c += 1
                    op['sigval'] = c
        emap = {'pe': 'tensor', 'act': 'scalar', 'dve': 'vector', 'pool': 'gpsimd', 'sp': 'sync'}

        def run(e, E):
            for op in self.ops[e]:
                for key, seq in op['waits']:
                    if isinstance(key, str):
                        E.wait_ge(self.eng_sem[key], self.ops[key][seq]['sigval'])
                    else:
                        E.wait_ge(self.lane_sem[key[1]], 16 * seq)
                inst = op['fn'](E)
                if op['dma']:
                    inst.then_inc(self.lane_sem[op['tok'][0][1]], 16)
                elif op['signal']:
                    inst.then_inc(self.eng_sem[e], 1)
            if e == 'sp':
                for i, cnt in enumerate(self.lane_count):
                    if cnt:
                        E.wait_ge(self.lane_sem[i], 16 * cnt)

        with nc.Block() as block:
            for e in ENGS:
                getattr(block, emap[e])(lambda E, e=e: run(e, E))

    def stats(self):
        return {e: (len(self.ops[e]), sum(len(o['waits']) for o in self.ops[e])) for e in ENGS}


class Ring:
    def __init__(self, items):
        self.items = list(items)
        self.i = 0

    def next(self):
        r = self.items[self.i]
        self.i = (self.i + 1) % len(self.items)
        return r


D = 1024
L = 4
TL = 2048
TC = 256
T = TL + TC
NKT = T // 128
TBS = [(0, 512), (512, 512), (1024, 512), (1536, 512), (2048, 256)]
IN_COLS = 2464
DFF = 2816
NCH = DFF // 128
EPS = 1e-6
BIG = 30000.0
NA_KT = {0: range(0, 6), 1: range(2, 10), 2: range(6, 14), 3: range(10, 16)}
NAW = 22 * 64
FFG = [(0, 4), (4, 4), (8, 4), (12, 4), (16, 4), (20, 2)]

PV = {}
_c = 0
for _n, _w in [('g_mix', 8), ('g_ffn', 8), ('qa_g', 2), ('kva_g', 1), ('mq_g', 1), ('mk_g', 1),
               ('dq_g', 1), ('dk_g', 1), ('dsub_g', 1), ('nq_g', 1), ('nk_g', 1), ('gq_g', 1), ('gk_g', 1),
               ('cw0', 44), ('cw1', 44), ('cw2', 44), ('cb', 44), ('lvec', 128)]:
    PV[_n] = (_c, _w)
    _c += _w
NV = _c


def lambda_init(l):
    return 0.8 - 0.6 * math.exp(-0.3 * l)


class StopBuild(Exception):
    pass


def build_program(n_layers=L, dbg=False, stop_after=None):
    nc = bass.Bass("TRN2", target_bir_lowering=False)

    def ck(name):
        if stop_after == name:
            raise StopBuild()

    def din(name, shape):
        return nc.dram_tensor(name, list(shape), F32, kind="ExternalInput").ap()

    xc_d = din("xc", [T, D])
    cT_d = din("cT", [128, 16])
    wmod_d = din("w_mod", [L, D, 6 * D])
    bmod_d = din("b_mod", [L, 6 * D])
    win_d = din("w_in", [L, D, IN_COLS])
    wout_d = din("w_out", [L, D, D])
    wuq_d = din("w_uq", [L, 256, 384])
    wukv_d = din("w_ukv", [L, 128, 512])
    wup_d = din("w_up", [L, D, 2 * DFF])
    wdn_d = din("w_down", [L, DFF, D])
    pv_d = din("pv", [128, L * NV])
    cf_d = din("cf32", [128, 256])
    cb_d = din("cbf", [128, 6 * 128])
    aug_d = din("aug", [2, 32, T])
    tabs_d = din("tabs", [3, 128, 2, TL])
    nab_d = din("nab", [L, 4, 128, NAW])
    y_d = nc.dram_tensor("y", [TL, D], F32, kind="ExternalOutput").ap()
    xT_d = nc.dram_tensor("xT_scr", [128, 8, T], F32).ap()
    if dbg:
        dbg_d = nc.dram_tensor("dbg", [128, 8, T], F32, kind="ExternalOutput").ap()

    ARENA_F32 = 53000
    arena = nc.alloc_sbuf_tensor("arena", [128, ARENA_F32], F32)
    psum = nc.alloc_psum_tensor("psum", [128, 4096], F32)
    S = Sched(nc)

    pos = [0]

    def alloc_b(nbytes):
        a = pos[0]
        n = (nbytes + 255) // 256 * 256
        pos[0] += n
        assert pos[0] <= ARENA_F32 * 4, pos[0]
        return a

    def f32v(boff, n):
        return arena[:, boff // 4: boff // 4 + n]

    def bf16v(boff, n):
        return arena[:, boff // 4: boff // 4 + (n + 1) // 2].bitcast(BF16)

    def PB(i):
        return psum[:, i * 512:(i + 1) * 512]

    ident = f32v(alloc_b(512), 128)
    ones_f = f32v(alloc_b(512), 128)
    cbf = bf16v(alloc_b(6 * 256), 6 * 128).rearrange("p (a b) -> p a b", a=6)
    allones, blk32, blk64, Rmla, Rdiff, Rgqa = [cbf[:, i, :] for i in range(6)]
    pv = f32v(alloc_b(L * NV * 4), L * NV).rearrange("p (l n) -> p l n", l=L)
    modT = f32v(alloc_b(L * 96 * 4), L * 96).rearrange("p (l s w) -> p l s w", l=L, s=48)
    cT = f32v(alloc_b(64), 16)
    scT_f = f32v(alloc_b(64), 16)
    scT = bf16v(alloc_b(32), 16).rearrange("p (a b) -> p a b", a=8)
    Gv = f32v(alloc_b(4 * 16 * 4), 64).rearrange("p (a j w) -> p a j w", a=2, j=8)
    misc = f32v(alloc_b(64 * 4), 64)
    hT = bf16v(alloc_b(8 * T * 2), 8 * T).rearrange("p (a t) -> p a t", a=8)
    mix_off = alloc_b(8 * T * 2)
    mixT = bf16v(mix_off, 8 * T).rearrange("p (a t) -> p a t", a=8)
    qkv_off = alloc_b(3 * 4 * T * 2)
    QT = bf16v(qkv_off, 4 * T).rearrange("p (a t) -> p a t", a=4)
    KT = bf16v(qkv_off + 4 * T * 2, 4 * T).rearrange("p (a t) -> p a t", a=4)
    VA = bf16v(qkv_off + 8 * T * 2, NKT * 4 * 128).rearrange("p (k h c) -> p k h c", k=NKT, h=4)
    xblk = f32v(qkv_off + 16384, 8 * 512).rearrange("p (a n) -> p a n", a=8)
    xblk2 = f32v(qkv_off, 8 * 512).rearrange("p (a n) -> p a n", a=8)
    xld = f32v(qkv_off + 32768, 4 * 1024).rearrange("p (a n) -> p a n", a=4)
    GN = 4
    fo = mix_off
    actT = bf16v(fo, GN * T).rearrange("p (a t) -> p a t", a=GN); fo += GN * T * 2
    UW = T + 4
    UWP = (UW * 4 + 255) // 256 * 256
    ubuf = [f32v(fo + i * UWP, UW) for i in range(2)]; fo += 2 * UWP
    assert fo <= qkv_off + 512, (fo, qkv_off)
    WUPB = 8 * 2 * GN * 128 * 2
    wup_bufs = [bf16v(qkv_off + 32768, 8 * 2 * GN * 128).rearrange("p (k g n) -> p k g n", k=8, g=2),
                bf16v(qkv_off + 512, 8 * 2 * GN * 128).rearrange("p (k g n) -> p k g n", k=8, g=2)]
    assert qkv_off + 32768 + WUPB <= qkv_off + 3 * 4 * T * 2 and 512 + WUPB <= 32768
    fo = qkv_off
    assert fo <= qkv_off + 3 * 4 * T * 2, (fo, qkv_off + 3 * 4 * T * 2)
    tab_off = alloc_b(2 * TL * 4)
    ropeT = f32v(tab_off, 2 * TL).rearrange("p (a t) -> p a t", a=2)
    natab = bf16v(tab_off, 4 * NAW).rearrange("p (h n) -> p h n", h=4)
    wA = bf16v(alloc_b(8 * 768 * 2), 8 * 768).rearrange("p (a n) -> p a n", a=8)
    wout_sb = bf16v(tab_off, 8 * 1024).rearrange("p (a n) -> p a n", a=8)
    ubuf2 = [f32v(tab_off + i * UWP, UW) for i in range(2)]
    assert 2 * UWP <= 2 * TL * 4 + 8 * 768 * 2
    wuq_sb = bf16v(alloc_b(2 * 384 * 2), 2 * 384).rearrange("p (a n) -> p a n", a=2)
    wukv_sb = bf16v(alloc_b(512 * 2), 512)
    ckvnT = bf16v(alloc_b(T * 2), T)
    cqn = bf16v(alloc_b(2 * 512 * 2), 2 * 512).rearrange("p (a n) -> p a n", a=2)
    r1_off = tab_off + 2 * UWP
    wdn_bufs = [bf16v(r1_off + i * GN * 1024 * 2, GN * 1024).rearrange("p (a n) -> p a n", a=GN) for i in range(2)]
    tmp_off = alloc_b(24 * 1024)
    assert r1_off + 2 * GN * 1024 * 2 <= tmp_off, (r1_off, tmp_off)
    krs_buf = f32v(alloc_b(2048), 512)
    sq_extra = alloc_b(2048)

    def tf(i):
        return f32v(tmp_off + i * 2048, 512)

    def tb16(i, half=0):
        return bf16v(tmp_off + i * 2048 + half * 1024, 512)

    kraw = Ring([tf(0), tf(1)])
    sqr = Ring([tb16(2, 0), tb16(2, 1), bf16v(sq_extra, 512), bf16v(sq_extra + 1024, 512)])
    lnr = Ring([tf(3), tf(4)])
    rsr = Ring([tf(5), tf(6)])
    qnr = Ring([tb16(7, 0), tb16(7, 1)])
    t1r = Ring([tf(8), tf(9)])
    t2r = Ring([tf(10), tf(11)])
    Pr = Ring([tb16(0, 0), tb16(0, 1), tb16(1, 0), tb16(1, 1)])
    P2r = Ring([tb16(2, 0), tb16(2, 1)])
    Osb = Ring([tf(3), tf(4)])
    zrow = Ring([tf(5), tf(6)])
    ABo = [tf(7), tf(8), tf(9)]
    xring = Ring([tf(6), tf(7), tf(8), tf(9), tf(10), tf(11)])
    ctile = [tf(0), tf(1), tf(2), tf(3), tf(4), tf(5)]
    nastg = f32v(tmp_off, NAW)
    wmr = Ring([bf16v(tmp_off + i * 4096, 2048) for i in range(3)])
    m_sb = f32v(qkv_off, 6 * D)
    bm_sb = f32v(qkv_off + 24576, 6 * D)

    def mm(out, lhsT, rhs, start=True, stop=True):
        S.add('pe', lambda E: E.matmul(out, lhsT, rhs, start=start, stop=stop), reads=[lhsT, rhs], writes=[out])

    def tr(out, in_, idn):
        S.add('pe', lambda E: E.transpose(out, in_, idn), reads=[in_, idn], writes=[out])

    def act(out, in_, func, scale=None, bias=None, eng='act'):
        kw = {}
        rd = [in_]
        if scale is not None:
            kw['scale'] = scale
            if not isinstance(scale, float):
                rd.append(scale)
        if bias is not None:
            kw['bias'] = bias
            if not isinstance(bias, float):
                rd.append(bias)
        S.add('act', lambda E: E.activation(out, in_, func, **kw), reads=rd, writes=[out])

    def stt(out, in0, scalar, in1, op0, op1):
        rd = [in0, in1] + ([] if isinstance(scalar, float) else [scalar])
        S.add('dve', lambda E: E.scalar_tensor_tensor(out=out, in0=in0, scalar=scalar, in1=in1, op0=op0, op1=op1),
              reads=rd, writes=[out])

    def tt(out, in0, in1, op, eng='dve'):
        S.add(eng, lambda E: E.tensor_tensor(out=out, in0=in0, in1=in1, op=op), reads=[in0, in1], writes=[out])

    def ts(out, in0, s1, s2, op0, op1=None, eng='dve'):
        rd = [in0] + [s for s in (s1, s2) if s is not None and not isinstance(s, float)]
        if op1 is None:
            S.add(eng, lambda E: E.tensor_scalar(out=out, in0=in0, scalar1=s1, scalar2=None, op0=op0), reads=rd, writes=[out])
        else:
            S.add(eng, lambda E: E.tensor_scalar(out=out, in0=in0, scalar1=s1, scalar2=s2, op0=op0, op1=op1), reads=rd, writes=[out])

    def cp(out, in_, eng='dve'):
        S.add(eng, lambda E: E.tensor_copy(out=out, in_=in_), reads=[in_], writes=[out])

    def vcopy(out, in_, bank, use_act):
        if use_act:
            S.add('act', lambda E: E.activation(out, in_, AF.Copy), reads=[bank], writes=[out])
        else:
            S.add('dve', lambda E: E.tensor_copy(out=out, in_=in_), reads=[bank], writes=[out])

    def mset(ap, val, eng='dve'):
        S.add(eng, lambda E: E.memset(ap, val), writes=[ap])

    def dma(q, out, in_, reads=None, writes=None):
        r = [in_] if reads is None else reads
        w = [out] if writes is None else writes
        r = [a for a in r if isinstance(a, (str, tuple)) or str(a.space) != 'DRAM']
        w = [a for a in w if isinstance(a, (str, tuple)) or str(a.space) != 'DRAM']
        S.add(q, lambda E: E.dma_start(out=out, in_=in_), reads=r, writes=w, dma=True)

    def xkey(tbi):
        return "x:%d" % tbi

    def xkeys(tbi):
        return ["x:%d" % tbi] + ["xf:%d:%d" % (tbi, j) for j in range(8)]

    try:
        import os as _os
        if _os.environ.get('SKIP_CONST'):
            raise StopBuild()
        dma('sp', ident, cf_d[:, 0:128])
        dma('sp', ones_f, cf_d[:, 128:256])
        dma('pool', cbf, cb_d.rearrange("p (a b) -> p a b", a=6))
        dma('sp', pv, pv_d.rearrange("p (l n) -> p l n", l=L))
        dma('sp', cT, cT_d)
        act(scT_f, cT, AF.Silu)
        cp(scT, scT_f.rearrange("p (a b) -> p a b", a=8))

        ck('c0')
        for tbi, (t0, n) in enumerate(TBS):
            ntt = n // 128
            dma('sp', xld[:, 0:ntt, :], xc_d[t0:t0 + n, :].rearrange("(a p) d -> p a d", p=128))
            for j in range(8):
                pb = PB(j % 2)
                for a in range(ntt):
                    tr(pb[:, a * 128:(a + 1) * 128], xld[:, a, j * 128:(j + 1) * 128], ident)
                if j % 2 == 0:
                    act(xblk[:, j, 0:n], pb[:, 0:n], AF.Copy)
                else:
                    cp(xblk[:, j, 0:n], pb[:, 0:n])
            dma('sp', xT_d[:, :, t0:t0 + n], xblk[:, :, 0:n], writes=xkeys(tbi))

        ck('xt')
        for l in range(n_layers):
            dma('sp', bm_sb[0:1, :], bmod_d[l:l + 1, :])
            dma('sp', bm_sb[1:2, :], bmod_d[l:l + 1, :])
            for cg in range(3):
                for kc in range(8):
                    piece = wmr.next()
                    dma('pool', piece, wmod_d[l, kc * 128:(kc + 1) * 128, cg * 2048:(cg + 1) * 2048])
                    for i in range(4):
                        mm(PB(i)[0:2, :], scT[:, kc, :], piece[:, i * 512:(i + 1) * 512], start=(kc == 0), stop=(kc == 7))
                for i in range(4):
                    c0 = cg * 2048 + i * 512
                    tt(m_sb[0:2, c0:c0 + 512], PB(i)[0:2, :], bm_sb[0:2, c0:c0 + 512], ALU.add)
            pt = PB(4)
            for s in range(48):
                tr(pt[:, 2 * s:2 * s + 2], m_sb[0:2, s * 128:(s + 1) * 128], ident[0:2, 0:2])
            cp(modT[:, l].rearrange("p s w -> p (s w)"), pt[:, 0:96])

        ck('mod')
        def rstd_from(src, R, N, ones_m, inv_d):
            sq = sqr.next()[0:R, 0:N]
            act(sq, src, AF.Square)
            ssp = ss_ring.next()[0:R, 0:N]
            mm(ssp, ones_m, sq)
            ln = lnr.next()[0:R, 0:N]
            act(ln, ssp, AF.Ln, scale=float(inv_d), bias=float(EPS))
            rs = rsr.next()[0:R, 0:N]
            act(rs, ln, AF.Exp, scale=-0.5)
            return rs

        def norm_rope(src, R, N, ones_m, inv_d, gain, out, rope=None):
            rs = rstd_from(src, R, N, ones_m, inv_d)
            if rope is None:
                stt(out, src, gain, rs, ALU.mult, ALU.mult)
                return
            Rm, cosT, sinT = rope
            qn = qnr.next()[0:R, 0:N]
            stt(qn, src, gain, rs, ALU.mult, ALU.mult)
            rp = rot_ring.next()[0:R, 0:N]
            mm(rp, Rm, qn)
            t1 = t1r.next()[0:R, 0:N]
            tt(t1, qn, cosT, ALU.mult, eng='pool')
            t2 = t2r.next()[0:R, 0:N]
            tt(t2, rp, sinT, ALU.mult)
            tt(out, t1, t2, ALU.add, eng='pool')

        def chain(src_fn, R, N, ones_m, inv_d, gain, out, rope=None):
            st = {}

            def A():
                st['src'] = src_fn()
                sq = sqr.next()[0:R, 0:N]
                act(sq, st['src'], AF.Square)
                st['sq'] = sq

            def B():
                ssp = ss_ring.next()[0:R, 0:N]
                mm(ssp, ones_m, st['sq'])
                ln = lnr.next()[0:R, 0:N]
                act(ln, ssp, AF.Ln, scale=float(inv_d), bias=float(EPS))
                rs = rsr.next()[0:R, 0:N]
                act(rs, ln, AF.Exp, scale=-0.5)
                if rope is None:
                    stt(out, st['src'], gain, rs, ALU.mult, ALU.mult)
                else:
                    qn = qnr.next()[0:R, 0:N]
                    stt(qn, st['src'], gain, rs, ALU.mult, ALU.mult)
                    st['qn'] = qn

            def C():
                Rm, cosT, sinT = rope
                qn = st['qn']
                rp = rot_ring.next()[0:R, 0:N]
                mm(rp, Rm, qn)
                t1 = t1r.next()[0:R, 0:N]
                tt(t1, qn, cosT, ALU.mult, eng='pool')
                t2 = t2r.next()[0:R, 0:N]
                tt(t2, rp, sinT, ALU.mult)
                tt(out, t1, t2, ALU.add, eng='pool')

            return [A, B, C if rope is not None else None]

        def run_pipe(tiles):
            n = len(tiles)
            for step in range(n + 2):
                if step < n and tiles[step][0]:
                    tiles[step][0]()
                if 0 <= step - 1 < n and tiles[step - 1][1]:
                    tiles[step - 1][1]()
                if 0 <= step - 2 < n and tiles[step - 2][2]:
                    tiles[step - 2][2]()

        def modulate(l, which, compute=None):
            xbs = [xblk, xblk2]

            def load(tbi):
                t0_, n_ = TBS[tbi]
                dma('sp', xbs[tbi % 2][:, :, 0:n_], xT_d[:, :, t0_:t0_ + n_], reads=xkeys(tbi))

            load(0)
            for tbi, (t0, n) in enumerate(TBS):
                w = 0 if tbi < 4 else 1
                xb = xbs[tbi % 2]
                if tbi + 1 < len(TBS):
                    load(tbi + 1)
                if compute is not None:
                    compute(tbi, t0, n, xb)
                ssp = ss_ring.next()[:, 0:n]
                for j in range(8):
                    sq = sqr.next()[:, 0:n]
                    act(sq, xb[:, j, 0:n], AF.Square)
                    mm(ssp, allones, sq, start=(j == 0), stop=(j == 7))
                ln = lnr.next()[:, 0:n]
                act(ln, ssp, AF.Ln, scale=1.0 / D, bias=float(EPS))
                rs = rsr.next()[:, 0:n]
                act(rs, ln, AF.Exp, scale=-0.5)
                for j in range(8):
                    t1 = t1r.next()[:, 0:n]
                    stt(t1, xb[:, j, 0:n], Gv[:, which, j, w:w + 1], rs, ALU.mult, ALU.mult)
                    shift = modT[:, l, (3 * which) * 8 + j, w:w + 1]
                    act(hT[:, j, t0:t0 + n], t1, AF.Identity, bias=shift)


        pending = []

        def flush_pending():
            while pending:
                pending.pop(0)()

        def attention(q_of, keytiles, N, scale, vrows, LOOK=2):
            Op = o_ring.next()
            nk = len(keytiles)
            Sps = {}

            def qk(i):
                Sps[i] = s_ring.next()[:, 0:N]
                mm(Sps[i], keytiles[i][0], q_of)

            for i in range(min(LOOK, nk)):
                qk(i)
            flush_pending()
            for i, (k_ap, v_ap, tab) in enumerate(keytiles):
                if i + LOOK < nk:
                    qk(i + LOOK)
                Sp = Sps.pop(i)
                P = Pr.next()[:, 0:N]
                act(P, Sp, AF.Exp, scale=float(scale))
                if tab is not None:
                    P2 = P2r.next()[:, 0:N]
                    tt(P2, P, tab[:, 0:N], ALU.mult)
                    P = P2
                mm(Op[0:vrows, 0:N], v_ap, P, start=(i == 0), stop=(i == nk - 1))
            return Op

        def normalize(Op, odd, N, out_sb, bcr=None):
            zp = 0 if odd else 64
            r0 = 64 if odd else 0
            zr = zrow.next()
            act(zr[zp:zp + 1, 0:N], Op[zp:zp + 1, 0:N], AF.Ln)
            zr2 = zrow.next()
            act(zr2[zp:zp + 1, 0:N], zr[zp:zp + 1, 0:N], AF.Exp, scale=-1.0)
            bc = (bcr or bc_ring).next()
            mm(bc[:, 0:N], ones_f[zp:zp + 1, :], zr2[zp:zp + 1, 0:N])
            osb = Osb.next()
            act(osb[r0:r0 + 64, 0:N], Op[r0:r0 + 64, 0:N], AF.Copy)
            tt(out_sb[r0:r0 + 64, 0:N], osb[r0:r0 + 64, 0:N], bc[r0:r0 + 64, 0:N], ALU.mult)

        def v_slot(kt, h):
            odd = h % 2
            return VA[:, kt, h, 0:128] if odd else VA[:, kt, h, 0:65]

        def v_dst(kt, h):
            odd = h % 2
            return VA[:, kt, h, 64:128] if odd else VA[:, kt, h, 0:64]

        def init_va():
            mset(VA.rearrange("p k h c -> p (k h c)"), 0.0, eng='dve')
            for h in range(4):
                col = 0 if h % 2 else 64
                mset(VA[:, :, h, col:col + 1], 1.0, eng='dve')

        for l in range(n_layers):
            with_ctx = l < L - 1
            li = lambda_init(l)
            q_tbs = TBS if with_ctx else TBS[:4]
            ss_ring = Ring([PB(3), PB(4)])
            rot_ring = Ring([PB(5), PB(6)])
            raw_ring = Ring([PB(0), PB(1), PB(2)])
            vp_ring = Ring([PB(int(_os.environ.get("VPB", "6")))])
            s_ring = Ring([PB(0), PB(1), PB(2)])
            o_ring = Ring([PB(3), PB(4)])
            bc_ring = Ring([PB(5)])
            ds_ring = Ring([PB(0), PB(1), PB(2), PB(3)])
            do_ring = Ring([PB(4), PB(5), PB(6)])
            dscratch = [PB(6)]

            def pvc(name, j=0, rows=128):
                c0, w = PV[name]
                return pv[0:rows, l, c0 + j:c0 + j + 1]

            for which, gname in ((0, 'g_mix'), (1, 'g_ffn')):
                c0, _ = PV[gname]
                for w in range(2):
                    sc = modT[:, l, (3 * which + 1) * 8:(3 * which + 2) * 8, w]
                    stt(Gv[:, which, :, w], sc, 1.0, pv[:, l, c0:c0 + 8], ALU.add, ALU.mult)
            c0, _ = PV['lvec']
            lv = pv[:, l, c0:c0 + 128].rearrange("p (a d) -> p a d", a=4)
            prod = tf(0)[:, 0:64].rearrange("p (a d) -> p a d", a=2)
            tt(prod[:, 0, :], lv[:, 0, :], lv[:, 1, :], ALU.mult)
            tt(prod[:, 1, :], lv[:, 2, :], lv[:, 3, :], ALU.mult)
            S.add('dve', lambda E, prod=prod: E.tensor_reduce(out=misc[:, 0:2], in_=prod, axis=AX.X, op=ALU.add),
                  reads=[prod], writes=[misc[:, 0:2]])
            act(misc[:, 2:4], misc[:, 0:2], AF.Exp)
            tt(misc[:, 4:5], misc[:, 3:4], misc[:, 2:3], ALU.subtract)
            ts(misc[:, 5:6], misc[:, 4:5], float(-li), None, ALU.add)
            ts(misc[:, 6:7], pvc('dsub_g'), float(1.0 - li), None, ALU.mult)
            nlam = misc[:, 5:6]
            gsub = misc[:, 6:7]

            modulate(l, 0)
            ck('m1')
            init_va()
            ck('va')

            def proj_fm(out_ps, wcols, t0, n):
                for kc in range(8):
                    mm(out_ps, wA[:, kc, wcols[0]:wcols[1]], hT[:, kc, t0:t0 + n], start=(kc == 0), stop=(kc == 7))

            def proj_v(col0, heads, ncols_per_head=64, slot_of=None):
                nh = len(heads)
                for kt in range(NKT):
                    vp = vp_ring.next()
                    for kc in range(8):
                        mm(vp[:, 0:nh * 64], hT[:, kc, kt * 128:(kt + 1) * 128], wA[:, kc, col0:col0 + nh * 64],
                           start=(kc == 0), stop=(kc == 7))
                    for i, hs in enumerate(heads):
                        for h in hs:
                            vcopy(v_dst(kt, h), vp[:, i * 64:(i + 1) * 64], vp, True)

            def run_attention(mixer, head_q, head_k, krows, scale, tab_of=None, na=False, diff=False):
                for h in range(4):
                    odd = h % 2
                    chunk = 2 * mixer + h // 2
                    for tbi, (t0, n) in enumerate(q_tbs):
                        if tbi < 4:
                            if na:
                                kts = list(NA_KT[tbi]) + [16, 17]
                            else:
                                kts = list(range(NKT))
                        else:
                            kts = [16, 17]
                        vrows = 128 if odd else 65
                        if not diff:
                            tiles = []
                            for kt in kts:
                                tab = None
                                if na and kt < 16:
                                    i0 = 8 * tbi - 2 * kt + 7
                                    tab = natab[:, h, (i0 + 3) * 64:(i0 + 3) * 64 + 512]
                                tiles.append((KT[0:krows, head_k(h), kt * 128:(kt + 1) * 128], v_slot(kt, h), tab))
                            Op = attention(QT[0:krows, h, t0:t0 + n], tiles, n, scale, vrows)
                            pending.append(lambda Op=Op, odd=odd, n=n, chunk=chunk, t0=t0: normalize(Op, odd, n, mixT[:, chunk, t0:t0 + n]))
                        else:
                            r0 = 64 if odd else 0
                            Ops = [do_ring.next(), do_ring.next()]
                            nk = len(kts)
                            Sps = {}

                            def qk2(i, h=h, t0=t0, n=n, kts=kts):
                                kt = kts[i]
                                Sps[i] = []
                                for pr in range(2):
                                    sp_ = ds_ring.next()[:, 0:n]
                                    mm(sp_, KT[32 * pr:32 * pr + 32, h, kt * 128:(kt + 1) * 128], QT[32 * pr:32 * pr + 32, h, t0:t0 + n])
                                    Sps[i].append(sp_)

                            qk2(0)
                            dscratch[0] = Ops[0]
                            flush_pending()
                            for i, kt in enumerate(kts):
                                if i + 1 < nk:
                                    qk2(i + 1)
                                sp2 = Sps.pop(i)
                                for pr in range(2):
                                    P = Pr.next()[:, 0:n]
                                    act(P, sp2[pr], AF.Exp, scale=float(scale))
                                    mm(Ops[pr][0:vrows, 0:n], v_slot(kt, h), P, start=(i == 0), stop=(i == nk - 1))

                            def fin(Ops=Ops, odd=odd, n=n, r0=r0, chunk=chunk, t0=t0):
                                scr_ring = Ring([dscratch[0]])
                                normalize(Ops[0], odd, n, ABo[0], bcr=scr_ring)
                                normalize(Ops[1], odd, n, ABo[1], bcr=scr_ring)
                                o = ABo[2]
                                stt(o[r0:r0 + 64, 0:n], ABo[1][r0:r0 + 64, 0:n], nlam[r0:r0 + 64, :], ABo[0][r0:r0 + 64, 0:n], ALU.mult, ALU.add)
                                sq = sqr.next()
                                act(sq[r0:r0 + 64, 0:n], o[r0:r0 + 64, 0:n], AF.Square)
                                ssp = dscratch[0]
                                mm(ssp[r0:r0 + 64, 0:n], blk64[r0:r0 + 64, r0:r0 + 64], sq[r0:r0 + 64, 0:n])
                                ln = lnr.next()
                                act(ln[r0:r0 + 64, 0:n], ssp[r0:r0 + 64, 0:n], AF.Ln, scale=1.0 / 64, bias=float(EPS))
                                rs = rsr.next()
                                act(rs[r0:r0 + 64, 0:n], ln[r0:r0 + 64, 0:n], AF.Exp, scale=-0.5)
                                stt(mixT[r0:r0 + 64, chunk, t0:t0 + n], o[r0:r0 + 64, 0:n], gsub[r0:r0 + 64, :], rs[r0:r0 + 64, 0:n],
                                    ALU.mult, ALU.mult)
                            pending.append(fin)
                    if diff:
                        dscratch[0] = do_ring.next()
                    flush_pending()

            dma('pool', wA[:, :, 0:416], win_d[l, :, 0:416].rearrange("(a p) n -> p a n", p=128))
            dma('pool', wuq_sb, wuq_d[l].rearrange("(a p) n -> p a n", p=128))
            dma('pool', wukv_sb, wukv_d[l])
            dma('sp', ropeT, tabs_d[0])
            tiles = []
            for tbi, (t0, n) in enumerate(TBS):
                lat = tbi < 4
                rope = (Rmla[0:96, 0:96], ropeT[0:96, 0, t0:t0 + n], ropeT[0:96, 1, t0:t0 + n]) if lat else None
                st = {}

                def A_cq(t0=t0, n=n, st=st):
                    st['raws'] = []
                    for c in range(2):
                        rp = raw_ring.next()[:, 0:n]
                        proj_fm(rp, (c * 128, (c + 1) * 128), t0, n)
                        st['raws'].append(rp)

                def B_cq(t0=t0, n=n, st=st):
                    ssp = ss_ring.next()[:, 0:n]
                    for c in range(2):
                        sq = sqr.next()[:, 0:n]
                        act(sq, st['raws'][c], AF.Square)
                        mm(ssp, allones, sq, start=(c == 0), stop=(c == 1))
                    ln = lnr.next()[:, 0:n]
                    act(ln, ssp, AF.Ln, scale=1.0 / 256, bias=float(EPS))
                    rs = rsr.next()[:, 0:n]
                    act(rs, ln, AF.Exp, scale=-0.5)
                    for c in range(2):
                        stt(cqn[:, c, 0:n], st['raws'][c], pvc('qa_g', c), rs, ALU.mult, ALU.mult)

                tiles.append([A_cq, B_cq, None])

                def src_ckv(t0=t0, n=n):
                    rp = raw_ring.next()[:, 0:n]
                    proj_fm(rp, (256, 384), t0, n)
                    return rp
                tiles.append(chain(src_ckv, 128, n, allones, 1.0 / 128, pvc('kva_g'), ckvnT[:, t0:t0 + n], None))

                def A_kr(t0=t0, n=n):
                    rpk = raw_ring.next()
                    for kc in range(8):
                        mm(rpk[64:96, 0:n], wA[:, kc, 384:416], hT[:, kc, t0:t0 + n], start=(kc == 0), stop=(kc == 7))
                    act(krs_buf[64:96, 0:n], rpk[64:96, 0:n], AF.Copy)
                tiles.append([A_kr, None, None])

                for h in range(4):
                    def src_q(h=h, n=n):
                        rp = raw_ring.next()[0:96, 0:n]
                        for c in range(2):
                            mm(rp, wuq_sb[:, c, h * 96:(h + 1) * 96], cqn[:, c, 0:n], start=(c == 0), stop=(c == 1))
                        return rp
                    tiles.append(chain(src_q, 96, n, allones[0:96, 0:96], 1.0 / 96, pvc('mq_g', 0, 96), QT[0:96, h, t0:t0 + n], rope))
                for h in range(4):
                    def src_k(h=h, t0=t0, n=n):
                        rp = raw_ring.next()[0:64, 0:n]
                        mm(rp, wukv_sb[:, h * 128:h * 128 + 64], ckvnT[:, t0:t0 + n])
                        kr_ = kraw.next()
                        act(kr_[0:64, 0:n], rp, AF.Copy)
                        cp(kr_[64:96, 0:n], krs_buf[64:96, 0:n])
                        return kr_[0:96, 0:n]
                    tiles.append(chain(src_k, 96, n, allones[0:96, 0:96], 1.0 / 96, pvc('mk_g', 0, 96), KT[0:96, h, t0:t0 + n], rope))
            run_pipe(tiles)
            ck('mlap')
            for kt in range(NKT):
                vp = vp_ring.next()
                for h in range(4):
                    mm(vp[:, h * 64:(h + 1) * 64], ckvnT[:, kt * 128:(kt + 1) * 128], wukv_sb[:, h * 128 + 64:h * 128 + 128])
                for h in range(4):
                    if _os.environ.get('NOVCP'):
                        continue
                    vcopy(v_dst(kt, h), vp[:, h * 64:(h + 1) * 64], vp, True)
            ck('mlav')
            run_attention(0, None, lambda h: h, 96, 96 ** -0.5)
            ck('mla')

            dma('pool', wA[:, :, 0:768], win_d[l, :, 416:1184].rearrange("(a p) n -> p a n", p=128))
            dma('sp', ropeT, tabs_d[1])
            tiles = []
            for tbi, (t0, n) in enumerate(TBS):
                lat = tbi < 4
                rope = (Rdiff[0:64, 0:64], ropeT[0:64, 0, t0:t0 + n], ropeT[0:64, 1, t0:t0 + n]) if lat else None
                for h in range(4):
                    for (cb, gname, dst) in ((0, 'dq_g', QT), (256, 'dk_g', KT)):
                        def src(h=h, cb=cb, t0=t0, n=n):
                            rp = raw_ring.next()[0:64, 0:n]
                            proj_fm(rp, (cb + h * 64, cb + h * 64 + 64), t0, n)
                            return rp
                        tiles.append(chain(src, 64, n, blk32[0:64, 0:64], 1.0 / 32, pvc(gname, 0, 64), dst[0:64, h, t0:t0 + n], rope))
            run_pipe(tiles)
            proj_v(512, [[0], [1], [2], [3]])
            run_attention(1, None, lambda h: h, 32, 32 ** -0.5, diff=True)
            ck('diff')

            dma('pool', wA[:, :, 0:768], win_d[l, :, 1184:1952].rearrange("(a p) n -> p a n", p=128))
            for h in range(4):
                dma('sp', nastg, nab_d[l, h])
                act(natab[:, h, :], nastg, AF.Exp)
                for (c0_, c1_) in ((0, TL), (TL, T)):
                    dma('pool', QT[64:96, h, c0_:c1_], aug_d[0][:, c0_:c1_])
                    dma('pool', KT[64:96, h, c0_:c1_], aug_d[1][:, c0_:c1_])
            tiles = []
            for tbi, (t0, n) in enumerate(TBS):
                for h in range(4):
                    for (cb, gname, dst) in ((0, 'nq_g', QT), (256, 'nk_g', KT)):
                        def src(h=h, cb=cb, t0=t0, n=n):
                            rp = raw_ring.next()[0:64, 0:n]
                            proj_fm(rp, (cb + h * 64, cb + h * 64 + 64), t0, n)
                            return rp
                        tiles.append(chain(src, 64, n, allones[0:64, 0:64], 1.0 / 64, pvc(gname, 0, 64), dst[0:64, h, t0:t0 + n], None))
            run_pipe(tiles)
            proj_v(512, [[0], [1], [2], [3]])
            run_attention(2, None, lambda h: h, 96, 64 ** -0.5, na=True)
            ck('na')

            dma('pool', wA[:, :, 0:512], win_d[l, :, 1952:2464].rearrange("(a p) n -> p a n", p=128))
            for g in range(2):
                for r in range(2):
                    c0_ = 512 + (2 * g + r) * 64
                    dma('pool', wA[:, :, c0_:c0_ + 64],
                        win_d[l, :, 1952 + 256 + g * 64:1952 + 256 + (g + 1) * 64].rearrange("(a p) n -> p a n", p=128))
            dma('sp', ropeT, tabs_d[2])
            tiles = []
            for tbi, (t0, n) in enumerate(TBS):
                lat = tbi < 4
                rope = (Rgqa, ropeT[:, 0, t0:t0 + n], ropeT[:, 1, t0:t0 + n]) if lat else None
                for (cb, gname, dst) in ((0, 'gq_g', QT), (512, 'gk_g', KT)):
                    for j in range(2):
                        def src(j=j, cb=cb, t0=t0, n=n):
                            rp = raw_ring.next()[:, 0:n]
                            proj_fm(rp, (cb + j * 128, cb + (j + 1) * 128), t0, n)
                            return rp
                        tiles.append(chain(src, 128, n, blk64, 1.0 / 64, pvc(gname), dst[:, j, t0:t0 + n], rope))
            run_pipe(tiles)
            ck('gqap')
            proj_v(384, [[0, 1], [2, 3]])
            ck('gqav')
            for j in range(2):
                for tbi, (t0, n) in enumerate(q_tbs):
                    kts = list(range(NKT)) if tbi < 4 else [16, 17]
                    nk = len(kts)
                    Ops = [do_ring.next(), do_ring.next()]
                    Sps = {}

                    def qk2(i, j=j, t0=t0, n=n, kts=kts):
                        kt = kts[i]
                        Sps[i] = []
                        for r in range(2):
                            sp_ = ds_ring.next()[:, 0:n]
                            mm(sp_, KT[64 * r:64 * r + 64, j, kt * 128:(kt + 1) * 128], QT[64 * r:64 * r + 64, j, t0:t0 + n])
                            Sps[i].append(sp_)

                    qk2(0)
                    dscratch[0] = Ops[0]
                    flush_pending()
                    for i, kt in enumerate(kts):
                        if i + 1 < nk:
                            qk2(i + 1)
                        sp2 = Sps.pop(i)
                        for r in range(2):
                            h = 2 * j + r
                            P = Pr.next()[:, 0:n]
                            act(P, sp2[r], AF.Exp, scale=float(64 ** -0.5))
                            mm(Ops[r][0:(128 if r else 65), 0:n], v_slot(kt, h), P, start=(i == 0), stop=(i == nk - 1))

                    def fin(Ops=Ops, j=j, n=n, t0=t0):
                        scr_ring = Ring([dscratch[0]])
                        for r in range(2):
                            normalize(Ops[r], r, n, mixT[:, 6 + j, t0:t0 + n], bcr=scr_ring)
                    pending.append(fin)
                dscratch[0] = do_ring.next()
                flush_pending()
            ck('gqa')

            def load_group(gi):
                g0_, gn_ = FFG[gi]
                wn_ = gn_ * 128
                wb = wup_bufs[gi % 2]
                dma('pool', wb[:, :, 0, 0:wn_], wup_d[l, :, g0_ * 128:g0_ * 128 + wn_].rearrange("(a p) n -> p a n", p=128))
                dma('pool', wb[:, :, 1, 0:wn_], wup_d[l, :, DFF + g0_ * 128:DFF + g0_ * 128 + wn_].rearrange("(a p) n -> p a n", p=128))
                dma('pool', wdn_bufs[gi % 2][:, 0:gn_, :], wdn_d[l, g0_ * 128:(g0_ + gn_) * 128, :].rearrange("(a p) n -> p a n", p=128))

            load_group(0)

            dma('pool', wout_sb, wout_d[l].rearrange("(a p) n -> p a n", p=128))
            all_ps = Ring([PB(i) for i in range(7)])
            ss_ring = Ring([PB(5), PB(6)])
            op_ring = Ring([PB(i) for i in range(5)])

            def outproj_compute(tbi, t0, n, xb):
                if tbi == 4 and not with_ctx:
                    return
                w = 0 if tbi < 4 else 1
                for j in range(8):
                    pb = op_ring.next()[:, 0:n]
                    for c in range(8):
                        mm(pb, wout_sb[:, c, j * 128:(j + 1) * 128], mixT[:, c, t0:t0 + n], start=(c == 0), stop=(c == 7))
                    stt(xb[:, j, 0:n], pb, modT[:, l, 2 * 8 + j, w:w + 1], xb[:, j, 0:n], ALU.mult, ALU.add)
                dma('act', xT_d[:, :, t0:t0 + n], xb[:, :, 0:n], writes=xkeys(tbi))

            modulate(l, 1, outproj_compute)

            ck('outproj')
            f_tbs = TBS if with_ctx else TBS[:4]
            for ub in ubuf + ubuf2:
                mset(ub[:, 0:1], 0.0, eng='pool')
                mset(ub[:, 2049:2051], 0.0, eng='pool')
                mset(ub[:, 2307:2308], 0.0, eng='pool')
            up_ring = Ring([PB(0), PB(1), PB(2), PB(3)])
            dn_ring = Ring([PB(4), PB(5), PB(6)])
            cring = Ring(ctile)

            def ucol(t0):
                return (1 + t0) if t0 < TL else (2051 + t0 - TL)

            def up_tile(c, gi, cl, t0, n):
                ub = ubuf if c % 2 == 0 else ubuf2
                wup_sb = wup_bufs[gi % 2]
                u0 = ucol(t0)
                for ag in range(2):
                    pb = up_ring.next()[:, 0:n]
                    for kc in range(8):
                        mm(pb, wup_sb[:, kc, ag, cl * 128:(cl + 1) * 128], hT[:, kc, t0:t0 + n], start=(kc == 0), stop=(kc == 7))
                    act(ub[ag][:, u0:u0 + n], pb, AF.Copy)

            silu_pending = []

            def flush_silu():
                while silu_pending:
                    silu_pending.pop(0)()

            def taps_tile(c, cl, t0, n):
                ub = ubuf if c % 2 == 0 else ubuf2
                u0 = ucol(t0)
                outs = []
                for ag in range(2):
                    col = c if ag == 0 else NCH + c
                    ct = cring.next()[:, 0:n]
                    act(ct, ub[ag][:, u0:u0 + n], AF.Identity, scale=pvc('cw1', col), bias=pvc('cb', col))
                    stt(ct, ub[ag][:, u0 - 1:u0 - 1 + n], pvc('cw0', col), ct, ALU.mult, ALU.add)
                    stt(ct, ub[ag][:, u0 + 1:u0 + 1 + n], pvc('cw2', col), ct, ALU.mult, ALU.add)
                    outs.append(ct)
                    if ag == 0:
                        flush_silu()

                def gate(outs=outs, cl=cl, t0=t0, n=n):
                    act(outs[1], outs[1], AF.Silu)
                    tt(actT[:, cl, t0:t0 + n], outs[1], outs[0], ALU.mult, eng='pool')
                silu_pending.append(gate)

            def down_group(gi):
                flush_silu()
                gn = FFG[gi][1]
                wdn_sb = wdn_bufs[gi % 2]
                dtiles = [(tbi, t0, n, j) for tbi, (t0, n) in enumerate(f_tbs) for j in range(8)]
                LA = 4
                xts = {}

                def xload(i):
                    tbi_, t0_, n_, j_ = dtiles[i]
                    xt_ = xring.next()[:, 0:n_]
                    dma('sp', xt_, xT_d[:, j_, t0_:t0_ + n_], reads=[xkey(tbi_), "xf:%d:%d" % (tbi_, j_)])
                    xts[i] = xt_

                for i in range(min(LA, len(dtiles))):
                    xload(i)
                for i, (tbi, t0, n, j) in enumerate(dtiles):
                    if i + LA < len(dtiles):
                        xload(i + LA)
                    w = 0 if tbi < 4 else 1
                    pb = dn_ring.next()[:, 0:n]
                    for cl in range(gn):
                        mm(pb, wdn_sb[:, cl, j * 128:(j + 1) * 128], actT[:, cl, t0:t0 + n], start=(cl == 0), stop=(cl == gn - 1))
                    xt = xts.pop(i)
                    stt(xt, pb, modT[:, l, 5 * 8 + j, w:w + 1], xt, ALU.mult, ALU.add)
                    dma('act', xT_d[:, j, t0:t0 + n], xt, writes=["xf:%d:%d" % (tbi, j)])

            chunk_info = [(g0 + cl, gi, cl, gn) for gi, (g0, gn) in enumerate(FFG) for cl in range(gn)]
            for idx, (c, gi, cl, gn) in enumerate(chunk_info):
                prev = chunk_info[idx - 1] if idx > 0 else None
                for (t0, n) in f_tbs:
                    up_tile(c, gi, cl, t0, n)
                    if prev is not None:
                        taps_tile(prev[0], prev[2], t0, n)
                if cl == 0:
                    if prev is not None:
                        down_group(prev[1])
                    if gi + 1 < len(FFG):
                        load_group(gi + 1)
            last = chunk_info[-1]
            for (t0, n) in f_tbs:
                taps_tile(last[0], last[2], t0, n)
            down_group(last[1])

    except StopBuild:
        pass
    import os as _os
    for tbi, (t0, n) in enumerate(TBS[:4]):
        dma('sp', xblk[:, :, 0:n], xT_d[:, :, t0:t0 + n], reads=xkeys(tbi))
        if _os.environ.get('SKIP_FINAL'):
            dma('sp', y_d[t0:t0 + n, :].rearrange("(a p) d -> p a d", p=128), xblk[:, 0:4, :].rearrange("p a (b c) -> p (a b) c", b=1)[:, :, :].rearrange("p a c -> p a c") if False else xld[:, 0:4, :])
            continue
        for a in range(4):
            for half in range(2):
                pb = PB((a * 2 + half) % int(_os.environ.get("NB", "6")))
                for jj in range(4):
                    j = half * 4 + jj
                    tr(pb[:, jj * 128:(jj + 1) * 128], xblk[:, j, a * 128:(a + 1) * 128], ident)
                if half == 0:
                    act(xld[:, a, 0:512], pb, AF.Copy)
                else:
                    cp(xld[:, a, 512:1024], pb)
        dma('sp', y_d[t0:t0 + n, :].rearrange("(a p) d -> p a d", p=128), xld[:, 0:4, :])
    if dbg:
        for tbi, (t0, n) in enumerate(TBS):
            dma('sp', xblk[:, :, 0:n], xT_d[:, :, t0:t0 + n], reads=xkeys(tbi))
            dma('sp', dbg_d[:, :, t0:t0 + n], xblk[:, :, 0:n])
    S.emit()
    return nc, S


def _rope_tab(rot):
    nf = rot // 4
    inv = np.power(np.float32(10000.0), -np.arange(nf, dtype=np.float32) / np.float32(nf)).astype(np.float32)
    t = np.arange(TL)
    row = (t // 64).astype(np.float32)
    col = (t % 64).astype(np.float32)
    ar = row[:, None] * inv
    ac = col[:, None] * inv
    ang = np.concatenate([ar, ar, ac, ac], axis=-1).astype(np.float32)
    return np.cos(ang).T.astype(np.float32), np.sin(ang).T.astype(np.float32)


def _constants():
    cf = np.zeros((128, 256), np.float32)
    cf[:, 0:128] = np.eye(128, dtype=np.float32)
    cf[:, 128:256] = 1.0
    cb = np.zeros((128, 6, 128), np.float32)
    cb[:, 0, :] = 1.0
    for b in range(4):
        cb[b * 32:(b + 1) * 32, 1, b * 32:(b + 1) * 32] = 1.0
    for b in range(2):
        cb[b * 64:(b + 1) * 64, 2, b * 64:(b + 1) * 64] = 1.0

    def add_rot(Rm, base, n):
        for i in range(n):
            Rm[base + i + n, base + i] = -1.0
            Rm[base + i, base + i + n] = 1.0
            Rm[base + 3 * n + i, base + 2 * n + i] = -1.0
            Rm[base + 2 * n + i, base + 3 * n + i] = 1.0
    add_rot(cb[:, 3, :], 64, 8)
    add_rot(cb[:, 4, :], 0, 8)
    add_rot(cb[:, 4, :], 32, 8)
    add_rot(cb[:, 5, :], 0, 16)
    add_rot(cb[:, 5, :], 64, 16)
    aug = np.zeros((2, 32, T), np.float32)
    for q in range(TL):
        qr = q // 64
        rs = min(max(qr - 4, 0), 24)
        aug[0, :, q] = -BIG
        aug[0, rs:rs + 8, q] = 0.0
        aug[1, qr, q] = 1.0
    tabs = np.zeros((3, 128, 2, TL), np.float32)
    c32, s32 = _rope_tab(32)
    c64, s64 = _rope_tab(64)
    tabs[0, 0:64, 0, :] = 1.0
    tabs[0, 64:96, 0, :] = c32
    tabs[0, 64:96, 1, :] = s32
    tabs[1, 0:32, 0, :] = c32
    tabs[1, 32:64, 0, :] = c32
    tabs[1, 0:32, 1, :] = s32
    tabs[1, 32:64, 1, :] = s32
    tabs[2, 0:64, 0, :] = c64
    tabs[2, 0:64, 1, :] = s64
    tabs[2, 64:128, 0, :] = c64
    tabs[2, 64:128, 1, :] = s64
    return cf, cb.reshape(128, 768), aug, tabs


def _na_index():
    idx = np.full((128, 22, 64), 15 * 31, np.int64)
    for p in range(128):
        half, kc = p // 64, p % 64
        for pos in range(22):
            i = pos - 3 - half
            if i < 0 or i > 14:
                continue
            for qc in range(64):
                ws = min(max(qc - 8, 0), 48)
                if ws <= kc < ws + 16:
                    co = min(max(kc - qc + 15, 0), 30)
                    idx[p, pos, qc] = (14 - i) * 31 + co
    return idx.reshape(128, NAW)


def _tile_rows(v, reps, rows=128):
    out = np.zeros((rows,), np.float32)
    t = np.tile(np.asarray(v, np.float32), reps)
    out[:t.shape[0]] = t
    return out


def _prep_shared(inp):
    cf, cb, aug, tabs = _constants()
    pvA = np.zeros((128, L, NV), np.float32)

    def put(name, l, arr2d):
        c0, w = PV[name]
        pvA[:, l, c0:c0 + w] = arr2d

    for l in range(L):
        put('g_mix', l, inp['g_mix'][l].reshape(8, 128).T)
        put('g_ffn', l, inp['g_ffn'][l].reshape(8, 128).T)
        put('qa_g', l, inp['mla_q_a_g'][l].reshape(2, 128).T)
        put('kva_g', l, inp['mla_kv_a_g'][l].reshape(1, 128).T)
        put('mq_g', l, _tile_rows(inp['mla_q_g'][l], 1)[:, None])
        put('mk_g', l, _tile_rows(inp['mla_k_g'][l], 1)[:, None])
        put('dq_g', l, _tile_rows(inp['diff_q_g'][l], 4)[:, None])
        put('dk_g', l, _tile_rows(inp['diff_k_g'][l], 4)[:, None])
        put('dsub_g', l, _tile_rows(inp['diff_subln_g'][l], 2)[:, None])
        put('nq_g', l, _tile_rows(inp['na_q_g'][l], 2)[:, None])
        put('nk_g', l, _tile_rows(inp['na_k_g'][l], 2)[:, None])
        put('gq_g', l, _tile_rows(inp['gqa_q_g'][l], 2)[:, None])
        put('gk_g', l, _tile_rows(inp['gqa_k_g'][l], 2)[:, None])
        for i in range(3):
            put('cw%d' % i, l, inp['conv_w'][l, i].reshape(44, 128).T)
        put('cb', l, inp['conv_b'][l].reshape(44, 128).T)
        lv = np.concatenate([inp['diff_lq1'][l], inp['diff_lk1'][l], inp['diff_lq2'][l], inp['diff_lk2'][l]]).astype(np.float32)
        put('lvec', l, np.broadcast_to(lv[None, :], (128, 128)))
    idx = _na_index()
    nab = np.zeros((L, 4, 128, NAW), np.float32)
    for l in range(L):
        for h in range(4):
            src = np.concatenate([np.asarray(inp['na_rpb'][l, h], np.float32).ravel(), np.array([-10000.0], np.float32)])
            nab[l, h] = src[idx]
    f = lambda a: np.ascontiguousarray(np.asarray(a, np.float32))
    return {
        "w_mod": f(inp['w_mod']), "b_mod": f(inp['b_mod']), "w_in": f(inp['w_in']), "w_out": f(inp['w_out']),
        "w_uq": f(inp['mla_w_uq']), "w_ukv": f(inp['mla_w_ukv']), "w_up": f(inp['w_up']), "w_down": f(inp['w_down']),
        "pv": np.ascontiguousarray(pvA.reshape(128, L * NV)), "cf32": cf, "cbf": np.ascontiguousarray(cb),
        "aug": aug, "tabs": tabs, "nab": nab,
    }


_CACHE = {}


def kernel(**inp):
    n_layers = inp.pop('_n_layers', L)
    dbg = inp.pop('_dbg', False)
    stop = inp.pop('_stop', None)
    ncores = inp.pop('_ncores', 8)
    key = (n_layers, dbg, stop)
    if key not in _CACHE:
        _CACHE[key] = build_program(n_layers, dbg, stop)[0]
    nc = _CACHE[key]
    shared = _prep_shared(inp)
    x = np.asarray(inp['x'], np.float32)
    ctx = np.asarray(inp['ctx'], np.float32)
    c = np.asarray(inp['c'], np.float32)
    cc = np.asarray(inp['c_ctx'], np.float32)
    in_maps = []
    for b in range(ncores):
        m = dict(shared)
        m["xc"] = np.ascontiguousarray(np.concatenate([x[b], ctx[b]], axis=0))
        cT = np.zeros((128, 8, 2), np.float32)
        cT[:, :, 0] = c[b].reshape(8, 128).T
        cT[:, :, 1] = cc.reshape(8, 128).T
        m["cT"] = np.ascontiguousarray(cT.reshape(128, 16))
        in_maps.append(m)
    res = run_bass_kernel_spmd(nc, in_maps, core_ids=list(range(ncores)))
    out = np.stack([np.asarray(r["y"], np.float32) for r in res.results], axis=0)
    if dbg:
        kernel.dbg = [np.asarray(r["dbg"], np.float32) for r in res.results]
    return out
```
